# Optimizing a Trainium2 kernel written in Bass

```python
import math
import jax, jax.numpy as jnp
from jax import lax
import numpy as np

D_MODEL = 1024
BATCH = 8
SEQ = 2048
DEPTH = 2
DEC_BATCH = 128
DEC_SEQ = 4
PAST_LEN = 16384
PAGE_SIZE = 128

CHUNK = 64
BR_W = D_MODEL // 2
N_BRANCH = 3
M_HEADS = 4
M_HD = BR_W // M_HEADS
M_W = M_HEADS * M_HD
G_HEADS = 4
G_DK = BR_W // (2 * G_HEADS)
G_DV = BR_W // G_HEADS
G_KW = G_HEADS * G_DK
G_VW = G_HEADS * G_DV
G_RANK = 16
G_TAU = 16.0
S_HD = 64
S_HEADS = BR_W // S_HD
S_W = S_HEADS * S_HD
S_GROUPS = 2
S_HG = S_HEADS // S_GROUPS
S_STATE = 64
S_CONV = 4
S_XBC = S_W + 2 * S_GROUPS * S_STATE
P_HEADS = 8
P_NKEYS = 128
P_EXPERTS = P_NKEYS * P_NKEYS
P_QDIM = 256
P_HALF = P_QDIM // 2
P_TOPK = 16
P_BLOCK = 128
DN_ALPHA = (2.0 * DEPTH) ** 0.25
DN_BETA = (8.0 * DEPTH) ** -0.25
EPS = 1e-5

IN_SIZES = (M_W, M_W, M_W, M_W, M_HEADS, M_HEADS,
            G_KW, G_KW, G_VW, G_VW, G_RANK,
            S_W, S_XBC, S_HEADS,
            N_BRANCH * D_MODEL)
IN_W = sum(IN_SIZES)

kernel_name = 'hybrid_mlstm_gla_ssd_peer_step'


def _split_cols(z):
    out, start = [], 0
    for size in IN_SIZES:
        out.append(z[..., start:start + size])
        start += size
    return out


def _chunk_len(L):
    c = min(CHUNK, L)
    while L % c:
        c -= 1
    return c


def _to_chunks(a, c):
    B, L = a.shape[:2]
    a = a.reshape((B, L // c, c) + a.shape[2:])
    return jnp.moveaxis(a, 1, 0)


def _from_chunks(a):
    a = jnp.moveaxis(a, 0, 1)
    return a.reshape((a.shape[0], a.shape[1] * a.shape[2]) + a.shape[3:])


def _rms(x):
    return x * lax.rsqrt(jnp.mean(jnp.square(x), -1, keepdims=True) + EPS)


def _layernorm(x, g, b):
    xf = x.astype(jnp.float32)
    mu = xf.mean(-1, keepdims=True)
    var = jnp.mean(jnp.square(xf - mu), -1, keepdims=True)
    return (xf - mu) * lax.rsqrt(var + EPS) * g.astype(jnp.float32) + b.astype(jnp.float32)


def _mlstm(q, k, v, log_i, log_f, C0, n0, m0):
    c = _chunk_len(q.shape[1])
    causal = jnp.tril(jnp.ones((c, c), bool))

    def step(carry, xs):
        C, n, m = carry
        qc, kc, vc, lic, lfc = xs
        b = jnp.cumsum(lfc, axis=1)
        a = b + m[:, None, :]
        d = b[:, :, None, :] - b[:, None, :, :] + lic[:, None, :, :]
        d = jnp.where(causal[None, :, :, None], d, -jnp.inf)
        m_t = jnp.maximum(a, jnp.max(d, axis=2))
        s = jnp.einsum('bthd,bshd->btsh', qc, kc) * jnp.exp(d - m_t[:, :, None, :])
        e_in = jnp.exp(a - m_t)
        num = (jnp.einsum('btsh,bshd->bthd', s, vc)
               + e_in[..., None] * jnp.einsum('bhvd,bthd->bthv', C, qc))
        den = jnp.sum(s, axis=2) + e_in * jnp.einsum('bhd,bthd->bth', n, qc)
        h = num / jnp.maximum(jnp.abs(den), jnp.exp(-m_t))[..., None]
        b_last = b[:, -1, :]
        g = b_last[:, None, :] - b + lic
        m_new = jnp.maximum(b_last + m, jnp.max(g, axis=1))
        e_c = jnp.exp(b_last + m - m_new)
        wg = jnp.exp(g - m_new[:, None, :])
        C_new = e_c[..., None, None] * C + jnp.einsum('bsh,bshv,bshd->bhvd', wg, vc, kc)
        n_new = e_c[..., None] * n + jnp.einsum('bsh,bshd->bhd', wg, kc)
        return (C_new, n_new, m_new), h

    xs = tuple(_to_chunks(t, c) for t in (q, k, v, log_i, log_f))
    (C, n, m), h = lax.scan(step, (C0, n0, m0), xs)
    return _from_chunks(h), C, n, m


def _gla(q, k, v, log_a, S0):
    c = _chunk_len(q.shape[1])
    causal = jnp.tril(jnp.ones((c, c), bool))

    def step(S, xs):
        qc, kc, vc, lac = xs
        lam = jnp.cumsum(lac, axis=1)
        diff = lam[:, :, None] - lam[:, None, :]
        decay = jnp.exp(jnp.where(causal[None, :, :, None, None], diff, -jnp.inf))
        att = jnp.einsum('bthk,bshk,btshk->btsh', qc, kc, decay)
        o = (jnp.einsum('btsh,bshv->bthv', att, vc)
             + jnp.einsum('bthk,bhkv->bthv', qc * jnp.exp(lam), S))
        lam_last = lam[:, -1]
        S_new = (jnp.exp(lam_last)[..., None] * S
                 + jnp.einsum('bshk,bshv->bhkv', kc * jnp.exp(lam_last[:, None] - lam), vc))
        return S_new, o

    xs = tuple(_to_chunks(t, c) for t in (q, k, v, log_a))
    S, o = lax.scan(step, S0, xs)
    return _from_chunks(o), S


def _ssd(x, dt, Bm, Cm, A, h0):
    c = _chunk_len(x.shape[1])
    causal = jnp.tril(jnp.ones((c, c), bool))

    def step(h, xs):
        xc, dtc, bc, cc = xs
        lam = jnp.cumsum(dtc * A, axis=1)
        diff = lam[:, :, None] - lam[:, None, :]
        decay = jnp.exp(jnp.where(causal[None, :, :, None, None], diff, -jnp.inf))
        cb = jnp.einsum('btgn,bsgn->btsg', cc, bc)
        w = cb[..., None] * decay * dtc[:, None]
        y = (jnp.einsum('btsgh,bsghp->btghp', w, xc)
             + jnp.einsum('btgn,bghpn->btghp', cc, h) * jnp.exp(lam)[..., None])
        lam_last = lam[:, -1]
        ws = jnp.exp(lam_last[:, None] - lam) * dtc
        h_new = (jnp.exp(lam_last)[..., None, None] * h
                 + jnp.einsum('bsgh,bsghp,bsgn->bghpn', ws, xc, bc))
        return h_new, y

    xs = tuple(_to_chunks(t, c) for t in (x, dt, Bm, Cm))
    h, y = lax.scan(step, h0, xs)
    return _from_chunks(y), h


def _peer(x, p_wq, p_keys, p_u, p_v):
    f32 = jnp.float32
    B, L, D = x.shape
    T = B * L
    xf = x.reshape(T, D)
    q = jnp.matmul(xf, p_wq).astype(f32).reshape(T, P_HEADS, 2, P_HALF)
    sc = jnp.einsum('thjc,hjkc->thjk', q, p_keys.astype(f32))
    s, i = lax.top_k(sc, P_TOPK)
    cand = s[:, :, 0, :, None] + s[:, :, 1, None, :]
    cidx = i[:, :, 0, :, None] * P_NKEYS + i[:, :, 1, None, :]
    top_s, pos = lax.top_k(cand.reshape(T, P_HEADS, P_TOPK * P_TOPK), P_TOPK)
    eidx = jnp.take_along_axis(cidx.reshape(T, P_HEADS, P_TOPK * P_TOPK), pos, axis=-1)
    g = jax.nn.softmax(top_s, axis=-1)
    nb = -(-T // P_BLOCK)
    pad = nb * P_BLOCK - T
    xb = jnp.pad(xf, ((0, pad), (0, 0))).reshape(nb, P_BLOCK, D)
    eb = jnp.pad(eidx.reshape(T, P_HEADS * P_TOPK), ((0, pad), (0, 0))).reshape(nb, P_BLOCK, -1)
    gb = jnp.pad(g.reshape(T, P_HEADS * P_TOPK), ((0, pad), (0, 0))).reshape(nb, P_BLOCK, -1)

    def block(args):
        xt, et, gt = args
        u = jnp.take(p_u, et, axis=0)
        act = jax.nn.gelu(jnp.einsum('tkd,td->tk', u, xt).astype(f32), approximate=False) * gt
        vv = jnp.take(p_v, et, axis=0)
        return jnp.einsum('tk,tkd->td', act, vv.astype(f32))

    out = lax.map(block, (xb, eb, gb))
    return out.reshape(nb * P_BLOCK, D)[:T].reshape(B, L, D)


def _layer(x, C0, n0, m0, S0, h0, buf0,
           w_in, m_i_bias, m_f_bias, m_norm, g_a_up, g_a_bias, g_norm,
           s_conv_w, s_conv_b, s_dt_bias, s_A_log, s_D, s_norm,
           w_branch, w_out, ln1_g, ln1_b, p_wq, p_keys, p_u, p_v, ln2_g, ln2_b):
    f32 = jnp.float32
    B, L, _ = x.shape
    z = jnp.matmul(x, w_in).astype(f32)
    (mq, mk, mv, mo, mi, mf, gq, gk, gv, gr, ga, sz, sxbc, sdt, gate) = _split_cols(z)

    q = mq.reshape(B, L, M_HEADS, M_HD) * (M_HD ** -0.5)
    k = mk.reshape(B, L, M_HEADS, M_HD)
    v = mv.reshape(B, L, M_HEADS, M_HD)
    log_i = mi + m_i_bias.astype(f32)
    log_f = jax.nn.log_sigmoid(mf + m_f_bias.astype(f32))
    hm, C1, n1, m1 = _mlstm(q, k, v, log_i, log_f,
                            C0.astype(f32), n0.astype(f32), m0.astype(f32))
    mu = hm.mean(-1, keepdims=True)
    hm = (hm - mu) * lax.rsqrt(jnp.mean(jnp.square(hm - mu), -1, keepdims=True) + EPS)
    hm = hm.reshape(B, L, M_W) * m_norm.astype(f32) * jax.nn.sigmoid(mo)

    q = gq.reshape(B, L, G_HEADS, G_DK) * (G_DK ** -0.5)
    k = gk.reshape(B, L, G_HEADS, G_DK)
    v = gv.reshape(B, L, G_HEADS, G_DV)
    log_a = jax.nn.log_sigmoid(jnp.matmul(ga, g_a_up.astype(f32)) + g_a_bias.astype(f32)) / G_TAU
    hg, S1 = _gla(q, k, v, log_a.reshape(B, L, G_HEADS, G_DK), S0.astype(f32))
    hg = _rms(hg).reshape(B, L, G_VW) * g_norm.astype(f32) * jax.nn.silu(gr)

    xp = jnp.concatenate([buf0.astype(f32), sxbc], axis=1)
    cw = s_conv_w.astype(f32)
    conv = s_conv_b.astype(f32) + sum(cw[j] * xp[:, j:j + L] for j in range(S_CONV))
    buf1 = xp[:, L:]
    act = jax.nn.silu(conv)
    xs = act[..., :S_W].reshape(B, L, S_GROUPS, S_HG, S_HD)
    Bm = act[..., S_W:S_W + S_GROUPS * S_STATE].reshape(B, L, S_GROUPS, S_STATE)
    Cm = act[..., S_W + S_GROUPS * S_STATE:].reshape(B, L, S_GROUPS, S_STATE)
    dt = jax.nn.softplus(sdt + s_dt_bias.astype(f32)).reshape(B, L, S_GROUPS, S_HG)
    A = -jnp.exp(s_A_log.astype(f32)).reshape(S_GROUPS, S_HG)
    ys, h1 = _ssd(xs, dt, Bm, Cm, A,
                  h0.astype(f32).reshape(B, S_GROUPS, S_HG, S_HD, S_STATE))
    ys = ys + s_D.astype(f32).reshape(S_GROUPS, S_HG, 1) * xs
    ys = ys.reshape(B, L, S_W) * jax.nn.silu(sz)
    hs = _rms(ys.reshape(B, L, S_GROUPS, S_W // S_GROUPS)).reshape(B, L, S_W) * s_norm.astype(f32)
    h1 = h1.reshape(B, S_HEADS, S_HD, S_STATE)

    br = jnp.stack([hm, hg, hs], axis=2)
    proj = jnp.einsum('blnc,ncd->blnd', br, w_branch.astype(f32))
    gates = jax.nn.sigmoid(gate.reshape(B, L, N_BRANCH, D_MODEL))
    mixed = jnp.sum(gates * proj, axis=2)
    y = jnp.matmul(mixed, w_out.astype(f32))
    h = _layernorm(DN_ALPHA * x.astype(f32) + y, ln1_g, ln1_b)

    h = _layernorm(DN_ALPHA * h + _peer(h, p_wq, p_keys, p_u, p_v), ln2_g, ln2_b)
    return (h.astype(x.dtype), C1.astype(C0.dtype), n1.astype(n0.dtype), m1.astype(m0.dtype),
            S1.astype(S0.dtype), h1.astype(h0.dtype), buf1.astype(buf0.dtype))


def _zero_states(b, dtype):
    return (jnp.zeros((b, M_HEADS, M_HD, M_HD), dtype),
            jnp.zeros((b, M_HEADS, M_HD), dtype),
            jnp.zeros((b, M_HEADS), dtype),
            jnp.zeros((b, G_HEADS, G_DK, G_DV), dtype),
            jnp.zeros((b, S_HEADS, S_HD, S_STATE), dtype),
            jnp.zeros((b, S_CONV - 1, S_XBC), dtype))


def setup_inputs(seed: int = 0) -> dict:
    key = jax.random.key(seed)
    ks = iter(jax.random.split(key, 48))
    nrm = lambda shape, s=1.0: jax.random.normal(next(ks), shape, jnp.float32) * s
    uni = lambda shape, lo, hi: jax.random.uniform(next(ks), shape, jnp.float32, lo, hi)
    Dd = DEPTH
    dt0 = jnp.exp(uni((Dd, S_HEADS), math.log(1e-3), math.log(1e-1)))
    return {
        'x_prompt': nrm((BATCH, SEQ, D_MODEL)),
        'x_sample': nrm((DEC_BATCH, DEC_SEQ, D_MODEL)),
        'state_mlstm_C': nrm((Dd, DEC_BATCH, M_HEADS, M_HD, M_HD), 0.1),
        'state_mlstm_n': nrm((Dd, DEC_BATCH, M_HEADS, M_HD), 0.5),
        'state_mlstm_m': uni((Dd, DEC_BATCH, M_HEADS), 0.0, 3.0),
        'state_gla_S': nrm((Dd, DEC_BATCH, G_HEADS, G_DK, G_DV), 0.1),
        'state_ssm_h': nrm((Dd, DEC_BATCH, S_HEADS, S_HD, S_STATE), 0.1),
        'state_conv': nrm((Dd, DEC_BATCH, S_CONV - 1, S_XBC)),
        'w_in': nrm((Dd, D_MODEL, IN_W), D_MODEL ** -0.5),
        'm_i_bias': nrm((Dd, M_HEADS), 0.1),
        'm_f_bias': uni((Dd, M_HEADS), 3.0, 6.0),
        'm_norm': 1.0 + nrm((Dd, M_W), 0.02),
        'g_a_up': nrm((Dd, G_RANK, G_KW), G_RANK ** -0.5),
        'g_a_bias': nrm((Dd, G_KW), 0.1),
        'g_norm': 1.0 + nrm((Dd, G_VW), 0.02),
        's_conv_w': nrm((Dd, S_CONV, S_XBC), S_CONV ** -0.5),
        's_conv_b': nrm((Dd, S_XBC), 0.01),
        's_dt_bias': dt0 + jnp.log(-jnp.expm1(-dt0)),
        's_A_log': jnp.log(uni((Dd, S_HEADS), 1.0, 16.0)),
        's_D': 1.0 + nrm((Dd, S_HEADS), 0.1),
        's_norm': 1.0 + nrm((Dd, S_W), 0.02),
        'w_branch': nrm((Dd, N_BRANCH, BR_W, D_MODEL), DN_BETA * BR_W ** -0.5),
        'w_out': nrm((Dd, D_MODEL, D_MODEL), DN_BETA * D_MODEL ** -0.5),
        'ln1_g': 1.0 + nrm((Dd, D_MODEL), 0.01),
        'ln1_b': nrm((Dd, D_MODEL), 0.01),
        'p_wq': nrm((Dd, D_MODEL, P_HEADS * P_QDIM), D_MODEL ** -0.5),
        'p_keys': nrm((Dd, P_HEADS, 2, P_NKEYS, P_HALF), P_HALF ** -0.5),
        'p_u': nrm((Dd, P_EXPERTS, D_MODEL), D_MODEL ** -0.5),
        'p_v': nrm((Dd, P_EXPERTS, D_MODEL), DN_BETA * (P_HEADS * P_TOPK) ** -0.5),
        'ln2_g': 1.0 + nrm((Dd, D_MODEL), 0.01),
        'ln2_b': nrm((Dd, D_MODEL), 0.01),
    }


def reference(x_prompt, x_sample, state_mlstm_C, state_mlstm_n, state_mlstm_m,
              state_gla_S, state_ssm_h, state_conv,
              w_in, m_i_bias, m_f_bias, m_norm, g_a_up, g_a_bias, g_norm,
              s_conv_w, s_conv_b, s_dt_bias, s_A_log, s_D, s_norm,
              w_branch, w_out, ln1_g, ln1_b, p_wq, p_keys, p_u, p_v, ln2_g, ln2_b):
    hp, hs = x_prompt, x_sample
    new_p = [[] for _ in range(6)]
    new_s = [[] for _ in range(6)]
    for l in range(DEPTH):
        w = (w_in[l], m_i_bias[l], m_f_bias[l], m_norm[l], g_a_up[l], g_a_bias[l], g_norm[l],
             s_conv_w[l], s_conv_b[l], s_dt_bias[l], s_A_log[l], s_D[l], s_norm[l],
             w_branch[l], w_out[l], ln1_g[l], ln1_b[l], p_wq[l], p_keys[l], p_u[l], p_v[l],
             ln2_g[l], ln2_b[l])
        outp = _layer(hp, *_zero_states(hp.shape[0], state_mlstm_C.dtype), *w)
        outs = _layer(hs, state_mlstm_C[l], state_mlstm_n[l], state_mlstm_m[l],
                      state_gla_S[l], state_ssm_h[l], state_conv[l], *w)
        hp, hs = outp[0], outs[0]
        for j in range(6):
            new_p[j].append(outp[j + 1])
            new_s[j].append(outs[j + 1])
    P = [jnp.stack(a, axis=0) for a in new_p]
    S = [jnp.stack(a, axis=0) for a in new_s]
    return (hp, hs, P[0], P[1], P[2], P[3], P[4], P[5], S[0], S[1], S[2], S[3], S[4], S[5])
```

```python
import contextlib
import os
import numpy as np
import concourse.bass as bass
import concourse.mybir as mybir
from concourse.bass_utils import run_bass_kernel_spmd

F32 = mybir.dt.float32
BF16 = mybir.dt.bfloat16
I32 = mybir.dt.int32
U32 = mybir.dt.uint32
AF = mybir.ActivationFunctionType
ALU = mybir.AluOpType
AX = mybir.AxisListType

NCORES = 8
D = 1024
SEQ = 2048
NSEG = 16
LS = 4
TC = SEQ + NSEG * LS
INW = 7968
EPS = 1e-5
DN_ALPHA = 4.0 ** 0.25
DEPTH = 2
GST = int(os.environ.get('GST', '99'))
LASTM = int(os.environ.get('LASTM', '15'))

RP = {}
_o = 0
for _n, _s in [("mib", 4), ("mfb", 4), ("mnorm", 512), ("gab", 256), ("gnorm", 512), ("dtb", 8),
               ("alog", 8), ("sD", 8), ("snorm", 512)]:
    RP[_n] = (_o, _o + _s)
    _o += _s
RPW = _o

CO = {}
_o = 0
for _n, _s in [("ident", 128), ("leP", 128), ("gtP", 128), ("leS", 64), ("gtS", 64), ("ones", 128),
               ("segind", 16), ("iota16", 16), ("halfm", 2)]:
    CO[_n] = (_o, _o + _s)
    _o += _s
COW = _o


class Res:
    __slots__ = ("name", "w", "r")

    def __init__(self, name="r"):
        self.name = name
        self.w = None
        self.r = []


class _Rec:
    def __getattr__(self, name):
        def f(*a, **k):
            self.call = (name, a, k)
            return self
        return f


def _replay(fn):
    rec = _Rec()
    fn(rec)
    name, a, k = rec.call
    def run(eng):
        try:
            return getattr(eng, name)(*a, **k)
        except Exception:
            print("FAILED OP", name, {kk: (getattr(vv, "shape", vv), getattr(getattr(vv, "tensor", None), "name", None)) for kk, vv in k.items()})
            raise
    return run


class Sched:
    ENG = ("pe", "dve", "act", "pool", "sp")

    def __init__(self, nc, stack, n_dma_sems=32):
        self.nc = nc
        self.sem = {e: stack.enter_context(nc.semaphore("s_" + e)) for e in self.ENG}
        self.cnt = {e: 0 for e in self.ENG}
        self.prog = {e: [] for e in self.ENG}
        self.seen = {e: {} for e in self.ENG}
        self.dsem = [stack.enter_context(nc.semaphore("d%d" % i)) for i in range(n_dma_sems)]
        self.dval = [0] * n_dma_sems
        self.dnext = 0
        self.n_hw = n_dma_sems - 8
        self.dnext_sw = 0
        self.n_ins = 0

    def _wait(self, e, ev):
        if ev is None:
            return
        kind, key, val = ev
        if kind == "e" and key == e and e == "pe":
            return
        k = (kind, key)
        if self.seen[e].get(k, 0) >= val:
            return
        self.seen[e][k] = val
        sem = self.sem[key] if kind == "e" else self.dsem[key]
        self.prog[e].append(lambda eng, sem=sem, val=val: eng.wait_ge(sem, val))

    def _deps(self, e, reads, writes):
        for r in reads:
            self._wait(e, r.w)
        for w in writes:
            self._wait(e, w.w)
            for ev in w.r:
                self._wait(e, ev)

    def _commit(self, ev, reads, writes):
        for r in reads:
            r.r.append(ev)
            if len(r.r) > 16:
                d = {}
                for x in r.r:
                    k = (x[0], x[1])
                    if k not in d or d[k][2] < x[2]:
                        d[k] = x
                r.r = list(d.values())
        for w in writes:
            w.w = ev
            w.r = []

    def op(self, e, fn, reads=(), writes=()):
        self._deps(e, reads, writes)
        self.cnt[e] += 1
        ev = ("e", e, self.cnt[e])
        sem = self.sem[e]
        fn = _replay(fn)
        self.prog[e].append(lambda eng, fn=fn, sem=sem: fn(eng).then_inc(sem, 1))
        self._commit(ev, reads, writes)
        self.n_ins += 1
        return ev

    def dma(self, q, out, in_, reads=(), writes=(), fn=None, slow=False):
        self._deps(q, reads, writes)
        if q == "pool":
            i = self.n_hw + self.dnext_sw
            self.dnext_sw = (self.dnext_sw + 1) % 8
        else:
            i = self.dnext
            self.dnext = (self.dnext + 1) % self.n_hw
        if self.dval[i] > 0:
            self._wait(q, ("d", i, self.dval[i]))
        self.dval[i] += 16
        ev = ("d", i, self.dval[i])
        sem = self.dsem[i]
        if fn is None:
            if slow:
                fn = lambda eng, out=out, in_=in_: eng.dma_start(out=out, in_=in_, allow_slow_non_contiguous=True)
            else:
                fn = lambda eng, out=out, in_=in_: eng.dma_start(out=out, in_=in_)
        fn = _replay(fn)
        self.prog[q].append(lambda eng, fn=fn, sem=sem: fn(eng).then_inc(sem, 16))
        self._commit(ev, reads, writes)
        self.n_ins += 1
        return ev

    def finish(self):
        for i in range(len(self.dsem)):
            if self.dval[i] > 0:
                self._wait("sp", ("d", i, self.dval[i]))
        for e in ("pe", "dve", "act", "pool"):
            if self.cnt[e] > 0:
                self._wait("sp", ("e", e, self.cnt[e]))
        nc = self.nc
        progs = self.prog
        with nc.Block() as block:
            @block.tensor
            def _(eng):
                for f in progs["pe"]:
                    f(eng)

            @block.vector
            def _(eng):
                for f in progs["dve"]:
                    f(eng)

            @block.scalar
            def _(eng):
                for f in progs["act"]:
                    f(eng)

            @block.gpsimd
            def _(eng):
                for f in progs["pool"]:
                    f(eng)

            @block.sync
            def _(eng):
                for f in progs["sp"]:
                    f(eng)


def build(tiles=None, nlayers=DEPTH, do_peer=True, dbg=(), nexp=16384):
    nc = bass.Bass("TRN2", target_bir_lowering=False)
    di = lambda name, shape, dt=F32: nc.dram_tensor(name, list(shape), dt, kind="ExternalInput").ap()
    do = lambda name, shape, dt=F32: nc.dram_tensor(name, list(shape), dt, kind="ExternalOutput").ap()
    x_d = di("x", [TC, D])
    sC_d = di("sC", [DEPTH, NSEG, 4, 128, 128]); sn_d = di("sn", [DEPTH, NSEG, 4, 128]); sm_d = di("sm", [DEPTH, NSEG, 4])
    sS_d = di("sS", [DEPTH, NSEG, 256, 128]); sh_d = di("sh", [DEPTH, NSEG, 512, 64]); scv_d = di("scv", [DEPTH, NSEG * 3, 768])
    win_d = di("w_in", [DEPTH, D, INW]); wbr_d = di("w_br", [DEPTH, 1536, D]); wout_d = di("w_out", [DEPTH, D, D])
    wq_d = di("p_wq", [DEPTH, D, 2048]); keys_d = di("keysT", [DEPTH, 128, 16, 128])
    pu_d = [di("p_u%d" % l, [nexp, D]) for l in range(DEPTH)]; pv_d = [di("p_v%d" % l, [nexp, D]) for l in range(DEPTH)]
    rowp_d = di("rowp", [DEPTH, 128, RPW]); convp_d = di("convp", [DEPTH, 128, 6, 5]); gaup_d = di("gaup", [DEPTH, 16, 256])
    cst_d = di("cst", [128, COW]); segm_d = di("segm", [128, 1024]); lnp_d = di("lnp", [DEPTH, 2, 128, 2048])
    y_d = do("y", [TC, D])
    pC_d = do("pC", [DEPTH, 4, 128, 128]); pn_d = do("pn", [DEPTH, 4, 128]); pm_d = do("pm", [DEPTH, 4])
    pS_d = do("pS", [DEPTH, 256, 128]); ph_d = do("ph", [DEPTH, 512, 64]); pcv_d = do("pcv", [DEPTH, 3, 768])
    oC_d = do("oC", [DEPTH, NSEG, 4, 128, 128]); on_d = do("on", [DEPTH, NSEG, 4, 128]); om_d = do("om", [DEPTH, NSEG, 4])
    oS_d = do("oS", [DEPTH, NSEG, 256, 128]); oh_d = do("oh", [DEPTH, NSEG, 512, 64]); ocv_d = do("ocv", [DEPTH, NSEG * 3, 768])
    dbg_d = {k: do("dbg_" + k, shp) for k, shp in dbg}

    with contextlib.ExitStack() as st:
        S = Sched(nc, st)
        res = {}

        def sb(name, shape, dt=F32):
            t = st.enter_context(nc.sbuf_tensor("sb_" + name, list(shape), dt))
            res[name] = Res(name)
            return t

        def R(*names):
            return [res[n] for n in names]

        PS = [st.enter_context(nc.psum_tensor("ps%d" % i, [128, 512], F32)) for i in range(8)]
        PSR = [Res("ps%d" % i) for i in range(8)]
        psn = [0]

        def bank():
            i = psn[0]
            psn[0] = (i + 1) % 5
            return PS[i], PSR[i]

        cst = sb("cst", [128, COW])
        S.dma("sp", cst[:], cst_d, writes=R("cst"))
        C = lambda n: cst[:, CO[n][0]:CO[n][1]]
        cb = sb("cb", [128, 128 + 128 + 64 + 1024], BF16)
        S.op("dve", lambda e: e.tensor_copy(out=cb[:, 0:128], in_=C("leP")), R("cst"), R("cb"))
        S.op("dve", lambda e: e.tensor_copy(out=cb[:, 128:256], in_=C("ident")), R("cst"), R("cb"))
        S.op("dve", lambda e: e.tensor_copy(out=cb[:, 256:320], in_=C("leS")), R("cst"), R("cb"))
        ident = C("ident")
        ones = C("ones")

        X = sb("X", [128, D])
        xT = sb("xT", [128, 8, 128], BF16)
        rowp = sb("rowp", [128, RPW])
        lnrow = sb("lnrow", [128, 2048])
        convp = sb("convp", [128, 6, 5])
        gaupf = sb("gaupf", [16, 256]); gaupb = sb("gaupb", [16, 256], BF16)
        arow = sb("arow", [128, 8])
        wst = [sb("wst%d" % i, [128, 8, 528]) for i in range(2)]
        wbf = [sb("wbf%d" % i, [128, 8, 528], BF16) for i in range(2)]
        wsn = [0]
        mqT = sb("mqT", [128, 4, 128], BF16); mkT = sb("mkT", [128, 4, 128], BF16)
        mk_tm = sb("mk_tm", [128, 512], BF16); mv_tm = sb("mv_tm", [128, 512])
        mo_tm = sb("mo_tm", [128, 512]); mif = sb("mif", [128, 8])
        gqT = sb("gqT", [128, 2, 128]); gkT = sb("gkT", [128, 2, 128])
        gk_tm = sb("gk_tm", [128, 256]); gv_tm = sb("gv_tm", [128, 512], BF16); gr_tm = sb("gr_tm", [128, 512])
        gaT = sb("gaT", [16, 128], BF16)
        sz_tm = sb("sz_tm", [128, 512]); sdt = sb("sdt", [128, 8])
        xp = sb("xp", [128, 6, 3 + 128])
        br = sb("br", [128, 1536])
        brT = sb("brT", [128, 12, 128], BF16)
        CT = [sb("CT%d" % l, [128, 4, 129]) for l in range(DEPTH)]
        CTb = sb("CTb", [128, 4, 129], BF16)
        mrun = [sb("mrun%d" % l, [4, 1]) for l in range(DEPTH)]
        Sst = [sb("Sst%d" % l, [128, 2, 128]) for l in range(DEPTH)]
        Sb = sb("Sb", [128, 2, 128], BF16)
        hT = [sb("hT%d" % l, [128, 256]) for l in range(DEPTH)]
        hTb = sb("hTb", [128, 256], BF16)
        cvh = [sb("cvh%d" % l, [128, 6, 3]) for l in range(DEPTH)]
        for l in range(DEPTH):
            S.op("pool", lambda e, l=l: e.memset(CT[l][:], 0.0), (), R("CT%d" % l))
            S.op("pool", lambda e, l=l: e.memset(mrun[l][:], 0.0), (), R("mrun%d" % l))
            S.op("pool", lambda e, l=l: e.memset(Sst[l][:], 0.0), (), R("Sst%d" % l))
            S.op("pool", lambda e, l=l: e.memset(hT[l][:], 0.0), (), R("hT%d" % l))
            S.op("pool", lambda e, l=l: e.memset(cvh[l][:], 0.0), (), R("cvh%d" % l))
        t8 = sb("t8", [128, 64]); u8 = sb("u8", [128, 8]); w4 = sb("w4", [128, 4]); eb4 = sb("eb4", [128, 4])
        eB4 = sb("eB4", [128, 4]); uT = sb("uT", [4, 256]); sm4 = sb("sm4", [4, 64])
        SM = sb("SM", [128, 4, 128], BF16); VW = sb("VW", [128, 4, 129], BF16)
        hm = sb("hm", [128, 512]); hm2 = sb("hm2", [128, 512]); st4 = sb("st4", [128, 16])
        CTs = sb("CTs", [128, 129])
        big = sb("big", [128, 1024]); big2 = sb("big2", [128, 1024])
        bigb = sb("bigb", [128, 1024], BF16)
        S.dma("sp", big[:, :], segm_d, writes=R("big"))
        S.op("dve", lambda e: e.tensor_copy(out=cb[:, 320:1344], in_=big[:, :]), R("big"), R("cb"))
        nl = sb("nl", [128, 256]); eT = sb("eT", [128, 2, 2, 128]); qtT = sb("qtT", [128, 2, 128], BF16)
        ktT = sb("ktT", [128, 2, 128], BF16); kt_tm = sb("kt_tm", [128, 256], BF16)
        AM = sb("AM", [128, 4, 128], BF16); St = sb("St", [128, 128])
        actT = sb("actT", [128, 6, 128]); BCb = sb("BCb", [128, 2, 128], BF16)
        xs_tm = sb("xs_tm", [128, 512]); Bm_tm = sb("Bm_tm", [128, 128], BF16)
        dts = sb("dts", [128, 40]); cbm = sb("cbm", [128, 2, 128])
        xsb = sb("xsb", [128, 512], BF16); xsw = sb("xsw", [128, 512], BF16)
        eL = sb("eL", [128, 8])
        mixT = sb("mixT", [128, 8, 128], BF16); gsig = sb("gsig", [128, 128])
        lnt = sb("lnt", [128, 8])
        qTb = sb("qTb", [128, 16, 128], BF16)
        V16 = sb("V16", [128, 16, 16]); I16 = sb("I16", [128, 16, 16], U32); I16f = sb("I16f", [128, 16, 16])
        wk = sb("wk", [128, 256])
        TV = sb("TV", [128, 8, 16]); TPu = sb("TPu", [128, 8, 16], U32)
        pk = sb("pk", [128, 8, 128])
        eidx = sb("eidx", [128, 128], I32)
        thr16 = sb("thr16", [128, 16])
        S.op("dve", lambda e: e.tensor_scalar(out=thr16[:], in0=C("iota16"), scalar1=16.0, scalar2=16.0, op0=ALU.mult, op1=ALU.add), R("cst"), R("thr16"))
        sampbuf = {}
        sampres = {}
        SA = sb("SA", [128, 4096])
        SBb = sb("SBb", [128, 12288], BF16)
        sampbuf["n0T"] = sb("sp_n0T", [128, 4, 16]); sampres["n0T"] = res["sp_n0T"]
        SC = SA[:, 0:2048]
        res["SC"] = Res("SC")
        SBf = SBb[:, :].bitcast(F32)
        GU = [SBf[:, q * 1024:(q + 1) * 1024] for q in range(4)]
        pacc = SBf[:, 4096:5120]
        for q in range(4):
            res["GU%d" % q] = Res("GU%d" % q)
        res["acc"] = Res("acc")
        fsc = sb("fsc", [1, 4])
        peer_alias = ["SC", "GU0", "GU1", "GU2", "GU3", "acc"]

        def fence():
            S.op("dve", lambda e: e.memset(fsc[0:1, 0:1], 0.0), (), R("SA", "SBb", "fsc", *peer_alias))

        sampbuf["C0t"] = SA[:, 0:2048].rearrange("p (s d) -> p s d", d=128); sampres["C0t"] = res["SA"]
        sampbuf["Co"] = SA[:, 2048:4096].rearrange("p (s d) -> p s d", d=128); sampres["Co"] = res["SA"]
        sampbuf["mqS"] = SBb[:, 0:4096].rearrange("p (h t) -> p h t", t=1024); sampres["mqS"] = res["SBb"]
        sampbuf["CTbs"] = SBb[:, 4096:4096 + 2064].rearrange("p (s c) -> p s c", c=129); sampres["CTbs"] = res["SBb"]
        sampbuf["VWs"] = SBb[:, 6160:6160 + 2064].rearrange("p (s c) -> p s c", c=129); sampres["VWs"] = res["SBb"]

        def evac(eng, out, in_, rd, wr, scale=None):
            if eng == "act":
                if scale is None:
                    S.op("act", lambda e: e.copy(out=out, in_=in_), rd, wr)
                else:
                    S.op("act", lambda e: e.mul(out=out, in_=in_, mul=scale), rd, wr)
            else:
                if scale is None:
                    S.op("dve", lambda e: e.tensor_copy(out=out, in_=in_), rd, wr)
                else:
                    S.op("dve", lambda e: e.tensor_scalar(out=out, in0=in_, scalar1=scale, scalar2=None, op0=ALU.mult), rd, wr)

        def load_w(src_ap, width):
            i = wsn[0]
            wsn[0] = 1 - i
            S.dma("sp", wst[i][:, :, 0:width], src_ap.rearrange("(k p) n -> p k n", p=128), writes=R("wst%d" % i))
            S.op("pool", lambda e: e.tensor_copy(out=wbf[i][:, :, 0:width], in_=wst[i][:, :, 0:width]),
                 R("wst%d" % i), R("wbf%d" % i))
            return wbf[i], res["wbf%d" % i]

        def load_w2(src_ap, kch, width):
            i = wsn[0]
            wsn[0] = 1 - i
            fv = wst[i][:].rearrange("p k n -> p (k n)")[:, 0:kch * width].rearrange("p (k n) -> p k n", n=width)
            bv = wbf[i][:].rearrange("p k n -> p (k n)")[:, 0:kch * width].rearrange("p (k n) -> p k n", n=width)
            S.dma("sp", fv, src_ap.rearrange("(k p) n -> p k n", p=128), writes=R("wst%d" % i))
            S.op("pool", lambda e: e.tensor_copy(out=bv, in_=fv), R("wst%d" % i), R("wbf%d" % i))
            return bv, res["wbf%d" % i]

        def fm_piece(wb, wr_, a, b, N, kch=8, src=None, srcres=None):
            src = xT if src is None else src
            srcres = res["xT"] if srcres is None else srcres
            p, pr = bank()
            for k in range(kch):
                S.op("pe", lambda e, k=k: e.matmul(p[0:b - a, 0:N], lhsT=wb[:, k, a:b], rhs=src[:, k, 0:N],
                                                    start=(k == 0), stop=(k == kch - 1)), [wr_, srcres], [pr])
            return p, pr

        def tm_piece(wb, wr_, a, b, P, kch=8, src=None, srcres=None):
            src = xT if src is None else src
            srcres = res["xT"] if srcres is None else srcres
            p, pr = bank()
            for k in range(kch):
                S.op("pe", lambda e, k=k: e.matmul(p[0:P, 0:b - a], lhsT=src[:, k, 0:P], rhs=wb[:, k, a:b],
                                                    start=(k == 0), stop=(k == kch - 1)), [wr_, srcres], [pr])
            return p, pr

        def transpose_to(dst_fn, src, srcres, P, nch, dstres, eng="act"):
            for c0 in range(0, nch, 4):
                c1 = min(nch, c0 + 4)
                p, pr = bank()
                for c in range(c0, c1):
                    S.op("pe", lambda e, c=c: e.transpose(out=p[:, (c - c0) * 128:(c - c0) * 128 + P],
                                                          in_=src[0:P, c * 128:(c + 1) * 128], identity=ident[0:P, 0:P]),
                         [srcres, res["cst"]], [pr])
                pv = p[:, 0:(c1 - c0) * 128].rearrange("p (c t) -> p c t", t=128)[:, :, 0:P]
                evac(eng, dst_fn(c0, c1), pv, [pr], [dstres])

        def act_fn(out, in_, func, rd, wr, **kw):
            S.op("act", lambda e: e.activation(out=out, in_=in_, func=func, **kw), rd, wr)

        def softplus_neg(out, in_, rd, wr, tmp, tmpres, sign=-1.0):
            act_fn(tmp, in_, AF.Exp, rd, [tmpres], scale=sign)
            act_fn(out, tmp, AF.Ln, [tmpres], wr, bias=1.0)

        def rstd_from_ssq(out, ssq, n, rd, wr):
            act_fn(out, ssq, AF.Sqrt, rd, wr, scale=1.0 / n, bias=EPS)
            S.op("dve", lambda e: e.reciprocal(out=out, in_=out), wr, wr)

        def layernorm_tm(P, src, srcres, g_ap, b_ap, out, outres):
            S.op("dve", lambda e: e.tensor_reduce(out=lnt[0:P, 0:1], in_=src[0:P, :], axis=AX.X, op=ALU.add), [srcres], R("lnt"))
            S.op("dve", lambda e: e.tensor_scalar(out=lnt[0:P, 1:2], in0=lnt[0:P, 0:1], scalar1=-1.0 / D, scalar2=None, op0=ALU.mult), R("lnt"), R("lnt"))
            S.op("dve", lambda e: e.tensor_scalar(out=src[0:P, :], in0=src[0:P, :], scalar1=lnt[0:P, 1:2], scalar2=None, op0=ALU.add), [srcres] + R("lnt"), [srcres])
            S.op("dve", lambda e: e.tensor_tensor(out=big[0:P, :], in0=src[0:P, :], in1=src[0:P, :], op=ALU.mult), [srcres], R("big"))
            S.op("dve", lambda e: e.tensor_reduce(out=lnt[0:P, 2:3], in_=big[0:P, :], axis=AX.X, op=ALU.add), R("big"), R("lnt"))
            rstd_from_ssq(lnt[0:P, 3:4], lnt[0:P, 2:3], D, R("lnt"), R("lnt"))
            S.op("dve", lambda e: e.scalar_tensor_tensor(out=big[0:P, :], in0=src[0:P, :], scalar=lnt[0:P, 3:4], in1=g_ap, op0=ALU.mult, op1=ALU.mult),
                 [srcres] + R("lnt", "lnrow"), R("big"))
            S.op("dve", lambda e: e.tensor_tensor(out=out, in0=big[0:P, :], in1=b_ap, op=ALU.add), R("big", "lnrow"), [outres])

        dbg_i = [0]

        def dbg_out(name, ap, rd):
            if name in dbg_d:
                S.dma("sp", dbg_d[name][dbg_i[0], 0:ap.shape[0]], ap, reads=rd)

        all_tiles = [("p", i) for i in range(SEQ // 128)] + [("s", 0)]
        if tiles is not None:
            all_tiles = tiles
        last_prompt = SEQ // 128 - 1
        for kind, ti in all_tiles:
            samp = kind == "s"
            P = 64 if samp else 128
            t0 = SEQ if samp else ti * 128
            N = P
            S.dma("sp", X[0:P, :], x_d[t0:t0 + P, :], writes=R("X"))
            for l in range(nlayers):
                le_f = C("leS")[0:P, :] if samp else C("leP")
                gt_f = C("gtS")[0:P, :] if samp else C("gtP")
                le_b = cb[0:P, 256:320] if samp else cb[:, 0:128]
                S.dma("sp", rowp[:], rowp_d[l], writes=R("rowp"))
                S.dma("sp", convp[:], convp_d[l], writes=R("convp"))
                S.dma("sp", gaupf[:], gaup_d[l], writes=R("gaupf"))
                S.op("dve", lambda e: e.tensor_copy(out=gaupb[:], in_=gaupf[:]), R("gaupf"), R("gaupb"))
                rp = lambda n: rowp[0:P, RP[n][0]:RP[n][1]]
                act_fn(arow[:], rowp[:, RP["alog"][0]:RP["alog"][1]], AF.Exp, R("rowp"), R("arow"))
                transpose_to(lambda c0, c1: xT[:, c0:c1, 0:P], X, res["X"], P, 8, res["xT"])
                W = lambda a, b: win_d[l, :, a:b]
                wb, wr_ = load_w(W(0, 512), 512)
                for h in range(4):
                    p, pr = fm_piece(wb, wr_, h * 128, (h + 1) * 128, N)
                    evac("act", mqT[:, h, 0:N], p[:, 0:N], [pr], R("mqT"), scale=128 ** -0.5)
                wb, wr_ = load_w(W(512, 1024), 512)
                for h in range(4):
                    p, pr = fm_piece(wb, wr_, h * 128, (h + 1) * 128, N)
                    evac("act", mkT[:, h, 0:N], p[:, 0:N], [pr], R("mkT"))
                p, pr = tm_piece(wb, wr_, 0, 512, P)
                evac("dve", mk_tm[0:P, :], p[0:P, :], [pr], R("mk_tm"))
                wb, wr_ = load_w(W(1024, 1536), 512)
                p, pr = tm_piece(wb, wr_, 0, 512, P)
                evac("act", mv_tm[0:P, :], p[0:P, :], [pr], R("mv_tm"))
                wb, wr_ = load_w(W(1536, 2048), 512)
                p, pr = tm_piece(wb, wr_, 0, 512, P)
                evac("dve", mo_tm[0:P, :], p[0:P, :], [pr], R("mo_tm"))
                wb, wr_ = load_w(W(2048, 2568), 520)
                p, pr = tm_piece(wb, wr_, 0, 8, P)
                evac("act", mif[0:P, :], p[0:P, 0:8], [pr], R("mif"))
                for j in range(2):
                    p, pr = fm_piece(wb, wr_, 8 + j * 128, 8 + (j + 1) * 128, N)
                    evac("act", gqT[:, j, 0:N], p[:, 0:N], [pr], R("gqT"), scale=64 ** -0.5)
                    p, pr = fm_piece(wb, wr_, 264 + j * 128, 264 + (j + 1) * 128, N)
                    evac("dve", gkT[:, j, 0:N], p[:, 0:N], [pr], R("gkT"))
                p, pr = tm_piece(wb, wr_, 264, 520, P)
                evac("act", gk_tm[0:P, :], p[0:P, 0:256], [pr], R("gk_tm"))
                wb, wr_ = load_w(W(2568, 3080), 512)
                p, pr = tm_piece(wb, wr_, 0, 512, P)
                evac("dve", gv_tm[0:P, :], p[0:P, :], [pr], R("gv_tm"))
                wb, wr_ = load_w(W(3080, 3608), 528)
                p, pr = tm_piece(wb, wr_, 0, 512, P)
                evac("act", gr_tm[0:P, :], p[0:P, :], [pr], R("gr_tm"))
                p, pr = fm_piece(wb, wr_, 512, 528, N)
                evac("dve", gaT[0:16, 0:N], p[0:16, 0:N], [pr], R("gaT"))
                wb, wr_ = load_w(W(3608, 4120), 512)
                p, pr = tm_piece(wb, wr_, 0, 512, P)
                evac("act", sz_tm[0:P, :], p[0:P, :], [pr], R("sz_tm"))
                if samp:
                    cv0 = big2[0:48, 0:768]
                    S.dma("sp", cv0, scv_d[l], writes=R("big2"))
                    for c0 in (0, 4):
                        p, pr = bank()
                        for c in range(c0, min(6, c0 + 4)):
                            S.op("pe", lambda e, c=c: e.transpose(out=p[:, (c - c0) * 128:(c - c0) * 128 + 48], in_=cv0[:, c * 128:(c + 1) * 128],
                                                                  identity=ident[0:48, 0:48]), R("big2", "cst"), [pr])
                        ncn = min(6, c0 + 4) - c0
                        pv = p[:, 0:ncn * 128].rearrange("p (c t) -> p c t", t=128)[:, :, 0:48].rearrange("p c (s j) -> p c s j", j=3)
                        dst = xp[:, c0:c0 + ncn, 0:NSEG * 7].rearrange("p c (s j) -> p c s j", j=7)[:, :, :, 0:3]
                        evac("dve", dst, pv, [pr], R("xp"))
                else:
                    S.op("dve", lambda e: e.tensor_copy(out=xp[:, :, 0:3], in_=cvh[l][:]), R("cvh%d" % l), R("xp"))
                wb, wr_ = load_w(W(4120, 4632), 512)
                wb2, wr2 = load_w(W(4632, 4896), 264)
                for c in range(6):
                    if c < 4:
                        p, pr = fm_piece(wb, wr_, c * 128, (c + 1) * 128, N)
                    else:
                        p, pr = fm_piece(wb2, wr2, (c - 4) * 128, (c - 3) * 128, N)
                    if samp:
                        dst = xp[:, c, 0:NSEG * 7].rearrange("p (s j) -> p s j", j=7)[:, :, 3:7]
                        evac("act", dst, p[:, 0:N].rearrange("p (s j) -> p s j", j=4), [pr], R("xp"))
                    else:
                        evac("act", xp[:, c, 3:3 + N], p[:, 0:N], [pr], R("xp"))
                p, pr = tm_piece(wb2, wr2, 256, 264, P)
                evac("dve", sdt[0:P, :], p[0:P, 0:8], [pr], R("sdt"))
                if not samp:
                    S.op("dve", lambda e: e.tensor_copy(out=cvh[l][:], in_=xp[:, :, N:N + 3]), R("xp"), R("cvh%d" % l))
                if samp or (ti == last_prompt and (LASTM & 1)):
                    nr = 48 if samp else 3
                    cvt = big[:, 0:6 * 48].rearrange("p (c r) -> p c r", r=48)
                    if samp:
                        srcv = xp[:, :, 0:NSEG * 7].rearrange("p c (s j) -> p c s j", j=7)[:, :, :, 4:7]
                        S.op("dve", lambda e: e.tensor_copy(out=cvt.rearrange("p c (s j) -> p c s j", j=3), in_=srcv), R("xp"), R("big"))
                    else:
                        S.op("dve", lambda e: e.tensor_copy(out=cvt[:, :, 0:3], in_=xp[:, :, N:N + 3]), R("xp"), R("big"))
                    for c0 in (0, 4):
                        p, pr = bank()
                        ncn = min(6, c0 + 4) - c0
                        for c in range(c0, c0 + ncn):
                            S.op("pe", lambda e, c=c: e.transpose(out=p[0:nr, (c - c0) * 128:(c - c0 + 1) * 128], in_=cvt[:, c, 0:nr], identity=ident),
                                 R("big", "cst"), [pr])
                        evac("act", big2[0:nr, c0 * 128:(c0 + ncn) * 128], p[0:nr, 0:ncn * 128], [pr], R("big2"))
                    S.dma("sp", (ocv_d[l] if samp else pcv_d[l]), big2[0:nr, 0:768], reads=R("big2"))

                nseg = NSEG if samp else 1
                L = P // nseg
                S.op("dve", lambda e: e.tensor_tensor(out=t8[0:P, 0:4], in0=mif[0:P, 0:4], in1=rp("mib"), op=ALU.add), R("mif", "rowp"), R("t8"))
                S.op("dve", lambda e: e.tensor_tensor(out=t8[0:P, 4:8], in0=mif[0:P, 4:8], in1=rp("mfb"), op=ALU.add), R("mif", "rowp"), R("t8"))
                softplus_neg(t8[0:P, 12:16], t8[0:P, 4:8], R("t8"), R("t8"), t8[0:P, 8:12], res["t8"])
                p, pr = bank()
                S.op("pe", lambda e: e.matmul(p[0:P, 0:4], lhsT=le_f, rhs=t8[0:P, 12:16], start=True, stop=True), R("cst", "t8"), [pr])
                S.op("pe", lambda e: e.matmul(p[:, 8:12], lhsT=ones[0:P, :], rhs=t8[0:P, 12:16], start=True, stop=True), R("cst", "t8"), [pr])
                S.op("dve", lambda e: e.tensor_tensor(out=u8[0:P, 0:4], in0=p[0:P, 0:4], in1=t8[0:P, 0:4], op=ALU.add), [pr] + R("t8"), R("u8"))
                S.op("dve", lambda e: e.tensor_copy(out=u8[0:P, 4:8], in_=p[0:P, 0:4]), [pr], R("u8"))
                act_fn(w4[0:P, :], u8[0:P, 0:4], AF.Exp, R("u8"), R("w4"))
                act_fn(eb4[0:P, :], p[0:P, 0:4], AF.Exp, [pr], R("eb4"), scale=-1.0)
                act_fn(eB4[:, :], p[:, 8:12], AF.Exp, [pr], R("eB4"), scale=-1.0)
                p2, pr2 = bank()
                S.op("pe", lambda e: e.transpose(out=p2[0:4, 0:P], in_=u8[0:P, 0:4], identity=ident[0:P, 0:P]), R("u8", "cst"), [pr2])
                S.op("pe", lambda e: e.transpose(out=p2[0:4, 128:128 + P], in_=u8[0:P, 4:8], identity=ident[0:P, 0:P]), R("u8", "cst"), [pr2])
                S.op("dve", lambda e: e.tensor_copy(out=uT[:, :], in_=p2[0:4, 0:256]), [pr2], R("uT"))
                S.op("dve", lambda e: e.tensor_reduce(out=sm4[:, 0:nseg], in_=uT[:, 0:P].rearrange("h (s j) -> h s j", j=L), axis=AX.X, op=ALU.max), R("uT"), R("sm4"))
                blast = uT[:, 128:128 + P].rearrange("h (s j) -> h s j", j=L)[:, :, L - 1]
                if not samp:
                    S.op("dve", lambda e: e.tensor_tensor(out=mrun[l][:], in0=mrun[l][:], in1=sm4[:, 0:1], op=ALU.max), R("mrun%d" % l, "sm4"), R("mrun%d" % l))
                    S.op("dve", lambda e: e.tensor_tensor(out=mrun[l][:], in0=mrun[l][:], in1=blast, op=ALU.subtract), R("mrun%d" % l, "uT"), R("mrun%d" % l))
                else:
                    S.dma("sp", sm4[:, 16:32], sm_d[l].rearrange("s h -> h s"), writes=R("sm4"), slow=True)
                    S.op("dve", lambda e: e.tensor_tensor(out=sm4[:, 32:48], in0=sm4[:, 0:16], in1=sm4[:, 16:32], op=ALU.max), R("sm4"), R("sm4"))
                    S.op("dve", lambda e: e.tensor_tensor(out=sm4[:, 48:64], in0=sm4[:, 32:48], in1=blast, op=ALU.subtract), R("sm4", "uT"), R("sm4"))
                    S.dma("sp", om_d[l].rearrange("s h -> h s"), sm4[:, 48:64], reads=R("sm4"), slow=True)
                    act_fn(sm4[:, 0:16], sm4[:, 32:48], AF.Exp, R("sm4"), R("sm4"), scale=-1.0)
                    act_fn(sm4[:, 48:64], sm4[:, 16:32], AF.Exp, R("sm4"), R("sm4"))
                    for q, (a0) in enumerate((0, 48)):
                        S.op("dve", lambda e, q=q, a0=a0: e.tensor_tensor(
                            out=hm2[0:4, q * 64:(q + 1) * 64].rearrange("p (h s) -> p h s", s=16),
                            in0=sm4[:, a0:a0 + 16].unsqueeze(1).to_broadcast([4, 4, 16]),
                            in1=ident[0:4, 0:4].unsqueeze(2).to_broadcast([4, 4, 16]), op=ALU.mult), R("sm4", "cst"), R("hm2"))
                    p3, pr3 = bank()
                    S.op("pe", lambda e: e.matmul(p3[:, 0:128], lhsT=ones[0:4, :], rhs=hm2[0:4, 0:128], start=True, stop=True), R("cst", "hm2"), [pr3])
                    S.op("dve", lambda e: e.tensor_copy(out=big2[:, 0:128], in_=p3[:, 0:128]), [pr3], R("big2"))
                    S.op("dve", lambda e: e.tensor_tensor(out=big2[:, 128:192], in0=big2[:, 0:64], in1=big2[:, 64:128], op=ALU.mult), R("big2"), R("big2"))
                for h in range(4):
                    p, pr = bank()
                    S.op("pe", lambda e, h=h: e.matmul(p[0:P, 0:P], lhsT=mkT[:, h, 0:P], rhs=mqT[:, h, 0:P], start=True, stop=True), R("mkT", "mqT"), [pr])
                    S.op("dve", lambda e, h=h: e.tensor_tensor(out=SM[0:P, h, 0:P], in0=p[0:P, 0:P], in1=le_f[:, 0:P], op=ALU.mult), [pr] + R("cst"), R("SM"))
                S.op("dve", lambda e: e.tensor_tensor(out=VW[0:P, :, 0:128], in0=mv_tm[0:P, :].rearrange("p (h v) -> p h v", v=128),
                                                      in1=w4[0:P, :].unsqueeze(2).to_broadcast([P, 4, 128]), op=ALU.mult), R("mv_tm", "w4"), R("VW"))
                S.op("dve", lambda e: e.tensor_copy(out=VW[0:P, :, 128:129], in_=w4[0:P, :].unsqueeze(2)), R("w4"), R("VW"))
                if not samp:
                    S.op("act", lambda e: e.copy(out=CTb[:], in_=CT[l][:]), R("CT%d" % l), R("CTb"))
                else:
                    pass
                if samp:
                    CTbs = sampbuf["CTbs"]
                    C0t = sampbuf["C0t"]
                    n0T = sampbuf["n0T"]
                    for hh in range(4):
                        S.dma("sp", n0T[:, hh, :], sn_d[l, :, hh, :].rearrange("s d -> d s"), writes=[sampres["n0T"]], slow=True)
                    mqS = sampbuf["mqS"]
                    S.op("dve", lambda e: e.tensor_tensor(out=mqS[:].rearrange("p h (s t) -> p h s t", t=64),
                                                          in0=mqT[:, :, 0:64].unsqueeze(2).to_broadcast([128, 4, 16, 64]),
                                                          in1=cb[:, 320:1344].rearrange("p (s t) -> p s t", t=64).unsqueeze(1).to_broadcast([128, 4, 16, 64]),
                                                          op=ALU.mult), R("mqT", "cb"), [sampres["mqS"]])
                nd_banks = [(PS[6], PSR[6]), (PS[7], PSR[7])]
                for h in range(4):
                    ndp, ndr = nd_banks[h // 2]
                    nd = ndp[0:P, (h % 2) * 129:(h % 2) * 129 + 129]
                    S.op("pe", lambda e, h=h, nd=nd: e.matmul(nd, lhsT=SM[0:P, h, 0:P], rhs=VW[0:P, h, :], start=True, stop=False), R("SM", "VW"), [ndr])
                    if not samp:
                        S.op("pe", lambda e, h=h, nd=nd: e.matmul(nd, lhsT=mqT[:, h, 0:P], rhs=CTb[:, h, :], start=False, stop=True), R("mqT", "CTb"), [ndr])
                    else:
                        S.dma("sp", C0t[:], sC_d[l, :, h].rearrange("s v d -> v s d"), writes=[sampres["C0t"]])
                        for s in range(NSEG):
                            p, pr = bank()
                            S.op("pe", lambda e, s=s: e.transpose(out=p[:, 0:128], in_=C0t[:, s, :], identity=ident), [sampres["C0t"]] + R("cst"), [pr])
                            S.op("dve", lambda e, s=s, h=h: e.tensor_scalar(out=CTbs[:, s, 0:128], in0=p[:, 0:128], scalar1=big2[:, 64 + h * 16 + s:64 + h * 16 + s + 1],
                                                                           scalar2=None, op0=ALU.mult), [pr] + R("big2"), [sampres["CTbs"]])
                        S.op("dve", lambda e, h=h: e.tensor_tensor(out=CTbs[:, :, 128], in0=n0T[:, h, :], in1=big2[:, 64 + h * 16:64 + h * 16 + 16], op=ALU.mult),
                             [sampres["n0T"]] + R("big2"), [sampres["CTbs"]])
                        for s in range(NSEG):
                            S.op("pe", lambda e, s=s, h=h, nd=nd: e.matmul(nd, lhsT=mqS[:, h, s * 64:(s + 1) * 64], rhs=CTbs[:, s, :], start=False, stop=(s == NSEG - 1)),
                                 [sampres["mqS"], sampres["CTbs"]], [ndr])
                        VWs = sampbuf["VWs"]
                        S.op("dve", lambda e, h=h: e.tensor_tensor(out=VWs[0:64, :, :], in0=VW[0:64, h, :].unsqueeze(1).to_broadcast([64, 16, 129]),
                                                                   in1=C("segind")[0:64, :].unsqueeze(2).to_broadcast([64, 16, 129]), op=ALU.mult),
                             R("VW", "cst"), [sampres["VWs"]])
                        Co = sampbuf["Co"]
                        for s in range(NSEG):
                            p, pr = bank()
                            S.op("pe", lambda e, s=s, h=h: e.matmul(p[:, 0:128], lhsT=VWs[0:64, s, 0:128], rhs=mk_tm[0:64, h * 128:(h + 1) * 128], start=True, stop=True),
                                 [sampres["VWs"]] + R("mk_tm"), [pr])
                            col = h * 16 + s
                            S.op("dve", lambda e, s=s, col=col: e.tensor_scalar(out=St[:, :], in0=C0t[:, s, :], scalar1=big2[:, 128 + col:129 + col], scalar2=None, op0=ALU.mult),
                                 [sampres["C0t"]] + R("big2"), R("St"))
                            S.op("dve", lambda e, s=s, col=col: e.scalar_tensor_tensor(out=Co[:, s, :], in0=p[:, 0:128], scalar=big2[:, col:col + 1], in1=St[:, :],
                                                                                      op0=ALU.mult, op1=ALU.add), [pr] + R("big2", "St"), [sampres["Co"]])
                        S.dma("sp", oC_d[l, :, h].rearrange("s v d -> v s d"), Co[:], reads=[sampres["Co"]])
                        S.op("dve", lambda e, h=h: e.tensor_scalar(out=bigb[0:64, 0:16], in0=C("segind")[0:64, :], scalar1=w4[0:64, h:h + 1], scalar2=None, op0=ALU.mult),
                             R("cst", "w4"), R("bigb"))
                        p, pr = bank()
                        S.op("pe", lambda e, h=h: e.matmul(p[:, 0:16], lhsT=mk_tm[0:64, h * 128:(h + 1) * 128], rhs=bigb[0:64, 0:16], start=True, stop=True), R("mk_tm", "bigb"), [pr])
                        S.op("dve", lambda e, h=h: e.tensor_tensor(out=hm2[:, 256 + h * 16:256 + (h + 1) * 16], in0=n0T[:, h, :], in1=big2[:, 128 + h * 16:128 + (h + 1) * 16], op=ALU.mult),
                             [sampres["n0T"]] + R("big2"), R("hm2"))
                        S.op("dve", lambda e, h=h: e.tensor_tensor(out=hm2[:, 320 + h * 16:320 + (h + 1) * 16], in0=p[:, 0:16], in1=big2[:, h * 16:(h + 1) * 16], op=ALU.mult),
                             [pr] + R("big2"), R("hm2"))
                if samp:
                    S.op("dve", lambda e: e.tensor_tensor(out=hm2[:, 256:320], in0=hm2[:, 256:320], in1=hm2[:, 320:384], op=ALU.add), R("hm2"), R("hm2"))
                    for hh in range(4):
                        S.dma("sp", on_d[l, :, hh, :].rearrange("s d -> d s"), hm2[:, 256 + hh * 16:256 + (hh + 1) * 16], reads=R("hm2"), slow=True)
                for j in range(2):
                    ndp, ndr = nd_banks[j]
                    S.op("dve", lambda e, j=j, ndp=ndp: e.tensor_tensor(out=st4[0:P, 2 * j:2 * j + 2], in0=ndp[0:P, 0:258].rearrange("p (h c) -> p h c", c=129)[:, :, 128],
                                                                        in1=eb4[0:P, 2 * j:2 * j + 2], op=ALU.mult), [ndr] + R("eb4"), R("st4"))
                S.op("dve", lambda e: e.tensor_scalar(out=st4[0:P, 4:8], in0=st4[0:P, 0:4], scalar1=-1.0, scalar2=1.0, op0=ALU.mult, op1=ALU.max), R("st4"), R("st4"))
                S.op("dve", lambda e: e.tensor_tensor(out=st4[0:P, 4:8], in0=st4[0:P, 4:8], in1=st4[0:P, 0:4], op=ALU.max), R("st4"), R("st4"))
                S.op("dve", lambda e: e.reciprocal(out=st4[0:P, 8:12], in_=st4[0:P, 4:8]), R("st4"), R("st4"))
                S.op("dve", lambda e: e.tensor_tensor(out=st4[0:P, 12:16], in0=st4[0:P, 8:12], in1=eb4[0:P, :], op=ALU.mult), R("st4", "eb4"), R("st4"))
                for h in range(4):
                    ndp, ndr = nd_banks[h // 2]
                    S.op("dve", lambda e, h=h, ndp=ndp: e.tensor_scalar(out=hm[0:P, h * 128:(h + 1) * 128], in0=ndp[0:P, (h % 2) * 129:(h % 2) * 129 + 128],
                                                                        scalar1=st4[0:P, 12 + h:13 + h], scalar2=None, op0=ALU.mult), [ndr] + R("st4"), R("hm"))
                hm3 = hm[0:P, :].rearrange("p (h v) -> p h v", v=128)
                S.op("dve", lambda e: e.tensor_reduce(out=st4[0:P, 0:4], in_=hm3, axis=AX.X, op=ALU.add), R("hm"), R("st4"))
                S.op("dve", lambda e: e.tensor_scalar(out=st4[0:P, 0:4], in0=st4[0:P, 0:4], scalar1=1.0 / 128, scalar2=None, op0=ALU.mult), R("st4"), R("st4"))
                S.op("dve", lambda e: e.tensor_tensor(out=hm3, in0=hm3, in1=st4[0:P, 0:4].unsqueeze(2).to_broadcast([P, 4, 128]), op=ALU.subtract), R("hm", "st4"), R("hm"))
                S.op("dve", lambda e: e.tensor_tensor(out=hm2[0:P, 0:512], in0=hm[0:P, :], in1=hm[0:P, :], op=ALU.mult), R("hm"), R("hm2"))
                S.op("dve", lambda e: e.tensor_reduce(out=st4[0:P, 4:8], in_=hm2[0:P, 0:512].rearrange("p (h v) -> p h v", v=128), axis=AX.X, op=ALU.add), R("hm2"), R("st4"))
                rstd_from_ssq(st4[0:P, 8:12], st4[0:P, 4:8], 128, R("st4"), R("st4"))
                S.op("dve", lambda e: e.tensor_tensor(out=hm3, in0=hm3, in1=st4[0:P, 8:12].unsqueeze(2).to_broadcast([P, 4, 128]), op=ALU.mult), R("hm", "st4"), R("hm"))
                S.op("dve", lambda e: e.tensor_tensor(out=hm[0:P, :], in0=hm[0:P, :], in1=rp("mnorm"), op=ALU.mult), R("hm", "rowp"), R("hm"))
                act_fn(hm2[0:P, 0:512], mo_tm[0:P, :], AF.Sigmoid, R("mo_tm"), R("hm2"))
                S.op("dve", lambda e: e.tensor_tensor(out=br[0:P, 0:512], in0=hm[0:P, :], in1=hm2[0:P, 0:512], op=ALU.mult), R("hm", "hm2"), R("br"))
                if not samp:
                    for h in range(4):
                        p, pr = bank()
                        S.op("pe", lambda e, h=h: e.matmul(p[:, 0:129], lhsT=mk_tm[0:P, h * 128:(h + 1) * 128], rhs=VW[0:P, h, :], start=True, stop=True), R("mk_tm", "VW"), [pr])
                        S.op("dve", lambda e, h=h: e.tensor_scalar(out=CTs[:, :], in0=CT[l][:, h, :], scalar1=eB4[:, h:h + 1], scalar2=None, op0=ALU.mult),
                             R("CT%d" % l, "eB4"), R("CTs"))
                        S.op("dve", lambda e, h=h: e.scalar_tensor_tensor(out=CT[l][:, h, :], in0=p[:, 0:129], scalar=eB4[:, h:h + 1], in1=CTs[:, :], op0=ALU.mult, op1=ALU.add),
                             [pr] + R("eB4", "CTs"), R("CT%d" % l))
                    if ti == last_prompt and (LASTM & 2):
                        act_fn(sm4[:, 0:1], mrun[l][:], AF.Exp, R("mrun%d" % l), R("sm4"), scale=-1.0)
                        S.op("dve", lambda e: e.tensor_tensor(out=hm2[0:4, 0:4], in0=sm4[:, 0:1].to_broadcast([4, 4]), in1=ident[0:4, 0:4], op=ALU.mult), R("sm4", "cst"), R("hm2"))
                        p3, pr3 = bank()
                        S.op("pe", lambda e: e.matmul(p3[:, 0:4], lhsT=ones[0:4, :], rhs=hm2[0:4, 0:4], start=True, stop=True), R("cst", "hm2"), [pr3])
                        S.op("dve", lambda e: e.tensor_copy(out=st4[:, 0:4], in_=p3[:, 0:4]), [pr3], R("st4"))
                        for h in range(4):
                            p, pr = bank()
                            S.op("pe", lambda e, h=h: e.transpose(out=p[:, 0:128], in_=CT[l][:, h, 0:128], identity=ident), R("CT%d" % l, "cst"), [pr])
                            S.op("dve", lambda e, h=h: e.tensor_scalar(out=hm2[:, h * 128:(h + 1) * 128], in0=p[:, 0:128], scalar1=st4[:, h:h + 1], scalar2=None, op0=ALU.mult),
                                 [pr] + R("st4"), R("hm2"))
                        S.dma("sp", pC_d[l].rearrange("h v d -> v h d"), hm2[:, 0:512].rearrange("p (h d) -> p h d", d=128), reads=R("hm2"))
                        S.op("dve", lambda e: e.tensor_tensor(out=st4[:, 4:8], in0=CT[l][:, :, 128], in1=st4[:, 0:4], op=ALU.mult), R("CT%d" % l, "st4"), R("st4"))
                        S.dma("sp", pn_d[l].rearrange("h d -> d h"), st4[:, 4:8], reads=R("st4"), slow=True)
                        S.dma("sp", pm_d[l].rearrange("(h o) -> h o", o=1), mrun[l][:], reads=R("mrun%d" % l), slow=True)


                p, pr = bank()
                S.op("pe", lambda e: e.matmul(p[0:P, 0:256], lhsT=gaT[0:16, 0:P], rhs=gaupb[0:16, :], start=True, stop=True), R("gaT", "gaupb"), [pr])
                S.op("dve", lambda e: e.tensor_tensor(out=big[0:P, 0:256], in0=p[0:P, 0:256], in1=rp("gab"), op=ALU.add), [pr] + R("rowp"), R("big"))
                softplus_neg(nl[0:P, :], big[0:P, 0:256], R("big"), R("nl"), big[0:P, 256:512], res["big"])
                S.op("dve", lambda e: e.tensor_scalar(out=nl[0:P, :], in0=nl[0:P, :], scalar1=1.0 / 16, scalar2=None, op0=ALU.mult), R("nl"), R("nl"))
                p1, pr1 = bank()
                S.op("pe", lambda e: e.matmul(p1[0:P, 0:256], lhsT=le_f, rhs=nl[0:P, :], start=True, stop=True), R("cst", "nl"), [pr1])
                p2, pr2 = bank()
                for j in range(2):
                    S.op("pe", lambda e, j=j: e.matmul(p2[:, j * 128:j * 128 + P], lhsT=nl[0:P, j * 128:(j + 1) * 128], rhs=le_f[:, 0:P], start=True, stop=True), R("cst", "nl"), [pr2])
                for j in range(2):
                    act_fn(eT[:, 0, j, 0:P], p2[:, j * 128:j * 128 + P], AF.Exp, [pr2], R("eT"))
                    act_fn(eT[:, 1, j, 0:P], p2[:, j * 128:j * 128 + P], AF.Exp, [pr2], R("eT"), scale=-1.0)
                S.op("dve", lambda e: e.tensor_tensor(out=qtT[:, :, 0:P], in0=gqT[:, :, 0:P], in1=eT[:, 1, :, 0:P], op=ALU.mult), R("gqT", "eT"), R("qtT"))
                S.op("dve", lambda e: e.tensor_tensor(out=ktT[:, :, 0:P], in0=gkT[:, :, 0:P], in1=eT[:, 0, :, 0:P], op=ALU.mult), R("gkT", "eT"), R("ktT"))
                act_fn(big[0:P, 256:512], p1[0:P, 0:256], AF.Exp, [pr1], R("big"))
                S.op("dve", lambda e: e.tensor_tensor(out=kt_tm[0:P, :], in0=gk_tm[0:P, :], in1=big[0:P, 256:512], op=ALU.mult), R("gk_tm", "big"), R("kt_tm"))
                for h in range(4 if GST >= 2 else 0):
                    j, off = h // 2, (h % 2) * 64
                    p, pr = bank()
                    S.op("pe", lambda e, j=j, off=off: e.matmul(p[0:P, 0:P], lhsT=ktT[off:off + 64, j, 0:P], rhs=qtT[off:off + 64, j, 0:P], start=True, stop=True), R("ktT", "qtT"), [pr])
                    S.op("dve", lambda e, h=h: e.tensor_tensor(out=AM[0:P, h, 0:P], in0=p[0:P, 0:P], in1=le_f[:, 0:P], op=ALU.mult), [pr] + R("cst"), R("AM"))
                po, por = PS[6], PSR[6]
                if not samp:
                    S.op("act", lambda e: e.copy(out=Sb[:], in_=Sst[l][:]), R("Sst%d" % l), R("Sb"))
                else:
                    S0t = SA[:, :].rearrange("p (s j v) -> p s j v", j=2, v=128)
                    S0b = SBb[:, 0:4096].rearrange("p (s j v) -> p s j v", j=2, v=128)
                    qtS = SBb[:, 4096:6144].rearrange("p (j s t) -> p j s t", s=16, t=64)
                    qtS2 = SBb[:, 6144:8192].rearrange("p (j s t) -> p j s t", s=16, t=64)
                    ktS = SBb[0:64, 8192:12288].rearrange("p (s c) -> p s c", c=256)
                    for j in range(2):
                        S.dma("sp", S0t[:, :, j, :], sS_d[l, :, j * 128:(j + 1) * 128, :].rearrange("s p v -> p s v"), writes=R("SA"))
                    S.op("act", lambda e: e.copy(out=SBb[:, 0:4096], in_=SA[:, :]), R("SA"), R("SBb"))
                    S.op("dve", lambda e: e.tensor_tensor(out=qtS, in0=qtT[:, :, 0:64].unsqueeze(2).to_broadcast([128, 2, 16, 64]),
                                                          in1=cb[:, 320:1344].rearrange("p (s t) -> p s t", t=64).unsqueeze(1).to_broadcast([128, 2, 16, 64]), op=ALU.mult),
                         R("qtT", "cb"), R("SBb"))
                    S.op("dve", lambda e: e.tensor_scalar(out=SBb[:, 6144:8192], in0=SBb[:, 4096:6144], scalar1=C("halfm")[:, 1:2], scalar2=None, op0=ALU.mult), R("SBb", "cst"), R("SBb"))
                    S.op("dve", lambda e: e.tensor_scalar(out=SBb[:, 4096:6144], in0=SBb[:, 4096:6144], scalar1=C("halfm")[:, 0:1], scalar2=None, op0=ALU.mult), R("SBb", "cst"), R("SBb"))
                    S.op("dve", lambda e: e.tensor_tensor(out=ktS, in0=kt_tm[0:64, :].unsqueeze(1).to_broadcast([64, 16, 256]),
                                                          in1=C("segind")[0:64, :].unsqueeze(2).to_broadcast([64, 16, 256]), op=ALU.mult), R("kt_tm", "cst"), R("SBb"))
                for h in range(4 if GST >= 3 else 0):
                    j, off = h // 2, (h % 2) * 64
                    og = po[0:P, h * 128:(h + 1) * 128]
                    S.op("pe", lambda e, h=h, og=og: e.matmul(og, lhsT=AM[0:P, h, 0:P], rhs=gv_tm[0:P, h * 128:(h + 1) * 128], start=True, stop=False), R("AM", "gv_tm"), [por])
                    if not samp:
                        S.op("pe", lambda e, j=j, off=off, og=og: e.matmul(og, lhsT=qtT[off:off + 64, j, 0:P], rhs=Sb[off:off + 64, j, :], start=False, stop=True), R("qtT", "Sb"), [por])
                    else:
                        for s_ in range(NSEG):
                            qq = qtS if off == 0 else qtS2
                            S.op("pe", lambda e, j=j, qq=qq, og=og, s_=s_: e.matmul(og, lhsT=qq[:, j, s_, :], rhs=S0b[:, s_, j, :], start=False, stop=(s_ == NSEG - 1)),
                                 R("SBb"), [por])
                evac("act", hm[0:P, :], po[0:P, :], [por], R("hm"))
                S.op("dve", lambda e: e.tensor_tensor(out=hm2[0:P, 0:512], in0=hm[0:P, :], in1=hm[0:P, :], op=ALU.mult), R("hm"), R("hm2"))
                S.op("dve", lambda e: e.tensor_reduce(out=st4[0:P, 4:8], in_=hm2[0:P, 0:512].rearrange("p (h v) -> p h v", v=128), axis=AX.X, op=ALU.add), R("hm2"), R("st4"))
                rstd_from_ssq(st4[0:P, 8:12], st4[0:P, 4:8], 128, R("st4"), R("st4"))
                S.op("dve", lambda e: e.tensor_tensor(out=hm3, in0=hm3, in1=st4[0:P, 8:12].unsqueeze(2).to_broadcast([P, 4, 128]), op=ALU.mult), R("hm", "st4"), R("hm"))
                S.op("dve", lambda e: e.tensor_tensor(out=hm[0:P, :], in0=hm[0:P, :], in1=rp("gnorm"), op=ALU.mult), R("hm", "rowp"), R("hm"))
                act_fn(hm2[0:P, 0:512], gr_tm[0:P, :], AF.Silu, R("gr_tm"), R("hm2"))
                S.op("dve", lambda e: e.tensor_tensor(out=br[0:P, 512:1024], in0=hm[0:P, :], in1=hm2[0:P, 0:512], op=ALU.mult), R("hm", "hm2"), R("br"))
                for s_ in range(nseg if GST >= 5 else 0):
                    for j in range(2):
                        p, pr = bank()
                        klhs = (ktS[0:64, s_, j * 128:(j + 1) * 128] if samp else kt_tm[0:P, j * 128:(j + 1) * 128])
                        srcres = R("SBb") if samp else R("kt_tm")
                        for half in range(2):
                            S.op("pe", lambda e, half=half, klhs=klhs: e.matmul(p[:, half * 128:(half + 1) * 128], lhsT=klhs, rhs=gv_tm[0:P, (2 * j + half) * 128:(2 * j + half + 1) * 128],
                                                                               start=True, stop=True), srcres + R("gv_tm"), [pr])
                        lastc = s_ * L + L - 1
                        for half in range(2):
                            rows = slice(half * 64, (half + 1) * 64)
                            el = eT[rows, 1, j, lastc:lastc + 1]
                            if samp:
                                sv = S0t[rows, s_, j, :]
                                svr = R("SA")
                            else:
                                sv = Sst[l][rows, j, :]
                                svr = R("Sst%d" % l)
                            if GST >= 6:
                                S.op("dve", lambda e, rows=rows, el=el, sv=sv: e.tensor_scalar(out=St[rows, :], in0=sv, scalar1=el, scalar2=None, op0=ALU.mult), svr + R("eT"), R("St"))
                            if GST >= 7:
                                S.op("dve", lambda e, rows=rows, el=el, sv=sv, half=half: e.scalar_tensor_tensor(out=sv, in0=p[rows, half * 128:(half + 1) * 128], scalar=el, in1=St[rows, :],
                                                                                                              op0=ALU.mult, op1=ALU.add), [pr] + R("eT", "St"), svr)
                if samp:
                    for j in range(2):
                        S.dma("sp", oS_d[l, :, j * 128:(j + 1) * 128, :].rearrange("s p v -> p s v"), S0t[:, :, j, :], reads=R("SA"))
                elif ti == last_prompt and (LASTM & 4):
                    S.dma("sp", pS_d[l].rearrange("(j p) v -> p j v", p=128), Sst[l][:], reads=R("Sst%d" % l))

                for c in range(6):
                    if samp:
                        xv = xp[:, c, 0:NSEG * 7].rearrange("p (s j) -> p s j", j=7)
                        xin = [xv[:, :, j:j + 4] for j in range(4)]
                        acc = actT[:, c, 0:64].rearrange("p (s j) -> p s j", j=4)
                    else:
                        xin = [xp[:, c, j:j + N] for j in range(4)]
                        acc = actT[:, c, 0:N]
                    S.op("dve", lambda e, c=c, xin=xin, acc=acc: e.tensor_scalar(out=acc, in0=xin[0], scalar1=convp[:, c, 0:1], scalar2=convp[:, c, 4:5], op0=ALU.mult, op1=ALU.add),
                         R("xp", "convp"), R("actT"))
                    for j in range(1, 4):
                        S.op("dve", lambda e, c=c, j=j, xin=xin, acc=acc: e.scalar_tensor_tensor(out=acc, in0=xin[j], scalar=convp[:, c, j:j + 1], in1=acc, op0=ALU.mult, op1=ALU.add),
                             R("xp", "convp", "actT"), R("actT"))
                act_fn(actT[:, :, 0:N], actT[:, :, 0:N], AF.Silu, R("actT"), R("actT"))
                S.op("dve", lambda e: e.tensor_copy(out=BCb[:, :, 0:N], in_=actT[:, 4:6, 0:N]), R("actT"), R("BCb"))
                for c0, c1 in ((0, 4), (4, 5)):
                    p, pr = bank()
                    for c in range(c0, c1):
                        S.op("pe", lambda e, c=c: e.transpose(out=p[0:P, (c - c0) * 128:(c - c0 + 1) * 128], in_=actT[:, c, 0:P], identity=ident), R("actT", "cst"), [pr])
                    if c0 == 0:
                        evac("act", xs_tm[0:P, :], p[0:P, 0:512], [pr], R("xs_tm"))
                    else:
                        evac("dve", Bm_tm[0:P, :], p[0:P, 0:128], [pr], R("Bm_tm"))
                S.op("dve", lambda e: e.tensor_tensor(out=dts[0:P, 0:8], in0=sdt[0:P, :], in1=rp("dtb"), op=ALU.add), R("sdt", "rowp"), R("dts"))
                softplus_neg(dts[0:P, 16:24], dts[0:P, 0:8], R("dts"), R("dts"), dts[0:P, 8:16], res["dts"], sign=1.0)
                S.op("dve", lambda e: e.scalar_tensor_tensor(out=dts[0:P, 24:32], in0=dts[0:P, 16:24], scalar=-1.0, in1=arow[0:P, :], op0=ALU.mult, op1=ALU.mult), R("dts", "arow"), R("dts"))
                dt_ = dts[0:P, 16:24]
                dA = dts[0:P, 24:32]
                p, pr = bank()
                S.op("pe", lambda e: e.matmul(p[0:P, 0:8], lhsT=le_f, rhs=dA, start=True, stop=True), R("cst", "dts"), [pr])
                S.op("pe", lambda e: e.matmul(p[0:P, 8:16], lhsT=gt_f, rhs=dA, start=True, stop=True), R("cst", "dts"), [pr])
                S.op("pe", lambda e: e.matmul(p[:, 16:24], lhsT=ones[0:P, :], rhs=dA, start=True, stop=True), R("cst", "dts"), [pr])
                S.op("dve", lambda e: e.tensor_copy(out=t8[0:P, 16:24], in_=p[0:P, 0:8]), [pr], R("t8"))
                lam = t8[0:P, 16:24]
                act_fn(t8[0:P, 24:32], p[0:P, 0:8], AF.Exp, [pr], R("t8"))
                act_fn(t8[0:P, 32:40], p[0:P, 8:16], AF.Exp, [pr], R("t8"))
                S.op("dve", lambda e: e.tensor_tensor(out=t8[0:P, 32:40], in0=t8[0:P, 32:40], in1=dt_, op=ALU.mult), R("t8", "dts"), R("t8"))
                act_fn(eL[:, :], p[:, 16:24], AF.Exp, [pr], R("eL"))
                S.op("dve", lambda e: e.tensor_tensor(out=xsw[0:P, :].rearrange("p (h c) -> p h c", c=64), in0=xs_tm[0:P, :].rearrange("p (h c) -> p h c", c=64),
                                                      in1=t8[0:P, 32:40].unsqueeze(2).to_broadcast([P, 8, 64]), op=ALU.mult), R("xs_tm", "t8"), R("xsw"))
                S.op("dve", lambda e: e.tensor_tensor(out=xsb[0:P, :].rearrange("p (h c) -> p h c", c=64), in0=xs_tm[0:P, :].rearrange("p (h c) -> p h c", c=64),
                                                      in1=dt_.unsqueeze(2).to_broadcast([P, 8, 64]), op=ALU.mult), R("xs_tm", "dts"), R("xsb"))
                pi0, pi0r = PS[6], PSR[6]
                pi1, pi1r = PS[5], PSR[5]
                py, pyr = PS[7], PSR[7]
                if not samp:
                    S.op("act", lambda e: e.copy(out=hTb[:], in_=hT[l][:]), R("hT%d" % l), R("hTb"))
                    for g, (pi, pir) in enumerate(((pi0, pi0r), (pi1, pi1r))):
                        S.op("pe", lambda e, g=g, pi=pi: e.matmul(pi[0:P, 0:256], lhsT=BCb[g * 64:(g + 1) * 64, 1, 0:P], rhs=hTb[g * 64:(g + 1) * 64, :], start=True, stop=True),
                             R("BCb", "hTb"), [pir])
                else:
                    hTbs = SBb[:, 0:4096].rearrange("p (s c) -> p s c", c=256)
                    CmS = SBb[:, 4096:5120].rearrange("p (s t) -> p s t", t=64)
                    xswS = SBb[0:64, 5120:9216].rearrange("p (s c) -> p s c", c=512)
                    h0t = SA[:, :].rearrange("p (s q c) -> p s q c", q=4, c=128)
                    S.op("dve", lambda e: e.tensor_copy(out=big[0:64, 0:512].rearrange("p (h c) -> p h c", c=64), in_=dA.unsqueeze(2).to_broadcast([64, 8, 64])), R("dts"), R("big"))
                    pe_, per_ = bank()
                    for q in range(4):
                        S.op("pe", lambda e, q=q: e.matmul(pe_[:, q * 16:(q + 1) * 16], lhsT=big[0:64, q * 128:(q + 1) * 128], rhs=C("segind")[0:64, :], start=True, stop=True), R("big", "cst"), [per_])
                    act_fn(hm2[:, 0:64], pe_[:, 0:64], AF.Exp, [per_], R("hm2"))
                    eLs = hm2[:, 0:64].rearrange("p (q s) -> p q s", s=16)
                    S.op("dve", lambda e: e.tensor_tensor(out=CmS, in0=BCb[:, 1, 0:64].unsqueeze(1).to_broadcast([128, 16, 64]),
                                                          in1=cb[:, 320:1344].rearrange("p (s t) -> p s t", t=64), op=ALU.mult), R("BCb", "cb"), R("SBb"))
                    for ps_ in range(2):
                        s0 = ps_ * 8
                        S.op("pool", lambda e: e.memset(SA[:, :], 0.0), (), R("SA"))
                        for q in range(4):
                            oq = 0 if q < 2 else 64
                            S.dma("sp", h0t[:, :, q, oq:oq + 64], sh_d[l, s0:s0 + 8, q * 128:(q + 1) * 128, :].rearrange("s r n -> r s n"), writes=R("SA"))
                        for s_ in range(8):
                            p, pr = bank()
                            for q in range(4):
                                S.op("pe", lambda e, q=q, s_=s_: e.transpose(out=p[:, q * 128:(q + 1) * 128], in_=h0t[:, s_, q, :], identity=ident), R("SA", "cst"), [pr])
                            evac("act", hTbs[0:64, s0 + s_, :], p[0:64, 0:256], [pr], R("SBb"))
                            evac("dve", hTbs[64:128, s0 + s_, :], p[64:128, 256:512], [pr], R("SBb"))
                        S.op("dve", lambda e: e.tensor_tensor(out=xswS, in0=xsw[0:64, :].unsqueeze(1).to_broadcast([64, 8, 512]),
                                                              in1=C("segind")[0:64, s0:s0 + 8].unsqueeze(2).to_broadcast([64, 8, 512]), op=ALU.mult), R("xsw", "cst"), R("SBb"))
                        for s_ in range(8):
                            p, pr = bank()
                            for q in range(4):
                                g = q // 2
                                S.op("pe", lambda e, q=q, s_=s_, g=g: e.matmul(p[:, q * 64:(q + 1) * 64], lhsT=xswS[0:64, s_, q * 128:(q + 1) * 128], rhs=Bm_tm[0:64, g * 64:(g + 1) * 64],
                                                                                start=True, stop=True), R("SBb", "Bm_tm"), [pr])
                            for q in range(4):
                                oq = 0 if q < 2 else 64
                                hv = h0t[:, s_, q, oq:oq + 64]
                                S.op("dve", lambda e, q=q, s_=s_, hv=hv: e.scalar_tensor_tensor(out=hv, in0=hv, scalar=eLs[:, q, s0 + s_:s0 + s_ + 1], in1=p[:, q * 64:(q + 1) * 64],
                                                                                              op0=ALU.mult, op1=ALU.add), [pr] + R("SA", "hm2"), R("SA"))
                        for q in range(4):
                            oq = 0 if q < 2 else 64
                            S.dma("sp", oh_d[l, s0:s0 + 8, q * 128:(q + 1) * 128, :].rearrange("s r n -> r s n"), h0t[:, :, q, oq:oq + 64], reads=R("SA"))
                    for g, (pi, pir) in enumerate(((pi0, pi0r), (pi1, pi1r))):
                        for s_ in range(NSEG):
                            S.op("pe", lambda e, g=g, pi=pi, s_=s_: e.matmul(pi[0:P, 0:256], lhsT=CmS[g * 64:(g + 1) * 64, s_, :], rhs=hTbs[g * 64:(g + 1) * 64, s_, :],
                                                                             start=(s_ == 0), stop=(s_ == NSEG - 1)), R("SBb"), [pir])
                Y = big[0:P, 0:8 * P].rearrange("p (h t) -> p h t", t=P)
                S.op("dve", lambda e: e.tensor_tensor(out=Y, in0=le_f[:, 0:P].unsqueeze(1).to_broadcast([P, 8, P]), in1=dA.unsqueeze(2).to_broadcast([P, 8, P]), op=ALU.mult),
                     R("cst", "dts"), R("big"))
                for g in range(2):
                    pL, pLr = bank()
                    S.op("pe", lambda e, g=g: e.matmul(pL[0:P, 0:4 * P], lhsT=ones[0:P, 0:P], rhs=big[0:P, g * 4 * P:(g + 1) * 4 * P], start=True, stop=True), R("cst", "big"), [pLr])
                    S.op("dve", lambda e, g=g: e.tensor_tensor(out=big2[0:P, g * 4 * P:(g + 1) * 4 * P].rearrange("p (h t) -> p h t", t=P), in0=pL[0:P, 0:4 * P].rearrange("p (h t) -> p h t", t=P),
                                                               in1=lam[:, 4 * g:4 * g + 4].unsqueeze(2).to_broadcast([P, 4, P]), op=ALU.subtract), [pLr] + R("t8"), R("big2"))
                S.op("dve", lambda e: e.tensor_scalar(out=big2[0:P, 0:8 * P], in0=big2[0:P, 0:8 * P], scalar1=0.0, scalar2=None, op0=ALU.min), R("big2"), R("big2"))
                act_fn(big2[0:P, 0:8 * P], big2[0:P, 0:8 * P], AF.Exp, R("big2"), R("big2"))
                for g in range(2):
                    p, pr = bank()
                    S.op("pe", lambda e, g=g: e.matmul(p[0:P, 0:P], lhsT=BCb[g * 64:(g + 1) * 64, 0, 0:P], rhs=BCb[g * 64:(g + 1) * 64, 1, 0:P], start=True, stop=True), R("BCb"), [pr])
                    S.op("dve", lambda e, g=g: e.tensor_tensor(out=cbm[0:P, g, 0:P], in0=p[0:P, 0:P], in1=le_f[:, 0:P], op=ALU.mult), [pr] + R("cst"), R("cbm"))
                S.op("dve", lambda e: e.tensor_tensor(out=bigb[0:P, 0:8 * P].rearrange("p (g h t) -> p g h t", g=2, t=P), in0=big2[0:P, 0:8 * P].rearrange("p (g h t) -> p g h t", g=2, t=P),
                                                      in1=cbm[0:P, :, 0:P].unsqueeze(2).to_broadcast([P, 2, 4, P]), op=ALU.mult), R("big2", "cbm"), R("bigb"))
                for h in range(8):
                    S.op("pe", lambda e, h=h: e.matmul(py[0:P, h * 64:(h + 1) * 64], lhsT=bigb[0:P, h * P:(h + 1) * P], rhs=xsb[0:P, h * 64:(h + 1) * 64], start=True, stop=True), R("bigb", "xsb"), [pyr])
                for g, (pi, pir) in enumerate(((pi0, pi0r), (pi1, pi1r))):
                    S.op("dve", lambda e, g=g, pi=pi: e.tensor_tensor(out=hm[0:P, g * 256:(g + 1) * 256].rearrange("p (h c) -> p h c", c=64), in0=pi[0:P, 0:256].rearrange("p (h c) -> p h c", c=64),
                                                                      in1=t8[0:P, 24 + 4 * g:28 + 4 * g].unsqueeze(2).to_broadcast([P, 4, 64]), op=ALU.mult), [pir] + R("t8"), R("hm"))
                S.op("dve", lambda e: e.tensor_tensor(out=hm[0:P, :], in0=py[0:P, 0:512], in1=hm[0:P, :], op=ALU.add), [pyr] + R("hm"), R("hm"))
                S.op("dve", lambda e: e.tensor_tensor(out=hm2[0:P, 0:512].rearrange("p (h c) -> p h c", c=64), in0=xs_tm[0:P, :].rearrange("p (h c) -> p h c", c=64),
                                                      in1=rp("sD").unsqueeze(2).to_broadcast([P, 8, 64]), op=ALU.mult), R("xs_tm", "rowp"), R("hm2"))
                S.op("dve", lambda e: e.tensor_tensor(out=hm[0:P, :], in0=hm[0:P, :], in1=hm2[0:P, 0:512], op=ALU.add), R("hm", "hm2"), R("hm"))
                act_fn(hm2[0:P, 0:512], sz_tm[0:P, :], AF.Silu, R("sz_tm"), R("hm2"))
                S.op("dve", lambda e: e.tensor_tensor(out=hm[0:P, :], in0=hm[0:P, :], in1=hm2[0:P, 0:512], op=ALU.mult), R("hm", "hm2"), R("hm"))
                S.op("dve", lambda e: e.tensor_tensor(out=hm2[0:P, 0:512], in0=hm[0:P, :], in1=hm[0:P, :], op=ALU.mult), R("hm"), R("hm2"))
                S.op("dve", lambda e: e.tensor_reduce(out=st4[0:P, 4:6], in_=hm2[0:P, 0:512].rearrange("p (g c) -> p g c", c=256), axis=AX.X, op=ALU.add), R("hm2"), R("st4"))
                rstd_from_ssq(st4[0:P, 8:10], st4[0:P, 4:6], 256, R("st4"), R("st4"))
                S.op("dve", lambda e: e.tensor_tensor(out=hm[0:P, :].rearrange("p (g c) -> p g c", c=256), in0=hm[0:P, :].rearrange("p (g c) -> p g c", c=256),
                                                      in1=st4[0:P, 8:10].unsqueeze(2).to_broadcast([P, 2, 256]), op=ALU.mult), R("hm", "st4"), R("hm"))
                S.op("dve", lambda e: e.tensor_tensor(out=br[0:P, 1024:1536], in0=hm[0:P, :], in1=rp("snorm"), op=ALU.mult), R("hm", "rowp"), R("br"))
                if not samp:
                    p, pr = bank()
                    for g in range(2):
                        S.op("pe", lambda e, g=g: e.matmul(p[:, g * 256:(g + 1) * 256], lhsT=Bm_tm[0:P, :], rhs=xsw[0:P, g * 256:(g + 1) * 256], start=True, stop=True), R("Bm_tm", "xsw"), [pr])
                    for g in range(2):
                        rows = slice(g * 64, (g + 1) * 64)
                        S.op("dve", lambda e, g=g, rows=rows: e.tensor_tensor(out=big[rows, 0:256].rearrange("p (h c) -> p h c", c=64), in0=hT[l][rows, :].rearrange("p (h c) -> p h c", c=64),
                                                                              in1=eL[rows, 4 * g:4 * g + 4].unsqueeze(2).to_broadcast([64, 4, 64]), op=ALU.mult), R("hT%d" % l, "eL"), R("big"))
                        S.op("dve", lambda e, g=g, rows=rows: e.tensor_tensor(out=hT[l][rows, :], in0=p[rows, g * 256:(g + 1) * 256], in1=big[rows, 0:256], op=ALU.add), [pr] + R("big"), R("hT%d" % l))
                    if ti == last_prompt and (LASTM & 8):
                        p, pr = bank()
                        for half in range(2):
                            S.op("pe", lambda e, half=half: e.transpose(out=p[:, half * 128:(half + 1) * 128], in_=hT[l][:, half * 128:(half + 1) * 128], identity=ident),
                                 R("hT%d" % l, "cst"), [pr])
                        evac("act", hm2[:, 0:256], p[:, 0:256], [pr], R("hm2"))
                        phv = ph_d[l].rearrange("(g f r) n -> f r g n", g=2, f=2, r=128)
                        for half in range(2):
                            S.dma("sp", phv[half], hm2[:, half * 128:(half + 1) * 128].rearrange("p (g n) -> p g n", n=64), reads=R("hm2"))

                dbg_out("br", br[0:P, :], R("br"))
                transpose_to(lambda c0, c1: brT[:, c0:c1, 0:P], br, res["br"], P, 12, res["brT"], eng="dve")
                mixacc = big[:, :].rearrange("p (c t) -> p c t", t=128)
                gs8 = big2[:, :].rearrange("p (c t) -> p c t", t=128)
                for n in range(3):
                    for gh in range(2):
                        c0 = 4896 + n * 1024 + gh * 512
                        wg, wgr = load_w(W(c0, c0 + 512), 512)
                        for dq in range(4):
                            dc = gh * 4 + dq
                            pg, pgr = fm_piece(wg, wgr, dq * 128, (dq + 1) * 128, N)
                            act_fn(gs8[:, dc, 0:N], pg[:, 0:N], AF.Sigmoid, [pgr], R("big2"))
                    wbn, wbnr = load_w2(wbr_d[l, n * 512:(n + 1) * 512, :], 4, 1024)
                    for dc in range(8):
                        pa, par = bank()
                        for k in range(4):
                            S.op("pe", lambda e, k=k, dc=dc: e.matmul(pa[:, 0:N], lhsT=wbn[:, k, dc * 128:(dc + 1) * 128], rhs=brT[:, n * 4 + k, 0:N], start=(k == 0), stop=(k == 3)),
                                 [wbnr] + R("brT"), [par])
                        if n == 0:
                            S.op("dve", lambda e, dc=dc: e.tensor_tensor(out=mixacc[:, dc, 0:N], in0=pa[:, 0:N], in1=gs8[:, dc, 0:N], op=ALU.mult), [par] + R("big2"), R("big"))
                        else:
                            S.op("dve", lambda e, dc=dc: e.tensor_tensor(out=hm2[:, 0:N], in0=pa[:, 0:N], in1=gs8[:, dc, 0:N], op=ALU.mult), [par] + R("big2"), R("hm2"))
                            S.op("dve", lambda e, dc=dc: e.tensor_tensor(out=mixacc[:, dc, 0:N], in0=mixacc[:, dc, 0:N], in1=hm2[:, 0:N], op=ALU.add), R("big", "hm2"), R("big"))
                S.op("act", lambda e: e.copy(out=mixT[:, :, 0:N], in_=mixacc[:, :, 0:N]), R("big"), R("mixT"))
                for gh in range(2):
                    wo, wor = load_w(wout_d[l, :, gh * 512:(gh + 1) * 512], 512)
                    p, pr = tm_piece(wo, wor, 0, 512, P, src=mixT, srcres=res["mixT"])
                    S.op("dve", lambda e, gh=gh: e.scalar_tensor_tensor(out=X[0:P, gh * 512:(gh + 1) * 512], in0=X[0:P, gh * 512:(gh + 1) * 512], scalar=DN_ALPHA, in1=p[0:P, 0:512],
                                                                        op0=ALU.mult, op1=ALU.add), [pr] + R("X"), R("X"))
                S.dma("sp", lnrow[:], lnp_d[l, 0], writes=R("lnrow"))
                layernorm_tm(P, X, res["X"], lnrow[0:P, 0:1024], lnrow[0:P, 1024:2048], X[0:P, :], res["X"])
                dbg_out("h1", X[0:P, :], R("X"))

                if do_peer:
                    fence()
                    transpose_to(lambda c0, c1: xT[:, c0:c1, 0:P], X, res["X"], P, 8, res["xT"])
                    for gq_ in range(4):
                        wq, wqr = load_w(wq_d[l, :, gq_ * 512:(gq_ + 1) * 512], 512)
                        for c in range(4):
                            p, pr = fm_piece(wq, wqr, c * 128, (c + 1) * 128, N)
                            evac("act", qTb[:, gq_ * 4 + c, 0:N], p[:, 0:N], [pr], R("qTb"))
                    i = wsn[0]
                    wsn[0] = 1 - i
                    kf = wst[i][:].rearrange("p k n -> p (k n)")[:, 0:2048]
                    kb = wbf[i][:].rearrange("p k n -> p (k n)")[:, 0:2048]
                    S.dma("sp", kf, keys_d[l].rearrange("c a k -> c (a k)"), writes=R("wst%d" % i))
                    S.op("pool", lambda e: e.tensor_copy(out=kb, in_=kf), R("wst%d" % i), R("wbf%d" % i))
                    kbr = res["wbf%d" % i]
                    for c0 in range(0, 16, 4):
                        p, pr = bank()
                        for c in range(c0, c0 + 4):
                            S.op("pe", lambda e, c=c: e.matmul(p[0:P, (c - c0) * 128:(c - c0 + 1) * 128], lhsT=qTb[:, c, 0:P], rhs=kb[:, c * 128:(c + 1) * 128], start=True, stop=True),
                                 R("qTb") + [kbr], [pr])
                        evac("dve", SC[0:P, c0 * 128:(c0 + 4) * 128], p[0:P, 0:512], [pr], R("SC"))

                    def top16(src, width, vout, iout):
                        wv = wk[0:P, 0:width]
                        S.op("dve", lambda e: e.max(out=vout[:, 0:8], in_=src), R("SC"), R("V16"))
                        S.op("dve", lambda e: e.max_index(out=iout[:, 0:8], in_max=vout[:, 0:8], in_values=src), R("SC", "V16"), R("I16"))
                        S.op("dve", lambda e: e.match_replace(out=wv, in_to_replace=vout[:, 0:8], in_values=src, imm_value=-1e30), R("SC", "V16"), R("wk"))
                        S.op("dve", lambda e: e.max(out=vout[:, 8:16], in_=wv), R("wk"), R("V16"))
                        S.op("dve", lambda e: e.max_index(out=iout[:, 8:16], in_max=vout[:, 8:16], in_values=wv), R("wk", "V16"), R("I16"))

                    for c in range(16):
                        top16(SC[0:P, c * 128:(c + 1) * 128], 128, V16[0:P, c, :], I16[0:P, c, :])
                    S.op("dve", lambda e: e.tensor_copy(out=I16f[0:P], in_=I16[0:P]), R("I16"), R("I16f"))
                    V16v = V16[0:P].rearrange("p (h j) a -> p h j a", j=2)
                    I16v = I16f[0:P].rearrange("p (h j) a -> p h j a", j=2)
                    cand = SC[0:P, :].rearrange("p (h a b) -> p h a b", a=16, b=16)
                    S.op("dve", lambda e: e.tensor_tensor(out=cand, in0=V16v[:, :, 0, :].unsqueeze(3).to_broadcast([P, 8, 16, 16]),
                                                          in1=V16v[:, :, 1, :].unsqueeze(2).to_broadcast([P, 8, 16, 16]), op=ALU.add), R("V16"), R("SC"))
                    for h in range(8):
                        top16(SC[0:P, h * 256:(h + 1) * 256], 256, TV[0:P, h, :], TPu[0:P, h, :])
                    TPf, af, bf_, i0f, i1f, gg, dots, actg = [pk[0:P, q, :].rearrange("p (h k) -> p h k", k=16) for q in range(8)]
                    S.op("dve", lambda e: e.tensor_copy(out=TPf, in_=TPu[0:P]), R("I16", "V16"), R("pk"))
                    bv4 = big[0:P, :].rearrange("p (h a b) -> p h a b", a=16, b=16)
                    for hh in range(2):
                        hs_ = slice(4 * hh, 4 * hh + 4)
                        S.op("dve", lambda e, hs_=hs_: e.tensor_tensor(out=bv4, in0=TPf[:, hs_, :].unsqueeze(3).to_broadcast([P, 4, 16, 16]),
                                                                       in1=thr16[0:P, :].unsqueeze(1).unsqueeze(1).to_broadcast([P, 4, 16, 16]), op=ALU.is_ge), R("pk", "thr16"), R("big"))
                        S.op("dve", lambda e, hs_=hs_: e.tensor_reduce(out=af[:, hs_, :], in_=bv4, axis=AX.X, op=ALU.add), R("big"), R("pk"))
                    S.op("dve", lambda e: e.scalar_tensor_tensor(out=pk[0:P, 2, :], in0=pk[0:P, 1, :], scalar=-16.0, in1=pk[0:P, 0, :], op0=ALU.mult, op1=ALU.add), R("pk"), R("pk"))
                    for (src_, jj, dst_) in ((af, 0, i0f), (bf_, 1, i1f)):
                        for hh in range(2):
                            hs_ = slice(4 * hh, 4 * hh + 4)
                            S.op("dve", lambda e, hs_=hs_, src_=src_: e.tensor_tensor(out=bv4, in0=src_[:, hs_, :].unsqueeze(3).to_broadcast([P, 4, 16, 16]),
                                                                                      in1=C("iota16")[0:P, :].unsqueeze(1).unsqueeze(1).to_broadcast([P, 4, 16, 16]), op=ALU.is_equal),
                                 R("pk", "cst"), R("big"))
                            S.op("dve", lambda e, hs_=hs_, jj=jj: e.tensor_tensor(out=bv4, in0=bv4, in1=I16v[:, hs_, jj, :].unsqueeze(2).to_broadcast([P, 4, 16, 16]), op=ALU.mult),
                                 R("big", "I16f"), R("big"))
                            S.op("dve", lambda e, hs_=hs_, dst_=dst_: e.tensor_reduce(out=dst_[:, hs_, :], in_=bv4, axis=AX.X, op=ALU.add), R("big"), R("pk"))
                    S.op("dve", lambda e: e.scalar_tensor_tensor(out=pk[0:P, 0, :], in0=pk[0:P, 3, :], scalar=128.0, in1=pk[0:P, 4, :], op0=ALU.mult, op1=ALU.add), R("pk"), R("pk"))
                    S.op("dve", lambda e: e.tensor_copy(out=eidx[0:P, :], in_=pk[0:P, 0, :]), R("pk"), R("eidx"))
                    S.op("dve", lambda e: e.tensor_tensor(out=gg, in0=TV[0:P], in1=TV[0:P, :, 0:1].to_broadcast([P, 8, 16]), op=ALU.subtract), R("V16"), R("pk"))
                    act_fn(pk[0:P, 5, :], pk[0:P, 5, :], AF.Exp, R("pk"), R("pk"))
                    S.op("dve", lambda e: e.tensor_reduce(out=st4[0:P, 0:8], in_=gg, axis=AX.X, op=ALU.add), R("pk"), R("st4"))
                    S.op("dve", lambda e: e.reciprocal(out=st4[0:P, 0:8], in_=st4[0:P, 0:8]), R("st4"), R("st4"))
                    S.op("dve", lambda e: e.tensor_tensor(out=gg, in0=gg, in1=st4[0:P, 0:8].unsqueeze(2).to_broadcast([P, 8, 16]), op=ALU.mult), R("pk", "st4"), R("pk"))
                    GUr = [res["GU%d" % q] for q in range(4)]
                    gn = [0]

                    def gather(tab, slot):
                        q = gn[0] % 4
                        gn[0] += 1
                        S.dma("pool", None, None, reads=R("eidx"), writes=[GUr[q]],
                              fn=lambda e, q=q: e.indirect_dma_start(out=GU[q][0:P, :], out_offset=None, in_=tab,
                                                                     in_offset=bass.IndirectOffsetOnAxis(ap=eidx[0:P, slot:slot + 1], axis=0)))
                        return GU[q], GUr[q]

                    for slot in range(128):
                        gu, gur = gather(pu_d[l][:, :], slot)
                        S.op("dve", lambda e, gu=gu, slot=slot: e.tensor_tensor(out=big[0:P, :], in0=gu[0:P, :], in1=X[0:P, :], op=ALU.mult), [gur] + R("X"), R("big"))
                        S.op("dve", lambda e, slot=slot: e.tensor_reduce(out=pk[0:P, 6, slot:slot + 1], in_=big[0:P, :], axis=AX.X, op=ALU.add), R("big"), R("pk"))
                    act_fn(pk[0:P, 7, :], pk[0:P, 6, :], AF.Gelu, R("pk"), R("pk"))
                    S.op("dve", lambda e: e.tensor_tensor(out=pk[0:P, 7, :], in0=pk[0:P, 7, :], in1=pk[0:P, 5, :], op=ALU.mult), R("pk"), R("pk"))
                    for slot in range(128):
                        gu, gur = gather(pv_d[l][:, :], slot)
                        if slot == 0:
                            S.op("dve", lambda e, gu=gu, slot=slot: e.tensor_scalar(out=pacc[0:P, :], in0=gu[0:P, :], scalar1=pk[0:P, 7, slot:slot + 1], scalar2=None, op0=ALU.mult),
                                 [gur] + R("pk"), R("acc"))
                        else:
                            S.op("dve", lambda e, gu=gu, slot=slot: e.scalar_tensor_tensor(out=pacc[0:P, :], in0=gu[0:P, :], scalar=pk[0:P, 7, slot:slot + 1], in1=pacc[0:P, :],
                                                                                        op0=ALU.mult, op1=ALU.add), [gur] + R("pk", "acc"), R("acc"))
                    S.op("dve", lambda e: e.scalar_tensor_tensor(out=X[0:P, :], in0=X[0:P, :], scalar=DN_ALPHA, in1=pacc[0:P, :], op0=ALU.mult, op1=ALU.add), R("X", "acc"), R("X"))
                    S.dma("sp", lnrow[:], lnp_d[l, 1], writes=R("lnrow"))
                    layernorm_tm(P, X, res["X"], lnrow[0:P, 0:1024], lnrow[0:P, 1024:2048], X[0:P, :], res["X"])
                    fence()
            S.dma("sp", y_d[t0:t0 + P, :], X[0:P, :], reads=R("X"))
            dbg_i[0] += 1
        S.finish()
    return nc


def make_consts():
    c = np.zeros((128, COW), np.float32)
    def put(n, a):
        c[:a.shape[0], CO[n][0]:CO[n][1]] = a
    s = np.arange(128)
    put("ident", np.eye(128, dtype=np.float32))
    put("leP", (s[:, None] <= s[None, :]).astype(np.float32))
    put("gtP", (s[:, None] > s[None, :]).astype(np.float32))
    q = np.arange(64)
    same = (q[:, None] // 4) == (q[None, :] // 4)
    put("leS", (same & (q[:, None] <= q[None, :])).astype(np.float32))
    put("gtS", (same & (q[:, None] > q[None, :])).astype(np.float32))
    put("ones", np.ones((128, 128), np.float32))
    put("segind", ((q[:, None] // 4) == np.arange(16)[None, :]).astype(np.float32))
    put("iota16", np.broadcast_to(np.arange(16, dtype=np.float32), (128, 16)))
    segm = ((np.arange(64)[None, :] // 4) == np.arange(16)[:, None]).astype(np.float32).reshape(1, 1024)
    put("halfm", np.stack([(s < 64), (s >= 64)], 1).astype(np.float32))
    return c, np.ascontiguousarray(np.broadcast_to(segm, (128, 1024))).astype(np.float32)

def prep(inp, core, nexp=16384):
    f = lambda a: np.ascontiguousarray(np.asarray(a, dtype=np.float32))
    x = np.concatenate([f(inp["x_prompt"][core]), f(inp["x_sample"][core * 16:(core + 1) * 16]).reshape(64, 1024)], 0)
    sl = slice(core * 16, (core + 1) * 16)
    d = dict(x=x)
    d["sC"] = f(inp["state_mlstm_C"][:, sl]); d["sn"] = f(inp["state_mlstm_n"][:, sl]); d["sm"] = f(inp["state_mlstm_m"][:, sl])
    d["sS"] = f(inp["state_gla_S"][:, sl]).reshape(2, 16, 256, 128)
    d["sh"] = f(inp["state_ssm_h"][:, sl]).reshape(2, 16, 512, 64)
    d["scv"] = f(inp["state_conv"][:, sl]).reshape(2, 48, 768)
    d["w_in"] = f(inp["w_in"]); d["w_br"] = f(inp["w_branch"]).reshape(2, 1536, 1024); d["w_out"] = f(inp["w_out"])
    d["p_wq"] = f(inp["p_wq"])
    d["keysT"] = f(np.transpose(np.asarray(inp["p_keys"]).reshape(2, 16, 128, 128), (0, 3, 1, 2)))
    for l in range(2):
        d["p_u%d" % l] = f(inp["p_u"][l][:nexp]); d["p_v%d" % l] = f(inp["p_v"][l][:nexp])
    rows = []
    for l in range(2):
        r = np.concatenate([f(inp[k][l]).reshape(-1) for k in ["m_i_bias", "m_f_bias", "m_norm", "g_a_bias", "g_norm", "s_dt_bias",
                                                              "s_A_log", "s_D", "s_norm"]])
        rows.append(np.broadcast_to(r, (128, r.size)))
    d["rowp"] = f(np.stack(rows))
    lnp = np.stack([np.stack([np.concatenate([f(inp["ln1_g"][l]), f(inp["ln1_b"][l])]), np.concatenate([f(inp["ln2_g"][l]), f(inp["ln2_b"][l])])]) for l in range(2)])
    d["lnp"] = f(np.broadcast_to(lnp[:, :, None, :], (2, 2, 128, 2048)))
    cw = f(inp["s_conv_w"]); cbb = f(inp["s_conv_b"])
    cp = np.concatenate([cw, cbb[:, None, :]], 1)
    d["convp"] = f(np.transpose(cp.reshape(2, 5, 6, 128), (0, 3, 2, 1)))
    d["gaup"] = f(inp["g_a_up"])
    d["cst"], d["segm"] = make_consts()
    return d


NEXP_USED = 16384


def kernel(**inp):
    nc = build(nexp=NEXP_USED)
    maps = [prep(inp, c, nexp=NEXP_USED) for c in range(NCORES)]
    r = run_bass_kernel_spmd(nc, maps, core_ids=list(range(NCORES))).results
    f = np.float32
    yp = np.stack([r[c]["y"][0:SEQ] for c in range(NCORES)], 0).astype(f)
    ys = np.concatenate([r[c]["y"][SEQ:].reshape(NSEG, LS, D) for c in range(NCORES)], 0).astype(f)
    st = lambda k, shp: np.stack([r[c][k].reshape(shp) for c in range(NCORES)], 1).astype(f)
    ct = lambda k, shp: np.concatenate([r[c][k].reshape(shp) for c in range(NCORES)], 1).astype(f)
    return (yp, ys,
            st("pC", (2, 4, 128, 128)), st("pn", (2, 4, 128)), st("pm", (2, 4)),
            st("pS", (2, 4, 64, 128)), st("ph", (2, 8, 64, 64)), st("pcv", (2, 3, 768)),
            ct("oC", (2, 16, 4, 128, 128)), ct("on", (2, 16, 4, 128)), ct("om", (2, 16, 4)),
            ct("oS", (2, 16, 4, 64, 128)), ct("oh", (2, 16, 8, 64, 64)), ct("ocv", (2, 16, 3, 768)))
```

```python
import contextlib
import os
import numpy as np
import concourse.bass as bass
import concourse.mybir as mybir
from concourse.bass_utils import run_bass_kernel_spmd

F32 = mybir.dt.float32
BF16 = mybir.dt.bfloat16
I32 = mybir.dt.int32
U32 = mybir.dt.uint32
AF = mybir.ActivationFunctionType
ALU = mybir.AluOpType
AX = mybir.AxisListType

NCORES = 8
D = 1024
SEQ = 2048
NSEG = 16
LS = 4
TC = SEQ + NSEG * LS
INW = 7968
EPS = 1e-5
DN_ALPHA = 4.0 ** 0.25
DEPTH = 2
GST = int(os.environ.get('GST', '99'))
LASTM = int(os.environ.get('LASTM', '15'))

RP = {}
_o = 0
for _n, _s in [("mib", 4), ("mfb", 4), ("mnorm", 512), ("gab", 256), ("gnorm", 512), ("dtb", 8),
               ("alog", 8), ("sD", 8), ("snorm", 512)]:
    RP[_n] = (_o, _o + _s)
    _o += _s
RPW = _o

CO = {}
_o = 0
for _n, _s in [("ident", 128), ("leP", 128), ("gtP", 128), ("leS", 64), ("gtS", 64), ("ones", 128),
               ("segind", 16), ("iota16", 16), ("halfm", 2)]:
    CO[_n] = (_o, _o + _s)
    _o += _s
COW = _o


class Res:
    __slots__ = ("name", "w", "r")

    def __init__(self, name="r"):
        self.name = name
        self.w = None
        self.r = []


class _Rec:
    def __getattr__(self, name):
        def f(*a, **k):
            self.call = (name, a, k)
            return self
        return f


def _replay(fn):
    rec = _Rec()
    fn(rec)
    name, a, k = rec.call
    def run(eng):
        try:
            return getattr(eng, name)(*a, **k)
        except Exception:
            print("FAILED OP", name, {kk: (getattr(vv, "shape", vv), getattr(getattr(vv, "tensor", None), "name", None)) for kk, vv in k.items()})
            raise
    return run


class Sched:
    ENG = ("pe", "dve", "act", "pool", "sp")

    def __init__(self, nc, stack, n_dma_sems=32):
        self.nc = nc
        self.sem = {e: stack.enter_context(nc.semaphore("s_" + e)) for e in self.ENG}
        self.cnt = {e: 0 for e in self.ENG}
        self.prog = {e: [] for e in self.ENG}
        self.seen = {e: {} for e in self.ENG}
        self.dsem = [stack.enter_context(nc.semaphore("d%d" % i)) for i in range(n_dma_sems)]
        self.dval = [0] * n_dma_sems
        self.dnext = 0
        self.n_hw = n_dma_sems - 8
        self.dnext_sw = 0
        self.n_ins = 0

    def _wait(self, e, ev):
        if ev is None:
            return
        kind, key, val = ev
        if kind == "e" and key == e and e == "pe":
            return
        k = (kind, key)
        if self.seen[e].get(k, 0) >= val:
            return
        self.seen[e][k] = val
        sem = self.sem[key] if kind == "e" else self.dsem[key]
        self.prog[e].append(lambda eng, sem=sem, val=val: eng.wait_ge(sem, val))

    def _deps(self, e, reads, writes):
        for r in reads:
            self._wait(e, r.w)
        for w in writes:
            self._wait(e, w.w)
            for ev in w.r:
                self._wait(e, ev)

    def _commit(self, ev, reads, writes):
        for r in reads:
            r.r.append(ev)
            if len(r.r) > 16:
                d = {}
                for x in r.r:
                    k = (x[0], x[1])
                    if k not in d or d[k][2] < x[2]:
                        d[k] = x
                r.r = list(d.values())
        for w in writes:
            w.w = ev
            w.r = []

    def op(self, e, fn, reads=(), writes=()):
        self._deps(e, reads, writes)
        self.cnt[e] += 1
        ev = ("e", e, self.cnt[e])
        sem = self.sem[e]
        fn = _replay(fn)
        self.prog[e].append(lambda eng, fn=fn, sem=sem: fn(eng).then_inc(sem, 1))
        self._commit(ev, reads, writes)
        self.n_ins += 1
        return ev

    def dma(self, q, out, in_, reads=(), writes=(), fn=None, slow=False):
        self._deps(q, reads, writes)
        if q == "pool":
            i = self.n_hw + self.dnext_sw
            self.dnext_sw = (self.dnext_sw + 1) % 8
        else:
            i = self.dnext
            self.dnext = (self.dnext + 1) % self.n_hw
        if self.dval[i] > 0:
            self._wait(q, ("d", i, self.dval[i]))
        self.dval[i] += 16
        ev = ("d", i, self.dval[i])
        sem = self.dsem[i]
        if fn is None:
            if slow:
                fn = lambda eng, out=out, in_=in_: eng.dma_start(out=out, in_=in_, allow_slow_non_contiguous=True)
            else:
                fn = lambda eng, out=out, in_=in_: eng.dma_start(out=out, in_=in_)
        fn = _replay(fn)
        self.prog[q].append(lambda eng, fn=fn, sem=sem: fn(eng).then_inc(sem, 16))
        self._commit(ev, reads, writes)
        self.n_ins += 1
        return ev

    def finish(self):
        for i in range(len(self.dsem)):
            if self.dval[i] > 0:
                self._wait("sp", ("d", i, self.dval[i]))
        for e in ("pe", "dve", "act", "pool"):
            if self.cnt[e] > 0:
                self._wait("sp", ("e", e, self.cnt[e]))
        nc = self.nc
        progs = self.prog
        with nc.Block() as block:
            @block.tensor
            def _(eng):
                for f in progs["pe"]:
                    f(eng)

            @block.vector
            def _(eng):
                for f in progs["dve"]:
                    f(eng)

            @block.scalar
            def _(eng):
                for f in progs["act"]:
                    f(eng)

            @block.gpsimd
            def _(eng):
                for f in progs["pool"]:
                    f(eng)

            @block.sync
            def _(eng):
                for f in progs["sp"]:
                    f(eng)


def build(tiles=None, nlayers=DEPTH, do_peer=True, dbg=(), nexp=16384, preconv=True):
    nc = bass.Bass("TRN2", target_bir_lowering=False)
    di = lambda name, shape, dt=F32: nc.dram_tensor(name, list(shape), dt, kind="ExternalInput").ap()
    do = lambda name, shape, dt=F32: nc.dram_tensor(name, list(shape), dt, kind="ExternalOutput").ap()
    x_d = di("x", [TC, D])
    sC_d = di("sC", [DEPTH, NSEG, 4, 128, 128]); sn_d = di("sn", [DEPTH, NSEG, 4, 128]); sm_d = di("sm", [DEPTH, NSEG, 4])
    sS_d = di("sS", [DEPTH, NSEG, 256, 128]); sh_d = di("sh", [DEPTH, NSEG, 512, 64]); scv_d = di("scv", [DEPTH, NSEG * 3, 768])
    win_d = di("w_in", [DEPTH, D, INW]); wbr_d = di("w_br", [DEPTH, 1536, D]); wout_d = di("w_out", [DEPTH, D, D])
    wq_d = di("p_wq", [DEPTH, D, 2048]); keys_d = di("keysT", [DEPTH, 128, 16, 128])
    pu_d = [di("p_u%d" % l, [nexp, D]) for l in range(DEPTH)]; pv_d = [di("p_v%d" % l, [nexp, D]) for l in range(DEPTH)]
    rowp_d = di("rowp", [DEPTH, 128, RPW]); convp_d = di("convp", [DEPTH, 128, 6, 5]); gaup_d = di("gaup", [DEPTH, 16, 256])
    cst_d = di("cst", [128, COW]); segm_d = di("segm", [128, 1024]); lnp_d = di("lnp", [DEPTH, 2, 128, 2048])
    y_d = do("y", [TC, D])
    pC_d = do("pC", [DEPTH, 4, 128, 128]); pn_d = do("pn", [DEPTH, 4, 128]); pm_d = do("pm", [DEPTH, 4])
    pS_d = do("pS", [DEPTH, 256, 128]); ph_d = do("ph", [DEPTH, 512, 64]); pcv_d = do("pcv", [DEPTH, 3, 768])
    oC_d = do("oC", [DEPTH, NSEG, 4, 128, 128]); on_d = do("on", [DEPTH, NSEG, 4, 128]); om_d = do("om", [DEPTH, NSEG, 4])
    oS_d = do("oS", [DEPTH, NSEG, 256, 128]); oh_d = do("oh", [DEPTH, NSEG, 512, 64]); ocv_d = do("ocv", [DEPTH, NSEG * 3, 768])
    dbg_d = {k: do("dbg_" + k, shp) for k, shp in dbg}
    dscr = lambda name, shape: nc.dram_tensor(name, list(shape), BF16).ap()
    wins = dscr("wins", [DEPTH, D, INW]); wbrs = dscr("wbrs", [DEPTH, 1536, D]); wouts = dscr("wouts", [DEPTH, D, D])
    wqs = dscr("wqs", [DEPTH, D, 2048]); keyss = dscr("keyss", [DEPTH, 128, 2048])
    pus = [dscr("pus%d" % l, [nexp, D]) for l in range(DEPTH)]; pvs = [dscr("pvs%d" % l, [nexp, D]) for l in range(DEPTH)]

    with contextlib.ExitStack() as st:
        S = Sched(nc, st)
        res = {}

        def sb(name, shape, dt=F32):
            t = st.enter_context(nc.sbuf_tensor("sb_" + name, list(shape), dt))
            res[name] = Res(name)
            return t

        def R(*names):
            return [res[n] for n in names]

        PS = [st.enter_context(nc.psum_tensor("ps%d" % i, [128, 512], F32)) for i in range(8)]
        PSR = [Res("ps%d" % i) for i in range(8)]
        psn = [0]

        def bank():
            i = psn[0]
            psn[0] = (i + 1) % 5
            return PS[i], PSR[i]

        cst = sb("cst", [128, COW])
        S.dma("sp", cst[:], cst_d, writes=R("cst"))
        C = lambda n: cst[:, CO[n][0]:CO[n][1]]
        cb = sb("cb", [128, 128 + 128 + 64 + 1024], BF16)
        S.op("dve", lambda e: e.tensor_copy(out=cb[:, 0:128], in_=C("leP")), R("cst"), R("cb"))
        S.op("dve", lambda e: e.tensor_copy(out=cb[:, 128:256], in_=C("ident")), R("cst"), R("cb"))
        S.op("dve", lambda e: e.tensor_copy(out=cb[:, 256:320], in_=C("leS")), R("cst"), R("cb"))
        ident = C("ident")
        ones = C("ones")

        X = sb("X", [128, D])
        xT = sb("xT", [128, 8, 128], BF16)
        rowp = sb("rowp", [128, RPW])
        lnrow = sb("lnrow", [128, 2048])
        convp = sb("convp", [128, 6, 5])
        gaupf = sb("gaupf", [16, 256]); gaupb = sb("gaupb", [16, 256], BF16)
        arow = sb("arow", [128, 8])
        wst = [sb("wst%d" % i, [128, 8, 528]) for i in range(2)]
        wbf = [sb("wbf%d" % i, [128, 8, 528], BF16) for i in range(2)]
        wsn = [0]
        mqT = sb("mqT", [128, 4, 128], BF16); mkT = sb("mkT", [128, 4, 128], BF16)
        mk_tm = sb("mk_tm", [128, 512], BF16); mv_tm = sb("mv_tm", [128, 512])
        mo_tm = sb("mo_tm", [128, 512]); mif = sb("mif", [128, 8])
        gqT = sb("gqT", [128, 2, 128]); gkT = sb("gkT", [128, 2, 128])
        gk_tm = sb("gk_tm", [128, 256]); gv_tm = sb("gv_tm", [128, 512], BF16); gr_tm = sb("gr_tm", [128, 512])
        gaT = sb("gaT", [16, 128], BF16)
        sz_tm = sb("sz_tm", [128, 512]); sdt = sb("sdt", [128, 8])
        xp = sb("xp", [128, 6, 3 + 128])
        br = sb("br", [128, 1536])
        brT = sb("brT", [128, 12, 128], BF16)
        CT = [sb("CT%d" % l, [128, 4, 129]) for l in range(DEPTH)]
        CTb = sb("CTb", [128, 4, 129], BF16)
        mrun = [sb("mrun%d" % l, [4, 1]) for l in range(DEPTH)]
        Sst = [sb("Sst%d" % l, [128, 2, 128]) for l in range(DEPTH)]
        Sb = sb("Sb", [128, 2, 128], BF16)
        hT = [sb("hT%d" % l, [128, 256]) for l in range(DEPTH)]
        hTb = sb("hTb", [128, 256], BF16)
        cvh = [sb("cvh%d" % l, [128, 6, 3]) for l in range(DEPTH)]
        for l in range(DEPTH):
            S.op("pool", lambda e, l=l: e.memset(CT[l][:], 0.0), (), R("CT%d" % l))
            S.op("pool", lambda e, l=l: e.memset(mrun[l][:], 0.0), (), R("mrun%d" % l))
            S.op("pool", lambda e, l=l: e.memset(Sst[l][:], 0.0), (), R("Sst%d" % l))
            S.op("pool", lambda e, l=l: e.memset(hT[l][:], 0.0), (), R("hT%d" % l))
            S.op("pool", lambda e, l=l: e.memset(cvh[l][:], 0.0), (), R("cvh%d" % l))
        t8 = sb("t8", [128, 64]); u8 = sb("u8", [128, 8]); w4 = sb("w4", [128, 4]); eb4 = sb("eb4", [128, 4])
        eB4 = sb("eB4", [128, 4]); uT = sb("uT", [4, 256]); sm4 = sb("sm4", [4, 64])
        SM = sb("SM", [128, 4, 128], BF16); VW = sb("VW", [128, 4, 129], BF16)
        hm = sb("hm", [128, 512]); hm2 = sb("hm2", [128, 512]); st4 = sb("st4", [128, 16])
        CTs = sb("CTs", [128, 129])
        big = sb("big", [128, 1024]); big2 = sb("big2", [128, 1024])
        bigb = sb("bigb", [128, 1024], BF16)
        S.dma("sp", big[:, :], segm_d, writes=R("big"))
        S.op("dve", lambda e: e.tensor_copy(out=cb[:, 320:1344], in_=big[:, :]), R("big"), R("cb"))
        nl = sb("nl", [128, 256]); eT = sb("eT", [128, 2, 2, 128]); qtT = sb("qtT", [128, 2, 128], BF16)
        ktT = sb("ktT", [128, 2, 128], BF16); kt_tm = sb("kt_tm", [128, 256], BF16)
        AM = sb("AM", [128, 4, 128], BF16); St = sb("St", [128, 128])
        actT = sb("actT", [128, 6, 128]); BCb = sb("BCb", [128, 2, 128], BF16)
        xs_tm = sb("xs_tm", [128, 512]); Bm_tm = sb("Bm_tm", [128, 128], BF16)
        dts = sb("dts", [128, 40]); cbm = sb("cbm", [128, 2, 128])
        xsb = sb("xsb", [128, 512], BF16); xsw = sb("xsw", [128, 512], BF16)
        eL = sb("eL", [128, 8])
        mixT = sb("mixT", [128, 8, 128], BF16); gsig = sb("gsig", [128, 128])
        lnt = sb("lnt", [128, 8])
        qTb = sb("qTb", [128, 16, 128], BF16)
        V16 = sb("V16", [128, 16, 16]); I16 = sb("I16", [128, 16, 16], U32); I16f = sb("I16f", [128, 16, 16])
        wk = sb("wk", [128, 256])
        TV = sb("TV", [128, 8, 16]); TPu = sb("TPu", [128, 8, 16], U32)
        pk = sb("pk", [128, 8, 128])
        eidx = sb("eidx", [128, 128], I32)
        thr16 = sb("thr16", [128, 16])
        S.op("dve", lambda e: e.tensor_scalar(out=thr16[:], in0=C("iota16"), scalar1=16.0, scalar2=16.0, op0=ALU.mult, op1=ALU.add), R("cst"), R("thr16"))
        sampbuf = {}
        sampres = {}
        SA = sb("SA", [128, 4096])
        SBb = sb("SBb", [128, 12288], BF16)
        sampbuf["n0T"] = sb("sp_n0T", [128, 4, 16]); sampres["n0T"] = res["sp_n0T"]
        SC = SA[:, 0:2048]
        res["SC"] = Res("SC")
        SBf = SBb[:, :].bitcast(F32)
        NGU = 8
        GU = [SBb[:, q * 1024:(q + 1) * 1024] for q in range(NGU)]
        pacc = SBf[:, 4096:5120]
        for q in range(NGU):
            res["GU%d" % q] = Res("GU%d" % q)
        res["acc"] = Res("acc")
        fsc = sb("fsc", [1, 4])
        peer_alias = ["SC", "acc"] + ["GU%d" % q for q in range(NGU)]

        def fence():
            S.op("dve", lambda e: e.memset(fsc[0:1, 0:1], 0.0), (), R("SA", "SBb", "fsc", *peer_alias))

        sampbuf["C0t"] = SA[:, 0:2048].rearrange("p (s d) -> p s d", d=128); sampres["C0t"] = res["SA"]
        sampbuf["Co"] = SA[:, 2048:4096].rearrange("p (s d) -> p s d", d=128); sampres["Co"] = res["SA"]
        sampbuf["mqS"] = SBb[:, 0:4096].rearrange("p (h t) -> p h t", t=1024); sampres["mqS"] = res["SBb"]
        sampbuf["CTbs"] = SBb[:, 4096:4096 + 2064].rearrange("p (s c) -> p s c", c=129); sampres["CTbs"] = res["SBb"]
        sampbuf["VWs"] = SBb[:, 6160:6160 + 2064].rearrange("p (s c) -> p s c", c=129); sampres["VWs"] = res["SBb"]

        def evac(eng, out, in_, rd, wr, scale=None):
            if eng == "act":
                if scale is None:
                    S.op("act", lambda e: e.copy(out=out, in_=in_), rd, wr)
                else:
                    S.op("act", lambda e: e.mul(out=out, in_=in_, mul=scale), rd, wr)
            else:
                if scale is None:
                    S.op("dve", lambda e: e.tensor_copy(out=out, in_=in_), rd, wr)
                else:
                    S.op("dve", lambda e: e.tensor_scalar(out=out, in0=in_, scalar1=scale, scalar2=None, op0=ALU.mult), rd, wr)

        def load_w2(src_ap, kch, width):
            i = wsn[0]
            wsn[0] = (i + 1) % 6
            bv = wpool[i][:, 0:kch * width].rearrange("p (k n) -> p k n", n=width)
            S.dma("sp", bv, src_ap.rearrange("(k p) n -> p k n", p=128), reads=R("scratch"), writes=[wpres[i]])
            return bv, wpres[i]

        def load_w(src_ap, width):
            return load_w2(src_ap, 8, width)

        def fm_piece(wb, wr_, a, b, N, kch=8, src=None, srcres=None):
            src = xT if src is None else src
            srcres = res["xT"] if srcres is None else srcres
            p, pr = bank()
            for k in range(kch):
                S.op("pe", lambda e, k=k: e.matmul(p[0:b - a, 0:N], lhsT=wb[:, k, a:b], rhs=src[:, k, 0:N],
                                                    start=(k == 0), stop=(k == kch - 1)), [wr_, srcres], [pr])
            return p, pr

        def tm_piece(wb, wr_, a, b, P, kch=8, src=None, srcres=None):
            src = xT if src is None else src
            srcres = res["xT"] if srcres is None else srcres
            p, pr = bank()
            for k in range(kch):
                S.op("pe", lambda e, k=k: e.matmul(p[0:P, 0:b - a], lhsT=src[:, k, 0:P], rhs=wb[:, k, a:b],
                                                    start=(k == 0), stop=(k == kch - 1)), [wr_, srcres], [pr])
            return p, pr

        def transpose_to(dst_fn, src, srcres, P, nch, dstres, eng="act"):
            for c0 in range(0, nch, 4):
                c1 = min(nch, c0 + 4)
                p, pr = bank()
                for c in range(c0, c1):
                    S.op("pe", lambda e, c=c: e.transpose(out=p[:, (c - c0) * 128:(c - c0) * 128 + P],
                                                          in_=src[0:P, c * 128:(c + 1) * 128], identity=ident[0:P, 0:P]),
                         [srcres, res["cst"]], [pr])
                pv = p[:, 0:(c1 - c0) * 128].rearrange("p (c t) -> p c t", t=128)[:, :, 0:P]
                evac(eng, dst_fn(c0, c1), pv, [pr], [dstres])

        def act_fn(out, in_, func, rd, wr, **kw):
            S.op("act", lambda e: e.activation(out=out, in_=in_, func=func, **kw), rd, wr)

        def softplus_neg(out, in_, rd, wr, tmp, tmpres, sign=-1.0):
            act_fn(tmp, in_, AF.Exp, rd, [tmpres], scale=sign)
            act_fn(out, tmp, AF.Ln, [tmpres], wr, bias=1.0)

        def rstd_from_ssq(out, ssq, n, rd, wr):
            act_fn(out, ssq, AF.Sqrt, rd, wr, scale=1.0 / n, bias=EPS)
            S.op("dve", lambda e: e.reciprocal(out=out, in_=out), wr, wr)

        def layernorm_tm(P, src, srcres, g_ap, b_ap, out, outres):
            S.op("dve", lambda e: e.tensor_reduce(out=lnt[0:P, 0:1], in_=src[0:P, :], axis=AX.X, op=ALU.add), [srcres], R("lnt"))
            S.op("dve", lambda e: e.tensor_scalar(out=lnt[0:P, 1:2], in0=lnt[0:P, 0:1], scalar1=-1.0 / D, scalar2=None, op0=ALU.mult), R("lnt"), R("lnt"))
            S.op("dve", lambda e: e.tensor_scalar(out=src[0:P, :], in0=src[0:P, :], scalar1=lnt[0:P, 1:2], scalar2=None, op0=ALU.add), [srcres] + R("lnt"), [srcres])
            S.op("dve", lambda e: e.tensor_tensor(out=big[0:P, :], in0=src[0:P, :], in1=src[0:P, :], op=ALU.mult), [srcres], R("big"))
            S.op("dve", lambda e: e.tensor_reduce(out=lnt[0:P, 2:3], in_=big[0:P, :], axis=AX.X, op=ALU.add), R("big"), R("lnt"))
            rstd_from_ssq(lnt[0:P, 3:4], lnt[0:P, 2:3], D, R("lnt"), R("lnt"))
            S.op("dve", lambda e: e.scalar_tensor_tensor(out=big[0:P, :], in0=src[0:P, :], scalar=lnt[0:P, 3:4], in1=g_ap, op0=ALU.mult, op1=ALU.mult),
                 [srcres] + R("lnt", "lnrow"), R("big"))
            S.op("dve", lambda e: e.tensor_tensor(out=out, in0=big[0:P, :], in1=b_ap, op=ALU.add), R("big", "lnrow"), [outres])

        dbg_i = [0]

        def dbg_out(name, ap, rd):
            if name in dbg_d:
                S.dma("sp", dbg_d[name][dbg_i[0], 0:ap.shape[0]], ap, reads=rd)

        cvn = [0]

        def convert(src2d, dst2d):
            n = src2d.shape[1]
            for a in range(0, n, 4096):
                b = min(n, a + 4096)
                i = cvn[0] % 2
                eng = ("dve", "act", "pool")[cvn[0] % 3]
                cvn[0] += 1
                fv = wst[i][:].rearrange("p k n -> p (k n)")[:, 0:b - a]
                bv = wbf[i][:].rearrange("p k n -> p (k n)")[:, 0:b - a]
                S.dma("sp", fv, src2d[:, a:b], writes=R("wst%d" % i))
                if eng == "act":
                    S.op("act", lambda e: e.copy(out=bv, in_=fv), R("wst%d" % i), R("wbf%d" % i))
                else:
                    S.op(eng, lambda e: e.tensor_copy(out=bv, in_=fv), R("wst%d" % i), R("wbf%d" % i))
                S.dma("sp", dst2d[:, a:b], bv, reads=R("wbf%d" % i), writes=R("scratch"))

        res["scratch"] = Res("scratch")
        flat = lambda ap: ap.rearrange("(p k) c -> p (k c)", p=128)
        if preconv:
            for l in range(nlayers):
                convert(flat(win_d[l]), flat(wins[l]))
                convert(flat(wbr_d[l]), flat(wbrs[l]))
                convert(flat(wout_d[l]), flat(wouts[l]))
                convert(flat(wq_d[l]), flat(wqs[l]))
                convert(keys_d[l].rearrange("c a k -> c (a k)"), keyss[l])
                if do_peer:
                    convert(flat(pu_d[l]), flat(pus[l]))
                    convert(flat(pv_d[l]), flat(pvs[l]))
        wpool = [wbf[0][:].rearrange("p k n -> p (k n)"), wbf[1][:].rearrange("p k n -> p (k n)")]
        for i in range(2):
            v = wst[i][:].rearrange("p k n -> p (k n)").bitcast(BF16)
            wpool += [v[:, 0:4224], v[:, 4224:8448]]
        wpres = [Res("wp%d" % i) for i in range(6)]
        S.op("dve", lambda e: e.memset(fsc[0:1, 1:2], 0.0), (), R("wst0", "wst1", "wbf0", "wbf1", "scratch", "fsc") + wpres)

        all_tiles = [("p", i) for i in range(SEQ // 128)] + [("s", 0)]
        if tiles is not None:
            all_tiles = tiles
        last_prompt = SEQ // 128 - 1
        for kind, ti in all_tiles:
            samp = kind == "s"
            P = 64 if samp else 128
            t0 = SEQ if samp else ti * 128
            N = P
            S.dma("sp", X[0:P, :], x_d[t0:t0 + P, :], writes=R("X"))
            for l in range(nlayers):
                le_f = C("leS")[0:P, :] if samp else C("leP")
                gt_f = C("gtS")[0:P, :] if samp else C("gtP")
                le_b = cb[0:P, 256:320] if samp else cb[:, 0:128]
                S.dma("sp", rowp[:], rowp_d[l], writes=R("rowp"))
                S.dma("sp", convp[:], convp_d[l], writes=R("convp"))
                S.dma("sp", gaupf[:], gaup_d[l], writes=R("gaupf"))
                S.op("dve", lambda e: e.tensor_copy(out=gaupb[:], in_=gaupf[:]), R("gaupf"), R("gaupb"))
                rp = lambda n: rowp[0:P, RP[n][0]:RP[n][1]]
                act_fn(arow[:], rowp[:, RP["alog"][0]:RP["alog"][1]], AF.Exp, R("rowp"), R("arow"))
                transpose_to(lambda c0, c1: xT[:, c0:c1, 0:P], X, res["X"], P, 8, res["xT"])
                W = lambda a, b: wins[l, :, a:b]
                wb, wr_ = load_w(W(0, 512), 512)
                for h in range(4):
                    p, pr = fm_piece(wb, wr_, h * 128, (h + 1) * 128, N)
                    evac("act", mqT[:, h, 0:N], p[:, 0:N], [pr], R("mqT"), scale=128 ** -0.5)
                wb, wr_ = load_w(W(512, 1024), 512)
                for h in range(4):
                    p, pr = fm_piece(wb, wr_, h * 128, (h + 1) * 128, N)
                    evac("act", mkT[:, h, 0:N], p[:, 0:N], [pr], R("mkT"))
                p, pr = tm_piece(wb, wr_, 0, 512, P)
                evac("dve", mk_tm[0:P, :], p[0:P, :], [pr], R("mk_tm"))
                wb, wr_ = load_w(W(1024, 1536), 512)
                p, pr = tm_piece(wb, wr_, 0, 512, P)
                evac("act", mv_tm[0:P, :], p[0:P, :], [pr], R("mv_tm"))
                wb, wr_ = load_w(W(1536, 2048), 512)
                p, pr = tm_piece(wb, wr_, 0, 512, P)
                evac("dve", mo_tm[0:P, :], p[0:P, :], [pr], R("mo_tm"))
                wb, wr_ = load_w(W(2048, 2568), 520)
                p, pr = tm_piece(wb, wr_, 0, 8, P)
                evac("act", mif[0:P, :], p[0:P, 0:8], [pr], R("mif"))
                for j in range(2):
                    p, pr = fm_piece(wb, wr_, 8 + j * 128, 8 + (j + 1) * 128, N)
                    evac("act", gqT[:, j, 0:N], p[:, 0:N], [pr], R("gqT"), scale=64 ** -0.5)
                    p, pr = fm_piece(wb, wr_, 264 + j * 128, 264 + (j + 1) * 128, N)
                    evac("dve", gkT[:, j, 0:N], p[:, 0:N], [pr], R("gkT"))
                p, pr = tm_piece(wb, wr_, 264, 520, P)
                evac("act", gk_tm[0:P, :], p[0:P, 0:256], [pr], R("gk_tm"))
                wb, wr_ = load_w(W(2568, 3080), 512)
                p, pr = tm_piece(wb, wr_, 0, 512, P)
                evac("dve", gv_tm[0:P, :], p[0:P, :], [pr], R("gv_tm"))
                wb, wr_ = load_w(W(3080, 3608), 528)
                p, pr = tm_piece(wb, wr_, 0, 512, P)
                evac("act", gr_tm[0:P, :], p[0:P, :], [pr], R("gr_tm"))
                p, pr = fm_piece(wb, wr_, 512, 528, N)
                evac("dve", gaT[0:16, 0:N], p[0:16, 0:N], [pr], R("gaT"))
                wb, wr_ = load_w(W(3608, 4120), 512)
                p, pr = tm_piece(wb, wr_, 0, 512, P)
                evac("act", sz_tm[0:P, :], p[0:P, :], [pr], R("sz_tm"))
                if samp:
                    cv0 = big2[0:48, 0:768]
                    S.dma("sp", cv0, scv_d[l], writes=R("big2"))
                    for c0 in (0, 4):
                        p, pr = bank()
                        for c in range(c0, min(6, c0 + 4)):
                            S.op("pe", lambda e, c=c: e.transpose(out=p[:, (c - c0) * 128:(c - c0) * 128 + 48], in_=cv0[:, c * 128:(c + 1) * 128],
                                                                  identity=ident[0:48, 0:48]), R("big2", "cst"), [pr])
                        ncn = min(6, c0 + 4) - c0
                        pv = p[:, 0:ncn * 128].rearrange("p (c t) -> p c t", t=128)[:, :, 0:48].rearrange("p c (s j) -> p c s j", j=3)
                        dst = xp[:, c0:c0 + ncn, 0:NSEG * 7].rearrange("p c (s j) -> p c s j", j=7)[:, :, :, 0:3]
                        evac("dve", dst, pv, [pr], R("xp"))
                else:
                    S.op("dve", lambda e: e.tensor_copy(out=xp[:, :, 0:3], in_=cvh[l][:]), R("cvh%d" % l), R("xp"))
                wb, wr_ = load_w(W(4120, 4632), 512)
                wb2, wr2 = load_w(W(4632, 4896), 264)
                for c in range(6):
                    if c < 4:
                        p, pr = fm_piece(wb, wr_, c * 128, (c + 1) * 128, N)
                    else:
                        p, pr = fm_piece(wb2, wr2, (c - 4) * 128, (c - 3) * 128, N)
                    if samp:
                        dst = xp[:, c, 0:NSEG * 7].rearrange("p (s j) -> p s j", j=7)[:, :, 3:7]
                        evac("act", dst, p[:, 0:N].rearrange("p (s j) -> p s j", j=4), [pr], R("xp"))
                    else:
                        evac("act", xp[:, c, 3:3 + N], p[:, 0:N], [pr], R("xp"))
                p, pr = tm_piece(wb2, wr2, 256, 264, P)
                evac("dve", sdt[0:P, :], p[0:P, 0:8], [pr], R("sdt"))
                if not samp:
                    S.op("dve", lambda e: e.tensor_copy(out=cvh[l][:], in_=xp[:, :, N:N + 3]), R("xp"), R("cvh%d" % l))
                if samp or (ti == last_prompt and (LASTM & 1)):
                    nr = 48 if samp else 3
                    cvt = big[:, 0:6 * 48].rearrange("p (c r) -> p c r", r=48)
                    if samp:
                        srcv = xp[:, :, 0:NSEG * 7].rearrange("p c (s j) -> p c s j", j=7)[:, :, :, 4:7]
                        S.op("dve", lambda e: e.tensor_copy(out=cvt.rearrange("p c (s j) -> p c s j", j=3), in_=srcv), R("xp"), R("big"))
                    else:
                        S.op("dve", lambda e: e.tensor_copy(out=cvt[:, :, 0:3], in_=xp[:, :, N:N + 3]), R("xp"), R("big"))
                    for c0 in (0, 4):
                        p, pr = bank()
                        ncn = min(6, c0 + 4) - c0
                        for c in range(c0, c0 + ncn):
                            S.op("pe", lambda e, c=c: e.transpose(out=p[0:nr, (c - c0) * 128:(c - c0 + 1) * 128], in_=cvt[:, c, 0:nr], identity=ident),
                                 R("big", "cst"), [pr])
                        evac("act", big2[0:nr, c0 * 128:(c0 + ncn) * 128], p[0:nr, 0:ncn * 128], [pr], R("big2"))
                    S.dma("sp", (ocv_d[l] if samp else pcv_d[l]), big2[0:nr, 0:768], reads=R("big2"))

                nseg = NSEG if samp else 1
                L = P // nseg
                S.op("dve", lambda e: e.tensor_tensor(out=t8[0:P, 0:4], in0=mif[0:P, 0:4], in1=rp("mib"), op=ALU.add), R("mif", "rowp"), R("t8"))
                S.op("dve", lambda e: e.tensor_tensor(out=t8[0:P, 4:8], in0=mif[0:P, 4:8], in1=rp("mfb"), op=ALU.add), R("mif", "rowp"), R("t8"))
                softplus_neg(t8[0:P, 12:16], t8[0:P, 4:8], R("t8"), R("t8"), t8[0:P, 8:12], res["t8"])
                p, pr = bank()
                S.op("pe", lambda e: e.matmul(p[0:P, 0:4], lhsT=le_f, rhs=t8[0:P, 12:16], start=True, stop=True), R("cst", "t8"), [pr])
                S.op("pe", lambda e: e.matmul(p[:, 8:12], lhsT=ones[0:P, :], rhs=t8[0:P, 12:16], start=True, stop=True), R("cst", "t8"), [pr])
                S.op("dve", lambda e: e.tensor_tensor(out=u8[0:P, 0:4], in0=p[0:P, 0:4], in1=t8[0:P, 0:4], op=ALU.add), [pr] + R("t8"), R("u8"))
                S.op("dve", lambda e: e.tensor_copy(out=u8[0:P, 4:8], in_=p[0:P, 0:4]), [pr], R("u8"))
                act_fn(w4[0:P, :], u8[0:P, 0:4], AF.Exp, R("u8"), R("w4"))
                act_fn(eb4[0:P, :], p[0:P, 0:4], AF.Exp, [pr], R("eb4"), scale=-1.0)
                act_fn(eB4[:, :], p[:, 8:12], AF.Exp, [pr], R("eB4"), scale=-1.0)
                p2, pr2 = bank()
                S.op("pe", lambda e: e.transpose(out=p2[0:4, 0:P], in_=u8[0:P, 0:4], identity=ident[0:P, 0:P]), R("u8", "cst"), [pr2])
                S.op("pe", lambda e: e.transpose(out=p2[0:4, 128:128 + P], in_=u8[0:P, 4:8], identity=ident[0:P, 0:P]), R("u8", "cst"), [pr2])
                S.op("dve", lambda e: e.tensor_copy(out=uT[:, :], in_=p2[0:4, 0:256]), [pr2], R("uT"))
                S.op("dve", lambda e: e.tensor_reduce(out=sm4[:, 0:nseg], in_=uT[:, 0:P].rearrange("h (s j) -> h s j", j=L), axis=AX.X, op=ALU.max), R("uT"), R("sm4"))
                blast = uT[:, 128:128 + P].rearrange("h (s j) -> h s j", j=L)[:, :, L - 1]
                if not samp:
                    S.op("dve", lambda e: e.tensor_tensor(out=mrun[l][:], in0=mrun[l][:], in1=sm4[:, 0:1], op=ALU.max), R("mrun%d" % l, "sm4"), R("mrun%d" % l))
                    S.op("dve", lambda e: e.tensor_tensor(out=mrun[l][:], in0=mrun[l][:], in1=blast, op=ALU.subtract), R("mrun%d" % l, "uT"), R("mrun%d" % l))
                else:
                    S.dma("sp", sm4[:, 16:32], sm_d[l].rearrange("s h -> h s"), writes=R("sm4"), slow=True)
                    S.op("dve", lambda e: e.tensor_tensor(out=sm4[:, 32:48], in0=sm4[:, 0:16], in1=sm4[:, 16:32], op=ALU.max), R("sm4"), R("sm4"))
                    S.op("dve", lambda e: e.tensor_tensor(out=sm4[:, 48:64], in0=sm4[:, 32:48], in1=blast, op=ALU.subtract), R("sm4", "uT"), R("sm4"))
                    S.dma("sp", om_d[l].rearrange("s h -> h s"), sm4[:, 48:64], reads=R("sm4"), slow=True)
                    act_fn(sm4[:, 0:16], sm4[:, 32:48], AF.Exp, R("sm4"), R("sm4"), scale=-1.0)
                    act_fn(sm4[:, 48:64], sm4[:, 16:32], AF.Exp, R("sm4"), R("sm4"))
                    for q, (a0) in enumerate((0, 48)):
                        S.op("dve", lambda e, q=q, a0=a0: e.tensor_tensor(
                            out=hm2[0:4, q * 64:(q + 1) * 64].rearrange("p (h s) -> p h s", s=16),
                            in0=sm4[:, a0:a0 + 16].unsqueeze(1).to_broadcast([4, 4, 16]),
                            in1=ident[0:4, 0:4].unsqueeze(2).to_broadcast([4, 4, 16]), op=ALU.mult), R("sm4", "cst"), R("hm2"))
                    p3, pr3 = bank()
                    S.op("pe", lambda e: e.matmul(p3[:, 0:128], lhsT=ones[0:4, :], rhs=hm2[0:4, 0:128], start=True, stop=True), R("cst", "hm2"), [pr3])
                    S.op("dve", lambda e: e.tensor_copy(out=big2[:, 0:128], in_=p3[:, 0:128]), [pr3], R("big2"))
                    S.op("dve", lambda e: e.tensor_tensor(out=big2[:, 128:192], in0=big2[:, 0:64], in1=big2[:, 64:128], op=ALU.mult), R("big2"), R("big2"))
                for h in range(4):
                    p, pr = bank()
                    S.op("pe", lambda e, h=h: e.matmul(p[0:P, 0:P], lhsT=mkT[:, h, 0:P], rhs=mqT[:, h, 0:P], start=True, stop=True), R("mkT", "mqT"), [pr])
                    S.op("dve", lambda e, h=h: e.tensor_tensor(out=SM[0:P, h, 0:P], in0=p[0:P, 0:P], in1=le_f[:, 0:P], op=ALU.mult), [pr] + R("cst"), R("SM"))
                S.op("dve", lambda e: e.tensor_tensor(out=VW[0:P, :, 0:128], in0=mv_tm[0:P, :].rearrange("p (h v) -> p h v", v=128),
                                                      in1=w4[0:P, :].unsqueeze(2).to_broadcast([P, 4, 128]), op=ALU.mult), R("mv_tm", "w4"), R("VW"))
                S.op("dve", lambda e: e.tensor_copy(out=VW[0:P, :, 128:129], in_=w4[0:P, :].unsqueeze(2)), R("w4"), R("VW"))
                if not samp:
                    S.op("act", lambda e: e.copy(out=CTb[:], in_=CT[l][:]), R("CT%d" % l), R("CTb"))
                else:
                    pass
                if samp:
                    CTbs = sampbuf["CTbs"]
                    C0t = sampbuf["C0t"]
                    n0T = sampbuf["n0T"]
                    for hh in range(4):
                        S.dma("sp", n0T[:, hh, :], sn_d[l, :, hh, :].rearrange("s d -> d s"), writes=[sampres["n0T"]], slow=True)
                    mqS = sampbuf["mqS"]
                    S.op("dve", lambda e: e.tensor_tensor(out=mqS[:].rearrange("p h (s t) -> p h s t", t=64),
                                                          in0=mqT[:, :, 0:64].unsqueeze(2).to_broadcast([128, 4, 16, 64]),
                                                          in1=cb[:, 320:1344].rearrange("p (s t) -> p s t", t=64).unsqueeze(1).to_broadcast([128, 4, 16, 64]),
                                                          op=ALU.mult), R("mqT", "cb"), [sampres["mqS"]])
                nd_banks = [(PS[6], PSR[6]), (PS[7], PSR[7])]
                for h in range(4):
                    ndp, ndr = nd_banks[h // 2]
                    nd = ndp[0:P, (h % 2) * 129:(h % 2) * 129 + 129]
                    S.op("pe", lambda e, h=h, nd=nd: e.matmul(nd, lhsT=SM[0:P, h, 0:P], rhs=VW[0:P, h, :], start=True, stop=False), R("SM", "VW"), [ndr])
                    if not samp:
                        S.op("pe", lambda e, h=h, nd=nd: e.matmul(nd, lhsT=mqT[:, h, 0:P], rhs=CTb[:, h, :], start=False, stop=True), R("mqT", "CTb"), [ndr])
                    else:
                        S.dma("sp", C0t[:], sC_d[l, :, h].rearrange("s v d -> v s d"), writes=[sampres["C0t"]])
                        for s in range(NSEG):
                            p, pr = bank()
                            S.op("pe", lambda e, s=s: e.transpose(out=p[:, 0:128], in_=C0t[:, s, :], identity=ident), [sampres["C0t"]] + R("cst"), [pr])
                            S.op("dve", lambda e, s=s, h=h: e.tensor_scalar(out=CTbs[:, s, 0:128], in0=p[:, 0:128], scalar1=big2[:, 64 + h * 16 + s:64 + h * 16 + s + 1],
                                                                           scalar2=None, op0=ALU.mult), [pr] + R("big2"), [sampres["CTbs"]])
                        S.op("dve", lambda e, h=h: e.tensor_tensor(out=CTbs[:, :, 128], in0=n0T[:, h, :], in1=big2[:, 64 + h * 16:64 + h * 16 + 16], op=ALU.mult),
                             [sampres["n0T"]] + R("big2"), [sampres["CTbs"]])
                        for s in range(NSEG):
                            S.op("pe", lambda e, s=s, h=h, nd=nd: e.matmul(nd, lhsT=mqS[:, h, s * 64:(s + 1) * 64], rhs=CTbs[:, s, :], start=False, stop=(s == NSEG - 1)),
                                 [sampres["mqS"], sampres["CTbs"]], [ndr])
                        VWs = sampbuf["VWs"]
                        S.op("dve", lambda e, h=h: e.tensor_tensor(out=VWs[0:64, :, :], in0=VW[0:64, h, :].unsqueeze(1).to_broadcast([64, 16, 129]),
                                                                   in1=C("segind")[0:64, :].unsqueeze(2).to_broadcast([64, 16, 129]), op=ALU.mult),
                             R("VW", "cst"), [sampres["VWs"]])
                        Co = sampbuf["Co"]
                        for s in range(NSEG):
                            p, pr = bank()
                            S.op("pe", lambda e, s=s, h=h: e.matmul(p[:, 0:128], lhsT=VWs[0:64, s, 0:128], rhs=mk_tm[0:64, h * 128:(h + 1) * 128], start=True, stop=True),
                                 [sampres["VWs"]] + R("mk_tm"), [pr])
                            col = h * 16 + s
                            S.op("dve", lambda e, s=s, col=col: e.tensor_scalar(out=St[:, :], in0=C0t[:, s, :], scalar1=big2[:, 128 + col:129 + col], scalar2=None, op0=ALU.mult),
                                 [sampres["C0t"]] + R("big2"), R("St"))
                            S.op("dve", lambda e, s=s, col=col: e.scalar_tensor_tensor(out=Co[:, s, :], in0=p[:, 0:128], scalar=big2[:, col:col + 1], in1=St[:, :],
                                                                                      op0=ALU.mult, op1=ALU.add), [pr] + R("big2", "St"), [sampres["Co"]])
                        S.dma("sp", oC_d[l, :, h].rearrange("s v d -> v s d"), Co[:], reads=[sampres["Co"]])
                        S.op("dve", lambda e, h=h: e.tensor_scalar(out=bigb[0:64, 0:16], in0=C("segind")[0:64, :], scalar1=w4[0:64, h:h + 1], scalar2=None, op0=ALU.mult),
                             R("cst", "w4"), R("bigb"))
                        p, pr = bank()
                        S.op("pe", lambda e, h=h: e.matmul(p[:, 0:16], lhsT=mk_tm[0:64, h * 128:(h + 1) * 128], rhs=bigb[0:64, 0:16], start=True, stop=True), R("mk_tm", "bigb"), [pr])
                        S.op("dve", lambda e, h=h: e.tensor_tensor(out=hm2[:, 256 + h * 16:256 + (h + 1) * 16], in0=n0T[:, h, :], in1=big2[:, 128 + h * 16:128 + (h + 1) * 16], op=ALU.mult),
                             [sampres["n0T"]] + R("big2"), R("hm2"))
                        S.op("dve", lambda e, h=h: e.tensor_tensor(out=hm2[:, 320 + h * 16:320 + (h + 1) * 16], in0=p[:, 0:16], in1=big2[:, h * 16:(h + 1) * 16], op=ALU.mult),
                             [pr] + R("big2"), R("hm2"))
                if samp:
                    S.op("dve", lambda e: e.tensor_tensor(out=hm2[:, 256:320], in0=hm2[:, 256:320], in1=hm2[:, 320:384], op=ALU.add), R("hm2"), R("hm2"))
                    for hh in range(4):
                        S.dma("sp", on_d[l, :, hh, :].rearrange("s d -> d s"), hm2[:, 256 + hh * 16:256 + (hh + 1) * 16], reads=R("hm2"), slow=True)
                for j in range(2):
                    ndp, ndr = nd_banks[j]
                    S.op("dve", lambda e, j=j, ndp=ndp: e.tensor_tensor(out=st4[0:P, 2 * j:2 * j + 2], in0=ndp[0:P, 0:258].rearrange("p (h c) -> p h c", c=129)[:, :, 128],
                                                                        in1=eb4[0:P, 2 * j:2 * j + 2], op=ALU.mult), [ndr] + R("eb4"), R("st4"))
                S.op("dve", lambda e: e.tensor_scalar(out=st4[0:P, 4:8], in0=st4[0:P, 0:4], scalar1=-1.0, scalar2=1.0, op0=ALU.mult, op1=ALU.max), R("st4"), R("st4"))
                S.op("dve", lambda e: e.tensor_tensor(out=st4[0:P, 4:8], in0=st4[0:P, 4:8], in1=st4[0:P, 0:4], op=ALU.max), R("st4"), R("st4"))
                S.op("dve", lambda e: e.reciprocal(out=st4[0:P, 8:12], in_=st4[0:P, 4:8]), R("st4"), R("st4"))
                S.op("dve", lambda e: e.tensor_tensor(out=st4[0:P, 12:16], in0=st4[0:P, 8:12], in1=eb4[0:P, :], op=ALU.mult), R("st4", "eb4"), R("st4"))
                for h in range(4):
                    ndp, ndr = nd_banks[h // 2]
                    S.op("dve", lambda e, h=h, ndp=ndp: e.tensor_scalar(out=hm[0:P, h * 128:(h + 1) * 128], in0=ndp[0:P, (h % 2) * 129:(h % 2) * 129 + 128],
                                                                        scalar1=st4[0:P, 12 + h:13 + h], scalar2=None, op0=ALU.mult), [ndr] + R("st4"), R("hm"))
                hm3 = hm[0:P, :].rearrange("p (h v) -> p h v", v=128)
                S.op("dve", lambda e: e.tensor_reduce(out=st4[0:P, 0:4], in_=hm3, axis=AX.X, op=ALU.add), R("hm"), R("st4"))
                S.op("dve", lambda e: e.tensor_scalar(out=st4[0:P, 0:4], in0=st4[0:P, 0:4], scalar1=1.0 / 128, scalar2=None, op0=ALU.mult), R("st4"), R("st4"))
                S.op("dve", lambda e: e.tensor_tensor(out=hm3, in0=hm3, in1=st4[0:P, 0:4].unsqueeze(2).to_broadcast([P, 4, 128]), op=ALU.subtract), R("hm", "st4"), R("hm"))
                S.op("dve", lambda e: e.tensor_tensor(out=hm2[0:P, 0:512], in0=hm[0:P, :], in1=hm[0:P, :], op=ALU.mult), R("hm"), R("hm2"))
                S.op("dve", lambda e: e.tensor_reduce(out=st4[0:P, 4:8], in_=hm2[0:P, 0:512].rearrange("p (h v) -> p h v", v=128), axis=AX.X, op=ALU.add), R("hm2"), R("st4"))
                rstd_from_ssq(st4[0:P, 8:12], st4[0:P, 4:8], 128, R("st4"), R("st4"))
                S.op("dve", lambda e: e.tensor_tensor(out=hm3, in0=hm3, in1=st4[0:P, 8:12].unsqueeze(2).to_broadcast([P, 4, 128]), op=ALU.mult), R("hm", "st4"), R("hm"))
                S.op("dve", lambda e: e.tensor_tensor(out=hm[0:P, :], in0=hm[0:P, :], in1=rp("mnorm"), op=ALU.mult), R("hm", "rowp"), R("hm"))
                act_fn(hm2[0:P, 0:512], mo_tm[0:P, :], AF.Sigmoid, R("mo_tm"), R("hm2"))
                S.op("dve", lambda e: e.tensor_tensor(out=br[0:P, 0:512], in0=hm[0:P, :], in1=hm2[0:P, 0:512], op=ALU.mult), R("hm", "hm2"), R("br"))
                if not samp:
                    for h in range(4):
                        p, pr = bank()
                        S.op("pe", lambda e, h=h: e.matmul(p[:, 0:129], lhsT=mk_tm[0:P, h * 128:(h + 1) * 128], rhs=VW[0:P, h, :], start=True, stop=True), R("mk_tm", "VW"), [pr])
                        S.op("dve", lambda e, h=h: e.tensor_scalar(out=CTs[:, :], in0=CT[l][:, h, :], scalar1=eB4[:, h:h + 1], scalar2=None, op0=ALU.mult),
                             R("CT%d" % l, "eB4"), R("CTs"))
                        S.op("dve", lambda e, h=h: e.scalar_tensor_tensor(out=CT[l][:, h, :], in0=p[:, 0:129], scalar=eB4[:, h:h + 1], in1=CTs[:, :], op0=ALU.mult, op1=ALU.add),
                             [pr] + R("eB4", "CTs"), R("CT%d" % l))
                    if ti == last_prompt and (LASTM & 2):
                        act_fn(sm4[:, 0:1], mrun[l][:], AF.Exp, R("mrun%d" % l), R("sm4"), scale=-1.0)
                        S.op("dve", lambda e: e.tensor_tensor(out=hm2[0:4, 0:4], in0=sm4[:, 0:1].to_broadcast([4, 4]), in1=ident[0:4, 0:4], op=ALU.mult), R("sm4", "cst"), R("hm2"))
                        p3, pr3 = bank()
                        S.op("pe", lambda e: e.matmul(p3[:, 0:4], lhsT=ones[0:4, :], rhs=hm2[0:4, 0:4], start=True, stop=True), R("cst", "hm2"), [pr3])
                        S.op("dve", lambda e: e.tensor_copy(out=st4[:, 0:4], in_=p3[:, 0:4]), [pr3], R("st4"))
                        for h in range(4):
                            p, pr = bank()
                            S.op("pe", lambda e, h=h: e.transpose(out=p[:, 0:128], in_=CT[l][:, h, 0:128], identity=ident), R("CT%d" % l, "cst"), [pr])
                            S.op("dve", lambda e, h=h: e.tensor_scalar(out=hm2[:, h * 128:(h + 1) * 128], in0=p[:, 0:128], scalar1=st4[:, h:h + 1], scalar2=None, op0=ALU.mult),
                                 [pr] + R("st4"), R("hm2"))
                        S.dma("sp", pC_d[l].rearrange("h v d -> v h d"), hm2[:, 0:512].rearrange("p (h d) -> p h d", d=128), reads=R("hm2"))
                        S.op("dve", lambda e: e.tensor_tensor(out=st4[:, 4:8], in0=CT[l][:, :, 128], in1=st4[:, 0:4], op=ALU.mult), R("CT%d" % l, "st4"), R("st4"))
                        S.dma("sp", pn_d[l].rearrange("h d -> d h"), st4[:, 4:8], reads=R("st4"), slow=True)
                        S.dma("sp", pm_d[l].rearrange("(h o) -> h o", o=1), mrun[l][:], reads=R("mrun%d" % l), slow=True)


                p, pr = bank()
                S.op("pe", lambda e: e.matmul(p[0:P, 0:256], lhsT=gaT[0:16, 0:P], rhs=gaupb[0:16, :], start=True, stop=True), R("gaT", "gaupb"), [pr])
                S.op("dve", lambda e: e.tensor_tensor(out=big[0:P, 0:256], in0=p[0:P, 0:256], in1=rp("gab"), op=ALU.add), [pr] + R("rowp"), R("big"))
                softplus_neg(nl[0:P, :], big[0:P, 0:256], R("big"), R("nl"), big[0:P, 256:512], res["big"])
                S.op("dve", lambda e: e.tensor_scalar(out=nl[0:P, :], in0=nl[0:P, :], scalar1=1.0 / 16, scalar2=None, op0=ALU.mult), R("nl"), R("nl"))
                p1, pr1 = bank()
                S.op("pe", lambda e: e.matmul(p1[0:P, 0:256], lhsT=le_f, rhs=nl[0:P, :], start=True, stop=True), R("cst", "nl"), [pr1])
                p2, pr2 = bank()
                for j in range(2):
                    S.op("pe", lambda e, j=j: e.matmul(p2[:, j * 128:j * 128 + P], lhsT=nl[0:P, j * 128:(j + 1) * 128], rhs=le_f[:, 0:P], start=True, stop=True), R("cst", "nl"), [pr2])
                for j in range(2):
                    act_fn(eT[:, 0, j, 0:P], p2[:, j * 128:j * 128 + P], AF.Exp, [pr2], R("eT"))
                    act_fn(eT[:, 1, j, 0:P], p2[:, j * 128:j * 128 + P], AF.Exp, [pr2], R("eT"), scale=-1.0)
                S.op("dve", lambda e: e.tensor_tensor(out=qtT[:, :, 0:P], in0=gqT[:, :, 0:P], in1=eT[:, 1, :, 0:P], op=ALU.mult), R("gqT", "eT"), R("qtT"))
                S.op("dve", lambda e: e.tensor_tensor(out=ktT[:, :, 0:P], in0=gkT[:, :, 0:P], in1=eT[:, 0, :, 0:P], op=ALU.mult), R("gkT", "eT"), R("ktT"))
                act_fn(big[0:P, 256:512], p1[0:P, 0:256], AF.Exp, [pr1], R("big"))
                S.op("dve", lambda e: e.tensor_tensor(out=kt_tm[0:P, :], in0=gk_tm[0:P, :], in1=big[0:P, 256:512], op=ALU.mult), R("gk_tm", "big"), R("kt_tm"))
                for h in range(4 if GST >= 2 else 0):
                    j, off = h // 2, (h % 2) * 64
                    p, pr = bank()
                    S.op("pe", lambda e, j=j, off=off: e.matmul(p[0:P, 0:P], lhsT=ktT[off:off + 64, j, 0:P], rhs=qtT[off:off + 64, j, 0:P], start=True, stop=True), R("ktT", "qtT"), [pr])
                    S.op("dve", lambda e, h=h: e.tensor_tensor(out=AM[0:P, h, 0:P], in0=p[0:P, 0:P], in1=le_f[:, 0:P], op=ALU.mult), [pr] + R("cst"), R("AM"))
                po, por = PS[6], PSR[6]
                if not samp:
                    S.op("act", lambda e: e.copy(out=Sb[:], in_=Sst[l][:]), R("Sst%d" % l), R("Sb"))
                else:
                    S0t = SA[:, :].rearrange("p (s j v) -> p s j v", j=2, v=128)
                    S0b = SBb[:, 0:4096].rearrange("p (s j v) -> p s j v", j=2, v=128)
                    qtS = SBb[:, 4096:6144].rearrange("p (j s t) -> p j s t", s=16, t=64)
                    qtS2 = SBb[:, 6144:8192].rearrange("p (j s t) -> p j s t", s=16, t=64)
                    ktS = SBb[0:64, 8192:12288].rearrange("p (s c) -> p s c", c=256)
                    for j in range(2):
                        S.dma("sp", S0t[:, :, j, :], sS_d[l, :, j * 128:(j + 1) * 128, :].rearrange("s p v -> p s v"), writes=R("SA"))
                    S.op("act", lambda e: e.copy(out=SBb[:, 0:4096], in_=SA[:, :]), R("SA"), R("SBb"))
                    S.op("dve", lambda e: e.tensor_tensor(out=qtS, in0=qtT[:, :, 0:64].unsqueeze(2).to_broadcast([128, 2, 16, 64]),
                                                          in1=cb[:, 320:1344].rearrange("p (s t) -> p s t", t=64).unsqueeze(1).to_broadcast([128, 2, 16, 64]), op=ALU.mult),
                         R("qtT", "cb"), R("SBb"))
                    S.op("dve", lambda e: e.tensor_scalar(out=SBb[:, 6144:8192], in0=SBb[:, 4096:6144], scalar1=C("halfm")[:, 1:2], scalar2=None, op0=ALU.mult), R("SBb", "cst"), R("SBb"))
                    S.op("dve", lambda e: e.tensor_scalar(out=SBb[:, 4096:6144], in0=SBb[:, 4096:6144], scalar1=C("halfm")[:, 0:1], scalar2=None, op0=ALU.mult), R("SBb", "cst"), R("SBb"))
                    S.op("dve", lambda e: e.tensor_tensor(out=ktS, in0=kt_tm[0:64, :].unsqueeze(1).to_broadcast([64, 16, 256]),
                                                          in1=C("segind")[0:64, :].unsqueeze(2).to_broadcast([64, 16, 256]), op=ALU.mult), R("kt_tm", "cst"), R("SBb"))
                for h in range(4 if GST >= 3 else 0):
                    j, off = h // 2, (h % 2) * 64
                    og = po[0:P, h * 128:(h + 1) * 128]
                    S.op("pe", lambda e, h=h, og=og: e.matmul(og, lhsT=AM[0:P, h, 0:P], rhs=gv_tm[0:P, h * 128:(h + 1) * 128], start=True, stop=False), R("AM", "gv_tm"), [por])
                    if not samp:
                        S.op("pe", lambda e, j=j, off=off, og=og: e.matmul(og, lhsT=qtT[off:off + 64, j, 0:P], rhs=Sb[off:off + 64, j, :], start=False, stop=True), R("qtT", "Sb"), [por])
                    else:
                        for s_ in range(NSEG):
                            qq = qtS if off == 0 else qtS2
                            S.op("pe", lambda e, j=j, qq=qq, og=og, s_=s_: e.matmul(og, lhsT=qq[:, j, s_, :], rhs=S0b[:, s_, j, :], start=False, stop=(s_ == NSEG - 1)),
                                 R("SBb"), [por])
                evac("act", hm[0:P, :], po[0:P, :], [por], R("hm"))
                S.op("dve", lambda e: e.tensor_tensor(out=hm2[0:P, 0:512], in0=hm[0:P, :], in1=hm[0:P, :], op=ALU.mult), R("hm"), R("hm2"))
                S.op("dve", lambda e: e.tensor_reduce(out=st4[0:P, 4:8], in_=hm2[0:P, 0:512].rearrange("p (h v) -> p h v", v=128), axis=AX.X, op=ALU.add), R("hm2"), R("st4"))
                rstd_from_ssq(st4[0:P, 8:12], st4[0:P, 4:8], 128, R("st4"), R("st4"))
                S.op("dve", lambda e: e.tensor_tensor(out=hm3, in0=hm3, in1=st4[0:P, 8:12].unsqueeze(2).to_broadcast([P, 4, 128]), op=ALU.mult), R("hm", "st4"), R("hm"))
                S.op("dve", lambda e: e.tensor_tensor(out=hm[0:P, :], in0=hm[0:P, :], in1=rp("gnorm"), op=ALU.mult), R("hm", "rowp"), R("hm"))
                act_fn(hm2[0:P, 0:512], gr_tm[0:P, :], AF.Silu, R("gr_tm"), R("hm2"))
                S.op("dve", lambda e: e.tensor_tensor(out=br[0:P, 512:1024], in0=hm[0:P, :], in1=hm2[0:P, 0:512], op=ALU.mult), R("hm", "hm2"), R("br"))
                for s_ in range(nseg if GST >= 5 else 0):
                    for j in range(2):
                        p, pr = bank()
                        klhs = (ktS[0:64, s_, j * 128:(j + 1) * 128] if samp else kt_tm[0:P, j * 128:(j + 1) * 128])
                        srcres = R("SBb") if samp else R("kt_tm")
                        for half in range(2):
                            S.op("pe", lambda e, half=half, klhs=klhs: e.matmul(p[:, half * 128:(half + 1) * 128], lhsT=klhs, rhs=gv_tm[0:P, (2 * j + half) * 128:(2 * j + half + 1) * 128],
                                                                               start=True, stop=True), srcres + R("gv_tm"), [pr])
                        lastc = s_ * L + L - 1
                        for half in range(2):
                            rows = slice(half * 64, (half + 1) * 64)
                            el = eT[rows, 1, j, lastc:lastc + 1]
                            if samp:
                                sv = S0t[rows, s_, j, :]
                                svr = R("SA")
                            else:
                                sv = Sst[l][rows, j, :]
                                svr = R("Sst%d" % l)
                            if GST >= 6:
                                S.op("dve", lambda e, rows=rows, el=el, sv=sv: e.tensor_scalar(out=St[rows, :], in0=sv, scalar1=el, scalar2=None, op0=ALU.mult), svr + R("eT"), R("St"))
                            if GST >= 7:
                                S.op("dve", lambda e, rows=rows, el=el, sv=sv, half=half: e.scalar_tensor_tensor(out=sv, in0=p[rows, half * 128:(half + 1) * 128], scalar=el, in1=St[rows, :],
                                                                                                              op0=ALU.mult, op1=ALU.add), [pr] + R("eT", "St"), svr)
                if samp:
                    for j in range(2):
                        S.dma("sp", oS_d[l, :, j * 128:(j + 1) * 128, :].rearrange("s p v -> p s v"), S0t[:, :, j, :], reads=R("SA"))
                elif ti == last_prompt and (LASTM & 4):
                    S.dma("sp", pS_d[l].rearrange("(j p) v -> p j v", p=128), Sst[l][:], reads=R("Sst%d" % l))

                for c in range(6):
                    if samp:
                        xv = xp[:, c, 0:NSEG * 7].rearrange("p (s j) -> p s j", j=7)
                        xin = [xv[:, :, j:j + 4] for j in range(4)]
                        acc = actT[:, c, 0:64].rearrange("p (s j) -> p s j", j=4)
                    else:
                        xin = [xp[:, c, j:j + N] for j in range(4)]
                        acc = actT[:, c, 0:N]
                    S.op("dve", lambda e, c=c, xin=xin, acc=acc: e.tensor_scalar(out=acc, in0=xin[0], scalar1=convp[:, c, 0:1], scalar2=convp[:, c, 4:5], op0=ALU.mult, op1=ALU.add),
                         R("xp", "convp"), R("actT"))
                    for j in range(1, 4):
                        S.op("dve", lambda e, c=c, j=j, xin=xin, acc=acc: e.scalar_tensor_tensor(out=acc, in0=xin[j], scalar=convp[:, c, j:j + 1], in1=acc, op0=ALU.mult, op1=ALU.add),
                             R("xp", "convp", "actT"), R("actT"))
                act_fn(actT[:, :, 0:N], actT[:, :, 0:N], AF.Silu, R("actT"), R("actT"))
                S.op("dve", lambda e: e.tensor_copy(out=BCb[:, :, 0:N], in_=actT[:, 4:6, 0:N]), R("actT"), R("BCb"))
                for c0, c1 in ((0, 4), (4, 5)):
                    p, pr = bank()
                    for c in range(c0, c1):
                        S.op("pe", lambda e, c=c: e.transpose(out=p[0:P, (c - c0) * 128:(c - c0 + 1) * 128], in_=actT[:, c, 0:P], identity=ident), R("actT", "cst"), [pr])
                    if c0 == 0:
                        evac("act", xs_tm[0:P, :], p[0:P, 0:512], [pr], R("xs_tm"))
                    else:
                        evac("dve", Bm_tm[0:P, :], p[0:P, 0:128], [pr], R("Bm_tm"))
                S.op("dve", lambda e: e.tensor_tensor(out=dts[0:P, 0:8], in0=sdt[0:P, :], in1=rp("dtb"), op=ALU.add), R("sdt", "rowp"), R("dts"))
                softplus_neg(dts[0:P, 16:24], dts[0:P, 0:8], R("dts"), R("dts"), dts[0:P, 8:16], res["dts"], sign=1.0)
                S.op("dve", lambda e: e.scalar_tensor_tensor(out=dts[0:P, 24:32], in0=dts[0:P, 16:24], scalar=-1.0, in1=arow[0:P, :], op0=ALU.mult, op1=ALU.mult), R("dts", "arow"), R("dts"))
                dt_ = dts[0:P, 16:24]
                dA = dts[0:P, 24:32]
                p, pr = bank()
                S.op("pe", lambda e: e.matmul(p[0:P, 0:8], lhsT=le_f, rhs=dA, start=True, stop=True), R("cst", "dts"), [pr])
                S.op("pe", lambda e: e.matmul(p[0:P, 8:16], lhsT=gt_f, rhs=dA, start=True, stop=True), R("cst", "dts"), [pr])
                S.op("pe", lambda e: e.matmul(p[:, 16:24], lhsT=ones[0:P, :], rhs=dA, start=True, stop=True), R("cst", "dts"), [pr])
                S.op("dve", lambda e: e.tensor_copy(out=t8[0:P, 16:24], in_=p[0:P, 0:8]), [pr], R("t8"))
                lam = t8[0:P, 16:24]
                act_fn(t8[0:P, 24:32], p[0:P, 0:8], AF.Exp, [pr], R("t8"))
                act_fn(t8[0:P, 32:40], p[0:P, 8:16], AF.Exp, [pr], R("t8"))
                S.op("dve", lambda e: e.tensor_tensor(out=t8[0:P, 32:40], in0=t8[0:P, 32:40], in1=dt_, op=ALU.mult), R("t8", "dts"), R("t8"))
                act_fn(eL[:, :], p[:, 16:24], AF.Exp, [pr], R("eL"))
                S.op("dve", lambda e: e.tensor_tensor(out=xsw[0:P, :].rearrange("p (h c) -> p h c", c=64), in0=xs_tm[0:P, :].rearrange("p (h c) -> p h c", c=64),
                                                      in1=t8[0:P, 32:40].unsqueeze(2).to_broadcast([P, 8, 64]), op=ALU.mult), R("xs_tm", "t8"), R("xsw"))
                S.op("dve", lambda e: e.tensor_tensor(out=xsb[0:P, :].rearrange("p (h c) -> p h c", c=64), in0=xs_tm[0:P, :].rearrange("p (h c) -> p h c", c=64),
                                                      in1=dt_.unsqueeze(2).to_broadcast([P, 8, 64]), op=ALU.mult), R("xs_tm", "dts"), R("xsb"))
                pi0, pi0r = PS[6], PSR[6]
                pi1, pi1r = PS[5], PSR[5]
                py, pyr = PS[7], PSR[7]
                if not samp:
                    S.op("act", lambda e: e.copy(out=hTb[:], in_=hT[l][:]), R("hT%d" % l), R("hTb"))
                    for g, (pi, pir) in enumerate(((pi0, pi0r), (pi1, pi1r))):
                        S.op("pe", lambda e, g=g, pi=pi: e.matmul(pi[0:P, 0:256], lhsT=BCb[g * 64:(g + 1) * 64, 1, 0:P], rhs=hTb[g * 64:(g + 1) * 64, :], start=True, stop=True),
                             R("BCb", "hTb"), [pir])
                else:
                    hTbs = SBb[:, 0:4096].rearrange("p (s c) -> p s c", c=256)
                    CmS = SBb[:, 4096:5120].rearrange("p (s t) -> p s t", t=64)
                    xswS = SBb[0:64, 5120:9216].rearrange("p (s c) -> p s c", c=512)
                    h0t = SA[:, :].rearrange("p (s q c) -> p s q c", q=4, c=128)
                    S.op("dve", lambda e: e.tensor_copy(out=big[0:64, 0:512].rearrange("p (h c) -> p h c", c=64), in_=dA.unsqueeze(2).to_broadcast([64, 8, 64])), R("dts"), R("big"))
                    pe_, per_ = bank()
                    for q in range(4):
                        S.op("pe", lambda e, q=q: e.matmul(pe_[:, q * 16:(q + 1) * 16], lhsT=big[0:64, q * 128:(q + 1) * 128], rhs=C("segind")[0:64, :], start=True, stop=True), R("big", "cst"), [per_])
                    act_fn(hm2[:, 0:64], pe_[:, 0:64], AF.Exp, [per_], R("hm2"))
                    eLs = hm2[:, 0:64].rearrange("p (q s) -> p q s", s=16)
                    S.op("dve", lambda e: e.tensor_tensor(out=CmS, in0=BCb[:, 1, 0:64].unsqueeze(1).to_broadcast([128, 16, 64]),
                                                          in1=cb[:, 320:1344].rearrange("p (s t) -> p s t", t=64), op=ALU.mult), R("BCb", "cb"), R("SBb"))
                    for ps_ in range(2):
                        s0 = ps_ * 8
                        S.op("pool", lambda e: e.memset(SA[:, :], 0.0), (), R("SA"))
                        for q in range(4):
                            oq = 0 if q < 2 else 64
                            S.dma("sp", h0t[:, :, q, oq:oq + 64], sh_d[l, s0:s0 + 8, q * 128:(q + 1) * 128, :].rearrange("s r n -> r s n"), writes=R("SA"))
                        for s_ in range(8):
                            p, pr = bank()
                            for q in range(4):
                                S.op("pe", lambda e, q=q, s_=s_: e.transpose(out=p[:, q * 128:(q + 1) * 128], in_=h0t[:, s_, q, :], identity=ident), R("SA", "cst"), [pr])
                            evac("act", hTbs[0:64, s0 + s_, :], p[0:64, 0:256], [pr], R("SBb"))
                            evac("dve", hTbs[64:128, s0 + s_, :], p[64:128, 256:512], [pr], R("SBb"))
                        S.op("dve", lambda e: e.tensor_tensor(out=xswS, in0=xsw[0:64, :].unsqueeze(1).to_broadcast([64, 8, 512]),
                                                              in1=C("segind")[0:64, s0:s0 + 8].unsqueeze(2).to_broadcast([64, 8, 512]), op=ALU.mult), R("xsw", "cst"), R("SBb"))
                        for s_ in range(8):
                            p, pr = bank()
                            for q in range(4):
                                g = q // 2
                                S.op("pe", lambda e, q=q, s_=s_, g=g: e.matmul(p[:, q * 64:(q + 1) * 64], lhsT=xswS[0:64, s_, q * 128:(q + 1) * 128], rhs=Bm_tm[0:64, g * 64:(g + 1) * 64],
                                                                                start=True, stop=True), R("SBb", "Bm_tm"), [pr])
                            for q in range(4):
                                oq = 0 if q < 2 else 64
                                hv = h0t[:, s_, q, oq:oq + 64]
                                S.op("dve", lambda e, q=q, s_=s_, hv=hv: e.scalar_tensor_tensor(out=hv, in0=hv, scalar=eLs[:, q, s0 + s_:s0 + s_ + 1], in1=p[:, q * 64:(q + 1) * 64],
                                                                                              op0=ALU.mult, op1=ALU.add), [pr] + R("SA", "hm2"), R("SA"))
                        for q in range(4):
                            oq = 0 if q < 2 else 64
                            S.dma("sp", oh_d[l, s0:s0 + 8, q * 128:(q + 1) * 128, :].rearrange("s r n -> r s n"), h0t[:, :, q, oq:oq + 64], reads=R("SA"))
                    for g, (pi, pir) in enumerate(((pi0, pi0r), (pi1, pi1r))):
                        for s_ in range(NSEG):
                            S.op("pe", lambda e, g=g, pi=pi, s_=s_: e.matmul(pi[0:P, 0:256], lhsT=CmS[g * 64:(g + 1) * 64, s_, :], rhs=hTbs[g * 64:(g + 1) * 64, s_, :],
                                                                             start=(s_ == 0), stop=(s_ == NSEG - 1)), R("SBb"), [pir])
                Y = big[0:P, 0:8 * P].rearrange("p (h t) -> p h t", t=P)
                S.op("dve", lambda e: e.tensor_tensor(out=Y, in0=le_f[:, 0:P].unsqueeze(1).to_broadcast([P, 8, P]), in1=dA.unsqueeze(2).to_broadcast([P, 8, P]), op=ALU.mult),
                     R("cst", "dts"), R("big"))
                for g in range(2):
                    pL, pLr = bank()
                    S.op("pe", lambda e, g=g: e.matmul(pL[0:P, 0:4 * P], lhsT=ones[0:P, 0:P], rhs=big[0:P, g * 4 * P:(g + 1) * 4 * P], start=True, stop=True), R("cst", "big"), [pLr])
                    S.op("dve", lambda e, g=g: e.tensor_tensor(out=big2[0:P, g * 4 * P:(g + 1) * 4 * P].rearrange("p (h t) -> p h t", t=P), in0=pL[0:P, 0:4 * P].rearrange("p (h t) -> p h t", t=P),
                                                               in1=lam[:, 4 * g:4 * g + 4].unsqueeze(2).to_broadcast([P, 4, P]), op=ALU.subtract), [pLr] + R("t8"), R("big2"))
                S.op("dve", lambda e: e.tensor_scalar(out=big2[0:P, 0:8 * P], in0=big2[0:P, 0:8 * P], scalar1=0.0, scalar2=None, op0=ALU.min), R("big2"), R("big2"))
                act_fn(big2[0:P, 0:8 * P], big2[0:P, 0:8 * P], AF.Exp, R("big2"), R("big2"))
                for g in range(2):
                    p, pr = bank()
                    S.op("pe", lambda e, g=g: e.matmul(p[0:P, 0:P], lhsT=BCb[g * 64:(g + 1) * 64, 0, 0:P], rhs=BCb[g * 64:(g + 1) * 64, 1, 0:P], start=True, stop=True), R("BCb"), [pr])
                    S.op("dve", lambda e, g=g: e.tensor_tensor(out=cbm[0:P, g, 0:P], in0=p[0:P, 0:P], in1=le_f[:, 0:P], op=ALU.mult), [pr] + R("cst"), R("cbm"))
                S.op("dve", lambda e: e.tensor_tensor(out=bigb[0:P, 0:8 * P].rearrange("p (g h t) -> p g h t", g=2, t=P), in0=big2[0:P, 0:8 * P].rearrange("p (g h t) -> p g h t", g=2, t=P),
                                                      in1=cbm[0:P, :, 0:P].unsqueeze(2).to_broadcast([P, 2, 4, P]), op=ALU.mult), R("big2", "cbm"), R("bigb"))
                for h in range(8):
                    S.op("pe", lambda e, h=h: e.matmul(py[0:P, h * 64:(h + 1) * 64], lhsT=bigb[0:P, h * P:(h + 1) * P], rhs=xsb[0:P, h * 64:(h + 1) * 64], start=True, stop=True), R("bigb", "xsb"), [pyr])
                for g, (pi, pir) in enumerate(((pi0, pi0r), (pi1, pi1r))):
                    S.op("dve", lambda e, g=g, pi=pi: e.tensor_tensor(out=hm[0:P, g * 256:(g + 1) * 256].rearrange("p (h c) -> p h c", c=64), in0=pi[0:P, 0:256].rearrange("p (h c) -> p h c", c=64),
                                                                      in1=t8[0:P, 24 + 4 * g:28 + 4 * g].unsqueeze(2).to_broadcast([P, 4, 64]), op=ALU.mult), [pir] + R("t8"), R("hm"))
                S.op("dve", lambda e: e.tensor_tensor(out=hm[0:P, :], in0=py[0:P, 0:512], in1=hm[0:P, :], op=ALU.add), [pyr] + R("hm"), R("hm"))
                S.op("dve", lambda e: e.tensor_tensor(out=hm2[0:P, 0:512].rearrange("p (h c) -> p h c", c=64), in0=xs_tm[0:P, :].rearrange("p (h c) -> p h c", c=64),
                                                      in1=rp("sD").unsqueeze(2).to_broadcast([P, 8, 64]), op=ALU.mult), R("xs_tm", "rowp"), R("hm2"))
                S.op("dve", lambda e: e.tensor_tensor(out=hm[0:P, :], in0=hm[0:P, :], in1=hm2[0:P, 0:512], op=ALU.add), R("hm", "hm2"), R("hm"))
                act_fn(hm2[0:P, 0:512], sz_tm[0:P, :], AF.Silu, R("sz_tm"), R("hm2"))
                S.op("dve", lambda e: e.tensor_tensor(out=hm[0:P, :], in0=hm[0:P, :], in1=hm2[0:P, 0:512], op=ALU.mult), R("hm", "hm2"), R("hm"))
                S.op("dve", lambda e: e.tensor_tensor(out=hm2[0:P, 0:512], in0=hm[0:P, :], in1=hm[0:P, :], op=ALU.mult), R("hm"), R("hm2"))
                S.op("dve", lambda e: e.tensor_reduce(out=st4[0:P, 4:6], in_=hm2[0:P, 0:512].rearrange("p (g c) -> p g c", c=256), axis=AX.X, op=ALU.add), R("hm2"), R("st4"))
                rstd_from_ssq(st4[0:P, 8:10], st4[0:P, 4:6], 256, R("st4"), R("st4"))
                S.op("dve", lambda e: e.tensor_tensor(out=hm[0:P, :].rearrange("p (g c) -> p g c", c=256), in0=hm[0:P, :].rearrange("p (g c) -> p g c", c=256),
                                                      in1=st4[0:P, 8:10].unsqueeze(2).to_broadcast([P, 2, 256]), op=ALU.mult), R("hm", "st4"), R("hm"))
                S.op("dve", lambda e: e.tensor_tensor(out=br[0:P, 1024:1536], in0=hm[0:P, :], in1=rp("snorm"), op=ALU.mult), R("hm", "rowp"), R("br"))
                if not samp:
                    p, pr = bank()
                    for g in range(2):
                        S.op("pe", lambda e, g=g: e.matmul(p[:, g * 256:(g + 1) * 256], lhsT=Bm_tm[0:P, :], rhs=xsw[0:P, g * 256:(g + 1) * 256], start=True, stop=True), R("Bm_tm", "xsw"), [pr])
                    for g in range(2):
                        rows = slice(g * 64, (g + 1) * 64)
                        S.op("dve", lambda e, g=g, rows=rows: e.tensor_tensor(out=big[rows, 0:256].rearrange("p (h c) -> p h c", c=64), in0=hT[l][rows, :].rearrange("p (h c) -> p h c", c=64),
                                                                              in1=eL[rows, 4 * g:4 * g + 4].unsqueeze(2).to_broadcast([64, 4, 64]), op=ALU.mult), R("hT%d" % l, "eL"), R("big"))
                        S.op("dve", lambda e, g=g, rows=rows: e.tensor_tensor(out=hT[l][rows, :], in0=p[rows, g * 256:(g + 1) * 256], in1=big[rows, 0:256], op=ALU.add), [pr] + R("big"), R("hT%d" % l))
                    if ti == last_prompt and (LASTM & 8):
                        p, pr = bank()
                        for half in range(2):
                            S.op("pe", lambda e, half=half: e.transpose(out=p[:, half * 128:(half + 1) * 128], in_=hT[l][:, half * 128:(half + 1) * 128], identity=ident),
                                 R("hT%d" % l, "cst"), [pr])
                        evac("act", hm2[:, 0:256], p[:, 0:256], [pr], R("hm2"))
                        phv = ph_d[l].rearrange("(g f r) n -> f r g n", g=2, f=2, r=128)
                        for half in range(2):
                            S.dma("sp", phv[half], hm2[:, half * 128:(half + 1) * 128].rearrange("p (g n) -> p g n", n=64), reads=R("hm2"))

                dbg_out("br", br[0:P, :], R("br"))
                transpose_to(lambda c0, c1: brT[:, c0:c1, 0:P], br, res["br"], P, 12, res["brT"], eng="dve")
                mixacc = big[:, :].rearrange("p (c t) -> p c t", t=128)
                gs8 = big2[:, :].rearrange("p (c t) -> p c t", t=128)
                for n in range(3):
                    for gh in range(2):
                        c0 = 4896 + n * 1024 + gh * 512
                        wg, wgr = load_w(W(c0, c0 + 512), 512)
                        for dq in range(4):
                            dc = gh * 4 + dq
                            pg, pgr = fm_piece(wg, wgr, dq * 128, (dq + 1) * 128, N)
                            act_fn(gs8[:, dc, 0:N], pg[:, 0:N], AF.Sigmoid, [pgr], R("big2"))
                    wbn, wbnr = load_w2(wbrs[l, n * 512:(n + 1) * 512, :], 4, 1024)
                    for dc in range(8):
                        pa, par = bank()
                        for k in range(4):
                            S.op("pe", lambda e, k=k, dc=dc: e.matmul(pa[:, 0:N], lhsT=wbn[:, k, dc * 128:(dc + 1) * 128], rhs=brT[:, n * 4 + k, 0:N], start=(k == 0), stop=(k == 3)),
                                 [wbnr] + R("brT"), [par])
                        if n == 0:
                            S.op("dve", lambda e, dc=dc: e.tensor_tensor(out=mixacc[:, dc, 0:N], in0=pa[:, 0:N], in1=gs8[:, dc, 0:N], op=ALU.mult), [par] + R("big2"), R("big"))
                        else:
                            S.op("dve", lambda e, dc=dc: e.tensor_tensor(out=hm2[:, 0:N], in0=pa[:, 0:N], in1=gs8[:, dc, 0:N], op=ALU.mult), [par] + R("big2"), R("hm2"))
                            S.op("dve", lambda e, dc=dc: e.tensor_tensor(out=mixacc[:, dc, 0:N], in0=mixacc[:, dc, 0:N], in1=hm2[:, 0:N], op=ALU.add), R("big", "hm2"), R("big"))
                S.op("act", lambda e: e.copy(out=mixT[:, :, 0:N], in_=mixacc[:, :, 0:N]), R("big"), R("mixT"))
                for gh in range(2):
                    wo, wor = load_w(wouts[l, :, gh * 512:(gh + 1) * 512], 512)
                    p, pr = tm_piece(wo, wor, 0, 512, P, src=mixT, srcres=res["mixT"])
                    S.op("dve", lambda e, gh=gh: e.scalar_tensor_tensor(out=X[0:P, gh * 512:(gh + 1) * 512], in0=X[0:P, gh * 512:(gh + 1) * 512], scalar=DN_ALPHA, in1=p[0:P, 0:512],
                                                                        op0=ALU.mult, op1=ALU.add), [pr] + R("X"), R("X"))
                S.dma("sp", lnrow[:], lnp_d[l, 0], writes=R("lnrow"))
                layernorm_tm(P, X, res["X"], lnrow[0:P, 0:1024], lnrow[0:P, 1024:2048], X[0:P, :], res["X"])
                dbg_out("h1", X[0:P, :], R("X"))

                if do_peer:
                    fence()
                    transpose_to(lambda c0, c1: xT[:, c0:c1, 0:P], X, res["X"], P, 8, res["xT"])
                    for gq_ in range(4):
                        wq, wqr = load_w(wqs[l, :, gq_ * 512:(gq_ + 1) * 512], 512)
                        for c in range(4):
                            p, pr = fm_piece(wq, wqr, c * 128, (c + 1) * 128, N)
                            evac("act", qTb[:, gq_ * 4 + c, 0:N], p[:, 0:N], [pr], R("qTb"))
                    i = wsn[0]
                    wsn[0] = (i + 1) % 6
                    kb = wpool[i][:, 0:2048]
                    S.dma("sp", kb, keyss[l], reads=R("scratch"), writes=[wpres[i]])
                    kbr = wpres[i]
                    for c0 in range(0, 16, 4):
                        p, pr = bank()
                        for c in range(c0, c0 + 4):
                            S.op("pe", lambda e, c=c: e.matmul(p[0:P, (c - c0) * 128:(c - c0 + 1) * 128], lhsT=qTb[:, c, 0:P], rhs=kb[:, c * 128:(c + 1) * 128], start=True, stop=True),
                                 R("qTb") + [kbr], [pr])
                        evac("dve", SC[0:P, c0 * 128:(c0 + 4) * 128], p[0:P, 0:512], [pr], R("SC"))

                    def top16(src, width, vout, iout):
                        wv = wk[0:P, 0:width]
                        S.op("dve", lambda e: e.max(out=vout[:, 0:8], in_=src), R("SC"), R("V16"))
                        S.op("dve", lambda e: e.max_index(out=iout[:, 0:8], in_max=vout[:, 0:8], in_values=src), R("SC", "V16"), R("I16"))
                        S.op("dve", lambda e: e.match_replace(out=wv, in_to_replace=vout[:, 0:8], in_values=src, imm_value=-1e30), R("SC", "V16"), R("wk"))
                        S.op("dve", lambda e: e.max(out=vout[:, 8:16], in_=wv), R("wk"), R("V16"))
                        S.op("dve", lambda e: e.max_index(out=iout[:, 8:16], in_max=vout[:, 8:16], in_values=wv), R("wk", "V16"), R("I16"))

                    for c in range(16):
                        top16(SC[0:P, c * 128:(c + 1) * 128], 128, V16[0:P, c, :], I16[0:P, c, :])
                    S.op("dve", lambda e: e.tensor_copy(out=I16f[0:P], in_=I16[0:P]), R("I16"), R("I16f"))
                    V16v = V16[0:P].rearrange("p (h j) a -> p h j a", j=2)
                    I16v = I16f[0:P].rearrange("p (h j) a -> p h j a", j=2)
                    cand = SC[0:P, :].rearrange("p (h a b) -> p h a b", a=16, b=16)
                    S.op("dve", lambda e: e.tensor_tensor(out=cand, in0=V16v[:, :, 0, :].unsqueeze(3).to_broadcast([P, 8, 16, 16]),
                                                          in1=V16v[:, :, 1, :].unsqueeze(2).to_broadcast([P, 8, 16, 16]), op=ALU.add), R("V16"), R("SC"))
                    for h in range(8):
                        top16(SC[0:P, h * 256:(h + 1) * 256], 256, TV[0:P, h, :], TPu[0:P, h, :])
                    TPf, af, bf_, i0f, i1f, gg, dots, actg = [pk[0:P, q, :].rearrange("p (h k) -> p h k", k=16) for q in range(8)]
                    S.op("dve", lambda e: e.tensor_copy(out=TPf, in_=TPu[0:P]), R("I16", "V16"), R("pk"))
                    bv4 = big[0:P, :].rearrange("p (h a b) -> p h a b", a=16, b=16)
                    for hh in range(2):
                        hs_ = slice(4 * hh, 4 * hh + 4)
                        S.op("dve", lambda e, hs_=hs_: e.tensor_tensor(out=bv4, in0=TPf[:, hs_, :].unsqueeze(3).to_broadcast([P, 4, 16, 16]),
                                                                       in1=thr16[0:P, :].unsqueeze(1).unsqueeze(1).to_broadcast([P, 4, 16, 16]), op=ALU.is_ge), R("pk", "thr16"), R("big"))
                        S.op("dve", lambda e, hs_=hs_: e.tensor_reduce(out=af[:, hs_, :], in_=bv4, axis=AX.X, op=ALU.add), R("big"), R("pk"))
                    S.op("dve", lambda e: e.scalar_tensor_tensor(out=pk[0:P, 2, :], in0=pk[0:P, 1, :], scalar=-16.0, in1=pk[0:P, 0, :], op0=ALU.mult, op1=ALU.add), R("pk"), R("pk"))
                    for (src_, jj, dst_) in ((af, 0, i0f), (bf_, 1, i1f)):
                        for hh in range(2):
                            hs_ = slice(4 * hh, 4 * hh + 4)
                            S.op("dve", lambda e, hs_=hs_, src_=src_: e.tensor_tensor(out=bv4, in0=src_[:, hs_, :].unsqueeze(3).to_broadcast([P, 4, 16, 16]),
                                                                                      in1=C("iota16")[0:P, :].unsqueeze(1).unsqueeze(1).to_broadcast([P, 4, 16, 16]), op=ALU.is_equal),
                                 R("pk", "cst"), R("big"))
                            S.op("dve", lambda e, hs_=hs_, jj=jj: e.tensor_tensor(out=bv4, in0=bv4, in1=I16v[:, hs_, jj, :].unsqueeze(2).to_broadcast([P, 4, 16, 16]), op=ALU.mult),
                                 R("big", "I16f"), R("big"))
                            S.op("dve", lambda e, hs_=hs_, dst_=dst_: e.tensor_reduce(out=dst_[:, hs_, :], in_=bv4, axis=AX.X, op=ALU.add), R("big"), R("pk"))
                    S.op("dve", lambda e: e.scalar_tensor_tensor(out=pk[0:P, 0, :], in0=pk[0:P, 3, :], scalar=128.0, in1=pk[0:P, 4, :], op0=ALU.mult, op1=ALU.add), R("pk"), R("pk"))
                    S.op("dve", lambda e: e.tensor_copy(out=eidx[0:P, :], in_=pk[0:P, 0, :]), R("pk"), R("eidx"))
                    S.op("dve", lambda e: e.tensor_tensor(out=gg, in0=TV[0:P], in1=TV[0:P, :, 0:1].to_broadcast([P, 8, 16]), op=ALU.subtract), R("V16"), R("pk"))
                    act_fn(pk[0:P, 5, :], pk[0:P, 5, :], AF.Exp, R("pk"), R("pk"))
                    S.op("dve", lambda e: e.tensor_reduce(out=st4[0:P, 0:8], in_=gg, axis=AX.X, op=ALU.add), R("pk"), R("st4"))
                    S.op("dve", lambda e: e.reciprocal(out=st4[0:P, 0:8], in_=st4[0:P, 0:8]), R("st4"), R("st4"))
                    S.op("dve", lambda e: e.tensor_tensor(out=gg, in0=gg, in1=st4[0:P, 0:8].unsqueeze(2).to_broadcast([P, 8, 16]), op=ALU.mult), R("pk", "st4"), R("pk"))
                    GUr = [res["GU%d" % q] for q in range(NGU)]
                    gn = [0]

                    def gather(tab, slot):
                        q = gn[0] % NGU
                        gn[0] += 1
                        S.dma("pool", None, None, reads=R("eidx", "scratch"), writes=[GUr[q]],
                              fn=lambda e, q=q: e.indirect_dma_start(out=GU[q][0:P, :], out_offset=None, in_=tab,
                                                                     in_offset=bass.IndirectOffsetOnAxis(ap=eidx[0:P, slot:slot + 1], axis=0)))
                        return GU[q], GUr[q]

                    for slot in range(128):
                        gu, gur = gather(pus[l][:, :], slot)
                        S.op("dve", lambda e, gu=gu, slot=slot: e.scalar_tensor_tensor(out=big[0:P, :], in0=gu[0:P, :], scalar=1.0, in1=X[0:P, :], op0=ALU.mult, op1=ALU.mult,
                                                                                    accum_out=pk[0:P, 6, slot:slot + 1]), [gur] + R("X"), R("big", "pk"))
                    act_fn(pk[0:P, 7, :], pk[0:P, 6, :], AF.Gelu, R("pk"), R("pk"))
                    S.op("dve", lambda e: e.tensor_tensor(out=pk[0:P, 7, :], in0=pk[0:P, 7, :], in1=pk[0:P, 5, :], op=ALU.mult), R("pk"), R("pk"))
                    for slot in range(128):
                        gu, gur = gather(pvs[l][:, :], slot)
                        if slot == 0:
                            S.op("dve", lambda e, gu=gu, slot=slot: e.tensor_scalar(out=pacc[0:P, :], in0=gu[0:P, :], scalar1=pk[0:P, 7, slot:slot + 1], scalar2=None, op0=ALU.mult),
                                 [gur] + R("pk"), R("acc"))
                        else:
                            S.op("dve", lambda e, gu=gu, slot=slot: e.scalar_tensor_tensor(out=pacc[0:P, :], in0=gu[0:P, :], scalar=pk[0:P, 7, slot:slot + 1], in1=pacc[0:P, :],
                                                                                        op0=ALU.mult, op1=ALU.add), [gur] + R("pk", "acc"), R("acc"))
                    S.op("dve", lambda e: e.scalar_tensor_tensor(out=X[0:P, :], in0=X[0:P, :], scalar=DN_ALPHA, in1=pacc[0:P, :], op0=ALU.mult, op1=ALU.add), R("X", "acc"), R("X"))
                    S.dma("sp", lnrow[:], lnp_d[l, 1], writes=R("lnrow"))
                    layernorm_tm(P, X, res["X"], lnrow[0:P, 0:1024], lnrow[0:P, 1024:2048], X[0:P, :], res["X"])
                    fence()
            S.dma("sp", y_d[t0:t0 + P, :], X[0:P, :], reads=R("X"))
            dbg_i[0] += 1
        S.finish()
    return nc


def make_consts():
    c = np.zeros((128, COW), np.float32)
    def put(n, a):
        c[:a.shape[0], CO[n][0]:CO[n][1]] = a
    s = np.arange(128)
    put("ident", np.eye(128, dtype=np.float32))
    put("leP", (s[:, None] <= s[None, :]).astype(np.float32))
    put("gtP", (s[:, None] > s[None, :]).astype(np.float32))
    q = np.arange(64)
    same = (q[:, None] // 4) == (q[None, :] // 4)
    put("leS", (same & (q[:, None] <= q[None, :])).astype(np.float32))
    put("gtS", (same & (q[:, None] > q[None, :])).astype(np.float32))
    put("ones", np.ones((128, 128), np.float32))
    put("segind", ((q[:, None] // 4) == np.arange(16)[None, :]).astype(np.float32))
    put("iota16", np.broadcast_to(np.arange(16, dtype=np.float32), (128, 16)))
    segm = ((np.arange(64)[None, :] // 4) == np.arange(16)[:, None]).astype(np.float32).reshape(1, 1024)
    put("halfm", np.stack([(s < 64), (s >= 64)], 1).astype(np.float32))
    return c, np.ascontiguousarray(np.broadcast_to(segm, (128, 1024))).astype(np.float32)

def prep(inp, core, nexp=16384):
    f = lambda a: np.ascontiguousarray(np.asarray(a, dtype=np.float32))
    x = np.concatenate([f(inp["x_prompt"][core]), f(inp["x_sample"][core * 16:(core + 1) * 16]).reshape(64, 1024)], 0)
    sl = slice(core * 16, (core + 1) * 16)
    d = dict(x=x)
    d["sC"] = f(inp["state_mlstm_C"][:, sl]); d["sn"] = f(inp["state_mlstm_n"][:, sl]); d["sm"] = f(inp["state_mlstm_m"][:, sl])
    d["sS"] = f(inp["state_gla_S"][:, sl]).reshape(2, 16, 256, 128)
    d["sh"] = f(inp["state_ssm_h"][:, sl]).reshape(2, 16, 512, 64)
    d["scv"] = f(inp["state_conv"][:, sl]).reshape(2, 48, 768)
    d["w_in"] = f(inp["w_in"]); d["w_br"] = f(inp["w_branch"]).reshape(2, 1536, 1024); d["w_out"] = f(inp["w_out"])
    d["p_wq"] = f(inp["p_wq"])
    d["keysT"] = f(np.transpose(np.asarray(inp["p_keys"]).reshape(2, 16, 128, 128), (0, 3, 1, 2)))
    for l in range(2):
        d["p_u%d" % l] = f(inp["p_u"][l][:nexp]); d["p_v%d" % l] = f(inp["p_v"][l][:nexp])
    rows = []
    for l in range(2):
        r = np.concatenate([f(inp[k][l]).reshape(-1) for k in ["m_i_bias", "m_f_bias", "m_norm", "g_a_bias", "g_norm", "s_dt_bias",
                                                              "s_A_log", "s_D", "s_norm"]])
        rows.append(np.broadcast_to(r, (128, r.size)))
    d["rowp"] = f(np.stack(rows))
    lnp = np.stack([np.stack([np.concatenate([f(inp["ln1_g"][l]), f(inp["ln1_b"][l])]), np.concatenate([f(inp["ln2_g"][l]), f(inp["ln2_b"][l])])]) for l in range(2)])
    d["lnp"] = f(np.broadcast_to(lnp[:, :, None, :], (2, 2, 128, 2048)))
    cw = f(inp["s_conv_w"]); cbb = f(inp["s_conv_b"])
    cp = np.concatenate([cw, cbb[:, None, :]], 1)
    d["convp"] = f(np.transpose(cp.reshape(2, 5, 6, 128), (0, 3, 2, 1)))
    d["gaup"] = f(inp["g_a_up"])
    d["cst"], d["segm"] = make_consts()
    return d


NEXP_USED = 16384


def kernel(**inp):
    nc = build(nexp=NEXP_USED)
    maps = [prep(inp, c, nexp=NEXP_USED) for c in range(NCORES)]
    r = run_bass_kernel_spmd(nc, maps, core_ids=list(range(NCORES))).results
    f = np.float32
    yp = np.stack([r[c]["y"][0:SEQ] for c in range(NCORES)], 0).astype(f)
    ys = np.concatenate([r[c]["y"][SEQ:].reshape(NSEG, LS, D) for c in range(NCORES)], 0).astype(f)
    st = lambda k, shp: np.stack([r[c][k].reshape(shp) for c in range(NCORES)], 1).astype(f)
    ct = lambda k, shp: np.concatenate([r[c][k].reshape(shp) for c in range(NCORES)], 1).astype(f)
    return (yp, ys,
            st("pC", (2, 4, 128, 128)), st("pn", (2, 4, 128)), st("pm", (2, 4)),
            st("pS", (2, 4, 64, 128)), st("ph", (2, 8, 64, 64)), st("pcv", (2, 3, 768)),
            ct("oC", (2, 16, 4, 128, 128)), ct("on", (2, 16, 4, 128)), ct("om", (2, 16, 4)),
            ct("oS", (2, 16, 4, 64, 128)), ct("oh", (2, 16, 8, 64, 64)), ct("ocv", (2, 16, 3, 768)))
```

```python
import contextlib
import os
import numpy as np
import concourse.bass as bass
import concourse.mybir as mybir
from concourse.bass_utils import run_bass_kernel_spmd

F32 = mybir.dt.float32
BF16 = mybir.dt.bfloat16
I32 = mybir.dt.int32
U32 = mybir.dt.uint32
AF = mybir.ActivationFunctionType
ALU = mybir.AluOpType
AX = mybir.AxisListType

NCORES = 8
D = 1024
SEQ = 2048
NSEG = 16
LS = 4
TC = SEQ + NSEG * LS
INW = 7968
EPS = 1e-5
DN_ALPHA = 4.0 ** 0.25
DEPTH = 2
GST = int(os.environ.get('GST', '99'))
LASTM = int(os.environ.get('LASTM', '15'))
FB_RATIO = int(os.environ.get('FB_RATIO', '5'))
SKEW = int(os.environ.get('SKEW', '2'))

RP = {}
_o = 0
for _n, _s in [("mib", 4), ("mfb", 4), ("mnorm", 512), ("gab", 256), ("gnorm", 512), ("dtb", 8),
               ("alog", 8), ("sD", 8), ("snorm", 512)]:
    RP[_n] = (_o, _o + _s)
    _o += _s
RPW = _o

CO = {}
_o = 0
for _n, _s in [("ident", 128), ("leP", 128), ("gtP", 128), ("leS", 64), ("gtS", 64), ("ones", 128),
               ("segind", 16), ("iota16", 16), ("halfm", 2)]:
    CO[_n] = (_o, _o + _s)
    _o += _s
COW = _o


class Res:
    __slots__ = ("name", "w", "r")

    def __init__(self, name="r"):
        self.name = name
        self.w = None
        self.r = []


class _Rec:
    def __getattr__(self, name):
        def f(*a, **k):
            self.call = (name, a, k)
            return self
        return f


def _replay(fn):
    rec = _Rec()
    fn(rec)
    name, a, k = rec.call
    def run(eng):
        try:
            return getattr(eng, name)(*a, **k)
        except Exception:
            print("FAILED OP", name, {kk: (getattr(vv, "shape", vv), getattr(getattr(vv, "tensor", None), "name", None)) for kk, vv in k.items()})
            raise
    return run


class Sched:
    ENG = ("pe", "dve", "act", "pool", "sp")

    def __init__(self, nc, stack, n_dma_sems=32):
        self.nc = nc
        self.sem = {e: stack.enter_context(nc.semaphore("s_" + e)) for e in self.ENG}
        self.cnt = {e: 0 for e in self.ENG}
        self.prog = {e: [] for e in self.ENG}
        self.seen = {e: {} for e in self.ENG}
        self.dsem = [stack.enter_context(nc.semaphore("d%d" % i)) for i in range(n_dma_sems)]
        self.dval = [0] * n_dma_sems
        self.dnext = 0
        self.n_hw = n_dma_sems - 8
        self.dnext_sw = 0
        self.n_ins = 0

    def _wait(self, e, ev):
        if ev is None:
            return
        kind, key, val = ev
        if kind == "e" and key == e and e == "pe":
            return
        k = (kind, key)
        if self.seen[e].get(k, 0) >= val:
            return
        self.seen[e][k] = val
        sem = self.sem[key] if kind == "e" else self.dsem[key]
        self.prog[e].append(lambda eng, sem=sem, val=val: eng.wait_ge(sem, val))

    def _deps(self, e, reads, writes):
        for r in reads:
            self._wait(e, r.w)
        for w in writes:
            self._wait(e, w.w)
            for ev in w.r:
                self._wait(e, ev)

    def _commit(self, ev, reads, writes):
        for r in reads:
            r.r.append(ev)
            if len(r.r) > 16:
                d = {}
                for x in r.r:
                    k = (x[0], x[1])
                    if k not in d or d[k][2] < x[2]:
                        d[k] = x
                r.r = list(d.values())
        for w in writes:
            w.w = ev
            w.r = []

    def op(self, e, fn, reads=(), writes=()):
        self._deps(e, reads, writes)
        self.cnt[e] += 1
        ev = ("e", e, self.cnt[e])
        sem = self.sem[e]
        fn = _replay(fn)
        self.prog[e].append(lambda eng, fn=fn, sem=sem: fn(eng).then_inc(sem, 1))
        self._commit(ev, reads, writes)
        self.n_ins += 1
        return ev

    def dma(self, q, out, in_, reads=(), writes=(), fn=None, slow=False):
        self._deps(q, reads, writes)
        if q == "pool":
            i = self.n_hw + self.dnext_sw
            self.dnext_sw = (self.dnext_sw + 1) % 8
        else:
            i = self.dnext
            self.dnext = (self.dnext + 1) % self.n_hw
        if self.dval[i] > 0:
            self._wait(q, ("d", i, self.dval[i]))
        self.dval[i] += 16
        ev = ("d", i, self.dval[i])
        sem = self.dsem[i]
        if fn is None:
            if slow:
                fn = lambda eng, out=out, in_=in_: eng.dma_start(out=out, in_=in_, allow_slow_non_contiguous=True)
            else:
                fn = lambda eng, out=out, in_=in_: eng.dma_start(out=out, in_=in_)
        fn = _replay(fn)
        self.prog[q].append(lambda eng, fn=fn, sem=sem: fn(eng).then_inc(sem, 16))
        self._commit(ev, reads, writes)
        self.n_ins += 1
        return ev

    def finish(self):
        for i in range(len(self.dsem)):
            if self.dval[i] > 0:
                self._wait("sp", ("d", i, self.dval[i]))
        for e in ("pe", "dve", "act", "pool"):
            if self.cnt[e] > 0:
                self._wait("sp", ("e", e, self.cnt[e]))
        nc = self.nc
        progs = self.prog
        with nc.Block() as block:
            @block.tensor
            def _(eng):
                for f in progs["pe"]:
                    f(eng)

            @block.vector
            def _(eng):
                for f in progs["dve"]:
                    f(eng)

            @block.scalar
            def _(eng):
                for f in progs["act"]:
                    f(eng)

            @block.gpsimd
            def _(eng):
                for f in progs["pool"]:
                    f(eng)

            @block.sync
            def _(eng):
                for f in progs["sp"]:
                    f(eng)


def build(tiles=None, nlayers=DEPTH, do_peer=True, dbg=(), nexp=16384, preconv=True):
    nc = bass.Bass("TRN2", target_bir_lowering=False)
    di = lambda name, shape, dt=F32: nc.dram_tensor(name, list(shape), dt, kind="ExternalInput").ap()
    do = lambda name, shape, dt=F32: nc.dram_tensor(name, list(shape), dt, kind="ExternalOutput").ap()
    x_d = di("x", [TC, D])
    sC_d = di("sC", [DEPTH, NSEG, 4, 128, 128]); sn_d = di("sn", [DEPTH, NSEG, 4, 128]); sm_d = di("sm", [DEPTH, NSEG, 4])
    sS_d = di("sS", [DEPTH, NSEG, 256, 128]); sh_d = di("sh", [DEPTH, NSEG, 512, 64]); scv_d = di("scv", [DEPTH, NSEG * 3, 768])
    win_d = di("w_in", [DEPTH, D, INW]); wbr_d = di("w_br", [DEPTH, 1536, D]); wout_d = di("w_out", [DEPTH, D, D])
    wq_d = di("p_wq", [DEPTH, D, 2048]); keys_d = di("keysT", [DEPTH, 128, 16, 128])
    pu_d = [di("p_u%d" % l, [nexp, D]) for l in range(DEPTH)]; pv_d = [di("p_v%d" % l, [nexp, D]) for l in range(DEPTH)]
    rowp_d = di("rowp", [DEPTH, 128, RPW]); convp_d = di("convp", [DEPTH, 128, 6, 5]); gaup_d = di("gaup", [DEPTH, 16, 256])
    cst_d = di("cst", [128, COW]); segm_d = di("segm", [128, 1024]); lnp_d = di("lnp", [DEPTH, 2, 128, 2048])
    y_d = do("y", [TC, D])
    pC_d = do("pC", [DEPTH, 4, 128, 128]); pn_d = do("pn", [DEPTH, 4, 128]); pm_d = do("pm", [DEPTH, 4])
    pS_d = do("pS", [DEPTH, 256, 128]); ph_d = do("ph", [DEPTH, 512, 64]); pcv_d = do("pcv", [DEPTH, 3, 768])
    oC_d = do("oC", [DEPTH, NSEG, 4, 128, 128]); on_d = do("on", [DEPTH, NSEG, 4, 128]); om_d = do("om", [DEPTH, NSEG, 4])
    oS_d = do("oS", [DEPTH, NSEG, 256, 128]); oh_d = do("oh", [DEPTH, NSEG, 512, 64]); ocv_d = do("ocv", [DEPTH, NSEG * 3, 768])
    dbg_d = {k: do("dbg_" + k, shp) for k, shp in dbg}
    dscr = lambda name, shape: nc.dram_tensor(name, list(shape), BF16).ap()
    wins = dscr("wins", [DEPTH, D, INW]); wbrs = dscr("wbrs", [DEPTH, 1536, D]); wouts = dscr("wouts", [DEPTH, D, D])
    wqs = dscr("wqs", [DEPTH, D, 2048]); keyss = dscr("keyss", [DEPTH, 128, 2048])
    uvs = [dscr("uvs%d" % l, [nexp, 2 * D]) for l in range(DEPTH)]

    with contextlib.ExitStack() as st:
        S = Sched(nc, st)
        res = {}

        def sb(name, shape, dt=F32):
            t = st.enter_context(nc.sbuf_tensor("sb_" + name, list(shape), dt))
            res[name] = Res(name)
            return t

        def R(*names):
            return [res[n] for n in names]

        PS = [st.enter_context(nc.psum_tensor("ps%d" % i, [128, 512], F32)) for i in range(8)]
        PSR = [Res("ps%d" % i) for i in range(8)]
        psn = [0]

        def bank():
            i = psn[0]
            psn[0] = (i + 1) % 3
            return PS[i], PSR[i]

        cst = sb("cst", [128, COW])
        S.dma("sp", cst[:], cst_d, writes=R("cst"))
        C = lambda n: cst[:, CO[n][0]:CO[n][1]]
        cb = sb("cb", [128, 128 + 128 + 64 + 1024], BF16)
        S.op("dve", lambda e: e.tensor_copy(out=cb[:, 0:128], in_=C("leP")), R("cst"), R("cb"))
        S.op("dve", lambda e: e.tensor_copy(out=cb[:, 128:256], in_=C("ident")), R("cst"), R("cb"))
        S.op("dve", lambda e: e.tensor_copy(out=cb[:, 256:320], in_=C("leS")), R("cst"), R("cb"))
        ident = C("ident")
        ones = C("ones")

        Xs = [sb("X0", [128, D]), sb("X1", [128, D])]
        xT = sb("xT", [128, 8, 128], BF16)
        rowp = sb("rowp", [128, RPW])
        lnrow = sb("lnrow", [128, 2048])
        res["lnrowF"] = Res("lnrowF"); res["lnrowB"] = Res("lnrowB")
        convp = sb("convp", [128, 6, 5])
        gaupf = sb("gaupf", [16, 256]); gaupb = sb("gaupb", [16, 256], BF16)
        arow = sb("arow", [128, 8])
        wst = [sb("wst%d" % i, [128, 8, 264]) for i in range(2)]
        wbf = [sb("wbf%d" % i, [128, 8, 528], BF16) for i in range(2)]
        wsn = [0]
        mqT = sb("mqT", [128, 4, 128], BF16); mkT = sb("mkT", [128, 4, 128], BF16)
        mk_tm = sb("mk_tm", [128, 512], BF16); mv_tm = sb("mv_tm", [128, 512])
        mo_tm = sb("mo_tm", [128, 512]); mif = sb("mif", [128, 8])
        gqT = sb("gqT", [128, 2, 128]); gkT = sb("gkT", [128, 2, 128])
        gk_tm = sb("gk_tm", [128, 256]); gv_tm = sb("gv_tm", [128, 512], BF16); gr_tm = sb("gr_tm", [128, 512])
        gaT = sb("gaT", [16, 128], BF16)
        sz_tm = sb("sz_tm", [128, 512]); sdt = sb("sdt", [128, 8])
        xp = sb("xp", [128, 6, 3 + 128])
        br = sb("br", [128, 1536])
        brT = sb("brT", [128, 12, 128], BF16)
        CT = [sb("CT%d" % l, [128, 4, 129]) for l in range(DEPTH)]
        CTb = sb("CTb", [128, 4, 129], BF16)
        mrun = [sb("mrun%d" % l, [4, 1]) for l in range(DEPTH)]
        Sst = [sb("Sst%d" % l, [128, 2, 128]) for l in range(DEPTH)]
        Sb = sb("Sb", [128, 2, 128], BF16)
        hT = [sb("hT%d" % l, [128, 256]) for l in range(DEPTH)]
        hTb = sb("hTb", [128, 256], BF16)
        cvh = [sb("cvh%d" % l, [128, 6, 3]) for l in range(DEPTH)]
        for l in range(DEPTH):
            S.op("pool", lambda e, l=l: e.memset(CT[l][:], 0.0), (), R("CT%d" % l))
            S.op("pool", lambda e, l=l: e.memset(mrun[l][:], 0.0), (), R("mrun%d" % l))
            S.op("pool", lambda e, l=l: e.memset(Sst[l][:], 0.0), (), R("Sst%d" % l))
            S.op("pool", lambda e, l=l: e.memset(hT[l][:], 0.0), (), R("hT%d" % l))
            S.op("pool", lambda e, l=l: e.memset(cvh[l][:], 0.0), (), R("cvh%d" % l))
        t8 = sb("t8", [128, 64]); u8 = sb("u8", [128, 8]); w4 = sb("w4", [128, 4]); eb4 = sb("eb4", [128, 4])
        eB4 = sb("eB4", [128, 4]); uT = sb("uT", [4, 256]); sm4 = sb("sm4", [4, 64])
        SM = sb("SM", [128, 4, 128], BF16); VW = sb("VW", [128, 4, 129], BF16)
        hm = sb("hm", [128, 512]); hm2 = sb("hm2", [128, 512]); st4 = sb("st4", [128, 16])
        CTs = sb("CTs", [128, 129])
        big = sb("big", [128, 1024]); big2 = sb("big2", [128, 1024])
        bigb = sb("bigb", [128, 1024], BF16)
        S.dma("sp", big[:, :], segm_d, writes=R("big"))
        S.op("dve", lambda e: e.tensor_copy(out=cb[:, 320:1344], in_=big[:, :]), R("big"), R("cb"))
        nl = sb("nl", [128, 256]); eT = sb("eT", [128, 2, 2, 128]); qtT = sb("qtT", [128, 2, 128], BF16)
        ktT = sb("ktT", [128, 2, 128], BF16); kt_tm = sb("kt_tm", [128, 256], BF16)
        AM = sb("AM", [128, 4, 128], BF16); St = sb("St", [128, 128])
        actT = sb("actT", [128, 6, 128]); BCb = sb("BCb", [128, 2, 128], BF16)
        xs_tm = sb("xs_tm", [128, 512]); Bm_tm = sb("Bm_tm", [128, 128], BF16)
        dts = sb("dts", [128, 40]); cbm = sb("cbm", [128, 2, 128])
        xsb = sb("xsb", [128, 512], BF16); xsw = sb("xsw", [128, 512], BF16)
        eL = sb("eL", [128, 8])
        mixT = sb("mixT", [128, 8, 128], BF16); gsig = sb("gsig", [128, 128])
        lnt = sb("lnt", [128, 8])
        qTb = sb("qTb", [128, 16, 128], BF16)
        V16 = sb("V16", [128, 16, 16]); I16 = sb("I16", [128, 16, 16], U32); I16f = sb("I16f", [128, 16, 16])
        wk = sb("wk", [128, 256])
        TV = sb("TV", [128, 8, 16]); TPu = sb("TPu", [128, 8, 16], U32)
        pks = [sb("pk0", [128, 8, 128]), sb("pk1", [128, 8, 128])]
        eidxs = [sb("eidx0", [128, 128], I32), sb("eidx1", [128, 128], I32)]
        lntB = sb("lntB", [128, 8])
        thr16 = sb("thr16", [128, 16])
        xb16 = sb("xb16", [128, 1024], BF16)
        prodb = [sb("prodb%d" % i, [128, 1024], BF16) for i in range(2)]
        dgb = [sb("dgb%d" % i, [128, 128], BF16) for i in range(4)]
        S.op("dve", lambda e: e.tensor_scalar(out=thr16[:], in0=C("iota16"), scalar1=16.0, scalar2=16.0, op0=ALU.mult, op1=ALU.add), R("cst"), R("thr16"))
        sampbuf = {}
        sampres = {}
        SA = sb("SA", [128, 4096])
        SBb = sb("SBb", [128, 12288], BF16)
        sampbuf["n0T"] = sb("sp_n0T", [128, 4, 16]); sampres["n0T"] = res["sp_n0T"]
        SC = SA[:, 0:2048]
        res["SC"] = Res("SC")
        SBf = SBb[:, :].bitcast(F32)
        NGU = 6
        GU = [SBb[:, q * 2048:(q + 1) * 2048] for q in range(NGU)]
        for q in range(NGU):
            res["GU%d" % q] = Res("GU%d" % q)
        res["acc"] = Res("acc")
        fsc = sb("fsc", [1, 4])
        peer_alias = ["SC", "acc"] + ["GU%d" % q for q in range(NGU)]

        def fence():
            S.op("dve", lambda e: e.memset(fsc[0:1, 0:1], 0.0), (), R("SA", "SBb", "fsc", *peer_alias))

        sampbuf["C0t"] = SA[:, 0:2048].rearrange("p (s d) -> p s d", d=128); sampres["C0t"] = res["SA"]
        sampbuf["Co"] = SA[:, 2048:4096].rearrange("p (s d) -> p s d", d=128); sampres["Co"] = res["SA"]
        sampbuf["mqS"] = SBb[:, 0:4096].rearrange("p (h t) -> p h t", t=1024); sampres["mqS"] = res["SBb"]
        sampbuf["CTbs"] = SBb[:, 4096:4096 + 2064].rearrange("p (s c) -> p s c", c=129); sampres["CTbs"] = res["SBb"]
        sampbuf["VWs"] = SBb[:, 6160:6160 + 2064].rearrange("p (s c) -> p s c", c=129); sampres["VWs"] = res["SBb"]

        def evac(eng, out, in_, rd, wr, scale=None):
            if eng == "act":
                if scale is None:
                    S.op("act", lambda e: e.copy(out=out, in_=in_), rd, wr)
                else:
                    S.op("act", lambda e: e.mul(out=out, in_=in_, mul=scale), rd, wr)
            else:
                if scale is None:
                    S.op("dve", lambda e: e.tensor_copy(out=out, in_=in_), rd, wr)
                else:
                    S.op("dve", lambda e: e.tensor_scalar(out=out, in0=in_, scalar1=scale, scalar2=None, op0=ALU.mult), rd, wr)

        def load_w2(src_ap, kch, width):
            i = wsn[0]
            wsn[0] = (i + 1) % NWP
            bv = wpool[i][:, 0:kch * width].rearrange("p (k n) -> p k n", n=width)
            S.dma("sp", bv, src_ap.rearrange("(k p) n -> p k n", p=128), reads=R("scratch"), writes=[wpres[i]])
            return bv, wpres[i]

        def load_w(src_ap, width):
            return load_w2(src_ap, 8, width)

        def fm_piece(wb, wr_, a, b, N, kch=8, src=None, srcres=None):
            src = xT if src is None else src
            srcres = res["xT"] if srcres is None else srcres
            p, pr = bank()
            for k in range(kch):
                S.op("pe", lambda e, k=k: e.matmul(p[0:b - a, 0:N], lhsT=wb[:, k, a:b], rhs=src[:, k, 0:N],
                                                    start=(k == 0), stop=(k == kch - 1)), [wr_, srcres], [pr])
            return p, pr

        def tm_piece(wb, wr_, a, b, P, kch=8, src=None, srcres=None):
            src = xT if src is None else src
            srcres = res["xT"] if srcres is None else srcres
            p, pr = bank()
            for k in range(kch):
                S.op("pe", lambda e, k=k: e.matmul(p[0:P, 0:b - a], lhsT=src[:, k, 0:P], rhs=wb[:, k, a:b],
                                                    start=(k == 0), stop=(k == kch - 1)), [wr_, srcres], [pr])
            return p, pr

        def transpose_to(dst_fn, src, srcres, P, nch, dstres, eng="act"):
            for c0 in range(0, nch, 4):
                c1 = min(nch, c0 + 4)
                p, pr = bank()
                for c in range(c0, c1):
                    S.op("pe", lambda e, c=c: e.transpose(out=p[:, (c - c0) * 128:(c - c0) * 128 + P],
                                                          in_=src[0:P, c * 128:(c + 1) * 128], identity=ident[0:P, 0:P]),
                         [srcres, res["cst"]], [pr])
                pv = p[:, 0:(c1 - c0) * 128].rearrange("p (c t) -> p c t", t=128)[:, :, 0:P]
                evac(eng, dst_fn(c0, c1), pv, [pr], [dstres])

        def act_fn(out, in_, func, rd, wr, **kw):
            S.op("act", lambda e: e.activation(out=out, in_=in_, func=func, **kw), rd, wr)

        def softplus_neg(out, in_, rd, wr, tmp, tmpres, sign=-1.0):
            act_fn(tmp, in_, AF.Exp, rd, [tmpres], scale=sign)
            act_fn(out, tmp, AF.Ln, [tmpres], wr, bias=1.0)

        def rstd_from_ssq(out, ssq, n, rd, wr):
            act_fn(out, ssq, AF.Sqrt, rd, wr, scale=1.0 / n, bias=EPS)
            S.op("dve", lambda e: e.reciprocal(out=out, in_=out), wr, wr)

        def layernorm_tm(P, src, srcres, gb_dram, out, outres, lnt_, lntn, scr, scrres, lnb, lnbn):
            LR = [res[lntn]]
            S.op("dve", lambda e: e.tensor_reduce(out=lnt_[0:P, 0:1], in_=src[0:P, :], axis=AX.X, op=ALU.add), [srcres], LR)
            S.op("dve", lambda e: e.tensor_scalar(out=lnt_[0:P, 1:2], in0=lnt_[0:P, 0:1], scalar1=-1.0 / D, scalar2=None, op0=ALU.mult), LR, LR)
            S.op("dve", lambda e: e.tensor_scalar(out=src[0:P, :], in0=src[0:P, :], scalar1=lnt_[0:P, 1:2], scalar2=None, op0=ALU.add), [srcres] + LR, [srcres])
            S.op("dve", lambda e: e.tensor_tensor(out=scr[0:P, :], in0=src[0:P, :], in1=src[0:P, :], op=ALU.mult), [srcres], [scrres])
            S.op("dve", lambda e: e.tensor_reduce(out=lnt_[0:P, 2:3], in_=scr[0:P, :], axis=AX.X, op=ALU.add), [scrres], LR)
            rstd_from_ssq(lnt_[0:P, 3:4], lnt_[0:P, 2:3], D, LR, LR)
            S.dma("sp", lnb, gb_dram[:, 0:1024], writes=[res[lnbn]])
            S.op("dve", lambda e: e.scalar_tensor_tensor(out=scr[0:P, :], in0=src[0:P, :], scalar=lnt_[0:P, 3:4], in1=lnb[0:P, :], op0=ALU.mult, op1=ALU.mult),
                 [srcres, res[lnbn]] + LR, [scrres])
            S.dma("sp", lnb, gb_dram[:, 1024:2048], writes=[res[lnbn]])
            S.op("dve", lambda e: e.tensor_tensor(out=out, in0=scr[0:P, :], in1=lnb[0:P, :], op=ALU.add), [scrres, res[lnbn]], [outres])

        dbg_i = [0]

        def dbg_out(name, ap, rd):
            if name in dbg_d:
                S.dma("sp", dbg_d[name][dbg_i[0], 0:ap.shape[0]], ap, reads=rd)

        cvn = [0]

        def convert(src2d, dst2d, three=False):
            n = 2048 if three else src2d.shape[1]
            for a in range(0, n, 2112):
                b = min(n, a + 2112)
                i = cvn[0] % 2
                eng = ("dve", "act", "pool")[cvn[0] % 3]
                cvn[0] += 1
                fv = wst[i][:].rearrange("p k n -> p (k n)")[:, 0:b - a]
                bv = wbf[i][:].rearrange("p k n -> p (k n)")[:, 0:b - a]
                if three:
                    S.dma("sp", fv.rearrange("p (k c) -> p k c", c=1024), src2d, writes=R("wst%d" % i))
                    S.op(eng, (lambda e: e.copy(out=bv, in_=fv)) if eng == "act" else (lambda e: e.tensor_copy(out=bv, in_=fv)), R("wst%d" % i), R("wbf%d" % i))
                    S.dma("sp", dst2d, bv.rearrange("p (k c) -> p k c", c=1024), reads=R("wbf%d" % i), writes=R("scratch"))
                    continue
                S.dma("sp", fv, src2d[:, a:b], writes=R("wst%d" % i))
                if eng == "act":
                    S.op("act", lambda e: e.copy(out=bv, in_=fv), R("wst%d" % i), R("wbf%d" % i))
                else:
                    S.op(eng, lambda e: e.tensor_copy(out=bv, in_=fv), R("wst%d" % i), R("wbf%d" % i))
                S.dma("sp", dst2d[:, a:b], bv, reads=R("wbf%d" % i), writes=R("scratch"))

        res["scratch"] = Res("scratch")
        flat = lambda ap: ap.rearrange("(p k) c -> p (k c)", p=128)
        if preconv:
            for l in range(nlayers):
                convert(flat(win_d[l]), flat(wins[l]))
                convert(flat(wbr_d[l]), flat(wbrs[l]))
                convert(flat(wout_d[l]), flat(wouts[l]))
                convert(flat(wq_d[l]), flat(wqs[l]))
                convert(keys_d[l].rearrange("c a k -> c (a k)"), keyss[l])
                if do_peer:
                    uv3 = uvs[l].rearrange("(p k) c -> p k c", p=128)
                    for half, src in enumerate((pu_d[l], pv_d[l])):
                        s3 = src.rearrange("(p k) c -> p k c", p=128)
                        for k0 in range(0, nexp // 128, 2):
                            convert(s3[:, k0:k0 + 2, :], uv3[:, k0:k0 + 2, half * D:(half + 1) * D], three=True)
        wpool = [wbf[0][:].rearrange("p k n -> p (k n)"), wbf[1][:].rearrange("p k n -> p (k n)")]
        for i in range(2):
            v = wst[i][:].rearrange("p k n -> p (k n)").bitcast(BF16)
            wpool += [v[:, 0:4224]]
        NWP = len(wpool)
        wpres = [Res("wp%d" % i) for i in range(NWP)]
        S.op("dve", lambda e: e.memset(fsc[0:1, 1:2], 0.0), (), R("wst0", "wst1", "wbf0", "wbf1", "scratch", "fsc") + wpres)

        all_tiles = [("p", i) for i in range(SEQ // 128)] + [("s", 0)]
        if tiles is not None:
            all_tiles = tiles
        last_prompt = SEQ // 128 - 1
        def tile_layer(kind, ti, l, par, tpar):
            samp = kind == "s"
            P = 64 if samp else 128
            t0 = SEQ if samp else ti * 128
            N = P
            X = Xs[par]; XN = "X%d" % par
            pk = pks[tpar]; PKN = "pk%d" % tpar
            eidx = eidxs[tpar]; EIN = "eidx%d" % tpar
            if l == 0:
                S.dma("sp", X[0:P, :], x_d[t0:t0 + P, :], writes=R(XN))
            yield
            le_f = C("leS")[0:P, :] if samp else C("leP")
            yield
            gt_f = C("gtS")[0:P, :] if samp else C("gtP")
            yield
            le_b = cb[0:P, 256:320] if samp else cb[:, 0:128]
            yield
            S.dma("sp", rowp[:], rowp_d[l], writes=R("rowp"))
            yield
            S.dma("sp", convp[:], convp_d[l], writes=R("convp"))
            yield
            S.dma("sp", gaupf[:], gaup_d[l], writes=R("gaupf"))
            yield
            S.op("dve", lambda e: e.tensor_copy(out=gaupb[:], in_=gaupf[:]), R("gaupf"), R("gaupb"))
            yield
            rp = lambda n: rowp[0:P, RP[n][0]:RP[n][1]]
            yield
            act_fn(arow[:], rowp[:, RP["alog"][0]:RP["alog"][1]], AF.Exp, R("rowp"), R("arow"))
            yield
            transpose_to(lambda c0, c1: xT[:, c0:c1, 0:P], X, res[XN], P, 8, res["xT"])
            yield
            W = lambda a, b: wins[l, :, a:b]
            yield
            wb, wr_ = load_w(W(0, 512), 512)
            yield
            for h in range(4):
                p, pr = fm_piece(wb, wr_, h * 128, (h + 1) * 128, N)
                evac("act", mqT[:, h, 0:N], p[:, 0:N], [pr], R("mqT"), scale=128 ** -0.5)
            yield
            wb, wr_ = load_w(W(512, 1024), 512)
            yield
            for h in range(4):
                p, pr = fm_piece(wb, wr_, h * 128, (h + 1) * 128, N)
                evac("act", mkT[:, h, 0:N], p[:, 0:N], [pr], R("mkT"))
            yield
            p, pr = tm_piece(wb, wr_, 0, 512, P)
            yield
            evac("dve", mk_tm[0:P, :], p[0:P, :], [pr], R("mk_tm"))
            yield
            wb, wr_ = load_w(W(1024, 1536), 512)
            yield
            p, pr = tm_piece(wb, wr_, 0, 512, P)
            yield
            evac("act", mv_tm[0:P, :], p[0:P, :], [pr], R("mv_tm"))
            yield
            wb, wr_ = load_w(W(1536, 2048), 512)
            yield
            p, pr = tm_piece(wb, wr_, 0, 512, P)
            yield
            evac("dve", mo_tm[0:P, :], p[0:P, :], [pr], R("mo_tm"))
            yield
            wb, wr_ = load_w(W(2048, 2568), 520)
            yield
            p, pr = tm_piece(wb, wr_, 0, 8, P)
            yield
            evac("act", mif[0:P, :], p[0:P, 0:8], [pr], R("mif"))
            yield
            for j in range(2):
                p, pr = fm_piece(wb, wr_, 8 + j * 128, 8 + (j + 1) * 128, N)
                evac("act", gqT[:, j, 0:N], p[:, 0:N], [pr], R("gqT"), scale=64 ** -0.5)
                p, pr = fm_piece(wb, wr_, 264 + j * 128, 264 + (j + 1) * 128, N)
                evac("dve", gkT[:, j, 0:N], p[:, 0:N], [pr], R("gkT"))
            yield
            p, pr = tm_piece(wb, wr_, 264, 520, P)
            yield
            evac("act", gk_tm[0:P, :], p[0:P, 0:256], [pr], R("gk_tm"))
            yield
            wb, wr_ = load_w(W(2568, 3080), 512)
            yield
            p, pr = tm_piece(wb, wr_, 0, 512, P)
            yield
            evac("dve", gv_tm[0:P, :], p[0:P, :], [pr], R("gv_tm"))
            yield
            wb, wr_ = load_w(W(3080, 3608), 528)
            yield
            p, pr = tm_piece(wb, wr_, 0, 512, P)
            yield
            evac("act", gr_tm[0:P, :], p[0:P, :], [pr], R("gr_tm"))
            yield
            p, pr = fm_piece(wb, wr_, 512, 528, N)
            yield
            evac("dve", gaT[0:16, 0:N], p[0:16, 0:N], [pr], R("gaT"))
            yield
            wb, wr_ = load_w(W(3608, 4120), 512)
            yield
            p, pr = tm_piece(wb, wr_, 0, 512, P)
            yield
            evac("act", sz_tm[0:P, :], p[0:P, :], [pr], R("sz_tm"))
            yield
            if samp:
                cv0 = big2[0:48, 0:768]
                S.dma("sp", cv0, scv_d[l], writes=R("big2"))
                for c0 in (0, 4):
                    p, pr = bank()
                    for c in range(c0, min(6, c0 + 4)):
                        S.op("pe", lambda e, c=c: e.transpose(out=p[:, (c - c0) * 128:(c - c0) * 128 + 48], in_=cv0[:, c * 128:(c + 1) * 128],
                                                              identity=ident[0:48, 0:48]), R("big2", "cst"), [pr])
                    ncn = min(6, c0 + 4) - c0
                    pv = p[:, 0:ncn * 128].rearrange("p (c t) -> p c t", t=128)[:, :, 0:48].rearrange("p c (s j) -> p c s j", j=3)
                    dst = xp[:, c0:c0 + ncn, 0:NSEG * 7].rearrange("p c (s j) -> p c s j", j=7)[:, :, :, 0:3]
                    evac("dve", dst, pv, [pr], R("xp"))
            else:
                S.op("dve", lambda e: e.tensor_copy(out=xp[:, :, 0:3], in_=cvh[l][:]), R("cvh%d" % l), R("xp"))
            yield
            wb, wr_ = load_w(W(4120, 4632), 512)
            yield
            wb2, wr2 = load_w(W(4632, 4896), 264)
            yield
            for c in range(6):
                if c < 4:
                    p, pr = fm_piece(wb, wr_, c * 128, (c + 1) * 128, N)
                else:
                    p, pr = fm_piece(wb2, wr2, (c - 4) * 128, (c - 3) * 128, N)
                if samp:
                    dst = xp[:, c, 0:NSEG * 7].rearrange("p (s j) -> p s j", j=7)[:, :, 3:7]
                    evac("act", dst, p[:, 0:N].rearrange("p (s j) -> p s j", j=4), [pr], R("xp"))
                else:
                    evac("act", xp[:, c, 3:3 + N], p[:, 0:N], [pr], R("xp"))
            yield
            p, pr = tm_piece(wb2, wr2, 256, 264, P)
            yield
            evac("dve", sdt[0:P, :], p[0:P, 0:8], [pr], R("sdt"))
            yield
            if not samp:
                S.op("dve", lambda e: e.tensor_copy(out=cvh[l][:], in_=xp[:, :, N:N + 3]), R("xp"), R("cvh%d" % l))
            yield
            if samp or (ti == last_prompt and (LASTM & 1)):
                nr = 48 if samp else 3
                cvt = big[:, 0:6 * 48].rearrange("p (c r) -> p c r", r=48)
                if samp:
                    srcv = xp[:, :, 0:NSEG * 7].rearrange("p c (s j) -> p c s j", j=7)[:, :, :, 4:7]
                    S.op("dve", lambda e: e.tensor_copy(out=cvt.rearrange("p c (s j) -> p c s j", j=3), in_=srcv), R("xp"), R("big"))
                else:
                    S.op("dve", lambda e: e.tensor_copy(out=cvt[:, :, 0:3], in_=xp[:, :, N:N + 3]), R("xp"), R("big"))
                for c0 in (0, 4):
                    p, pr = bank()
                    ncn = min(6, c0 + 4) - c0
                    for c in range(c0, c0 + ncn):
                        S.op("pe", lambda e, c=c: e.transpose(out=p[0:nr, (c - c0) * 128:(c - c0 + 1) * 128], in_=cvt[:, c, 0:nr], identity=ident),
                             R("big", "cst"), [pr])
                    evac("act", big2[0:nr, c0 * 128:(c0 + ncn) * 128], p[0:nr, 0:ncn * 128], [pr], R("big2"))
                S.dma("sp", (ocv_d[l] if samp else pcv_d[l]), big2[0:nr, 0:768], reads=R("big2"))

            yield
            nseg = NSEG if samp else 1
            yield
            L = P // nseg
            yield
            S.op("dve", lambda e: e.tensor_tensor(out=t8[0:P, 0:4], in0=mif[0:P, 0:4], in1=rp("mib"), op=ALU.add), R("mif", "rowp"), R("t8"))
            yield
            S.op("dve", lambda e: e.tensor_tensor(out=t8[0:P, 4:8], in0=mif[0:P, 4:8], in1=rp("mfb"), op=ALU.add), R("mif", "rowp"), R("t8"))
            yield
            softplus_neg(t8[0:P, 12:16], t8[0:P, 4:8], R("t8"), R("t8"), t8[0:P, 8:12], res["t8"])
            yield
            p, pr = bank()
            yield
            S.op("pe", lambda e: e.matmul(p[0:P, 0:4], lhsT=le_f, rhs=t8[0:P, 12:16], start=True, stop=True), R("cst", "t8"), [pr])
            yield
            S.op("pe", lambda e: e.matmul(p[:, 8:12], lhsT=ones[0:P, :], rhs=t8[0:P, 12:16], start=True, stop=True), R("cst", "t8"), [pr])
            yield
            S.op("dve", lambda e: e.tensor_tensor(out=u8[0:P, 0:4], in0=p[0:P, 0:4], in1=t8[0:P, 0:4], op=ALU.add), [pr] + R("t8"), R("u8"))
            yield
            S.op("dve", lambda e: e.tensor_copy(out=u8[0:P, 4:8], in_=p[0:P, 0:4]), [pr], R("u8"))
            yield
            act_fn(w4[0:P, :], u8[0:P, 0:4], AF.Exp, R("u8"), R("w4"))
            yield
            act_fn(eb4[0:P, :], p[0:P, 0:4], AF.Exp, [pr], R("eb4"), scale=-1.0)
            yield
            act_fn(eB4[:, :], p[:, 8:12], AF.Exp, [pr], R("eB4"), scale=-1.0)
            yield
            p2, pr2 = bank()
            yield
            S.op("pe", lambda e: e.transpose(out=p2[0:4, 0:P], in_=u8[0:P, 0:4], identity=ident[0:P, 0:P]), R("u8", "cst"), [pr2])
            yield
            S.op("pe", lambda e: e.transpose(out=p2[0:4, 128:128 + P], in_=u8[0:P, 4:8], identity=ident[0:P, 0:P]), R("u8", "cst"), [pr2])
            yield
            S.op("dve", lambda e: e.tensor_copy(out=uT[:, :], in_=p2[0:4, 0:256]), [pr2], R("uT"))
            yield
            S.op("dve", lambda e: e.tensor_reduce(out=sm4[:, 0:nseg], in_=uT[:, 0:P].rearrange("h (s j) -> h s j", j=L), axis=AX.X, op=ALU.max), R("uT"), R("sm4"))
            yield
            blast = uT[:, 128:128 + P].rearrange("h (s j) -> h s j", j=L)[:, :, L - 1]
            yield
            if not samp:
                S.op("dve", lambda e: e.tensor_tensor(out=mrun[l][:], in0=mrun[l][:], in1=sm4[:, 0:1], op=ALU.max), R("mrun%d" % l, "sm4"), R("mrun%d" % l))
                S.op("dve", lambda e: e.tensor_tensor(out=mrun[l][:], in0=mrun[l][:], in1=blast, op=ALU.subtract), R("mrun%d" % l, "uT"), R("mrun%d" % l))
            else:
                S.dma("sp", sm4[:, 16:32], sm_d[l].rearrange("s h -> h s"), writes=R("sm4"), slow=True)
                S.op("dve", lambda e: e.tensor_tensor(out=sm4[:, 32:48], in0=sm4[:, 0:16], in1=sm4[:, 16:32], op=ALU.max), R("sm4"), R("sm4"))
                S.op("dve", lambda e: e.tensor_tensor(out=sm4[:, 48:64], in0=sm4[:, 32:48], in1=blast, op=ALU.subtract), R("sm4", "uT"), R("sm4"))
                S.dma("sp", om_d[l].rearrange("s h -> h s"), sm4[:, 48:64], reads=R("sm4"), slow=True)
                act_fn(sm4[:, 0:16], sm4[:, 32:48], AF.Exp, R("sm4"), R("sm4"), scale=-1.0)
                act_fn(sm4[:, 48:64], sm4[:, 16:32], AF.Exp, R("sm4"), R("sm4"))
                for q, (a0) in enumerate((0, 48)):
                    S.op("dve", lambda e, q=q, a0=a0: e.tensor_tensor(
                        out=hm2[0:4, q * 64:(q + 1) * 64].rearrange("p (h s) -> p h s", s=16),
                        in0=sm4[:, a0:a0 + 16].unsqueeze(1).to_broadcast([4, 4, 16]),
                        in1=ident[0:4, 0:4].unsqueeze(2).to_broadcast([4, 4, 16]), op=ALU.mult), R("sm4", "cst"), R("hm2"))
                p3, pr3 = bank()
                S.op("pe", lambda e: e.matmul(p3[:, 0:128], lhsT=ones[0:4, :], rhs=hm2[0:4, 0:128], start=True, stop=True), R("cst", "hm2"), [pr3])
                S.op("dve", lambda e: e.tensor_copy(out=big2[:, 0:128], in_=p3[:, 0:128]), [pr3], R("big2"))
                S.op("dve", lambda e: e.tensor_tensor(out=big2[:, 128:192], in0=big2[:, 0:64], in1=big2[:, 64:128], op=ALU.mult), R("big2"), R("big2"))
            yield
            for h in range(4):
                p, pr = bank()
                S.op("pe", lambda e, h=h: e.matmul(p[0:P, 0:P], lhsT=mkT[:, h, 0:P], rhs=mqT[:, h, 0:P], start=True, stop=True), R("mkT", "mqT"), [pr])
                S.op("dve", lambda e, h=h: e.tensor_tensor(out=SM[0:P, h, 0:P], in0=p[0:P, 0:P], in1=le_f[:, 0:P], op=ALU.mult), [pr] + R("cst"), R("SM"))
            yield
            S.op("dve", lambda e: e.tensor_tensor(out=VW[0:P, :, 0:128], in0=mv_tm[0:P, :].rearrange("p (h v) -> p h v", v=128),
                                                  in1=w4[0:P, :].unsqueeze(2).to_broadcast([P, 4, 128]), op=ALU.mult), R("mv_tm", "w4"), R("VW"))
            yield
            S.op("dve", lambda e: e.tensor_copy(out=VW[0:P, :, 128:129], in_=w4[0:P, :].unsqueeze(2)), R("w4"), R("VW"))
            yield
            if not samp:
                S.op("act", lambda e: e.copy(out=CTb[:], in_=CT[l][:]), R("CT%d" % l), R("CTb"))
            else:
                pass
            yield
            if samp:
                CTbs = sampbuf["CTbs"]
                C0t = sampbuf["C0t"]
                n0T = sampbuf["n0T"]
                for hh in range(4):
                    S.dma("sp", n0T[:, hh, :], sn_d[l, :, hh, :].rearrange("s d -> d s"), writes=[sampres["n0T"]], slow=True)
                mqS = sampbuf["mqS"]
                S.op("dve", lambda e: e.tensor_tensor(out=mqS[:].rearrange("p h (s t) -> p h s t", t=64),
                                                      in0=mqT[:, :, 0:64].unsqueeze(2).to_broadcast([128, 4, 16, 64]),
                                                      in1=cb[:, 320:1344].rearrange("p (s t) -> p s t", t=64).unsqueeze(1).to_broadcast([128, 4, 16, 64]),
                                                      op=ALU.mult), R("mqT", "cb"), [sampres["mqS"]])
            yield
            nd_banks = [(PS[6], PSR[6]), (PS[7], PSR[7])]
            yield
            for h in range(4):
                ndp, ndr = nd_banks[h // 2]
                nd = ndp[0:P, (h % 2) * 129:(h % 2) * 129 + 129]
                S.op("pe", lambda e, h=h, nd=nd: e.matmul(nd, lhsT=SM[0:P, h, 0:P], rhs=VW[0:P, h, :], start=True, stop=False), R("SM", "VW"), [ndr])
                if not samp:
                    S.op("pe", lambda e, h=h, nd=nd: e.matmul(nd, lhsT=mqT[:, h, 0:P], rhs=CTb[:, h, :], start=False, stop=True), R("mqT", "CTb"), [ndr])
                else:
                    S.dma("sp", C0t[:], sC_d[l, :, h].rearrange("s v d -> v s d"), writes=[sampres["C0t"]])
                    for s in range(NSEG):
                        p, pr = bank()
                        S.op("pe", lambda e, s=s: e.transpose(out=p[:, 0:128], in_=C0t[:, s, :], identity=ident), [sampres["C0t"]] + R("cst"), [pr])
                        S.op("dve", lambda e, s=s, h=h: e.tensor_scalar(out=CTbs[:, s, 0:128], in0=p[:, 0:128], scalar1=big2[:, 64 + h * 16 + s:64 + h * 16 + s + 1],
                                                                       scalar2=None, op0=ALU.mult), [pr] + R("big2"), [sampres["CTbs"]])
                    S.op("dve", lambda e, h=h: e.tensor_tensor(out=CTbs[:, :, 128], in0=n0T[:, h, :], in1=big2[:, 64 + h * 16:64 + h * 16 + 16], op=ALU.mult),
                         [sampres["n0T"]] + R("big2"), [sampres["CTbs"]])
                    for s in range(NSEG):
                        S.op("pe", lambda e, s=s, h=h, nd=nd: e.matmul(nd, lhsT=mqS[:, h, s * 64:(s + 1) * 64], rhs=CTbs[:, s, :], start=False, stop=(s == NSEG - 1)),
                             [sampres["mqS"], sampres["CTbs"]], [ndr])
                    VWs = sampbuf["VWs"]
                    S.op("dve", lambda e, h=h: e.tensor_tensor(out=VWs[0:64, :, :], in0=VW[0:64, h, :].unsqueeze(1).to_broadcast([64, 16, 129]),
                                                               in1=C("segind")[0:64, :].unsqueeze(2).to_broadcast([64, 16, 129]), op=ALU.mult),
                         R("VW", "cst"), [sampres["VWs"]])
                    Co = sampbuf["Co"]
                    for s in range(NSEG):
                        p, pr = bank()
                        S.op("pe", lambda e, s=s, h=h: e.matmul(p[:, 0:128], lhsT=VWs[0:64, s, 0:128], rhs=mk_tm[0:64, h * 128:(h + 1) * 128], start=True, stop=True),
                             [sampres["VWs"]] + R("mk_tm"), [pr])
                        col = h * 16 + s
                        S.op("dve", lambda e, s=s, col=col: e.tensor_scalar(out=St[:, :], in0=C0t[:, s, :], scalar1=big2[:, 128 + col:129 + col], scalar2=None, op0=ALU.mult),
                             [sampres["C0t"]] + R("big2"), R("St"))
                        S.op("dve", lambda e, s=s, col=col: e.scalar_tensor_tensor(out=Co[:, s, :], in0=p[:, 0:128], scalar=big2[:, col:col + 1], in1=St[:, :],
                                                                                  op0=ALU.mult, op1=ALU.add), [pr] + R("big2", "St"), [sampres["Co"]])
                    S.dma("sp", oC_d[l, :, h].rearrange("s v d -> v s d"), Co[:], reads=[sampres["Co"]])
                    S.op("dve", lambda e, h=h: e.tensor_scalar(out=bigb[0:64, 0:16], in0=C("segind")[0:64, :], scalar1=w4[0:64, h:h + 1], scalar2=None, op0=ALU.mult),
                         R("cst", "w4"), R("bigb"))
                    p, pr = bank()
                    S.op("pe", lambda e, h=h: e.matmul(p[:, 0:16], lhsT=mk_tm[0:64, h * 128:(h + 1) * 128], rhs=bigb[0:64, 0:16], start=True, stop=True), R("mk_tm", "bigb"), [pr])
                    S.op("dve", lambda e, h=h: e.tensor_tensor(out=hm2[:, 256 + h * 16:256 + (h + 1) * 16], in0=n0T[:, h, :], in1=big2[:, 128 + h * 16:128 + (h + 1) * 16], op=ALU.mult),
                         [sampres["n0T"]] + R("big2"), R("hm2"))
                    S.op("dve", lambda e, h=h: e.tensor_tensor(out=hm2[:, 320 + h * 16:320 + (h + 1) * 16], in0=p[:, 0:16], in1=big2[:, h * 16:(h + 1) * 16], op=ALU.mult),
                         [pr] + R("big2"), R("hm2"))
            yield
            if samp:
                S.op("dve", lambda e: e.tensor_tensor(out=hm2[:, 256:320], in0=hm2[:, 256:320], in1=hm2[:, 320:384], op=ALU.add), R("hm2"), R("hm2"))
                for hh in range(4):
                    S.dma("sp", on_d[l, :, hh, :].rearrange("s d -> d s"), hm2[:, 256 + hh * 16:256 + (hh + 1) * 16], reads=R("hm2"), slow=True)
            yield
            for j in range(2):
                ndp, ndr = nd_banks[j]
                S.op("dve", lambda e, j=j, ndp=ndp: e.tensor_tensor(out=st4[0:P, 2 * j:2 * j + 2], in0=ndp[0:P, 0:258].rearrange("p (h c) -> p h c", c=129)[:, :, 128],
                                                                    in1=eb4[0:P, 2 * j:2 * j + 2], op=ALU.mult), [ndr] + R("eb4"), R("st4"))
            yield
            S.op("dve", lambda e: e.tensor_scalar(out=st4[0:P, 4:8], in0=st4[0:P, 0:4], scalar1=-1.0, scalar2=1.0, op0=ALU.mult, op1=ALU.max), R("st4"), R("st4"))
            yield
            S.op("dve", lambda e: e.tensor_tensor(out=st4[0:P, 4:8], in0=st4[0:P, 4:8], in1=st4[0:P, 0:4], op=ALU.max), R("st4"), R("st4"))
            yield
            S.op("dve", lambda e: e.reciprocal(out=st4[0:P, 8:12], in_=st4[0:P, 4:8]), R("st4"), R("st4"))
            yield
            S.op("dve", lambda e: e.tensor_tensor(out=st4[0:P, 12:16], in0=st4[0:P, 8:12], in1=eb4[0:P, :], op=ALU.mult), R("st4", "eb4"), R("st4"))
            yield
            for h in range(4):
                ndp, ndr = nd_banks[h // 2]
                S.op("dve", lambda e, h=h, ndp=ndp: e.tensor_scalar(out=hm[0:P, h * 128:(h + 1) * 128], in0=ndp[0:P, (h % 2) * 129:(h % 2) * 129 + 128],
                                                                    scalar1=st4[0:P, 12 + h:13 + h], scalar2=None, op0=ALU.mult), [ndr] + R("st4"), R("hm"))
            yield
            hm3 = hm[0:P, :].rearrange("p (h v) -> p h v", v=128)
            yield
            S.op("dve", lambda e: e.tensor_reduce(out=st4[0:P, 0:4], in_=hm3, axis=AX.X, op=ALU.add), R("hm"), R("st4"))
            yield
            S.op("dve", lambda e: e.tensor_scalar(out=st4[0:P, 0:4], in0=st4[0:P, 0:4], scalar1=1.0 / 128, scalar2=None, op0=ALU.mult), R("st4"), R("st4"))
            yield
            S.op("dve", lambda e: e.tensor_tensor(out=hm3, in0=hm3, in1=st4[0:P, 0:4].unsqueeze(2).to_broadcast([P, 4, 128]), op=ALU.subtract), R("hm", "st4"), R("hm"))
            yield
            S.op("dve", lambda e: e.tensor_tensor(out=hm2[0:P, 0:512], in0=hm[0:P, :], in1=hm[0:P, :], op=ALU.mult), R("hm"), R("hm2"))
            yield
            S.op("dve", lambda e: e.tensor_reduce(out=st4[0:P, 4:8], in_=hm2[0:P, 0:512].rearrange("p (h v) -> p h v", v=128), axis=AX.X, op=ALU.add), R("hm2"), R("st4"))
            yield
            rstd_from_ssq(st4[0:P, 8:12], st4[0:P, 4:8], 128, R("st4"), R("st4"))
            yield
            S.op("dve", lambda e: e.tensor_tensor(out=hm3, in0=hm3, in1=st4[0:P, 8:12].unsqueeze(2).to_broadcast([P, 4, 128]), op=ALU.mult), R("hm", "st4"), R("hm"))
            yield
            S.op("dve", lambda e: e.tensor_tensor(out=hm[0:P, :], in0=hm[0:P, :], in1=rp("mnorm"), op=ALU.mult), R("hm", "rowp"), R("hm"))
            yield
            act_fn(hm2[0:P, 0:512], mo_tm[0:P, :], AF.Sigmoid, R("mo_tm"), R("hm2"))
            yield
            S.op("dve", lambda e: e.tensor_tensor(out=br[0:P, 0:512], in0=hm[0:P, :], in1=hm2[0:P, 0:512], op=ALU.mult), R("hm", "hm2"), R("br"))
            yield
            if not samp:
                for h in range(4):
                    p, pr = bank()
                    S.op("pe", lambda e, h=h: e.matmul(p[:, 0:129], lhsT=mk_tm[0:P, h * 128:(h + 1) * 128], rhs=VW[0:P, h, :], start=True, stop=True), R("mk_tm", "VW"), [pr])
                    S.op("dve", lambda e, h=h: e.tensor_scalar(out=CTs[:, :], in0=CT[l][:, h, :], scalar1=eB4[:, h:h + 1], scalar2=None, op0=ALU.mult),
                         R("CT%d" % l, "eB4"), R("CTs"))
                    S.op("dve", lambda e, h=h: e.scalar_tensor_tensor(out=CT[l][:, h, :], in0=p[:, 0:129], scalar=eB4[:, h:h + 1], in1=CTs[:, :], op0=ALU.mult, op1=ALU.add),
                         [pr] + R("eB4", "CTs"), R("CT%d" % l))
                if ti == last_prompt and (LASTM & 2):
                    act_fn(sm4[:, 0:1], mrun[l][:], AF.Exp, R("mrun%d" % l), R("sm4"), scale=-1.0)
                    S.op("dve", lambda e: e.tensor_tensor(out=hm2[0:4, 0:4], in0=sm4[:, 0:1].to_broadcast([4, 4]), in1=ident[0:4, 0:4], op=ALU.mult), R("sm4", "cst"), R("hm2"))
                    p3, pr3 = bank()
                    S.op("pe", lambda e: e.matmul(p3[:, 0:4], lhsT=ones[0:4, :], rhs=hm2[0:4, 0:4], start=True, stop=True), R("cst", "hm2"), [pr3])
                    S.op("dve", lambda e: e.tensor_copy(out=st4[:, 0:4], in_=p3[:, 0:4]), [pr3], R("st4"))
                    for h in range(4):
                        p, pr = bank()
                        S.op("pe", lambda e, h=h: e.transpose(out=p[:, 0:128], in_=CT[l][:, h, 0:128], identity=ident), R("CT%d" % l, "cst"), [pr])
                        S.op("dve", lambda e, h=h: e.tensor_scalar(out=hm2[:, h * 128:(h + 1) * 128], in0=p[:, 0:128], scalar1=st4[:, h:h + 1], scalar2=None, op0=ALU.mult),
                             [pr] + R("st4"), R("hm2"))
                    S.dma("sp", pC_d[l].rearrange("h v d -> v h d"), hm2[:, 0:512].rearrange("p (h d) -> p h d", d=128), reads=R("hm2"))
                    S.op("dve", lambda e: e.tensor_tensor(out=st4[:, 4:8], in0=CT[l][:, :, 128], in1=st4[:, 0:4], op=ALU.mult), R("CT%d" % l, "st4"), R("st4"))
                    S.dma("sp", pn_d[l].rearrange("h d -> d h"), st4[:, 4:8], reads=R("st4"), slow=True)
                    S.dma("sp", pm_d[l].rearrange("(h o) -> h o", o=1), mrun[l][:], reads=R("mrun%d" % l), slow=True)


            yield
            p, pr = bank()
            yield
            S.op("pe", lambda e: e.matmul(p[0:P, 0:256], lhsT=gaT[0:16, 0:P], rhs=gaupb[0:16, :], start=True, stop=True), R("gaT", "gaupb"), [pr])
            yield
            S.op("dve", lambda e: e.tensor_tensor(out=big[0:P, 0:256], in0=p[0:P, 0:256], in1=rp("gab"), op=ALU.add), [pr] + R("rowp"), R("big"))
            yield
            softplus_neg(nl[0:P, :], big[0:P, 0:256], R("big"), R("nl"), big[0:P, 256:512], res["big"])
            yield
            S.op("dve", lambda e: e.tensor_scalar(out=nl[0:P, :], in0=nl[0:P, :], scalar1=1.0 / 16, scalar2=None, op0=ALU.mult), R("nl"), R("nl"))
            yield
            p1, pr1 = bank()
            yield
            S.op("pe", lambda e: e.matmul(p1[0:P, 0:256], lhsT=le_f, rhs=nl[0:P, :], start=True, stop=True), R("cst", "nl"), [pr1])
            yield
            p2, pr2 = bank()
            yield
            for j in range(2):
                S.op("pe", lambda e, j=j: e.matmul(p2[:, j * 128:j * 128 + P], lhsT=nl[0:P, j * 128:(j + 1) * 128], rhs=le_f[:, 0:P], start=True, stop=True), R("cst", "nl"), [pr2])
            yield
            for j in range(2):
                act_fn(eT[:, 0, j, 0:P], p2[:, j * 128:j * 128 + P], AF.Exp, [pr2], R("eT"))
                act_fn(eT[:, 1, j, 0:P], p2[:, j * 128:j * 128 + P], AF.Exp, [pr2], R("eT"), scale=-1.0)
            yield
            S.op("dve", lambda e: e.tensor_tensor(out=qtT[:, :, 0:P], in0=gqT[:, :, 0:P], in1=eT[:, 1, :, 0:P], op=ALU.mult), R("gqT", "eT"), R("qtT"))
            yield
            S.op("dve", lambda e: e.tensor_tensor(out=ktT[:, :, 0:P], in0=gkT[:, :, 0:P], in1=eT[:, 0, :, 0:P], op=ALU.mult), R("gkT", "eT"), R("ktT"))
            yield
            act_fn(big[0:P, 256:512], p1[0:P, 0:256], AF.Exp, [pr1], R("big"))
            yield
            S.op("dve", lambda e: e.tensor_tensor(out=kt_tm[0:P, :], in0=gk_tm[0:P, :], in1=big[0:P, 256:512], op=ALU.mult), R("gk_tm", "big"), R("kt_tm"))
            yield
            for h in range(4 if GST >= 2 else 0):
                j, off = h // 2, (h % 2) * 64
                p, pr = bank()
                S.op("pe", lambda e, j=j, off=off: e.matmul(p[0:P, 0:P], lhsT=ktT[off:off + 64, j, 0:P], rhs=qtT[off:off + 64, j, 0:P], start=True, stop=True), R("ktT", "qtT"), [pr])
                S.op("dve", lambda e, h=h: e.tensor_tensor(out=AM[0:P, h, 0:P], in0=p[0:P, 0:P], in1=le_f[:, 0:P], op=ALU.mult), [pr] + R("cst"), R("AM"))
            yield
            po, por = PS[6], PSR[6]
            yield
            if not samp:
                S.op("act", lambda e: e.copy(out=Sb[:], in_=Sst[l][:]), R("Sst%d" % l), R("Sb"))
            else:
                S0t = SA[:, :].rearrange("p (s j v) -> p s j v", j=2, v=128)
                S0b = SBb[:, 0:4096].rearrange("p (s j v) -> p s j v", j=2, v=128)
                qtS = SBb[:, 4096:6144].rearrange("p (j s t) -> p j s t", s=16, t=64)
                qtS2 = SBb[:, 6144:8192].rearrange("p (j s t) -> p j s t", s=16, t=64)
                ktS = SBb[0:64, 8192:12288].rearrange("p (s c) -> p s c", c=256)
                for j in range(2):
                    S.dma("sp", S0t[:, :, j, :], sS_d[l, :, j * 128:(j + 1) * 128, :].rearrange("s p v -> p s v"), writes=R("SA"))
                S.op("act", lambda e: e.copy(out=SBb[:, 0:4096], in_=SA[:, :]), R("SA"), R("SBb"))
                S.op("dve", lambda e: e.tensor_tensor(out=qtS, in0=qtT[:, :, 0:64].unsqueeze(2).to_broadcast([128, 2, 16, 64]),
                                                      in1=cb[:, 320:1344].rearrange("p (s t) -> p s t", t=64).unsqueeze(1).to_broadcast([128, 2, 16, 64]), op=ALU.mult),
                     R("qtT", "cb"), R("SBb"))
                S.op("dve", lambda e: e.tensor_scalar(out=SBb[:, 6144:8192], in0=SBb[:, 4096:6144], scalar1=C("halfm")[:, 1:2], scalar2=None, op0=ALU.mult), R("SBb", "cst"), R("SBb"))
                S.op("dve", lambda e: e.tensor_scalar(out=SBb[:, 4096:6144], in0=SBb[:, 4096:6144], scalar1=C("halfm")[:, 0:1], scalar2=None, op0=ALU.mult), R("SBb", "cst"), R("SBb"))
                S.op("dve", lambda e: e.tensor_tensor(out=ktS, in0=kt_tm[0:64, :].unsqueeze(1).to_broadcast([64, 16, 256]),
                                                      in1=C("segind")[0:64, :].unsqueeze(2).to_broadcast([64, 16, 256]), op=ALU.mult), R("kt_tm", "cst"), R("SBb"))
            yield
            for h in range(4 if GST >= 3 else 0):
                j, off = h // 2, (h % 2) * 64
                og = po[0:P, h * 128:(h + 1) * 128]
                S.op("pe", lambda e, h=h, og=og: e.matmul(og, lhsT=AM[0:P, h, 0:P], rhs=gv_tm[0:P, h * 128:(h + 1) * 128], start=True, stop=False), R("AM", "gv_tm"), [por])
                if not samp:
                    S.op("pe", lambda e, j=j, off=off, og=og: e.matmul(og, lhsT=qtT[off:off + 64, j, 0:P], rhs=Sb[off:off + 64, j, :], start=False, stop=True), R("qtT", "Sb"), [por])
                else:
                    for s_ in range(NSEG):
                        qq = qtS if off == 0 else qtS2
                        S.op("pe", lambda e, j=j, qq=qq, og=og, s_=s_: e.matmul(og, lhsT=qq[:, j, s_, :], rhs=S0b[:, s_, j, :], start=False, stop=(s_ == NSEG - 1)),
                             R("SBb"), [por])
            yield
            evac("act", hm[0:P, :], po[0:P, :], [por], R("hm"))
            yield
            S.op("dve", lambda e: e.tensor_tensor(out=hm2[0:P, 0:512], in0=hm[0:P, :], in1=hm[0:P, :], op=ALU.mult), R("hm"), R("hm2"))
            yield
            S.op("dve", lambda e: e.tensor_reduce(out=st4[0:P, 4:8], in_=hm2[0:P, 0:512].rearrange("p (h v) -> p h v", v=128), axis=AX.X, op=ALU.add), R("hm2"), R("st4"))
            yield
            rstd_from_ssq(st4[0:P, 8:12], st4[0:P, 4:8], 128, R("st4"), R("st4"))
            yield
            S.op("dve", lambda e: e.tensor_tensor(out=hm3, in0=hm3, in1=st4[0:P, 8:12].unsqueeze(2).to_broadcast([P, 4, 128]), op=ALU.mult), R("hm", "st4"), R("hm"))
            yield
            S.op("dve", lambda e: e.tensor_tensor(out=hm[0:P, :], in0=hm[0:P, :], in1=rp("gnorm"), op=ALU.mult), R("hm", "rowp"), R("hm"))
            yield
            act_fn(hm2[0:P, 0:512], gr_tm[0:P, :], AF.Silu, R("gr_tm"), R("hm2"))
            yield
            S.op("dve", lambda e: e.tensor_tensor(out=br[0:P, 512:1024], in0=hm[0:P, :], in1=hm2[0:P, 0:512], op=ALU.mult), R("hm", "hm2"), R("br"))
            yield
            for s_ in range(nseg if GST >= 5 else 0):
                for j in range(2):
                    p, pr = bank()
                    klhs = (ktS[0:64, s_, j * 128:(j + 1) * 128] if samp else kt_tm[0:P, j * 128:(j + 1) * 128])
                    srcres = R("SBb") if samp else R("kt_tm")
                    for half in range(2):
                        S.op("pe", lambda e, half=half, klhs=klhs: e.matmul(p[:, half * 128:(half + 1) * 128], lhsT=klhs, rhs=gv_tm[0:P, (2 * j + half) * 128:(2 * j + half + 1) * 128],
                                                                           start=True, stop=True), srcres + R("gv_tm"), [pr])
                    lastc = s_ * L + L - 1
                    for half in range(2):
                        rows = slice(half * 64, (half + 1) * 64)
                        el = eT[rows, 1, j, lastc:lastc + 1]
                        if samp:
                            sv = S0t[rows, s_, j, :]
                            svr = R("SA")
                        else:
                            sv = Sst[l][rows, j, :]
                            svr = R("Sst%d" % l)
                        if GST >= 6:
                            S.op("dve", lambda e, rows=rows, el=el, sv=sv: e.tensor_scalar(out=St[rows, :], in0=sv, scalar1=el, scalar2=None, op0=ALU.mult), svr + R("eT"), R("St"))
                        if GST >= 7:
                            S.op("dve", lambda e, rows=rows, el=el, sv=sv, half=half: e.scalar_tensor_tensor(out=sv, in0=p[rows, half * 128:(half + 1) * 128], scalar=el, in1=St[rows, :],
                                                                                                          op0=ALU.mult, op1=ALU.add), [pr] + R("eT", "St"), svr)
            yield
            if samp:
                for j in range(2):
                    S.dma("sp", oS_d[l, :, j * 128:(j + 1) * 128, :].rearrange("s p v -> p s v"), S0t[:, :, j, :], reads=R("SA"))
            elif ti == last_prompt and (LASTM & 4):
                S.dma("sp", pS_d[l].rearrange("(j p) v -> p j v", p=128), Sst[l][:], reads=R("Sst%d" % l))

            yield
            for c in range(6):
                if samp:
                    xv = xp[:, c, 0:NSEG * 7].rearrange("p (s j) -> p s j", j=7)
                    xin = [xv[:, :, j:j + 4] for j in range(4)]
                    acc = actT[:, c, 0:64].rearrange("p (s j) -> p s j", j=4)
                else:
                    xin = [xp[:, c, j:j + N] for j in range(4)]
                    acc = actT[:, c, 0:N]
                S.op("dve", lambda e, c=c, xin=xin, acc=acc: e.tensor_scalar(out=acc, in0=xin[0], scalar1=convp[:, c, 0:1], scalar2=convp[:, c, 4:5], op0=ALU.mult, op1=ALU.add),
                     R("xp", "convp"), R("actT"))
                for j in range(1, 4):
                    S.op("dve", lambda e, c=c, j=j, xin=xin, acc=acc: e.scalar_tensor_tensor(out=acc, in0=xin[j], scalar=convp[:, c, j:j + 1], in1=acc, op0=ALU.mult, op1=ALU.add),
                         R("xp", "convp", "actT"), R("actT"))
            yield
            act_fn(actT[:, :, 0:N], actT[:, :, 0:N], AF.Silu, R("actT"), R("actT"))
            yield
            S.op("dve", lambda e: e.tensor_copy(out=BCb[:, :, 0:N], in_=actT[:, 4:6, 0:N]), R("actT"), R("BCb"))
            yield
            for c0, c1 in ((0, 4), (4, 5)):
                p, pr = bank()
                for c in range(c0, c1):
                    S.op("pe", lambda e, c=c: e.transpose(out=p[0:P, (c - c0) * 128:(c - c0 + 1) * 128], in_=actT[:, c, 0:P], identity=ident), R("actT", "cst"), [pr])
                if c0 == 0:
                    evac("act", xs_tm[0:P, :], p[0:P, 0:512], [pr], R("xs_tm"))
                else:
                    evac("dve", Bm_tm[0:P, :], p[0:P, 0:128], [pr], R("Bm_tm"))
            yield
            S.op("dve", lambda e: e.tensor_tensor(out=dts[0:P, 0:8], in0=sdt[0:P, :], in1=rp("dtb"), op=ALU.add), R("sdt", "rowp"), R("dts"))
            yield
            softplus_neg(dts[0:P, 16:24], dts[0:P, 0:8], R("dts"), R("dts"), dts[0:P, 8:16], res["dts"], sign=1.0)
            yield
            S.op("dve", lambda e: e.scalar_tensor_tensor(out=dts[0:P, 24:32], in0=dts[0:P, 16:24], scalar=-1.0, in1=arow[0:P, :], op0=ALU.mult, op1=ALU.mult), R("dts", "arow"), R("dts"))
            yield
            dt_ = dts[0:P, 16:24]
            yield
            dA = dts[0:P, 24:32]
            yield
            p, pr = bank()
            yield
            S.op("pe", lambda e: e.matmul(p[0:P, 0:8], lhsT=le_f, rhs=dA, start=True, stop=True), R("cst", "dts"), [pr])
            yield
            S.op("pe", lambda e: e.matmul(p[0:P, 8:16], lhsT=gt_f, rhs=dA, start=True, stop=True), R("cst", "dts"), [pr])
            yield
            S.op("pe", lambda e: e.matmul(p[:, 16:24], lhsT=ones[0:P, :], rhs=dA, start=True, stop=True), R("cst", "dts"), [pr])
            yield
            S.op("dve", lambda e: e.tensor_copy(out=t8[0:P, 16:24], in_=p[0:P, 0:8]), [pr], R("t8"))
            yield
            lam = t8[0:P, 16:24]
            yield
            act_fn(t8[0:P, 24:32], p[0:P, 0:8], AF.Exp, [pr], R("t8"))
            yield
            act_fn(t8[0:P, 32:40], p[0:P, 8:16], AF.Exp, [pr], R("t8"))
            yield
            S.op("dve", lambda e: e.tensor_tensor(out=t8[0:P, 32:40], in0=t8[0:P, 32:40], in1=dt_, op=ALU.mult), R("t8", "dts"), R("t8"))
            yield
            act_fn(eL[:, :], p[:, 16:24], AF.Exp, [pr], R("eL"))
            yield
            S.op("dve", lambda e: e.tensor_tensor(out=xsw[0:P, :].rearrange("p (h c) -> p h c", c=64), in0=xs_tm[0:P, :].rearrange("p (h c) -> p h c", c=64),
                                                  in1=t8[0:P, 32:40].unsqueeze(2).to_broadcast([P, 8, 64]), op=ALU.mult), R("xs_tm", "t8"), R("xsw"))
            yield
            S.op("dve", lambda e: e.tensor_tensor(out=xsb[0:P, :].rearrange("p (h c) -> p h c", c=64), in0=xs_tm[0:P, :].rearrange("p (h c) -> p h c", c=64),
                                                  in1=dt_.unsqueeze(2).to_broadcast([P, 8, 64]), op=ALU.mult), R("xs_tm", "dts"), R("xsb"))
            yield
            pi0, pi0r = PS[6], PSR[6]
            yield
            pi1, pi1r = PS[5], PSR[5]
            yield
            py, pyr = PS[7], PSR[7]
            yield
            if not samp:
                S.op("act", lambda e: e.copy(out=hTb[:], in_=hT[l][:]), R("hT%d" % l), R("hTb"))
                for g, (pi, pir) in enumerate(((pi0, pi0r), (pi1, pi1r))):
                    S.op("pe", lambda e, g=g, pi=pi: e.matmul(pi[0:P, 0:256], lhsT=BCb[g * 64:(g + 1) * 64, 1, 0:P], rhs=hTb[g * 64:(g + 1) * 64, :], start=True, stop=True),
                         R("BCb", "hTb"), [pir])
            else:
                hTbs = SBb[:, 0:4096].rearrange("p (s c) -> p s c", c=256)
                CmS = SBb[:, 4096:5120].rearrange("p (s t) -> p s t", t=64)
                xswS = SBb[0:64, 5120:9216].rearrange("p (s c) -> p s c", c=512)
                h0t = SA[:, :].rearrange("p (s q c) -> p s q c", q=4, c=128)
                S.op("dve", lambda e: e.tensor_copy(out=big[0:64, 0:512].rearrange("p (h c) -> p h c", c=64), in_=dA.unsqueeze(2).to_broadcast([64, 8, 64])), R("dts"), R("big"))
                pe_, per_ = bank()
                for q in range(4):
                    S.op("pe", lambda e, q=q: e.matmul(pe_[:, q * 16:(q + 1) * 16], lhsT=big[0:64, q * 128:(q + 1) * 128], rhs=C("segind")[0:64, :], start=True, stop=True), R("big", "cst"), [per_])
                act_fn(hm2[:, 0:64], pe_[:, 0:64], AF.Exp, [per_], R("hm2"))
                eLs = hm2[:, 0:64].rearrange("p (q s) -> p q s", s=16)
                S.op("dve", lambda e: e.tensor_tensor(out=CmS, in0=BCb[:, 1, 0:64].unsqueeze(1).to_broadcast([128, 16, 64]),
                                                      in1=cb[:, 320:1344].rearrange("p (s t) -> p s t", t=64), op=ALU.mult), R("BCb", "cb"), R("SBb"))
                for ps_ in range(2):
                    s0 = ps_ * 8
                    S.op("pool", lambda e: e.memset(SA[:, :], 0.0), (), R("SA"))
                    for q in range(4):
                        oq = 0 if q < 2 else 64
                        S.dma("sp", h0t[:, :, q, oq:oq + 64], sh_d[l, s0:s0 + 8, q * 128:(q + 1) * 128, :].rearrange("s r n -> r s n"), writes=R("SA"))
                    for s_ in range(8):
                        p, pr = bank()
                        for q in range(4):
                            S.op("pe", lambda e, q=q, s_=s_: e.transpose(out=p[:, q * 128:(q + 1) * 128], in_=h0t[:, s_, q, :], identity=ident), R("SA", "cst"), [pr])
                        evac("act", hTbs[0:64, s0 + s_, :], p[0:64, 0:256], [pr], R("SBb"))
                        evac("dve", hTbs[64:128, s0 + s_, :], p[64:128, 256:512], [pr], R("SBb"))
                    S.op("dve", lambda e: e.tensor_tensor(out=xswS, in0=xsw[0:64, :].unsqueeze(1).to_broadcast([64, 8, 512]),
                                                          in1=C("segind")[0:64, s0:s0 + 8].unsqueeze(2).to_broadcast([64, 8, 512]), op=ALU.mult), R("xsw", "cst"), R("SBb"))
                    for s_ in range(8):
                        p, pr = bank()
                        for q in range(4):
                            g = q // 2
                            S.op("pe", lambda e, q=q, s_=s_, g=g: e.matmul(p[:, q * 64:(q + 1) * 64], lhsT=xswS[0:64, s_, q * 128:(q + 1) * 128], rhs=Bm_tm[0:64, g * 64:(g + 1) * 64],
                                                                            start=True, stop=True), R("SBb", "Bm_tm"), [pr])
                        for q in range(4):
                            oq = 0 if q < 2 else 64
                            hv = h0t[:, s_, q, oq:oq + 64]
                            S.op("dve", lambda e, q=q, s_=s_, hv=hv: e.scalar_tensor_tensor(out=hv, in0=hv, scalar=eLs[:, q, s0 + s_:s0 + s_ + 1], in1=p[:, q * 64:(q + 1) * 64],
                                                                                          op0=ALU.mult, op1=ALU.add), [pr] + R("SA", "hm2"), R("SA"))
                    for q in range(4):
                        oq = 0 if q < 2 else 64
                        S.dma("sp", oh_d[l, s0:s0 + 8, q * 128:(q + 1) * 128, :].rearrange("s r n -> r s n"), h0t[:, :, q, oq:oq + 64], reads=R("SA"))
                for g, (pi, pir) in enumerate(((pi0, pi0r), (pi1, pi1r))):
                    for s_ in range(NSEG):
                        S.op("pe", lambda e, g=g, pi=pi, s_=s_: e.matmul(pi[0:P, 0:256], lhsT=CmS[g * 64:(g + 1) * 64, s_, :], rhs=hTbs[g * 64:(g + 1) * 64, s_, :],
                                                                         start=(s_ == 0), stop=(s_ == NSEG - 1)), R("SBb"), [pir])
            yield
            Y = big[0:P, 0:8 * P].rearrange("p (h t) -> p h t", t=P)
            yield
            S.op("dve", lambda e: e.tensor_tensor(out=Y, in0=le_f[:, 0:P].unsqueeze(1).to_broadcast([P, 8, P]), in1=dA.unsqueeze(2).to_broadcast([P, 8, P]), op=ALU.mult),
                 R("cst", "dts"), R("big"))
            yield
            for g in range(2):
                pL, pLr = bank()
                S.op("pe", lambda e, g=g: e.matmul(pL[0:P, 0:4 * P], lhsT=ones[0:P, 0:P], rhs=big[0:P, g * 4 * P:(g + 1) * 4 * P], start=True, stop=True), R("cst", "big"), [pLr])
                S.op("dve", lambda e, g=g: e.tensor_tensor(out=big2[0:P, g * 4 * P:(g + 1) * 4 * P].rearrange("p (h t) -> p h t", t=P), in0=pL[0:P, 0:4 * P].rearrange("p (h t) -> p h t", t=P),
                                                           in1=lam[:, 4 * g:4 * g + 4].unsqueeze(2).to_broadcast([P, 4, P]), op=ALU.subtract), [pLr] + R("t8"), R("big2"))
            yield
            S.op("dve", lambda e: e.tensor_scalar(out=big2[0:P, 0:8 * P], in0=big2[0:P, 0:8 * P], scalar1=0.0, scalar2=None, op0=ALU.min), R("big2"), R("big2"))
            yield
            act_fn(big2[0:P, 0:8 * P], big2[0:P, 0:8 * P], AF.Exp, R("big2"), R("big2"))
            yield
            for g in range(2):
                p, pr = bank()
                S.op("pe", lambda e, g=g: e.matmul(p[0:P, 0:P], lhsT=BCb[g * 64:(g + 1) * 64, 0, 0:P], rhs=BCb[g * 64:(g + 1) * 64, 1, 0:P], start=True, stop=True), R("BCb"), [pr])
                S.op("dve", lambda e, g=g: e.tensor_tensor(out=cbm[0:P, g, 0:P], in0=p[0:P, 0:P], in1=le_f[:, 0:P], op=ALU.mult), [pr] + R("cst"), R("cbm"))
            yield
            S.op("dve", lambda e: e.tensor_tensor(out=bigb[0:P, 0:8 * P].rearrange("p (g h t) -> p g h t", g=2, t=P), in0=big2[0:P, 0:8 * P].rearrange("p (g h t) -> p g h t", g=2, t=P),
                                                  in1=cbm[0:P, :, 0:P].unsqueeze(2).to_broadcast([P, 2, 4, P]), op=ALU.mult), R("big2", "cbm"), R("bigb"))
            yield
            for h in range(8):
                S.op("pe", lambda e, h=h: e.matmul(py[0:P, h * 64:(h + 1) * 64], lhsT=bigb[0:P, h * P:(h + 1) * P], rhs=xsb[0:P, h * 64:(h + 1) * 64], start=True, stop=True), R("bigb", "xsb"), [pyr])
            yield
            for g, (pi, pir) in enumerate(((pi0, pi0r), (pi1, pi1r))):
                S.op("dve", lambda e, g=g, pi=pi: e.tensor_tensor(out=hm[0:P, g * 256:(g + 1) * 256].rearrange("p (h c) -> p h c", c=64), in0=pi[0:P, 0:256].rearrange("p (h c) -> p h c", c=64),
                                                                  in1=t8[0:P, 24 + 4 * g:28 + 4 * g].unsqueeze(2).to_broadcast([P, 4, 64]), op=ALU.mult), [pir] + R("t8"), R("hm"))
            yield
            S.op("dve", lambda e: e.tensor_tensor(out=hm[0:P, :], in0=py[0:P, 0:512], in1=hm[0:P, :], op=ALU.add), [pyr] + R("hm"), R("hm"))
            yield
            S.op("dve", lambda e: e.tensor_tensor(out=hm2[0:P, 0:512].rearrange("p (h c) -> p h c", c=64), in0=xs_tm[0:P, :].rearrange("p (h c) -> p h c", c=64),
                                                  in1=rp("sD").unsqueeze(2).to_broadcast([P, 8, 64]), op=ALU.mult), R("xs_tm", "rowp"), R("hm2"))
            yield
            S.op("dve", lambda e: e.tensor_tensor(out=hm[0:P, :], in0=hm[0:P, :], in1=hm2[0:P, 0:512], op=ALU.add), R("hm", "hm2"), R("hm"))
            yield
            act_fn(hm2[0:P, 0:512], sz_tm[0:P, :], AF.Silu, R("sz_tm"), R("hm2"))
            yield
            S.op("dve", lambda e: e.tensor_tensor(out=hm[0:P, :], in0=hm[0:P, :], in1=hm2[0:P, 0:512], op=ALU.mult), R("hm", "hm2"), R("hm"))
            yield
            S.op("dve", lambda e: e.tensor_tensor(out=hm2[0:P, 0:512], in0=hm[0:P, :], in1=hm[0:P, :], op=ALU.mult), R("hm"), R("hm2"))
            yield
            S.op("dve", lambda e: e.tensor_reduce(out=st4[0:P, 4:6], in_=hm2[0:P, 0:512].rearrange("p (g c) -> p g c", c=256), axis=AX.X, op=ALU.add), R("hm2"), R("st4"))
            yield
            rstd_from_ssq(st4[0:P, 8:10], st4[0:P, 4:6], 256, R("st4"), R("st4"))
            yield
            S.op("dve", lambda e: e.tensor_tensor(out=hm[0:P, :].rearrange("p (g c) -> p g c", c=256), in0=hm[0:P, :].rearrange("p (g c) -> p g c", c=256),
                                                  in1=st4[0:P, 8:10].unsqueeze(2).to_broadcast([P, 2, 256]), op=ALU.mult), R("hm", "st4"), R("hm"))
            yield
            S.op("dve", lambda e: e.tensor_tensor(out=br[0:P, 1024:1536], in0=hm[0:P, :], in1=rp("snorm"), op=ALU.mult), R("hm", "rowp"), R("br"))
            yield
            if not samp:
                p, pr = bank()
                for g in range(2):
                    S.op("pe", lambda e, g=g: e.matmul(p[:, g * 256:(g + 1) * 256], lhsT=Bm_tm[0:P, :], rhs=xsw[0:P, g * 256:(g + 1) * 256], start=True, stop=True), R("Bm_tm", "xsw"), [pr])
                for g in range(2):
                    rows = slice(g * 64, (g + 1) * 64)
                    S.op("dve", lambda e, g=g, rows=rows: e.tensor_tensor(out=big[rows, 0:256].rearrange("p (h c) -> p h c", c=64), in0=hT[l][rows, :].rearrange("p (h c) -> p h c", c=64),
                                                                          in1=eL[rows, 4 * g:4 * g + 4].unsqueeze(2).to_broadcast([64, 4, 64]), op=ALU.mult), R("hT%d" % l, "eL"), R("big"))
                    S.op("dve", lambda e, g=g, rows=rows: e.tensor_tensor(out=hT[l][rows, :], in0=p[rows, g * 256:(g + 1) * 256], in1=big[rows, 0:256], op=ALU.add), [pr] + R("big"), R("hT%d" % l))
                if ti == last_prompt and (LASTM & 8):
                    p, pr = bank()
                    for half in range(2):
                        S.op("pe", lambda e, half=half: e.transpose(out=p[:, half * 128:(half + 1) * 128], in_=hT[l][:, half * 128:(half + 1) * 128], identity=ident),
                             R("hT%d" % l, "cst"), [pr])
                    evac("act", hm2[:, 0:256], p[:, 0:256], [pr], R("hm2"))
                    phv = ph_d[l].rearrange("(g f r) n -> f r g n", g=2, f=2, r=128)
                    for half in range(2):
                        S.dma("sp", phv[half], hm2[:, half * 128:(half + 1) * 128].rearrange("p (g n) -> p g n", n=64), reads=R("hm2"))

            yield
            dbg_out("br", br[0:P, :], R("br"))
            yield
            transpose_to(lambda c0, c1: brT[:, c0:c1, 0:P], br, res["br"], P, 12, res["brT"], eng="dve")
            yield
            mixacc = big[:, :].rearrange("p (c t) -> p c t", t=128)
            yield
            gs8 = big2[:, :].rearrange("p (c t) -> p c t", t=128)
            yield
            for n in range(3):
                for gh in range(2):
                    c0 = 4896 + n * 1024 + gh * 512
                    wg, wgr = load_w(W(c0, c0 + 512), 512)
                    for dq in range(4):
                        dc = gh * 4 + dq
                        pg, pgr = fm_piece(wg, wgr, dq * 128, (dq + 1) * 128, N)
                        act_fn(gs8[:, dc, 0:N], pg[:, 0:N], AF.Sigmoid, [pgr], R("big2"))
                wbn, wbnr = load_w2(wbrs[l, n * 512:(n + 1) * 512, :], 4, 1024)
                for dc in range(8):
                    pa, par = bank()
                    for k in range(4):
                        S.op("pe", lambda e, k=k, dc=dc: e.matmul(pa[:, 0:N], lhsT=wbn[:, k, dc * 128:(dc + 1) * 128], rhs=brT[:, n * 4 + k, 0:N], start=(k == 0), stop=(k == 3)),
                             [wbnr] + R("brT"), [par])
                    if n == 0:
                        S.op("dve", lambda e, dc=dc: e.tensor_tensor(out=mixacc[:, dc, 0:N], in0=pa[:, 0:N], in1=gs8[:, dc, 0:N], op=ALU.mult), [par] + R("big2"), R("big"))
                    else:
                        S.op("dve", lambda e, dc=dc: e.tensor_tensor(out=hm2[:, 0:N], in0=pa[:, 0:N], in1=gs8[:, dc, 0:N], op=ALU.mult), [par] + R("big2"), R("hm2"))
                        S.op("dve", lambda e, dc=dc: e.tensor_tensor(out=mixacc[:, dc, 0:N], in0=mixacc[:, dc, 0:N], in1=hm2[:, 0:N], op=ALU.add), R("big", "hm2"), R("big"))
            yield
            S.op("act", lambda e: e.copy(out=mixT[:, :, 0:N], in_=mixacc[:, :, 0:N]), R("big"), R("mixT"))
            yield
            for gh in range(2):
                wo, wor = load_w(wouts[l, :, gh * 512:(gh + 1) * 512], 512)
                p, pr = tm_piece(wo, wor, 0, 512, P, src=mixT, srcres=res["mixT"])
                S.op("dve", lambda e, gh=gh: e.scalar_tensor_tensor(out=X[0:P, gh * 512:(gh + 1) * 512], in0=X[0:P, gh * 512:(gh + 1) * 512], scalar=DN_ALPHA, in1=p[0:P, 0:512],
                                                                    op0=ALU.mult, op1=ALU.add), [pr] + R(XN), R(XN))
            yield
            pass
            yield
            layernorm_tm(P, X, res[XN], lnp_d[l, 0], X[0:P, :], res[XN], lnt, "lnt", big, res["big"], lnrow[:, 0:1024], "lnrowF")
            yield
            dbg_out("h1", X[0:P, :], R(XN))

            yield
            if do_peer:
                (fence() if samp else None)
                transpose_to(lambda c0, c1: xT[:, c0:c1, 0:P], X, res[XN], P, 8, res["xT"])
                for gq_ in range(4):
                    wq, wqr = load_w(wqs[l, :, gq_ * 512:(gq_ + 1) * 512], 512)
                    for c in range(4):
                        p, pr = fm_piece(wq, wqr, c * 128, (c + 1) * 128, N)
                        evac("act", qTb[:, gq_ * 4 + c, 0:N], p[:, 0:N], [pr], R("qTb"))
                i = wsn[0]
                wsn[0] = (i + 1) % NWP
                kb = wpool[i][:, 0:2048]
                S.dma("sp", kb, keyss[l], reads=R("scratch"), writes=[wpres[i]])
                kbr = wpres[i]
                for c0 in range(0, 16, 4):
                    p, pr = bank()
                    for c in range(c0, c0 + 4):
                        S.op("pe", lambda e, c=c: e.matmul(p[0:P, (c - c0) * 128:(c - c0 + 1) * 128], lhsT=qTb[:, c, 0:P], rhs=kb[:, c * 128:(c + 1) * 128], start=True, stop=True),
                             R("qTb") + [kbr], [pr])
                    evac("dve", SC[0:P, c0 * 128:(c0 + 4) * 128], p[0:P, 0:512], [pr], R("SC"))

                def top16(src, width, vout, iout):
                    wv = wk[0:P, 0:width]
                    S.op("dve", lambda e: e.max(out=vout[:, 0:8], in_=src), R("SC"), R("V16"))
                    S.op("dve", lambda e: e.max_index(out=iout[:, 0:8], in_max=vout[:, 0:8], in_values=src), R("SC", "V16"), R("I16"))
                    S.op("dve", lambda e: e.match_replace(out=wv, in_to_replace=vout[:, 0:8], in_values=src, imm_value=-1e30), R("SC", "V16"), R("wk"))
                    S.op("dve", lambda e: e.max(out=vout[:, 8:16], in_=wv), R("wk"), R("V16"))
                    S.op("dve", lambda e: e.max_index(out=iout[:, 8:16], in_max=vout[:, 8:16], in_values=wv), R("wk", "V16"), R("I16"))

                for c in range(16):
                    top16(SC[0:P, c * 128:(c + 1) * 128], 128, V16[0:P, c, :], I16[0:P, c, :])
                S.op("dve", lambda e: e.tensor_copy(out=I16f[0:P], in_=I16[0:P]), R("I16"), R("I16f"))
                V16v = V16[0:P].rearrange("p (h j) a -> p h j a", j=2)
                I16v = I16f[0:P].rearrange("p (h j) a -> p h j a", j=2)
                cand = SC[0:P, :].rearrange("p (h a b) -> p h a b", a=16, b=16)
                S.op("dve", lambda e: e.tensor_tensor(out=cand, in0=V16v[:, :, 0, :].unsqueeze(3).to_broadcast([P, 8, 16, 16]),
                                                      in1=V16v[:, :, 1, :].unsqueeze(2).to_broadcast([P, 8, 16, 16]), op=ALU.add), R("V16"), R("SC"))
                for h in range(8):
                    top16(SC[0:P, h * 256:(h + 1) * 256], 256, TV[0:P, h, :], TPu[0:P, h, :])
                TPf, af, bf_, i0f, i1f, gg, dots, actg = [pk[0:P, q, :].rearrange("p (h k) -> p h k", k=16) for q in range(8)]
                S.op("dve", lambda e: e.tensor_copy(out=TPf, in_=TPu[0:P]), R("I16", "V16"), R(PKN))
                bv4 = big[0:P, :].rearrange("p (h a b) -> p h a b", a=16, b=16)
                for hh in range(2):
                    hs_ = slice(4 * hh, 4 * hh + 4)
                    S.op("dve", lambda e, hs_=hs_: e.tensor_tensor(out=bv4, in0=TPf[:, hs_, :].unsqueeze(3).to_broadcast([P, 4, 16, 16]),
                                                                   in1=thr16[0:P, :].unsqueeze(1).unsqueeze(1).to_broadcast([P, 4, 16, 16]), op=ALU.is_ge), R(PKN, "thr16"), R("big"))
                    S.op("dve", lambda e, hs_=hs_: e.tensor_reduce(out=af[:, hs_, :], in_=bv4, axis=AX.X, op=ALU.add), R("big"), R(PKN))
                S.op("dve", lambda e: e.scalar_tensor_tensor(out=pk[0:P, 2, :], in0=pk[0:P, 1, :], scalar=-16.0, in1=pk[0:P, 0, :], op0=ALU.mult, op1=ALU.add), R(PKN), R(PKN))
                for (src_, jj, dst_) in ((af, 0, i0f), (bf_, 1, i1f)):
                    for hh in range(2):
                        hs_ = slice(4 * hh, 4 * hh + 4)
                        S.op("dve", lambda e, hs_=hs_, src_=src_: e.tensor_tensor(out=bv4, in0=src_[:, hs_, :].unsqueeze(3).to_broadcast([P, 4, 16, 16]),
                                                                                  in1=C("iota16")[0:P, :].unsqueeze(1).unsqueeze(1).to_broadcast([P, 4, 16, 16]), op=ALU.is_equal),
                             R(PKN, "cst"), R("big"))
                        S.op("dve", lambda e, hs_=hs_, jj=jj: e.tensor_tensor(out=bv4, in0=bv4, in1=I16v[:, hs_, jj, :].unsqueeze(2).to_broadcast([P, 4, 16, 16]), op=ALU.mult),
                             R("big", "I16f"), R("big"))
                        S.op("dve", lambda e, hs_=hs_, dst_=dst_: e.tensor_reduce(out=dst_[:, hs_, :], in_=bv4, axis=AX.X, op=ALU.add), R("big"), R(PKN))
                S.op("dve", lambda e: e.scalar_tensor_tensor(out=pk[0:P, 0, :], in0=pk[0:P, 3, :], scalar=128.0, in1=pk[0:P, 4, :], op0=ALU.mult, op1=ALU.add), R(PKN), R(PKN))
                S.op("dve", lambda e: e.tensor_copy(out=eidx[0:P, :], in_=pk[0:P, 0, :]), R(PKN), R(EIN))
                S.op("dve", lambda e: e.tensor_tensor(out=gg, in0=TV[0:P], in1=TV[0:P, :, 0:1].to_broadcast([P, 8, 16]), op=ALU.subtract), R("V16"), R(PKN))
                act_fn(pk[0:P, 5, :], pk[0:P, 5, :], AF.Exp, R(PKN), R(PKN))
                S.op("dve", lambda e: e.tensor_reduce(out=st4[0:P, 0:8], in_=gg, axis=AX.X, op=ALU.add), R(PKN), R("st4"))
                S.op("dve", lambda e: e.reciprocal(out=st4[0:P, 0:8], in_=st4[0:P, 0:8]), R("st4"), R("st4"))
                S.op("dve", lambda e: e.tensor_tensor(out=gg, in0=gg, in1=st4[0:P, 0:8].unsqueeze(2).to_broadcast([P, 8, 16]), op=ALU.mult), R(PKN, "st4"), R(PKN))
                GUr = [res["GU%d" % q] for q in range(NGU)]
                gn = [0]

                def gather(tab, slot):
                    q = gn[0] % NGU
                    gn[0] += 1
                    S.dma("pool", None, None, reads=R(EIN, "scratch"), writes=[GUr[q]],
                          fn=lambda e, q=q: e.indirect_dma_start(out=GU[q][0:P, :], out_offset=None, in_=tab,
                                                                 in_offset=bass.IndirectOffsetOnAxis(ap=eidx[0:P, slot:slot + 1], axis=0)))
                    return GU[q], GUr[q]

                yield "SPLIT"
                S.op("dve", lambda e: e.tensor_copy(out=xb16[0:P, :], in_=X[0:P, :]), R(XN), R("xb16"))
                pvb = [(PS[3], PSR[3]), (PS[4], PSR[4])]
                slres = [Res("sl%d" % q) for q in range(128)]
                pend = []
                for slot in range(128):
                    if slot % 4 == 0:
                        yield
                    gu, gur = gather(uvs[l][:, :], slot)
                    pb, pbr = prodb[slot % 2], res["prodb%d" % (slot % 2)]
                    dg, dgr = dgb[slot % 4], res["dgb%d" % (slot % 4)]
                    sr = slres[slot]
                    S.op("dve", lambda e, gu=gu, pb=pb: e.tensor_tensor(out=pb[0:P, :], in0=gu[0:P, 0:1024], in1=xb16[0:P, :], op=ALU.mult), [gur] + R("xb16"), [pbr])
                    S.op("act", lambda e, pb=pb, slot=slot: e.activation(out=pb[0:P, :], in_=pb[0:P, :], func=AF.Copy, accum_out=pk[0:P, 6, slot:slot + 1]), [pbr], [pbr, sr])
                    S.op("act", lambda e, slot=slot: e.activation(out=pk[0:P, 7, slot:slot + 1], in_=pk[0:P, 6, slot:slot + 1], func=AF.Gelu), [sr], [sr])
                    pend.append((slot, gu, gur, dg, dgr, sr))
                    while pend and (len(pend) > SKEW or slot == 127):
                        slot2, gu2, gur2, dg2, dgr2, sr2 = pend.pop(0)
                        S.op("dve", lambda e, dg2=dg2, slot2=slot2: e.tensor_scalar(out=dg2[0:P, 0:P], in0=cb[0:P, 128:128 + P], scalar1=pk[0:P, 7, slot2:slot2 + 1], scalar2=pk[0:P, 5, slot2:slot2 + 1],
                                                                                  op0=ALU.mult, op1=ALU.mult), R("cb", PKN) + [sr2], [dgr2])
                        for half in range(2):
                            S.op("pe", lambda e, dg2=dg2, gu2=gu2, half=half, slot2=slot2: e.matmul(pvb[half][0][0:P, 0:512], lhsT=dg2[0:P, 0:P], rhs=gu2[0:P, 1024 + half * 512:1024 + (half + 1) * 512],
                                                                                                 start=(slot2 == 0), stop=(slot2 == 127)), [dgr2, gur2], [pvb[half][1]])
                for half in range(2):
                    S.op("dve", lambda e, half=half: e.scalar_tensor_tensor(out=X[0:P, half * 512:(half + 1) * 512], in0=X[0:P, half * 512:(half + 1) * 512], scalar=DN_ALPHA,
                                                                           in1=pvb[half][0][0:P, 0:512], op0=ALU.mult, op1=ALU.add), R(XN) + [pvb[half][1]], R(XN))
                pass
                layernorm_tm(P, X, res[XN], lnp_d[l, 1], X[0:P, :], res[XN], lntB, "lntB", SBf[:, 0:1024], res["GU0"], lnrow[:, 1024:2048], "lnrowB")
                (fence() if samp else None)
            if not do_peer:
                yield "SPLIT"
            if l == nlayers - 1:
                S.dma("sp", y_d[t0:t0 + P, :], X[0:P, :], reads=R(XN))
            dbg_i[0] += 1

        def drain(g):
            if g is not None:
                for _ in g:
                    pass

        prompt_tiles = [t for t in all_tiles if t[0] == "p"]
        samp_tiles = [t for t in all_tiles if t[0] == "s"]
        seq = []
        for a in range(0, len(prompt_tiles), 2):
            pair = prompt_tiles[a:a + 2]
            for l in range(nlayers):
                for j, t in enumerate(pair):
                    seq.append((t[0], t[1], l, j))
        prev = None
        for i, (kind, ti, l, par) in enumerate(seq):
            g = tile_layer(kind, ti, l, par, i % 2)
            cnt = 0
            for tok in g:
                if tok == "SPLIT":
                    break
                cnt += 1
                if prev is not None and cnt % FB_RATIO == 0:
                    if next(prev, "END") == "END":
                        prev = None
            drain(prev)
            prev = g
        drain(prev)
        for kind, ti in samp_tiles:
            for l in range(nlayers):
                drain(tile_layer(kind, ti, l, 0, 0))
        S.finish()
    return nc


def make_consts():
    c = np.zeros((128, COW), np.float32)
    def put(n, a):
        c[:a.shape[0], CO[n][0]:CO[n][1]] = a
    s = np.arange(128)
    put("ident", np.eye(128, dtype=np.float32))
    put("leP", (s[:, None] <= s[None, :]).astype(np.float32))
    put("gtP", (s[:, None] > s[None, :]).astype(np.float32))
    q = np.arange(64)
    same = (q[:, None] // 4) == (q[None, :] // 4)
    put("leS", (same & (q[:, None] <= q[None, :])).astype(np.float32))
    put("gtS", (same & (q[:, None] > q[None, :])).astype(np.float32))
    put("ones", np.ones((128, 128), np.float32))
    put("segind", ((q[:, None] // 4) == np.arange(16)[None, :]).astype(np.float32))
    put("iota16", np.broadcast_to(np.arange(16, dtype=np.float32), (128, 16)))
    segm = ((np.arange(64)[None, :] // 4) == np.arange(16)[:, None]).astype(np.float32).reshape(1, 1024)
    put("halfm", np.stack([(s < 64), (s >= 64)], 1).astype(np.float32))
    return c, np.ascontiguousarray(np.broadcast_to(segm, (128, 1024))).astype(np.float32)

def prep(inp, core, nexp=16384):
    f = lambda a: np.ascontiguousarray(np.asarray(a, dtype=np.float32))
    x = np.concatenate([f(inp["x_prompt"][core]), f(inp["x_sample"][core * 16:(core + 1) * 16]).reshape(64, 1024)], 0)
    sl = slice(core * 16, (core + 1) * 16)
    d = dict(x=x)
    d["sC"] = f(inp["state_mlstm_C"][:, sl]); d["sn"] = f(inp["state_mlstm_n"][:, sl]); d["sm"] = f(inp["state_mlstm_m"][:, sl])
    d["sS"] = f(inp["state_gla_S"][:, sl]).reshape(2, 16, 256, 128)
    d["sh"] = f(inp["state_ssm_h"][:, sl]).reshape(2, 16, 512, 64)
    d["scv"] = f(inp["state_conv"][:, sl]).reshape(2, 48, 768)
    d["w_in"] = f(inp["w_in"]); d["w_br"] = f(inp["w_branch"]).reshape(2, 1536, 1024); d["w_out"] = f(inp["w_out"])
    d["p_wq"] = f(inp["p_wq"])
    d["keysT"] = f(np.transpose(np.asarray(inp["p_keys"]).reshape(2, 16, 128, 128), (0, 3, 1, 2)))
    for l in range(2):
        d["p_u%d" % l] = f(inp["p_u"][l][:nexp]); d["p_v%d" % l] = f(inp["p_v"][l][:nexp])
    rows = []
    for l in range(2):
        r = np.concatenate([f(inp[k][l]).reshape(-1) for k in ["m_i_bias", "m_f_bias", "m_norm", "g_a_bias", "g_norm", "s_dt_bias",
                                                              "s_A_log", "s_D", "s_norm"]])
        rows.append(np.broadcast_to(r, (128, r.size)))
    d["rowp"] = f(np.stack(rows))
    lnp = np.stack([np.stack([np.concatenate([f(inp["ln1_g"][l]), f(inp["ln1_b"][l])]), np.concatenate([f(inp["ln2_g"][l]), f(inp["ln2_b"][l])])]) for l in range(2)])
    d["lnp"] = f(np.broadcast_to(lnp[:, :, None, :], (2, 2, 128, 2048)))
    cw = f(inp["s_conv_w"]); cbb = f(inp["s_conv_b"])
    cp = np.concatenate([cw, cbb[:, None, :]], 1)
    d["convp"] = f(np.transpose(cp.reshape(2, 5, 6, 128), (0, 3, 2, 1)))
    d["gaup"] = f(inp["g_a_up"])
    d["cst"], d["segm"] = make_consts()
    return d


NEXP_USED = 16384


def kernel(**inp):
    nc = build(nexp=NEXP_USED)
    maps = [prep(inp, c, nexp=NEXP_USED) for c in range(NCORES)]
    r = run_bass_kernel_spmd(nc, maps, core_ids=list(range(NCORES))).results
    f = np.float32
    yp = np.stack([r[c]["y"][0:SEQ] for c in range(NCORES)], 0).astype(f)
    ys = np.concatenate([r[c]["y"][SEQ:].reshape(NSEG, LS, D) for c in range(NCORES)], 0).astype(f)
    st = lambda k, shp: np.stack([r[c][k].reshape(shp) for c in range(NCORES)], 1).astype(f)
    ct = lambda k, shp: np.concatenate([r[c][k].reshape(shp) for c in range(NCORES)], 1).astype(f)
    return (yp, ys,
            st("pC", (2, 4, 128, 128)), st("pn", (2, 4, 128)), st("pm", (2, 4)),
            st("pS", (2, 4, 64, 128)), st("ph", (2, 8, 64, 64)), st("pcv", (2, 3, 768)),
            ct("oC", (2, 16, 4, 128, 128)), ct("on", (2, 16, 4, 128)), ct("om", (2, 16, 4)),
            ct("oS", (2, 16, 4, 64, 128)), ct("oh", (2, 16, 8, 64, 64)), ct("ocv", (2, 16, 3, 768)))
```

```python
import contextlib
import os
import numpy as np
import concourse.bass as bass
import concourse.mybir as mybir
from concourse.bass_utils import run_bass_kernel_spmd

F32 = mybir.dt.float32
BF16 = mybir.dt.bfloat16
I32 = mybir.dt.int32
U32 = mybir.dt.uint32
AF = mybir.ActivationFunctionType
ALU = mybir.AluOpType
AX = mybir.AxisListType

NCORES = 8
D = 1024
SEQ = 2048
NSEG = 16
LS = 4
TC = SEQ + NSEG * LS
INW = 7968
EPS = 1e-5
DN_ALPHA = 4.0 ** 0.25
DEPTH = 2
GST = int(os.environ.get('GST', '99'))
LASTM = int(os.environ.get('LASTM', '15'))
FB_RATIO = int(os.environ.get('FB_RATIO', '5'))
SKEW = int(os.environ.get('SKEW', '2'))
CVQ = os.environ.get('CVQ', 'act')
CVL = os.environ.get('CVL', 'sp').split(',')

RP = {}
_o = 0
for _n, _s in [("mib", 4), ("mfb", 4), ("mnorm", 512), ("gab", 256), ("gnorm", 512), ("dtb", 8),
               ("alog", 8), ("sD", 8), ("snorm", 512)]:
    RP[_n] = (_o, _o + _s)
    _o += _s
RPW = _o

CO = {}
_o = 0
for _n, _s in [("ident", 128), ("leP", 128), ("gtP", 128), ("leS", 64), ("gtS", 64), ("ones", 128),
               ("segind", 16), ("iota16", 16), ("halfm", 2)]:
    CO[_n] = (_o, _o + _s)
    _o += _s
COW = _o


class Res:
    __slots__ = ("name", "w", "r")

    def __init__(self, name="r"):
        self.name = name
        self.w = None
        self.r = []


class _Rec:
    def __getattr__(self, name):
        def f(*a, **k):
            self.call = (name, a, k)
            return self
        return f


def _replay(fn):
    rec = _Rec()
    fn(rec)
    name, a, k = rec.call
    def run(eng):
        try:
            return getattr(eng, name)(*a, **k)
        except Exception:
            print("FAILED OP", name, {kk: (getattr(vv, "shape", vv), getattr(getattr(vv, "tensor", None), "name", None)) for kk, vv in k.items()})
            raise
    return run


class Sched:
    ENG = ("pe", "dve", "act", "pool", "sp")

    def __init__(self, nc, stack, n_dma_sems=32):
        self.nc = nc
        self.sem = {e: stack.enter_context(nc.semaphore("s_" + e)) for e in self.ENG}
        self.cnt = {e: 0 for e in self.ENG}
        self.prog = {e: [] for e in self.ENG}
        self.seen = {e: {} for e in self.ENG}
        self.dsem = [stack.enter_context(nc.semaphore("d%d" % i)) for i in range(n_dma_sems)]
        self.dval = [0] * n_dma_sems
        self.dnext = 0
        self.n_hw = n_dma_sems - 8
        self.dnext_sw = 0
        self.n_ins = 0

    def _wait(self, e, ev):
        if ev is None:
            return
        kind, key, val = ev
        if kind == "e" and key == e and e == "pe":
            return
        k = (kind, key)
        if self.seen[e].get(k, 0) >= val:
            return
        self.seen[e][k] = val
        sem = self.sem[key] if kind == "e" else self.dsem[key]
        self.prog[e].append(lambda eng, sem=sem, val=val: eng.wait_ge(sem, val))

    def _deps(self, e, reads, writes):
        for r in reads:
            self._wait(e, r.w)
        for w in writes:
            self._wait(e, w.w)
            for ev in w.r:
                self._wait(e, ev)

    def _commit(self, ev, reads, writes):
        for r in reads:
            r.r.append(ev)
            if len(r.r) > 16:
                d = {}
                for x in r.r:
                    k = (x[0], x[1])
                    if k not in d or d[k][2] < x[2]:
                        d[k] = x
                r.r = list(d.values())
        for w in writes:
            w.w = ev
            w.r = []

    def op(self, e, fn, reads=(), writes=()):
        self._deps(e, reads, writes)
        self.cnt[e] += 1
        ev = ("e", e, self.cnt[e])
        sem = self.sem[e]
        fn = _replay(fn)
        self.prog[e].append(lambda eng, fn=fn, sem=sem: fn(eng).then_inc(sem, 1))
        self._commit(ev, reads, writes)
        self.n_ins += 1
        return ev

    def dma(self, q, out, in_, reads=(), writes=(), fn=None, slow=False):
        self._deps(q, reads, writes)
        if q == "pool":
            i = self.n_hw + self.dnext_sw
            self.dnext_sw = (self.dnext_sw + 1) % 8
        else:
            i = self.dnext
            self.dnext = (self.dnext + 1) % self.n_hw
        if self.dval[i] > 0:
            self._wait(q, ("d", i, self.dval[i]))
        self.dval[i] += 16
        ev = ("d", i, self.dval[i])
        sem = self.dsem[i]
        if fn is None:
            if slow:
                fn = lambda eng, out=out, in_=in_: eng.dma_start(out=out, in_=in_, allow_slow_non_contiguous=True)
            else:
                fn = lambda eng, out=out, in_=in_: eng.dma_start(out=out, in_=in_)
        fn = _replay(fn)
        self.prog[q].append(lambda eng, fn=fn, sem=sem: fn(eng).then_inc(sem, 16))
        self._commit(ev, reads, writes)
        self.n_ins += 1
        return ev

    def finish(self):
        for i in range(len(self.dsem)):
            if self.dval[i] > 0:
                self._wait("sp", ("d", i, self.dval[i]))
        for e in ("pe", "dve", "act", "pool"):
            if self.cnt[e] > 0:
                self._wait("sp", ("e", e, self.cnt[e]))
        nc = self.nc
        progs = self.prog
        with nc.Block() as block:
            @block.tensor
            def _(eng):
                for f in progs["pe"]:
                    f(eng)

            @block.vector
            def _(eng):
                for f in progs["dve"]:
                    f(eng)

            @block.scalar
            def _(eng):
                for f in progs["act"]:
                    f(eng)

            @block.gpsimd
            def _(eng):
                for f in progs["pool"]:
                    f(eng)

            @block.sync
            def _(eng):
                for f in progs["sp"]:
                    f(eng)


def build(tiles=None, nlayers=DEPTH, do_peer=True, dbg=(), nexp=16384, preconv=True):
    nc = bass.Bass("TRN2", target_bir_lowering=False)
    di = lambda name, shape, dt=F32: nc.dram_tensor(name, list(shape), dt, kind="ExternalInput").ap()
    do = lambda name, shape, dt=F32: nc.dram_tensor(name, list(shape), dt, kind="ExternalOutput").ap()
    x_d = di("x", [TC, D])
    sC_d = di("sC", [DEPTH, NSEG, 4, 128, 128]); sn_d = di("sn", [DEPTH, NSEG, 4, 128]); sm_d = di("sm", [DEPTH, NSEG, 4])
    sS_d = di("sS", [DEPTH, NSEG, 256, 128]); sh_d = di("sh", [DEPTH, NSEG, 512, 64]); scv_d = di("scv", [DEPTH, NSEG * 3, 768])
    win_d = di("w_in", [DEPTH, D, INW]); wbr_d = di("w_br", [DEPTH, 1536, D]); wout_d = di("w_out", [DEPTH, D, D])
    wq_d = di("p_wq", [DEPTH, D, 2048]); keys_d = di("keysT", [DEPTH, 128, 16, 128])
    pu_d = [di("p_u%d" % l, [nexp, D]) for l in range(DEPTH)]; pv_d = [di("p_v%d" % l, [nexp, D]) for l in range(DEPTH)]
    rowp_d = di("rowp", [DEPTH, 128, RPW]); convp_d = di("convp", [DEPTH, 128, 6, 5]); gaup_d = di("gaup", [DEPTH, 16, 256])
    cst_d = di("cst", [128, COW]); segm_d = di("segm", [128, 1024]); lnp_d = di("lnp", [DEPTH, 2, 128, 2048])
    y_d = do("y", [TC, D])
    pC_d = do("pC", [DEPTH, 4, 128, 128]); pn_d = do("pn", [DEPTH, 4, 128]); pm_d = do("pm", [DEPTH, 4])
    pS_d = do("pS", [DEPTH, 256, 128]); ph_d = do("ph", [DEPTH, 512, 64]); pcv_d = do("pcv", [DEPTH, 3, 768])
    oC_d = do("oC", [DEPTH, NSEG, 4, 128, 128]); on_d = do("on", [DEPTH, NSEG, 4, 128]); om_d = do("om", [DEPTH, NSEG, 4])
    oS_d = do("oS", [DEPTH, NSEG, 256, 128]); oh_d = do("oh", [DEPTH, NSEG, 512, 64]); ocv_d = do("ocv", [DEPTH, NSEG * 3, 768])
    dbg_d = {k: do("dbg_" + k, shp) for k, shp in dbg}
    dscr = lambda name, shape: nc.dram_tensor(name, list(shape), BF16).ap()
    wins = dscr("wins", [DEPTH, D, INW]); wbrs = dscr("wbrs", [DEPTH, 1536, D]); wouts = dscr("wouts", [DEPTH, D, D])
    wqs = dscr("wqs", [DEPTH, D, 2048]); keyss = dscr("keyss", [DEPTH, 128, 2048])
    uvs = [dscr("uvs%d" % l, [nexp, 2 * D]) for l in range(DEPTH)]

    with contextlib.ExitStack() as st:
        S = Sched(nc, st)
        res = {}

        def sb(name, shape, dt=F32):
            t = st.enter_context(nc.sbuf_tensor("sb_" + name, list(shape), dt))
            res[name] = Res(name)
            return t

        def R(*names):
            return [res[n] for n in names]

        PS = [st.enter_context(nc.psum_tensor("ps%d" % i, [128, 512], F32)) for i in range(8)]
        PSR = [Res("ps%d" % i) for i in range(8)]
        psn = [0]

        def bank():
            i = psn[0]
            psn[0] = (i + 1) % 3
            return PS[i], PSR[i]

        cst = sb("cst", [128, COW])
        S.dma("sp", cst[:], cst_d, writes=R("cst"))
        C = lambda n: cst[:, CO[n][0]:CO[n][1]]
        cb = sb("cb", [128, 128 + 128 + 64 + 1024], BF16)
        S.op("dve", lambda e: e.tensor_copy(out=cb[:, 0:128], in_=C("leP")), R("cst"), R("cb"))
        S.op("dve", lambda e: e.tensor_copy(out=cb[:, 128:256], in_=C("ident")), R("cst"), R("cb"))
        S.op("dve", lambda e: e.tensor_copy(out=cb[:, 256:320], in_=C("leS")), R("cst"), R("cb"))
        ident = C("ident")
        ones = C("ones")

        Xs = [sb("X0", [128, D]), sb("X1", [128, D])]
        xT = sb("xT", [128, 8, 128], BF16)
        rowp = sb("rowp", [128, RPW])
        lnrow = sb("lnrow", [128, 2048])
        res["lnrowF"] = Res("lnrowF"); res["lnrowB"] = Res("lnrowB")
        convp = sb("convp", [128, 6, 5])
        gaupf = sb("gaupf", [16, 256]); gaupb = sb("gaupb", [16, 256], BF16)
        arow = sb("arow", [128, 8])
        wst = [sb("wst%d" % i, [128, 8, 264]) for i in range(2)]
        wbf = [sb("wbf%d" % i, [128, 8, 528], BF16) for i in range(2)]
        wsn = [0]
        mqT = sb("mqT", [128, 4, 128], BF16); mkT = sb("mkT", [128, 4, 128], BF16)
        mk_tm = sb("mk_tm", [128, 512], BF16); mv_tm = sb("mv_tm", [128, 512])
        mo_tm = sb("mo_tm", [128, 512]); mif = sb("mif", [128, 8])
        gqT = sb("gqT", [128, 2, 128]); gkT = sb("gkT", [128, 2, 128])
        gk_tm = sb("gk_tm", [128, 256]); gv_tm = sb("gv_tm", [128, 512], BF16); gr_tm = sb("gr_tm", [128, 512])
        gaT = sb("gaT", [16, 128], BF16)
        sz_tm = sb("sz_tm", [128, 512]); sdt = sb("sdt", [128, 8])
        xp = sb("xp", [128, 6, 3 + 128])
        br = sb("br", [128, 1536])
        brT = sb("brT", [128, 12, 128], BF16)
        CT = [sb("CT%d" % l, [128, 4, 129]) for l in range(DEPTH)]
        CTb = sb("CTb", [128, 4, 129], BF16)
        mrun = [sb("mrun%d" % l, [4, 1]) for l in range(DEPTH)]
        Sst = [sb("Sst%d" % l, [128, 2, 128]) for l in range(DEPTH)]
        Sb = sb("Sb", [128, 2, 128], BF16)
        hT = [sb("hT%d" % l, [128, 256]) for l in range(DEPTH)]
        hTb = sb("hTb", [128, 256], BF16)
        cvh = [sb("cvh%d" % l, [128, 6, 3]) for l in range(DEPTH)]
        for l in range(DEPTH):
            S.op("pool", lambda e, l=l: e.memset(CT[l][:], 0.0), (), R("CT%d" % l))
            S.op("pool", lambda e, l=l: e.memset(mrun[l][:], 0.0), (), R("mrun%d" % l))
            S.op("pool", lambda e, l=l: e.memset(Sst[l][:], 0.0), (), R("Sst%d" % l))
            S.op("pool", lambda e, l=l: e.memset(hT[l][:], 0.0), (), R("hT%d" % l))
            S.op("pool", lambda e, l=l: e.memset(cvh[l][:], 0.0), (), R("cvh%d" % l))
        t8 = sb("t8", [128, 64]); u8 = sb("u8", [128, 8]); w4 = sb("w4", [128, 4]); eb4 = sb("eb4", [128, 4])
        eB4 = sb("eB4", [128, 4]); uT = sb("uT", [4, 256]); sm4 = sb("sm4", [4, 64])
        SM = sb("SM", [128, 4, 128], BF16); VW = sb("VW", [128, 4, 129], BF16)
        hm = sb("hm", [128, 512]); hm2 = sb("hm2", [128, 512]); st4 = sb("st4", [128, 16])
        CTs = sb("CTs", [128, 129])
        big = sb("big", [128, 1024]); big2 = sb("big2", [128, 1024])
        bigb = sb("bigb", [128, 1024], BF16)
        S.dma("sp", big[:, :], segm_d, writes=R("big"))
        S.op("dve", lambda e: e.tensor_copy(out=cb[:, 320:1344], in_=big[:, :]), R("big"), R("cb"))
        nl = sb("nl", [128, 256]); eT = sb("eT", [128, 2, 2, 128]); qtT = sb("qtT", [128, 2, 128], BF16)
        ktT = sb("ktT", [128, 2, 128], BF16); kt_tm = sb("kt_tm", [128, 256], BF16)
        AM = sb("AM", [128, 4, 128], BF16); St = sb("St", [128, 128])
        actT = sb("actT", [128, 6, 128]); BCb = sb("BCb", [128, 2, 128], BF16)
        xs_tm = sb("xs_tm", [128, 512]); Bm_tm = sb("Bm_tm", [128, 128], BF16)
        dts = sb("dts", [128, 40]); cbm = sb("cbm", [128, 2, 128])
        xsb = sb("xsb", [128, 512], BF16); xsw = sb("xsw", [128, 512], BF16)
        eL = sb("eL", [128, 8])
        mixT = sb("mixT", [128, 8, 128], BF16); gsig = sb("gsig", [128, 128])
        lnt = sb("lnt", [128, 8])
        qTb = sb("qTb", [128, 16, 128], BF16)
        V16 = sb("V16", [128, 16, 16]); I16 = sb("I16", [128, 16, 16], U32); I16f = sb("I16f", [128, 16, 16])
        wk = sb("wk", [128, 256])
        TV = sb("TV", [128, 8, 16]); TPu = sb("TPu", [128, 8, 16], U32)
        pks = [sb("pk0", [128, 8, 128]), sb("pk1", [128, 8, 128])]
        eidxs = [sb("eidx0", [128, 128], I32), sb("eidx1", [128, 128], I32)]
        lntB = sb("lntB", [128, 8])
        thr16 = sb("thr16", [128, 16])
        xb16 = sb("xb16", [128, 1024], BF16)
        prodb = [sb("prodb%d" % i, [128, 1024], BF16) for i in range(2)]
        dgb = [sb("dgb%d" % i, [128, 128], BF16) for i in range(4)]
        S.op("dve", lambda e: e.tensor_scalar(out=thr16[:], in0=C("iota16"), scalar1=16.0, scalar2=16.0, op0=ALU.mult, op1=ALU.add), R("cst"), R("thr16"))
        sampbuf = {}
        sampres = {}
        SA = sb("SA", [128, 4096])
        SBb = sb("SBb", [128, 12288], BF16)
        sampbuf["n0T"] = sb("sp_n0T", [128, 4, 16]); sampres["n0T"] = res["sp_n0T"]
        SC = SA[:, 0:2048]
        res["SC"] = Res("SC")
        SBf = SBb[:, :].bitcast(F32)
        NGU = 8
        SAb = SA[:, :].bitcast(BF16)
        GU = [SBb[:, q * 2048:(q + 1) * 2048] for q in range(6)] + [SAb[:, 4096 + q * 2048:4096 + (q + 1) * 2048] for q in range(2)]
        for q in range(NGU):
            res["GU%d" % q] = Res("GU%d" % q)
        res["acc"] = Res("acc")
        fsc = sb("fsc", [1, 4])
        peer_alias = ["SC", "acc"] + ["GU%d" % q for q in range(NGU)]

        def fence():
            S.op("dve", lambda e: e.memset(fsc[0:1, 0:1], 0.0), (), R("SA", "SBb", "fsc", *peer_alias))

        sampbuf["C0t"] = SA[:, 0:2048].rearrange("p (s d) -> p s d", d=128); sampres["C0t"] = res["SA"]
        sampbuf["Co"] = SA[:, 2048:4096].rearrange("p (s d) -> p s d", d=128); sampres["Co"] = res["SA"]
        sampbuf["mqS"] = SBb[:, 0:4096].rearrange("p (h t) -> p h t", t=1024); sampres["mqS"] = res["SBb"]
        sampbuf["CTbs"] = SBb[:, 4096:4096 + 2064].rearrange("p (s c) -> p s c", c=129); sampres["CTbs"] = res["SBb"]
        sampbuf["VWs"] = SBb[:, 6160:6160 + 2064].rearrange("p (s c) -> p s c", c=129); sampres["VWs"] = res["SBb"]

        def evac(eng, out, in_, rd, wr, scale=None):
            if eng == "act":
                if scale is None:
                    S.op("act", lambda e: e.copy(out=out, in_=in_), rd, wr)
                else:
                    S.op("act", lambda e: e.mul(out=out, in_=in_, mul=scale), rd, wr)
            else:
                if scale is None:
                    S.op("dve", lambda e: e.tensor_copy(out=out, in_=in_), rd, wr)
                else:
                    S.op("dve", lambda e: e.tensor_scalar(out=out, in0=in_, scalar1=scale, scalar2=None, op0=ALU.mult), rd, wr)

        def load_w2(src_ap, kch, width):
            i = wsn[0]
            wsn[0] = (i + 1) % NWP
            bv = wpool[i][:, 0:kch * width].rearrange("p (k n) -> p k n", n=width)
            S.dma("sp", bv, src_ap.rearrange("(k p) n -> p k n", p=128), reads=R("scratch"), writes=[wpres[i]])
            return bv, wpres[i]

        def load_w(src_ap, width):
            return load_w2(src_ap, 8, width)

        def fm_piece(wb, wr_, a, b, N, kch=8, src=None, srcres=None):
            src = xT if src is None else src
            srcres = res["xT"] if srcres is None else srcres
            p, pr = bank()
            for k in range(kch):
                S.op("pe", lambda e, k=k: e.matmul(p[0:b - a, 0:N], lhsT=wb[:, k, a:b], rhs=src[:, k, 0:N],
                                                    start=(k == 0), stop=(k == kch - 1)), [wr_, srcres], [pr])
            return p, pr

        def tm_piece(wb, wr_, a, b, P, kch=8, src=None, srcres=None):
            src = xT if src is None else src
            srcres = res["xT"] if srcres is None else srcres
            p, pr = bank()
            for k in range(kch):
                S.op("pe", lambda e, k=k: e.matmul(p[0:P, 0:b - a], lhsT=src[:, k, 0:P], rhs=wb[:, k, a:b],
                                                    start=(k == 0), stop=(k == kch - 1)), [wr_, srcres], [pr])
            return p, pr

        def transpose_to(dst_fn, src, srcres, P, nch, dstres, eng="act"):
            for c0 in range(0, nch, 4):
                c1 = min(nch, c0 + 4)
                p, pr = bank()
                for c in range(c0, c1):
                    S.op("pe", lambda e, c=c: e.transpose(out=p[:, (c - c0) * 128:(c - c0) * 128 + P],
                                                          in_=src[0:P, c * 128:(c + 1) * 128], identity=ident[0:P, 0:P]),
                         [srcres, res["cst"]], [pr])
                pv = p[:, 0:(c1 - c0) * 128].rearrange("p (c t) -> p c t", t=128)[:, :, 0:P]
                evac(eng, dst_fn(c0, c1), pv, [pr], [dstres])

        def act_fn(out, in_, func, rd, wr, **kw):
            S.op("act", lambda e: e.activation(out=out, in_=in_, func=func, **kw), rd, wr)

        def softplus_neg(out, in_, rd, wr, tmp, tmpres, sign=-1.0):
            act_fn(tmp, in_, AF.Exp, rd, [tmpres], scale=sign)
            act_fn(out, tmp, AF.Ln, [tmpres], wr, bias=1.0)

        def rstd_from_ssq(out, ssq, n, rd, wr):
            act_fn(out, ssq, AF.Sqrt, rd, wr, scale=1.0 / n, bias=EPS)
            S.op("dve", lambda e: e.reciprocal(out=out, in_=out), wr, wr)

        def layernorm_tm(P, src, srcres, gb_dram, out, outres, lnt_, lntn, scr, scrres, lnb, lnbn):
            LR = [res[lntn]]
            S.op("dve", lambda e: e.tensor_reduce(out=lnt_[0:P, 0:1], in_=src[0:P, :], axis=AX.X, op=ALU.add), [srcres], LR)
            S.op("dve", lambda e: e.tensor_scalar(out=lnt_[0:P, 1:2], in0=lnt_[0:P, 0:1], scalar1=-1.0 / D, scalar2=None, op0=ALU.mult), LR, LR)
            S.op("dve", lambda e: e.tensor_scalar(out=src[0:P, :], in0=src[0:P, :], scalar1=lnt_[0:P, 1:2], scalar2=None, op0=ALU.add), [srcres] + LR, [srcres])
            S.op("dve", lambda e: e.tensor_tensor(out=scr[0:P, :], in0=src[0:P, :], in1=src[0:P, :], op=ALU.mult), [srcres], [scrres])
            S.op("dve", lambda e: e.tensor_reduce(out=lnt_[0:P, 2:3], in_=scr[0:P, :], axis=AX.X, op=ALU.add), [scrres], LR)
            rstd_from_ssq(lnt_[0:P, 3:4], lnt_[0:P, 2:3], D, LR, LR)
            S.dma("sp", lnb, gb_dram[:, 0:1024], writes=[res[lnbn]])
            S.op("dve", lambda e: e.scalar_tensor_tensor(out=scr[0:P, :], in0=src[0:P, :], scalar=lnt_[0:P, 3:4], in1=lnb[0:P, :], op0=ALU.mult, op1=ALU.mult),
                 [srcres, res[lnbn]] + LR, [scrres])
            S.dma("sp", lnb, gb_dram[:, 1024:2048], writes=[res[lnbn]])
            S.op("dve", lambda e: e.tensor_tensor(out=out, in0=scr[0:P, :], in1=lnb[0:P, :], op=ALU.add), [scrres, res[lnbn]], [outres])

        dbg_i = [0]

        def dbg_out(name, ap, rd):
            if name in dbg_d:
                S.dma("sp", dbg_d[name][dbg_i[0], 0:ap.shape[0]], ap, reads=rd)

        cvn = [0]

        SAf = SA
        stF = [wst[0][:].rearrange("p k n -> p (k n)")[:, 0:2048], wst[1][:].rearrange("p k n -> p (k n)")[:, 0:2048], SAf[:, 0:2048], SAf[:, 2048:4096]]
        stB = [wbf[0][:].rearrange("p k n -> p (k n)")[:, 0:2048], wbf[1][:].rearrange("p k n -> p (k n)")[:, 0:2048], SBb[:, 0:2048], SBb[:, 2048:4096]]
        stFr = [Res("stF%d" % i) for i in range(4)]
        stBr = [Res("stB%d" % i) for i in range(4)]

        def convert(src2d, dst2d, three=False):
            n = 2048 if three else src2d.shape[1]
            for a in range(0, n, 2048):
                b = min(n, a + 2048)
                i = cvn[0] % 4
                eng = ("dve", "act", "pool")[cvn[0] % 3]
                cvn[0] += 1
                fv = stF[i][:, 0:b - a]
                bv = stB[i][:, 0:b - a]
                cp = (lambda e: e.copy(out=bv, in_=fv)) if eng == "act" else (lambda e: e.tensor_copy(out=bv, in_=fv))
                if three:
                    S.dma(CVL[cvn[0] % len(CVL)], fv.rearrange("p (k c) -> p k c", c=1024), src2d, writes=[stFr[i]])
                    S.op(eng, cp, [stFr[i]], [stBr[i]])
                    S.dma(CVQ, dst2d, bv.rearrange("p (k c) -> p k c", c=1024), reads=[stBr[i]], writes=R("scratch"))
                else:
                    S.dma(CVL[cvn[0] % len(CVL)], fv, src2d[:, a:b], writes=[stFr[i]])
                    S.op(eng, cp, [stFr[i]], [stBr[i]])
                    S.dma(CVQ, dst2d[:, a:b], bv, reads=[stBr[i]], writes=R("scratch"))

        res["scratch"] = Res("scratch")
        flat = lambda ap: ap.rearrange("(p k) c -> p (k c)", p=128)
        if preconv:
            for l in range(nlayers):
                convert(flat(win_d[l]), flat(wins[l]))
                convert(flat(wbr_d[l]), flat(wbrs[l]))
                convert(flat(wout_d[l]), flat(wouts[l]))
                convert(flat(wq_d[l]), flat(wqs[l]))
                convert(keys_d[l].rearrange("c a k -> c (a k)"), keyss[l])
                if do_peer:
                    uv3 = uvs[l].rearrange("(p k) c -> p k c", p=128)
                    for half, src in enumerate((pu_d[l], pv_d[l])):
                        s3 = src.rearrange("(p k) c -> p k c", p=128)
                        for k0 in range(0, nexp // 128, 2):
                            convert(s3[:, k0:k0 + 2, :], uv3[:, k0:k0 + 2, half * D:(half + 1) * D], three=True)
        wpool = [wbf[0][:].rearrange("p k n -> p (k n)"), wbf[1][:].rearrange("p k n -> p (k n)")]
        for i in range(2):
            v = wst[i][:].rearrange("p k n -> p (k n)").bitcast(BF16)
            wpool += [v[:, 0:4224]]
        NWP = len(wpool)
        wpres = [Res("wp%d" % i) for i in range(NWP)]
        S.op("dve", lambda e: e.memset(fsc[0:1, 1:2], 0.0), (), R("wst0", "wst1", "wbf0", "wbf1", "scratch", "fsc", "SA", "SBb", *peer_alias) + wpres + stFr + stBr)

        all_tiles = [("p", i) for i in range(SEQ // 128)] + [("s", 0)]
        if tiles is not None:
            all_tiles = tiles
        last_prompt = SEQ // 128 - 1
        def tile_layer(kind, ti, l, par, tpar):
            samp = kind == "s"
            P = 64 if samp else 128
            t0 = SEQ if samp else ti * 128
            N = P
            X = Xs[par]; XN = "X%d" % par
            pk = pks[tpar]; PKN = "pk%d" % tpar
            eidx = eidxs[tpar]; EIN = "eidx%d" % tpar
            if l == 0:
                S.dma("sp", X[0:P, :], x_d[t0:t0 + P, :], writes=R(XN))
            yield
            le_f = C("leS")[0:P, :] if samp else C("leP")
            yield
            gt_f = C("gtS")[0:P, :] if samp else C("gtP")
            yield
            le_b = cb[0:P, 256:320] if samp else cb[:, 0:128]
            yield
            S.dma("sp", rowp[:], rowp_d[l], writes=R("rowp"))
            yield
            S.dma("sp", convp[:], convp_d[l], writes=R("convp"))
            yield
            S.dma("sp", gaupf[:], gaup_d[l], writes=R("gaupf"))
            yield
            S.op("dve", lambda e: e.tensor_copy(out=gaupb[:], in_=gaupf[:]), R("gaupf"), R("gaupb"))
            yield
            rp = lambda n: rowp[0:P, RP[n][0]:RP[n][1]]
            yield
            act_fn(arow[:], rowp[:, RP["alog"][0]:RP["alog"][1]], AF.Exp, R("rowp"), R("arow"))
            yield
            transpose_to(lambda c0, c1: xT[:, c0:c1, 0:P], X, res[XN], P, 8, res["xT"])
            yield
            W = lambda a, b: wins[l, :, a:b]
            yield
            wb, wr_ = load_w(W(0, 512), 512)
            yield
            for h in range(4):
                p, pr = fm_piece(wb, wr_, h * 128, (h + 1) * 128, N)
                evac("act", mqT[:, h, 0:N], p[:, 0:N], [pr], R("mqT"), scale=128 ** -0.5)
            yield
            wb, wr_ = load_w(W(512, 1024), 512)
            yield
            for h in range(4):
                p, pr = fm_piece(wb, wr_, h * 128, (h + 1) * 128, N)
                evac("act", mkT[:, h, 0:N], p[:, 0:N], [pr], R("mkT"))
            yield
            p, pr = tm_piece(wb, wr_, 0, 512, P)
            yield
            evac("dve", mk_tm[0:P, :], p[0:P, :], [pr], R("mk_tm"))
            yield
            wb, wr_ = load_w(W(1024, 1536), 512)
            yield
            p, pr = tm_piece(wb, wr_, 0, 512, P)
            yield
            evac("act", mv_tm[0:P, :], p[0:P, :], [pr], R("mv_tm"))
            yield
            wb, wr_ = load_w(W(1536, 2048), 512)
            yield
            p, pr = tm_piece(wb, wr_, 0, 512, P)
            yield
            evac("dve", mo_tm[0:P, :], p[0:P, :], [pr], R("mo_tm"))
            yield
            wb, wr_ = load_w(W(2048, 2568), 520)
            yield
            p, pr = tm_piece(wb, wr_, 0, 8, P)
            yield
            evac("act", mif[0:P, :], p[0:P, 0:8], [pr], R("mif"))
            yield
            for j in range(2):
                p, pr = fm_piece(wb, wr_, 8 + j * 128, 8 + (j + 1) * 128, N)
                evac("act", gqT[:, j, 0:N], p[:, 0:N], [pr], R("gqT"), scale=64 ** -0.5)
                p, pr = fm_piece(wb, wr_, 264 + j * 128, 264 + (j + 1) * 128, N)
                evac("dve", gkT[:, j, 0:N], p[:, 0:N], [pr], R("gkT"))
            yield
            p, pr = tm_piece(wb, wr_, 264, 520, P)
            yield
            evac("act", gk_tm[0:P, :], p[0:P, 0:256], [pr], R("gk_tm"))
            yield
            wb, wr_ = load_w(W(2568, 3080), 512)
            yield
            p, pr = tm_piece(wb, wr_, 0, 512, P)
            yield
            evac("dve", gv_tm[0:P, :], p[0:P, :], [pr], R("gv_tm"))
            yield
            wb, wr_ = load_w(W(3080, 3608), 528)
            yield
            p, pr = tm_piece(wb, wr_, 0, 512, P)
            yield
            evac("act", gr_tm[0:P, :], p[0:P, :], [pr], R("gr_tm"))
            yield
            p, pr = fm_piece(wb, wr_, 512, 528, N)
            yield
            evac("dve", gaT[0:16, 0:N], p[0:16, 0:N], [pr], R("gaT"))
            yield
            wb, wr_ = load_w(W(3608, 4120), 512)
            yield
            p, pr = tm_piece(wb, wr_, 0, 512, P)
            yield
            evac("act", sz_tm[0:P, :], p[0:P, :], [pr], R("sz_tm"))
            yield
            if samp:
                cv0 = big2[0:48, 0:768]
                S.dma("sp", cv0, scv_d[l], writes=R("big2"))
                for c0 in (0, 4):
                    p, pr = bank()
                    for c in range(c0, min(6, c0 + 4)):
                        S.op("pe", lambda e, c=c: e.transpose(out=p[:, (c - c0) * 128:(c - c0) * 128 + 48], in_=cv0[:, c * 128:(c + 1) * 128],
                                                              identity=ident[0:48, 0:48]), R("big2", "cst"), [pr])
                    ncn = min(6, c0 + 4) - c0
                    pv = p[:, 0:ncn * 128].rearrange("p (c t) -> p c t", t=128)[:, :, 0:48].rearrange("p c (s j) -> p c s j", j=3)
                    dst = xp[:, c0:c0 + ncn, 0:NSEG * 7].rearrange("p c (s j) -> p c s j", j=7)[:, :, :, 0:3]
                    evac("dve", dst, pv, [pr], R("xp"))
            else:
                S.op("dve", lambda e: e.tensor_copy(out=xp[:, :, 0:3], in_=cvh[l][:]), R("cvh%d" % l), R("xp"))
            yield
            wb, wr_ = load_w(W(4120, 4632), 512)
            yield
            wb2, wr2 = load_w(W(4632, 4896), 264)
            yield
            for c in range(6):
                if c < 4:
                    p, pr = fm_piece(wb, wr_, c * 128, (c + 1) * 128, N)
                else:
                    p, pr = fm_piece(wb2, wr2, (c - 4) * 128, (c - 3) * 128, N)
                if samp:
                    dst = xp[:, c, 0:NSEG * 7].rearrange("p (s j) -> p s j", j=7)[:, :, 3:7]
                    evac("act", dst, p[:, 0:N].rearrange("p (s j) -> p s j", j=4), [pr], R("xp"))
                else:
                    evac("act", xp[:, c, 3:3 + N], p[:, 0:N], [pr], R("xp"))
            yield
            p, pr = tm_piece(wb2, wr2, 256, 264, P)
            yield
            evac("dve", sdt[0:P, :], p[0:P, 0:8], [pr], R("sdt"))
            yield
            if not samp:
                S.op("dve", lambda e: e.tensor_copy(out=cvh[l][:], in_=xp[:, :, N:N + 3]), R("xp"), R("cvh%d" % l))
            yield
            if samp or (ti == last_prompt and (LASTM & 1)):
                nr = 48 if samp else 3
                cvt = big[:, 0:6 * 48].rearrange("p (c r) -> p c r", r=48)
                if samp:
                    srcv = xp[:, :, 0:NSEG * 7].rearrange("p c (s j) -> p c s j", j=7)[:, :, :, 4:7]
                    S.op("dve", lambda e: e.tensor_copy(out=cvt.rearrange("p c (s j) -> p c s j", j=3), in_=srcv), R("xp"), R("big"))
                else:
                    S.op("dve", lambda e: e.tensor_copy(out=cvt[:, :, 0:3], in_=xp[:, :, N:N + 3]), R("xp"), R("big"))
                for c0 in (0, 4):
                    p, pr = bank()
                    ncn = min(6, c0 + 4) - c0
                    for c in range(c0, c0 + ncn):
                        S.op("pe", lambda e, c=c: e.transpose(out=p[0:nr, (c - c0) * 128:(c - c0 + 1) * 128], in_=cvt[:, c, 0:nr], identity=ident),
                             R("big", "cst"), [pr])
                    evac("act", big2[0:nr, c0 * 128:(c0 + ncn) * 128], p[0:nr, 0:ncn * 128], [pr], R("big2"))
                S.dma("sp", (ocv_d[l] if samp else pcv_d[l]), big2[0:nr, 0:768], reads=R("big2"))

            yield
            nseg = NSEG if samp else 1
            yield
            L = P // nseg
            yield
            S.op("dve", lambda e: e.tensor_tensor(out=t8[0:P, 0:4], in0=mif[0:P, 0:4], in1=rp("mib"), op=ALU.add), R("mif", "rowp"), R("t8"))
            yield
            S.op("dve", lambda e: e.tensor_tensor(out=t8[0:P, 4:8], in0=mif[0:P, 4:8], in1=rp("mfb"), op=ALU.add), R("mif", "rowp"), R("t8"))
            yield
            softplus_neg(t8[0:P, 12:16], t8[0:P, 4:8], R("t8"), R("t8"), t8[0:P, 8:12], res["t8"])
            yield
            p, pr = bank()
            yield
            S.op("pe", lambda e: e.matmul(p[0:P, 0:4], lhsT=le_f, rhs=t8[0:P, 12:16], start=True, stop=True), R("cst", "t8"), [pr])
            yield
            S.op("pe", lambda e: e.matmul(p[:, 8:12], lhsT=ones[0:P, :], rhs=t8[0:P, 12:16], start=True, stop=True), R("cst", "t8"), [pr])
            yield
            S.op("dve", lambda e: e.tensor_tensor(out=u8[0:P, 0:4], in0=p[0:P, 0:4], in1=t8[0:P, 0:4], op=ALU.add), [pr] + R("t8"), R("u8"))
            yield
            S.op("dve", lambda e: e.tensor_copy(out=u8[0:P, 4:8], in_=p[0:P, 0:4]), [pr], R("u8"))
            yield
            act_fn(w4[0:P, :], u8[0:P, 0:4], AF.Exp, R("u8"), R("w4"))
            yield
            act_fn(eb4[0:P, :], p[0:P, 0:4], AF.Exp, [pr], R("eb4"), scale=-1.0)
            yield
            act_fn(eB4[:, :], p[:, 8:12], AF.Exp, [pr], R("eB4"), scale=-1.0)
            yield
            p2, pr2 = bank()
            yield
            S.op("pe", lambda e: e.transpose(out=p2[0:4, 0:P], in_=u8[0:P, 0:4], identity=ident[0:P, 0:P]), R("u8", "cst"), [pr2])
            yield
            S.op("pe", lambda e: e.transpose(out=p2[0:4, 128:128 + P], in_=u8[0:P, 4:8], identity=ident[0:P, 0:P]), R("u8", "cst"), [pr2])
            yield
            S.op("dve", lambda e: e.tensor_copy(out=uT[:, :], in_=p2[0:4, 0:256]), [pr2], R("uT"))
            yield
            S.op("dve", lambda e: e.tensor_reduce(out=sm4[:, 0:nseg], in_=uT[:, 0:P].rearrange("h (s j) -> h s j", j=L), axis=AX.X, op=ALU.max), R("uT"), R("sm4"))
            yield
            blast = uT[:, 128:128 + P].rearrange("h (s j) -> h s j", j=L)[:, :, L - 1]
            yield
            if not samp:
                S.op("dve", lambda e: e.tensor_tensor(out=mrun[l][:], in0=mrun[l][:], in1=sm4[:, 0:1], op=ALU.max), R("mrun%d" % l, "sm4"), R("mrun%d" % l))
                S.op("dve", lambda e: e.tensor_tensor(out=mrun[l][:], in0=mrun[l][:], in1=blast, op=ALU.subtract), R("mrun%d" % l, "uT"), R("mrun%d" % l))
            else:
                S.dma("sp", sm4[:, 16:32], sm_d[l].rearrange("s h -> h s"), writes=R("sm4"), slow=True)
                S.op("dve", lambda e: e.tensor_tensor(out=sm4[:, 32:48], in0=sm4[:, 0:16], in1=sm4[:, 16:32], op=ALU.max), R("sm4"), R("sm4"))
                S.op("dve", lambda e: e.tensor_tensor(out=sm4[:, 48:64], in0=sm4[:, 32:48], in1=blast, op=ALU.subtract), R("sm4", "uT"), R("sm4"))
                S.dma("sp", om_d[l].rearrange("s h -> h s"), sm4[:, 48:64], reads=R("sm4"), slow=True)
                act_fn(sm4[:, 0:16], sm4[:, 32:48], AF.Exp, R("sm4"), R("sm4"), scale=-1.0)
                act_fn(sm4[:, 48:64], sm4[:, 16:32], AF.Exp, R("sm4"), R("sm4"))
                for q, (a0) in enumerate((0, 48)):
                    S.op("dve", lambda e, q=q, a0=a0: e.tensor_tensor(
                        out=hm2[0:4, q * 64:(q + 1) * 64].rearrange("p (h s) -> p h s", s=16),
                        in0=sm4[:, a0:a0 + 16].unsqueeze(1).to_broadcast([4, 4, 16]),
                        in1=ident[0:4, 0:4].unsqueeze(2).to_broadcast([4, 4, 16]), op=ALU.mult), R("sm4", "cst"), R("hm2"))
                p3, pr3 = bank()
                S.op("pe", lambda e: e.matmul(p3[:, 0:128], lhsT=ones[0:4, :], rhs=hm2[0:4, 0:128], start=True, stop=True), R("cst", "hm2"), [pr3])
                S.op("dve", lambda e: e.tensor_copy(out=big2[:, 0:128], in_=p3[:, 0:128]), [pr3], R("big2"))
                S.op("dve", lambda e: e.tensor_tensor(out=big2[:, 128:192], in0=big2[:, 0:64], in1=big2[:, 64:128], op=ALU.mult), R("big2"), R("big2"))
            yield
            for h in range(4):
                p, pr = bank()
                S.op("pe", lambda e, h=h: e.matmul(p[0:P, 0:P], lhsT=mkT[:, h, 0:P], rhs=mqT[:, h, 0:P], start=True, stop=True), R("mkT", "mqT"), [pr])
                S.op("dve", lambda e, h=h: e.tensor_tensor(out=SM[0:P, h, 0:P], in0=p[0:P, 0:P], in1=le_f[:, 0:P], op=ALU.mult), [pr] + R("cst"), R("SM"))
            yield
            S.op("dve", lambda e: e.tensor_tensor(out=VW[0:P, :, 0:128], in0=mv_tm[0:P, :].rearrange("p (h v) -> p h v", v=128),
                                                  in1=w4[0:P, :].unsqueeze(2).to_broadcast([P, 4, 128]), op=ALU.mult), R("mv_tm", "w4"), R("VW"))
            yield
            S.op("dve", lambda e: e.tensor_copy(out=VW[0:P, :, 128:129], in_=w4[0:P, :].unsqueeze(2)), R("w4"), R("VW"))
            yield
            if not samp:
                S.op("act", lambda e: e.copy(out=CTb[:], in_=CT[l][:]), R("CT%d" % l), R("CTb"))
            else:
                pass
            yield
            if samp:
                CTbs = sampbuf["CTbs"]
                C0t = sampbuf["C0t"]
                n0T = sampbuf["n0T"]
                for hh in range(4):
                    S.dma("sp", n0T[:, hh, :], sn_d[l, :, hh, :].rearrange("s d -> d s"), writes=[sampres["n0T"]], slow=True)
                mqS = sampbuf["mqS"]
                S.op("dve", lambda e: e.tensor_tensor(out=mqS[:].rearrange("p h (s t) -> p h s t", t=64),
                                                      in0=mqT[:, :, 0:64].unsqueeze(2).to_broadcast([128, 4, 16, 64]),
                                                      in1=cb[:, 320:1344].rearrange("p (s t) -> p s t", t=64).unsqueeze(1).to_broadcast([128, 4, 16, 64]),
                                                      op=ALU.mult), R("mqT", "cb"), [sampres["mqS"]])
            yield
            nd_banks = [(PS[6], PSR[6]), (PS[7], PSR[7])]
            yield
            for h in range(4):
                ndp, ndr = nd_banks[h // 2]
                nd = ndp[0:P, (h % 2) * 129:(h % 2) * 129 + 129]
                S.op("pe", lambda e, h=h, nd=nd: e.matmul(nd, lhsT=SM[0:P, h, 0:P], rhs=VW[0:P, h, :], start=True, stop=False), R("SM", "VW"), [ndr])
                if not samp:
                    S.op("pe", lambda e, h=h, nd=nd: e.matmul(nd, lhsT=mqT[:, h, 0:P], rhs=CTb[:, h, :], start=False, stop=True), R("mqT", "CTb"), [ndr])
                else:
                    S.dma("sp", C0t[:], sC_d[l, :, h].rearrange("s v d -> v s d"), writes=[sampres["C0t"]])
                    for s in range(NSEG):
                        p, pr = bank()
                        S.op("pe", lambda e, s=s: e.transpose(out=p[:, 0:128], in_=C0t[:, s, :], identity=ident), [sampres["C0t"]] + R("cst"), [pr])
                        S.op("dve", lambda e, s=s, h=h: e.tensor_scalar(out=CTbs[:, s, 0:128], in0=p[:, 0:128], scalar1=big2[:, 64 + h * 16 + s:64 + h * 16 + s + 1],
                                                                       scalar2=None, op0=ALU.mult), [pr] + R("big2"), [sampres["CTbs"]])
                    S.op("dve", lambda e, h=h: e.tensor_tensor(out=CTbs[:, :, 128], in0=n0T[:, h, :], in1=big2[:, 64 + h * 16:64 + h * 16 + 16], op=ALU.mult),
                         [sampres["n0T"]] + R("big2"), [sampres["CTbs"]])
                    for s in range(NSEG):
                        S.op("pe", lambda e, s=s, h=h, nd=nd: e.matmul(nd, lhsT=mqS[:, h, s * 64:(s + 1) * 64], rhs=CTbs[:, s, :], start=False, stop=(s == NSEG - 1)),
                             [sampres["mqS"], sampres["CTbs"]], [ndr])
                    VWs = sampbuf["VWs"]
                    S.op("dve", lambda e, h=h: e.tensor_tensor(out=VWs[0:64, :, :], in0=VW[0:64, h, :].unsqueeze(1).to_broadcast([64, 16, 129]),
                                                               in1=C("segind")[0:64, :].unsqueeze(2).to_broadcast([64, 16, 129]), op=ALU.mult),
                         R("VW", "cst"), [sampres["VWs"]])
                    Co = sampbuf["Co"]
                    for s in range(NSEG):
                        p, pr = bank()
                        S.op("pe", lambda e, s=s, h=h: e.matmul(p[:, 0:128], lhsT=VWs[0:64, s, 0:128], rhs=mk_tm[0:64, h * 128:(h + 1) * 128], start=True, stop=True),
                             [sampres["VWs"]] + R("mk_tm"), [pr])
                        col = h * 16 + s
                        S.op("dve", lambda e, s=s, col=col: e.tensor_scalar(out=St[:, :], in0=C0t[:, s, :], scalar1=big2[:, 128 + col:129 + col], scalar2=None, op0=ALU.mult),
                             [sampres["C0t"]] + R("big2"), R("St"))
                        S.op("dve", lambda e, s=s, col=col: e.scalar_tensor_tensor(out=Co[:, s, :], in0=p[:, 0:128], scalar=big2[:, col:col + 1], in1=St[:, :],
                                                                                  op0=ALU.mult, op1=ALU.add), [pr] + R("big2", "St"), [sampres["Co"]])
                    S.dma("sp", oC_d[l, :, h].rearrange("s v d -> v s d"), Co[:], reads=[sampres["Co"]])
                    S.op("dve", lambda e, h=h: e.tensor_scalar(out=bigb[0:64, 0:16], in0=C("segind")[0:64, :], scalar1=w4[0:64, h:h + 1], scalar2=None, op0=ALU.mult),
                         R("cst", "w4"), R("bigb"))
                    p, pr = bank()
                    S.op("pe", lambda e, h=h: e.matmul(p[:, 0:16], lhsT=mk_tm[0:64, h * 128:(h + 1) * 128], rhs=bigb[0:64, 0:16], start=True, stop=True), R("mk_tm", "bigb"), [pr])
                    S.op("dve", lambda e, h=h: e.tensor_tensor(out=hm2[:, 256 + h * 16:256 + (h + 1) * 16], in0=n0T[:, h, :], in1=big2[:, 128 + h * 16:128 + (h + 1) * 16], op=ALU.mult),
                         [sampres["n0T"]] + R("big2"), R("hm2"))
                    S.op("dve", lambda e, h=h: e.tensor_tensor(out=hm2[:, 320 + h * 16:320 + (h + 1) * 16], in0=p[:, 0:16], in1=big2[:, h * 16:(h + 1) * 16], op=ALU.mult),
                         [pr] + R("big2"), R("hm2"))
            yield
            if samp:
                S.op("dve", lambda e: e.tensor_tensor(out=hm2[:, 256:320], in0=hm2[:, 256:320], in1=hm2[:, 320:384], op=ALU.add), R("hm2"), R("hm2"))
                for hh in range(4):
                    S.dma("sp", on_d[l, :, hh, :].rearrange("s d -> d s"), hm2[:, 256 + hh * 16:256 + (hh + 1) * 16], reads=R("hm2"), slow=True)
            yield
            for j in range(2):
                ndp, ndr = nd_banks[j]
                S.op("dve", lambda e, j=j, ndp=ndp: e.tensor_tensor(out=st4[0:P, 2 * j:2 * j + 2], in0=ndp[0:P, 0:258].rearrange("p (h c) -> p h c", c=129)[:, :, 128],
                                                                    in1=eb4[0:P, 2 * j:2 * j + 2], op=ALU.mult), [ndr] + R("eb4"), R("st4"))
            yield
            S.op("dve", lambda e: e.tensor_scalar(out=st4[0:P, 4:8], in0=st4[0:P, 0:4], scalar1=-1.0, scalar2=1.0, op0=ALU.mult, op1=ALU.max), R("st4"), R("st4"))
            yield
            S.op("dve", lambda e: e.tensor_tensor(out=st4[0:P, 4:8], in0=st4[0:P, 4:8], in1=st4[0:P, 0:4], op=ALU.max), R("st4"), R("st4"))
            yield
            S.op("dve", lambda e: e.reciprocal(out=st4[0:P, 8:12], in_=st4[0:P, 4:8]), R("st4"), R("st4"))
            yield
            S.op("dve", lambda e: e.tensor_tensor(out=st4[0:P, 12:16], in0=st4[0:P, 8:12], in1=eb4[0:P, :], op=ALU.mult), R("st4", "eb4"), R("st4"))
            yield
            for h in range(4):
                ndp, ndr = nd_banks[h // 2]
                S.op("dve", lambda e, h=h, ndp=ndp: e.tensor_scalar(out=hm[0:P, h * 128:(h + 1) * 128], in0=ndp[0:P, (h % 2) * 129:(h % 2) * 129 + 128],
                                                                    scalar1=st4[0:P, 12 + h:13 + h], scalar2=None, op0=ALU.mult), [ndr] + R("st4"), R("hm"))
            yield
            hm3 = hm[0:P, :].rearrange("p (h v) -> p h v", v=128)
            yield
            S.op("dve", lambda e: e.tensor_reduce(out=st4[0:P, 0:4], in_=hm3, axis=AX.X, op=ALU.add), R("hm"), R("st4"))
            yield
            S.op("dve", lambda e: e.tensor_scalar(out=st4[0:P, 0:4], in0=st4[0:P, 0:4], scalar1=1.0 / 128, scalar2=None, op0=ALU.mult), R("st4"), R("st4"))
            yield
            S.op("dve", lambda e: e.tensor_tensor(out=hm3, in0=hm3, in1=st4[0:P, 0:4].unsqueeze(2).to_broadcast([P, 4, 128]), op=ALU.subtract), R("hm", "st4"), R("hm"))
            yield
            S.op("dve", lambda e: e.tensor_tensor(out=hm2[0:P, 0:512], in0=hm[0:P, :], in1=hm[0:P, :], op=ALU.mult), R("hm"), R("hm2"))
            yield
            S.op("dve", lambda e: e.tensor_reduce(out=st4[0:P, 4:8], in_=hm2[0:P, 0:512].rearrange("p (h v) -> p h v", v=128), axis=AX.X, op=ALU.add), R("hm2"), R("st4"))
            yield
            rstd_from_ssq(st4[0:P, 8:12], st4[0:P, 4:8], 128, R("st4"), R("st4"))
            yield
            S.op("dve", lambda e: e.tensor_tensor(out=hm3, in0=hm3, in1=st4[0:P, 8:12].unsqueeze(2).to_broadcast([P, 4, 128]), op=ALU.mult), R("hm", "st4"), R("hm"))
            yield
            S.op("dve", lambda e: e.tensor_tensor(out=hm[0:P, :], in0=hm[0:P, :], in1=rp("mnorm"), op=ALU.mult), R("hm", "rowp"), R("hm"))
            yield
            act_fn(hm2[0:P, 0:512], mo_tm[0:P, :], AF.Sigmoid, R("mo_tm"), R("hm2"))
            yield
            S.op("dve", lambda e: e.tensor_tensor(out=br[0:P, 0:512], in0=hm[0:P, :], in1=hm2[0:P, 0:512], op=ALU.mult), R("hm", "hm2"), R("br"))
            yield
            if not samp:
                for h in range(4):
                    p, pr = bank()
                    S.op("pe", lambda e, h=h: e.matmul(p[:, 0:129], lhsT=mk_tm[0:P, h * 128:(h + 1) * 128], rhs=VW[0:P, h, :], start=True, stop=True), R("mk_tm", "VW"), [pr])
                    S.op("dve", lambda e, h=h: e.tensor_scalar(out=CTs[:, :], in0=CT[l][:, h, :], scalar1=eB4[:, h:h + 1], scalar2=None, op0=ALU.mult),
                         R("CT%d" % l, "eB4"), R("CTs"))
                    S.op("dve", lambda e, h=h: e.scalar_tensor_tensor(out=CT[l][:, h, :], in0=p[:, 0:129], scalar=eB4[:, h:h + 1], in1=CTs[:, :], op0=ALU.mult, op1=ALU.add),
                         [pr] + R("eB4", "CTs"), R("CT%d" % l))
                if ti == last_prompt and (LASTM & 2):
                    act_fn(sm4[:, 0:1], mrun[l][:], AF.Exp, R("mrun%d" % l), R("sm4"), scale=-1.0)
                    S.op("dve", lambda e: e.tensor_tensor(out=hm2[0:4, 0:4], in0=sm4[:, 0:1].to_broadcast([4, 4]), in1=ident[0:4, 0:4], op=ALU.mult), R("sm4", "cst"), R("hm2"))
                    p3, pr3 = bank()
                    S.op("pe", lambda e: e.matmul(p3[:, 0:4], lhsT=ones[0:4, :], rhs=hm2[0:4, 0:4], start=True, stop=True), R("cst", "hm2"), [pr3])
                    S.op("dve", lambda e: e.tensor_copy(out=st4[:, 0:4], in_=p3[:, 0:4]), [pr3], R("st4"))
                    for h in range(4):
                        p, pr = bank()
                        S.op("pe", lambda e, h=h: e.transpose(out=p[:, 0:128], in_=CT[l][:, h, 0:128], identity=ident), R("CT%d" % l, "cst"), [pr])
                        S.op("dve", lambda e, h=h: e.tensor_scalar(out=hm2[:, h * 128:(h + 1) * 128], in0=p[:, 0:128], scalar1=st4[:, h:h + 1], scalar2=None, op0=ALU.mult),
                             [pr] + R("st4"), R("hm2"))
                    S.dma("sp", pC_d[l].rearrange("h v d -> v h d"), hm2[:, 0:512].rearrange("p (h d) -> p h d", d=128), reads=R("hm2"))
                    S.op("dve", lambda e: e.tensor_tensor(out=st4[:, 4:8], in0=CT[l][:, :, 128], in1=st4[:, 0:4], op=ALU.mult), R("CT%d" % l, "st4"), R("st4"))
                    S.dma("sp", pn_d[l].rearrange("h d -> d h"), st4[:, 4:8], reads=R("st4"), slow=True)
                    S.dma("sp", pm_d[l].rearrange("(h o) -> h o", o=1), mrun[l][:], reads=R("mrun%d" % l), slow=True)


            yield
            p, pr = bank()
            yield
            S.op("pe", lambda e: e.matmul(p[0:P, 0:256], lhsT=gaT[0:16, 0:P], rhs=gaupb[0:16, :], start=True, stop=True), R("gaT", "gaupb"), [pr])
            yield
            S.op("dve", lambda e: e.tensor_tensor(out=big[0:P, 0:256], in0=p[0:P, 0:256], in1=rp("gab"), op=ALU.add), [pr] + R("rowp"), R("big"))
            yield
            softplus_neg(nl[0:P, :], big[0:P, 0:256], R("big"), R("nl"), big[0:P, 256:512], res["big"])
            yield
            S.op("dve", lambda e: e.tensor_scalar(out=nl[0:P, :], in0=nl[0:P, :], scalar1=1.0 / 16, scalar2=None, op0=ALU.mult), R("nl"), R("nl"))
            yield
            p1, pr1 = bank()
            yield
            S.op("pe", lambda e: e.matmul(p1[0:P, 0:256], lhsT=le_f, rhs=nl[0:P, :], start=True, stop=True), R("cst", "nl"), [pr1])
            yield
            p2, pr2 = bank()
            yield
            for j in range(2):
                S.op("pe", lambda e, j=j: e.matmul(p2[:, j * 128:j * 128 + P], lhsT=nl[0:P, j * 128:(j + 1) * 128], rhs=le_f[:, 0:P], start=True, stop=True), R("cst", "nl"), [pr2])
            yield
            for j in range(2):
                act_fn(eT[:, 0, j, 0:P], p2[:, j * 128:j * 128 + P], AF.Exp, [pr2], R("eT"))
                act_fn(eT[:, 1, j, 0:P], p2[:, j * 128:j * 128 + P], AF.Exp, [pr2], R("eT"), scale=-1.0)
            yield
            S.op("dve", lambda e: e.tensor_tensor(out=qtT[:, :, 0:P], in0=gqT[:, :, 0:P], in1=eT[:, 1, :, 0:P], op=ALU.mult), R("gqT", "eT"), R("qtT"))
            yield
            S.op("dve", lambda e: e.tensor_tensor(out=ktT[:, :, 0:P], in0=gkT[:, :, 0:P], in1=eT[:, 0, :, 0:P], op=ALU.mult), R("gkT", "eT"), R("ktT"))
            yield
            act_fn(big[0:P, 256:512], p1[0:P, 0:256], AF.Exp, [pr1], R("big"))
            yield
            S.op("dve", lambda e: e.tensor_tensor(out=kt_tm[0:P, :], in0=gk_tm[0:P, :], in1=big[0:P, 256:512], op=ALU.mult), R("gk_tm", "big"), R("kt_tm"))
            yield
            for h in range(4 if GST >= 2 else 0):
                j, off = h // 2, (h % 2) * 64
                p, pr = bank()
                S.op("pe", lambda e, j=j, off=off: e.matmul(p[0:P, 0:P], lhsT=ktT[off:off + 64, j, 0:P], rhs=qtT[off:off + 64, j, 0:P], start=True, stop=True), R("ktT", "qtT"), [pr])
                S.op("dve", lambda e, h=h: e.tensor_tensor(out=AM[0:P, h, 0:P], in0=p[0:P, 0:P], in1=le_f[:, 0:P], op=ALU.mult), [pr] + R("cst"), R("AM"))
            yield
            po, por = PS[6], PSR[6]
            yield
            if not samp:
                S.op("act", lambda e: e.copy(out=Sb[:], in_=Sst[l][:]), R("Sst%d" % l), R("Sb"))
            else:
                S0t = SA[:, :].rearrange("p (s j v) -> p s j v", j=2, v=128)
                S0b = SBb[:, 0:4096].rearrange("p (s j v) -> p s j v", j=2, v=128)
                qtS = SBb[:, 4096:6144].rearrange("p (j s t) -> p j s t", s=16, t=64)
                qtS2 = SBb[:, 6144:8192].rearrange("p (j s t) -> p j s t", s=16, t=64)
                ktS = SBb[0:64, 8192:12288].rearrange("p (s c) -> p s c", c=256)
                for j in range(2):
                    S.dma("sp", S0t[:, :, j, :], sS_d[l, :, j * 128:(j + 1) * 128, :].rearrange("s p v -> p s v"), writes=R("SA"))
                S.op("act", lambda e: e.copy(out=SBb[:, 0:4096], in_=SA[:, :]), R("SA"), R("SBb"))
                S.op("dve", lambda e: e.tensor_tensor(out=qtS, in0=qtT[:, :, 0:64].unsqueeze(2).to_broadcast([128, 2, 16, 64]),
                                                      in1=cb[:, 320:1344].rearrange("p (s t) -> p s t", t=64).unsqueeze(1).to_broadcast([128, 2, 16, 64]), op=ALU.mult),
                     R("qtT", "cb"), R("SBb"))
                S.op("dve", lambda e: e.tensor_scalar(out=SBb[:, 6144:8192], in0=SBb[:, 4096:6144], scalar1=C("halfm")[:, 1:2], scalar2=None, op0=ALU.mult), R("SBb", "cst"), R("SBb"))
                S.op("dve", lambda e: e.tensor_scalar(out=SBb[:, 4096:6144], in0=SBb[:, 4096:6144], scalar1=C("halfm")[:, 0:1], scalar2=None, op0=ALU.mult), R("SBb", "cst"), R("SBb"))
                S.op("dve", lambda e: e.tensor_tensor(out=ktS, in0=kt_tm[0:64, :].unsqueeze(1).to_broadcast([64, 16, 256]),
                                                      in1=C("segind")[0:64, :].unsqueeze(2).to_broadcast([64, 16, 256]), op=ALU.mult), R("kt_tm", "cst"), R("SBb"))
            yield
            for h in range(4 if GST >= 3 else 0):
                j, off = h // 2, (h % 2) * 64
                og = po[0:P, h * 128:(h + 1) * 128]
                S.op("pe", lambda e, h=h, og=og: e.matmul(og, lhsT=AM[0:P, h, 0:P], rhs=gv_tm[0:P, h * 128:(h + 1) * 128], start=True, stop=False), R("AM", "gv_tm"), [por])
                if not samp:
                    S.op("pe", lambda e, j=j, off=off, og=og: e.matmul(og, lhsT=qtT[off:off + 64, j, 0:P], rhs=Sb[off:off + 64, j, :], start=False, stop=True), R("qtT", "Sb"), [por])
                else:
                    for s_ in range(NSEG):
                        qq = qtS if off == 0 else qtS2
                        S.op("pe", lambda e, j=j, qq=qq, og=og, s_=s_: e.matmul(og, lhsT=qq[:, j, s_, :], rhs=S0b[:, s_, j, :], start=False, stop=(s_ == NSEG - 1)),
                             R("SBb"), [por])
            yield
            evac("act", hm[0:P, :], po[0:P, :], [por], R("hm"))
            yield
            S.op("dve", lambda e: e.tensor_tensor(out=hm2[0:P, 0:512], in0=hm[0:P, :], in1=hm[0:P, :], op=ALU.mult), R("hm"), R("hm2"))
            yield
            S.op("dve", lambda e: e.tensor_reduce(out=st4[0:P, 4:8], in_=hm2[0:P, 0:512].rearrange("p (h v) -> p h v", v=128), axis=AX.X, op=ALU.add), R("hm2"), R("st4"))
            yield
            rstd_from_ssq(st4[0:P, 8:12], st4[0:P, 4:8], 128, R("st4"), R("st4"))
            yield
            S.op("dve", lambda e: e.tensor_tensor(out=hm3, in0=hm3, in1=st4[0:P, 8:12].unsqueeze(2).to_broadcast([P, 4, 128]), op=ALU.mult), R("hm", "st4"), R("hm"))
            yield
            S.op("dve", lambda e: e.tensor_tensor(out=hm[0:P, :], in0=hm[0:P, :], in1=rp("gnorm"), op=ALU.mult), R("hm", "rowp"), R("hm"))
            yield
            act_fn(hm2[0:P, 0:512], gr_tm[0:P, :], AF.Silu, R("gr_tm"), R("hm2"))
            yield
            S.op("dve", lambda e: e.tensor_tensor(out=br[0:P, 512:1024], in0=hm[0:P, :], in1=hm2[0:P, 0:512], op=ALU.mult), R("hm", "hm2"), R("br"))
            yield
            for s_ in range(nseg if GST >= 5 else 0):
                for j in range(2):
                    p, pr = bank()
                    klhs = (ktS[0:64, s_, j * 128:(j + 1) * 128] if samp else kt_tm[0:P, j * 128:(j + 1) * 128])
                    srcres = R("SBb") if samp else R("kt_tm")
                    for half in range(2):
                        S.op("pe", lambda e, half=half, klhs=klhs: e.matmul(p[:, half * 128:(half + 1) * 128], lhsT=klhs, rhs=gv_tm[0:P, (2 * j + half) * 128:(2 * j + half + 1) * 128],
                                                                           start=True, stop=True), srcres + R("gv_tm"), [pr])
                    lastc = s_ * L + L - 1
                    for half in range(2):
                        rows = slice(half * 64, (half + 1) * 64)
                        el = eT[rows, 1, j, lastc:lastc + 1]
                        if samp:
                            sv = S0t[rows, s_, j, :]
                            svr = R("SA")
                        else:
                            sv = Sst[l][rows, j, :]
                            svr = R("Sst%d" % l)
                        if GST >= 6:
                            S.op("dve", lambda e, rows=rows, el=el, sv=sv: e.tensor_scalar(out=St[rows, :], in0=sv, scalar1=el, scalar2=None, op0=ALU.mult), svr + R("eT"), R("St"))
                        if GST >= 7:
                            S.op("dve", lambda e, rows=rows, el=el, sv=sv, half=half: e.scalar_tensor_tensor(out=sv, in0=p[rows, half * 128:(half + 1) * 128], scalar=el, in1=St[rows, :],
                                                                                                          op0=ALU.mult, op1=ALU.add), [pr] + R("eT", "St"), svr)
            yield
            if samp:
                for j in range(2):
                    S.dma("sp", oS_d[l, :, j * 128:(j + 1) * 128, :].rearrange("s p v -> p s v"), S0t[:, :, j, :], reads=R("SA"))
            elif ti == last_prompt and (LASTM & 4):
                S.dma("sp", pS_d[l].rearrange("(j p) v -> p j v", p=128), Sst[l][:], reads=R("Sst%d" % l))

            yield
            for c in range(6):
                if samp:
                    xv = xp[:, c, 0:NSEG * 7].rearrange("p (s j) -> p s j", j=7)
                    xin = [xv[:, :, j:j + 4] for j in range(4)]
                    acc = actT[:, c, 0:64].rearrange("p (s j) -> p s j", j=4)
                else:
                    xin = [xp[:, c, j:j + N] for j in range(4)]
                    acc = actT[:, c, 0:N]
                S.op("dve", lambda e, c=c, xin=xin, acc=acc: e.tensor_scalar(out=acc, in0=xin[0], scalar1=convp[:, c, 0:1], scalar2=convp[:, c, 4:5], op0=ALU.mult, op1=ALU.add),
                     R("xp", "convp"), R("actT"))
                for j in range(1, 4):
                    S.op("dve", lambda e, c=c, j=j, xin=xin, acc=acc: e.scalar_tensor_tensor(out=acc, in0=xin[j], scalar=convp[:, c, j:j + 1], in1=acc, op0=ALU.mult, op1=ALU.add),
                         R("xp", "convp", "actT"), R("actT"))
            yield
            act_fn(actT[:, :, 0:N], actT[:, :, 0:N], AF.Silu, R("actT"), R("actT"))
            yield
            S.op("dve", lambda e: e.tensor_copy(out=BCb[:, :, 0:N], in_=actT[:, 4:6, 0:N]), R("actT"), R("BCb"))
            yield
            for c0, c1 in ((0, 4), (4, 5)):
                p, pr = bank()
                for c in range(c0, c1):
                    S.op("pe", lambda e, c=c: e.transpose(out=p[0:P, (c - c0) * 128:(c - c0 + 1) * 128], in_=actT[:, c, 0:P], identity=ident), R("actT", "cst"), [pr])
                if c0 == 0:
                    evac("act", xs_tm[0:P, :], p[0:P, 0:512], [pr], R("xs_tm"))
                else:
                    evac("dve", Bm_tm[0:P, :], p[0:P, 0:128], [pr], R("Bm_tm"))
            yield
            S.op("dve", lambda e: e.tensor_tensor(out=dts[0:P, 0:8], in0=sdt[0:P, :], in1=rp("dtb"), op=ALU.add), R("sdt", "rowp"), R("dts"))
            yield
            softplus_neg(dts[0:P, 16:24], dts[0:P, 0:8], R("dts"), R("dts"), dts[0:P, 8:16], res["dts"], sign=1.0)
            yield
            S.op("dve", lambda e: e.scalar_tensor_tensor(out=dts[0:P, 24:32], in0=dts[0:P, 16:24], scalar=-1.0, in1=arow[0:P, :], op0=ALU.mult, op1=ALU.mult), R("dts", "arow"), R("dts"))
            yield
            dt_ = dts[0:P, 16:24]
            yield
            dA = dts[0:P, 24:32]
            yield
            p, pr = bank()
            yield
            S.op("pe", lambda e: e.matmul(p[0:P, 0:8], lhsT=le_f, rhs=dA, start=True, stop=True), R("cst", "dts"), [pr])
            yield
            S.op("pe", lambda e: e.matmul(p[0:P, 8:16], lhsT=gt_f, rhs=dA, start=True, stop=True), R("cst", "dts"), [pr])
            yield
            S.op("pe", lambda e: e.matmul(p[:, 16:24], lhsT=ones[0:P, :], rhs=dA, start=True, stop=True), R("cst", "dts"), [pr])
            yield
            S.op("dve", lambda e: e.tensor_copy(out=t8[0:P, 16:24], in_=p[0:P, 0:8]), [pr], R("t8"))
            yield
            lam = t8[0:P, 16:24]
            yield
            act_fn(t8[0:P, 24:32], p[0:P, 0:8], AF.Exp, [pr], R("t8"))
            yield
            act_fn(t8[0:P, 32:40], p[0:P, 8:16], AF.Exp, [pr], R("t8"))
            yield
            S.op("dve", lambda e: e.tensor_tensor(out=t8[0:P, 32:40], in0=t8[0:P, 32:40], in1=dt_, op=ALU.mult), R("t8", "dts"), R("t8"))
            yield
            act_fn(eL[:, :], p[:, 16:24], AF.Exp, [pr], R("eL"))
            yield
            S.op("dve", lambda e: e.tensor_tensor(out=xsw[0:P, :].rearrange("p (h c) -> p h c", c=64), in0=xs_tm[0:P, :].rearrange("p (h c) -> p h c", c=64),
                                                  in1=t8[0:P, 32:40].unsqueeze(2).to_broadcast([P, 8, 64]), op=ALU.mult), R("xs_tm", "t8"), R("xsw"))
            yield
            S.op("dve", lambda e: e.tensor_tensor(out=xsb[0:P, :].rearrange("p (h c) -> p h c", c=64), in0=xs_tm[0:P, :].rearrange("p (h c) -> p h c", c=64),
                                                  in1=dt_.unsqueeze(2).to_broadcast([P, 8, 64]), op=ALU.mult), R("xs_tm", "dts"), R("xsb"))
            yield
            pi0, pi0r = PS[6], PSR[6]
            yield
            pi1, pi1r = PS[5], PSR[5]
            yield
            py, pyr = PS[7], PSR[7]
            yield
            if not samp:
                S.op("act", lambda e: e.copy(out=hTb[:], in_=hT[l][:]), R("hT%d" % l), R("hTb"))
                for g, (pi, pir) in enumerate(((pi0, pi0r), (pi1, pi1r))):
                    S.op("pe", lambda e, g=g, pi=pi: e.matmul(pi[0:P, 0:256], lhsT=BCb[g * 64:(g + 1) * 64, 1, 0:P], rhs=hTb[g * 64:(g + 1) * 64, :], start=True, stop=True),
                         R("BCb", "hTb"), [pir])
            else:
                hTbs = SBb[:, 0:4096].rearrange("p (s c) -> p s c", c=256)
                CmS = SBb[:, 4096:5120].rearrange("p (s t) -> p s t", t=64)
                xswS = SBb[0:64, 5120:9216].rearrange("p (s c) -> p s c", c=512)
                h0t = SA[:, :].rearrange("p (s q c) -> p s q c", q=4, c=128)
                S.op("dve", lambda e: e.tensor_copy(out=big[0:64, 0:512].rearrange("p (h c) -> p h c", c=64), in_=dA.unsqueeze(2).to_broadcast([64, 8, 64])), R("dts"), R("big"))
                pe_, per_ = bank()
                for q in range(4):
                    S.op("pe", lambda e, q=q: e.matmul(pe_[:, q * 16:(q + 1) * 16], lhsT=big[0:64, q * 128:(q + 1) * 128], rhs=C("segind")[0:64, :], start=True, stop=True), R("big", "cst"), [per_])
                act_fn(hm2[:, 0:64], pe_[:, 0:64], AF.Exp, [per_], R("hm2"))
                eLs = hm2[:, 0:64].rearrange("p (q s) -> p q s", s=16)
                S.op("dve", lambda e: e.tensor_tensor(out=CmS, in0=BCb[:, 1, 0:64].unsqueeze(1).to_broadcast([128, 16, 64]),
                                                      in1=cb[:, 320:1344].rearrange("p (s t) -> p s t", t=64), op=ALU.mult), R("BCb", "cb"), R("SBb"))
                for ps_ in range(2):
                    s0 = ps_ * 8
                    S.op("pool", lambda e: e.memset(SA[:, :], 0.0), (), R("SA"))
                    for q in range(4):
                        oq = 0 if q < 2 else 64
                        S.dma("sp", h0t[:, :, q, oq:oq + 64], sh_d[l, s0:s0 + 8, q * 128:(q + 1) * 128, :].rearrange("s r n -> r s n"), writes=R("SA"))
                    for s_ in range(8):
                        p, pr = bank()
                        for q in range(4):
                            S.op("pe", lambda e, q=q, s_=s_: e.transpose(out=p[:, q * 128:(q + 1) * 128], in_=h0t[:, s_, q, :], identity=ident), R("SA", "cst"), [pr])
                        evac("act", hTbs[0:64, s0 + s_, :], p[0:64, 0:256], [pr], R("SBb"))
                        evac("dve", hTbs[64:128, s0 + s_, :], p[64:128, 256:512], [pr], R("SBb"))
                    S.op("dve", lambda e: e.tensor_tensor(out=xswS, in0=xsw[0:64, :].unsqueeze(1).to_broadcast([64, 8, 512]),
                                                          in1=C("segind")[0:64, s0:s0 + 8].unsqueeze(2).to_broadcast([64, 8, 512]), op=ALU.mult), R("xsw", "cst"), R("SBb"))
                    for s_ in range(8):
                        p, pr = bank()
                        for q in range(4):
                            g = q // 2
                            S.op("pe", lambda e, q=q, s_=s_, g=g: e.matmul(p[:, q * 64:(q + 1) * 64], lhsT=xswS[0:64, s_, q * 128:(q + 1) * 128], rhs=Bm_tm[0:64, g * 64:(g + 1) * 64],
                                                                            start=True, stop=True), R("SBb", "Bm_tm"), [pr])
                        for q in range(4):
                            oq = 0 if q < 2 else 64
                            hv = h0t[:, s_, q, oq:oq + 64]
                            S.op("dve", lambda e, q=q, s_=s_, hv=hv: e.scalar_tensor_tensor(out=hv, in0=hv, scalar=eLs[:, q, s0 + s_:s0 + s_ + 1], in1=p[:, q * 64:(q + 1) * 64],
                                                                                          op0=ALU.mult, op1=ALU.add), [pr] + R("SA", "hm2"), R("SA"))
                    for q in range(4):
                        oq = 0 if q < 2 else 64
                        S.dma("sp", oh_d[l, s0:s0 + 8, q * 128:(q + 1) * 128, :].rearrange("s r n -> r s n"), h0t[:, :, q, oq:oq + 64], reads=R("SA"))
                for g, (pi, pir) in enumerate(((pi0, pi0r), (pi1, pi1r))):
                    for s_ in range(NSEG):
                        S.op("pe", lambda e, g=g, pi=pi, s_=s_: e.matmul(pi[0:P, 0:256], lhsT=CmS[g * 64:(g + 1) * 64, s_, :], rhs=hTbs[g * 64:(g + 1) * 64, s_, :],
                                                                         start=(s_ == 0), stop=(s_ == NSEG - 1)), R("SBb"), [pir])
            yield
            Y = big[0:P, 0:8 * P].rearrange("p (h t) -> p h t", t=P)
            yield
            S.op("dve", lambda e: e.tensor_tensor(out=Y, in0=le_f[:, 0:P].unsqueeze(1).to_broadcast([P, 8, P]), in1=dA.unsqueeze(2).to_broadcast([P, 8, P]), op=ALU.mult),
                 R("cst", "dts"), R("big"))
            yield
            for g in range(2):
                pL, pLr = bank()
                S.op("pe", lambda e, g=g: e.matmul(pL[0:P, 0:4 * P], lhsT=ones[0:P, 0:P], rhs=big[0:P, g * 4 * P:(g + 1) * 4 * P], start=True, stop=True), R("cst", "big"), [pLr])
                S.op("dve", lambda e, g=g: e.tensor_tensor(out=big2[0:P, g * 4 * P:(g + 1) * 4 * P].rearrange("p (h t) -> p h t", t=P), in0=pL[0:P, 0:4 * P].rearrange("p (h t) -> p h t", t=P),
                                                           in1=lam[:, 4 * g:4 * g + 4].unsqueeze(2).to_broadcast([P, 4, P]), op=ALU.subtract), [pLr] + R("t8"), R("big2"))
            yield
            S.op("dve", lambda e: e.tensor_scalar(out=big2[0:P, 0:8 * P], in0=big2[0:P, 0:8 * P], scalar1=0.0, scalar2=None, op0=ALU.min), R("big2"), R("big2"))
            yield
            act_fn(big2[0:P, 0:8 * P], big2[0:P, 0:8 * P], AF.Exp, R("big2"), R("big2"))
            yield
            for g in range(2):
                p, pr = bank()
                S.op("pe", lambda e, g=g: e.matmul(p[0:P, 0:P], lhsT=BCb[g * 64:(g + 1) * 64, 0, 0:P], rhs=BCb[g * 64:(g + 1) * 64, 1, 0:P], start=True, stop=True), R("BCb"), [pr])
                S.op("dve", lambda e, g=g: e.tensor_tensor(out=cbm[0:P, g, 0:P], in0=p[0:P, 0:P], in1=le_f[:, 0:P], op=ALU.mult), [pr] + R("cst"), R("cbm"))
            yield
            S.op("dve", lambda e: e.tensor_tensor(out=bigb[0:P, 0:8 * P].rearrange("p (g h t) -> p g h t", g=2, t=P), in0=big2[0:P, 0:8 * P].rearrange("p (g h t) -> p g h t", g=2, t=P),
                                                  in1=cbm[0:P, :, 0:P].unsqueeze(2).to_broadcast([P, 2, 4, P]), op=ALU.mult), R("big2", "cbm"), R("bigb"))
            yield
            for h in range(8):
                S.op("pe", lambda e, h=h: e.matmul(py[0:P, h * 64:(h + 1) * 64], lhsT=bigb[0:P, h * P:(h + 1) * P], rhs=xsb[0:P, h * 64:(h + 1) * 64], start=True, stop=True), R("bigb", "xsb"), [pyr])
            yield
            for g, (pi, pir) in enumerate(((pi0, pi0r), (pi1, pi1r))):
                S.op("dve", lambda e, g=g, pi=pi: e.tensor_tensor(out=hm[0:P, g * 256:(g + 1) * 256].rearrange("p (h c) -> p h c", c=64), in0=pi[0:P, 0:256].rearrange("p (h c) -> p h c", c=64),
                                                                  in1=t8[0:P, 24 + 4 * g:28 + 4 * g].unsqueeze(2).to_broadcast([P, 4, 64]), op=ALU.mult), [pir] + R("t8"), R("hm"))
            yield
            S.op("dve", lambda e: e.tensor_tensor(out=hm[0:P, :], in0=py[0:P, 0:512], in1=hm[0:P, :], op=ALU.add), [pyr] + R("hm"), R("hm"))
            yield
            S.op("dve", lambda e: e.tensor_tensor(out=hm2[0:P, 0:512].rearrange("p (h c) -> p h c", c=64), in0=xs_tm[0:P, :].rearrange("p (h c) -> p h c", c=64),
                                                  in1=rp("sD").unsqueeze(2).to_broadcast([P, 8, 64]), op=ALU.mult), R("xs_tm", "rowp"), R("hm2"))
            yield
            S.op("dve", lambda e: e.tensor_tensor(out=hm[0:P, :], in0=hm[0:P, :], in1=hm2[0:P, 0:512], op=ALU.add), R("hm", "hm2"), R("hm"))
            yield
            act_fn(hm2[0:P, 0:512], sz_tm[0:P, :], AF.Silu, R("sz_tm"), R("hm2"))
            yield
            S.op("dve", lambda e: e.tensor_tensor(out=hm[0:P, :], in0=hm[0:P, :], in1=hm2[0:P, 0:512], op=ALU.mult), R("hm", "hm2"), R("hm"))
            yield
            S.op("dve", lambda e: e.tensor_tensor(out=hm2[0:P, 0:512], in0=hm[0:P, :], in1=hm[0:P, :], op=ALU.mult), R("hm"), R("hm2"))
            yield
            S.op("dve", lambda e: e.tensor_reduce(out=st4[0:P, 4:6], in_=hm2[0:P, 0:512].rearrange("p (g c) -> p g c", c=256), axis=AX.X, op=ALU.add), R("hm2"), R("st4"))
            yield
            rstd_from_ssq(st4[0:P, 8:10], st4[0:P, 4:6], 256, R("st4"), R("st4"))
            yield
            S.op("dve", lambda e: e.tensor_tensor(out=hm[0:P, :].rearrange("p (g c) -> p g c", c=256), in0=hm[0:P, :].rearrange("p (g c) -> p g c", c=256),
                                                  in1=st4[0:P, 8:10].unsqueeze(2).to_broadcast([P, 2, 256]), op=ALU.mult), R("hm", "st4"), R("hm"))
            yield
            S.op("dve", lambda e: e.tensor_tensor(out=br[0:P, 1024:1536], in0=hm[0:P, :], in1=rp("snorm"), op=ALU.mult), R("hm", "rowp"), R("br"))
            yield
            if not samp:
                p, pr = bank()
                for g in range(2):
                    S.op("pe", lambda e, g=g: e.matmul(p[:, g * 256:(g + 1) * 256], lhsT=Bm_tm[0:P, :], rhs=xsw[0:P, g * 256:(g + 1) * 256], start=True, stop=True), R("Bm_tm", "xsw"), [pr])
                for g in range(2):
                    rows = slice(g * 64, (g + 1) * 64)
                    S.op("dve", lambda e, g=g, rows=rows: e.tensor_tensor(out=big[rows, 0:256].rearrange("p (h c) -> p h c", c=64), in0=hT[l][rows, :].rearrange("p (h c) -> p h c", c=64),
                                                                          in1=eL[rows, 4 * g:4 * g + 4].unsqueeze(2).to_broadcast([64, 4, 64]), op=ALU.mult), R("hT%d" % l, "eL"), R("big"))
                    S.op("dve", lambda e, g=g, rows=rows: e.tensor_tensor(out=hT[l][rows, :], in0=p[rows, g * 256:(g + 1) * 256], in1=big[rows, 0:256], op=ALU.add), [pr] + R("big"), R("hT%d" % l))
                if ti == last_prompt and (LASTM & 8):
                    p, pr = bank()
                    for half in range(2):
                        S.op("pe", lambda e, half=half: e.transpose(out=p[:, half * 128:(half + 1) * 128], in_=hT[l][:, half * 128:(half + 1) * 128], identity=ident),
                             R("hT%d" % l, "cst"), [pr])
                    evac("act", hm2[:, 0:256], p[:, 0:256], [pr], R("hm2"))
                    phv = ph_d[l].rearrange("(g f r) n -> f r g n", g=2, f=2, r=128)
                    for half in range(2):
                        S.dma("sp", phv[half], hm2[:, half * 128:(half + 1) * 128].rearrange("p (g n) -> p g n", n=64), reads=R("hm2"))

            yield
            dbg_out("br", br[0:P, :], R("br"))
            yield
            transpose_to(lambda c0, c1: brT[:, c0:c1, 0:P], br, res["br"], P, 12, res["brT"], eng="dve")
            yield
            mixacc = big[:, :].rearrange("p (c t) -> p c t", t=128)
            yield
            gs8 = big2[:, :].rearrange("p (c t) -> p c t", t=128)
            yield
            for n in range(3):
                for gh in range(2):
                    c0 = 4896 + n * 1024 + gh * 512
                    wg, wgr = load_w(W(c0, c0 + 512), 512)
                    for dq in range(4):
                        dc = gh * 4 + dq
                        pg, pgr = fm_piece(wg, wgr, dq * 128, (dq + 1) * 128, N)
                        act_fn(gs8[:, dc, 0:N], pg[:, 0:N], AF.Sigmoid, [pgr], R("big2"))
                wbn, wbnr = load_w2(wbrs[l, n * 512:(n + 1) * 512, :], 4, 1024)
                for dc in range(8):
                    pa, par = bank()
                    for k in range(4):
                        S.op("pe", lambda e, k=k, dc=dc: e.matmul(pa[:, 0:N], lhsT=wbn[:, k, dc * 128:(dc + 1) * 128], rhs=brT[:, n * 4 + k, 0:N], start=(k == 0), stop=(k == 3)),
                             [wbnr] + R("brT"), [par])
                    if n == 0:
                        S.op("dve", lambda e, dc=dc: e.tensor_tensor(out=mixacc[:, dc, 0:N], in0=pa[:, 0:N], in1=gs8[:, dc, 0:N], op=ALU.mult), [par] + R("big2"), R("big"))
                    else:
                        S.op("dve", lambda e, dc=dc: e.tensor_tensor(out=hm2[:, 0:N], in0=pa[:, 0:N], in1=gs8[:, dc, 0:N], op=ALU.mult), [par] + R("big2"), R("hm2"))
                        S.op("dve", lambda e, dc=dc: e.tensor_tensor(out=mixacc[:, dc, 0:N], in0=mixacc[:, dc, 0:N], in1=hm2[:, 0:N], op=ALU.add), R("big", "hm2"), R("big"))
            yield
            S.op("act", lambda e: e.copy(out=mixT[:, :, 0:N], in_=mixacc[:, :, 0:N]), R("big"), R("mixT"))
            yield
            for gh in range(2):
                wo, wor = load_w(wouts[l, :, gh * 512:(gh + 1) * 512], 512)
                p, pr = tm_piece(wo, wor, 0, 512, P, src=mixT, srcres=res["mixT"])
                S.op("dve", lambda e, gh=gh: e.scalar_tensor_tensor(out=X[0:P, gh * 512:(gh + 1) * 512], in0=X[0:P, gh * 512:(gh + 1) * 512], scalar=DN_ALPHA, in1=p[0:P, 0:512],
                                                                    op0=ALU.mult, op1=ALU.add), [pr] + R(XN), R(XN))
            yield
            pass
            yield
            layernorm_tm(P, X, res[XN], lnp_d[l, 0], X[0:P, :], res[XN], lnt, "lnt", big, res["big"], lnrow[:, 0:1024], "lnrowF")
            yield
            dbg_out("h1", X[0:P, :], R(XN))

            yield
            if do_peer:
                (fence() if samp else None)
                transpose_to(lambda c0, c1: xT[:, c0:c1, 0:P], X, res[XN], P, 8, res["xT"])
                for gq_ in range(4):
                    wq, wqr = load_w(wqs[l, :, gq_ * 512:(gq_ + 1) * 512], 512)
                    for c in range(4):
                        p, pr = fm_piece(wq, wqr, c * 128, (c + 1) * 128, N)
                        evac("act", qTb[:, gq_ * 4 + c, 0:N], p[:, 0:N], [pr], R("qTb"))
                i = wsn[0]
                wsn[0] = (i + 1) % NWP
                kb = wpool[i][:, 0:2048]
                S.dma("sp", kb, keyss[l], reads=R("scratch"), writes=[wpres[i]])
                kbr = wpres[i]
                for c0 in range(0, 16, 4):
                    p, pr = bank()
                    for c in range(c0, c0 + 4):
                        S.op("pe", lambda e, c=c: e.matmul(p[0:P, (c - c0) * 128:(c - c0 + 1) * 128], lhsT=qTb[:, c, 0:P], rhs=kb[:, c * 128:(c + 1) * 128], start=True, stop=True),
                             R("qTb") + [kbr], [pr])
                    evac("dve", SC[0:P, c0 * 128:(c0 + 4) * 128], p[0:P, 0:512], [pr], R("SC"))

                def top16(src, width, vout, iout):
                    wv = wk[0:P, 0:width]
                    S.op("dve", lambda e: e.max(out=vout[:, 0:8], in_=src), R("SC"), R("V16"))
                    S.op("dve", lambda e: e.max_index(out=iout[:, 0:8], in_max=vout[:, 0:8], in_values=src), R("SC", "V16"), R("I16"))
                    S.op("dve", lambda e: e.match_replace(out=wv, in_to_replace=vout[:, 0:8], in_values=src, imm_value=-1e30), R("SC", "V16"), R("wk"))
                    S.op("dve", lambda e: e.max(out=vout[:, 8:16], in_=wv), R("wk"), R("V16"))
                    S.op("dve", lambda e: e.max_index(out=iout[:, 8:16], in_max=vout[:, 8:16], in_values=wv), R("wk", "V16"), R("I16"))

                for c in range(16):
                    top16(SC[0:P, c * 128:(c + 1) * 128], 128, V16[0:P, c, :], I16[0:P, c, :])
                S.op("dve", lambda e: e.tensor_copy(out=I16f[0:P], in_=I16[0:P]), R("I16"), R("I16f"))
                V16v = V16[0:P].rearrange("p (h j) a -> p h j a", j=2)
                I16v = I16f[0:P].rearrange("p (h j) a -> p h j a", j=2)
                cand = SC[0:P, :].rearrange("p (h a b) -> p h a b", a=16, b=16)
                S.op("dve", lambda e: e.tensor_tensor(out=cand, in0=V16v[:, :, 0, :].unsqueeze(3).to_broadcast([P, 8, 16, 16]),
                                                      in1=V16v[:, :, 1, :].unsqueeze(2).to_broadcast([P, 8, 16, 16]), op=ALU.add), R("V16"), R("SC"))
                for h in range(8):
                    top16(SC[0:P, h * 256:(h + 1) * 256], 256, TV[0:P, h, :], TPu[0:P, h, :])
                TPf, af, bf_, i0f, i1f, gg, dots, actg = [pk[0:P, q, :].rearrange("p (h k) -> p h k", k=16) for q in range(8)]
                S.op("dve", lambda e: e.tensor_copy(out=TPf, in_=TPu[0:P]), R("I16", "V16"), R(PKN))
                bv4 = big[0:P, :].rearrange("p (h a b) -> p h a b", a=16, b=16)
                for hh in range(2):
                    hs_ = slice(4 * hh, 4 * hh + 4)
                    S.op("dve", lambda e, hs_=hs_: e.tensor_tensor(out=bv4, in0=TPf[:, hs_, :].unsqueeze(3).to_broadcast([P, 4, 16, 16]),
                                                                   in1=thr16[0:P, :].unsqueeze(1).unsqueeze(1).to_broadcast([P, 4, 16, 16]), op=ALU.is_ge), R(PKN, "thr16"), R("big"))
                    S.op("dve", lambda e, hs_=hs_: e.tensor_reduce(out=af[:, hs_, :], in_=bv4, axis=AX.X, op=ALU.add), R("big"), R(PKN))
                S.op("dve", lambda e: e.scalar_tensor_tensor(out=pk[0:P, 2, :], in0=pk[0:P, 1, :], scalar=-16.0, in1=pk[0:P, 0, :], op0=ALU.mult, op1=ALU.add), R(PKN), R(PKN))
                for (src_, jj, dst_) in ((af, 0, i0f), (bf_, 1, i1f)):
                    for hh in range(2):
                        hs_ = slice(4 * hh, 4 * hh + 4)
                        S.op("dve", lambda e, hs_=hs_, src_=src_: e.tensor_tensor(out=bv4, in0=src_[:, hs_, :].unsqueeze(3).to_broadcast([P, 4, 16, 16]),
                                                                                  in1=C("iota16")[0:P, :].unsqueeze(1).unsqueeze(1).to_broadcast([P, 4, 16, 16]), op=ALU.is_equal),
                             R(PKN, "cst"), R("big"))
                        S.op("dve", lambda e, hs_=hs_, jj=jj: e.tensor_tensor(out=bv4, in0=bv4, in1=I16v[:, hs_, jj, :].unsqueeze(2).to_broadcast([P, 4, 16, 16]), op=ALU.mult),
                             R("big", "I16f"), R("big"))
                        S.op("dve", lambda e, hs_=hs_, dst_=dst_: e.tensor_reduce(out=dst_[:, hs_, :], in_=bv4, axis=AX.X, op=ALU.add), R("big"), R(PKN))
                S.op("dve", lambda e: e.scalar_tensor_tensor(out=pk[0:P, 0, :], in0=pk[0:P, 3, :], scalar=128.0, in1=pk[0:P, 4, :], op0=ALU.mult, op1=ALU.add), R(PKN), R(PKN))
                S.op("dve", lambda e: e.tensor_copy(out=eidx[0:P, :], in_=pk[0:P, 0, :]), R(PKN), R(EIN))
                S.op("dve", lambda e: e.tensor_tensor(out=gg, in0=TV[0:P], in1=TV[0:P, :, 0:1].to_broadcast([P, 8, 16]), op=ALU.subtract), R("V16"), R(PKN))
                act_fn(pk[0:P, 5, :], pk[0:P, 5, :], AF.Exp, R(PKN), R(PKN))
                S.op("dve", lambda e: e.tensor_reduce(out=st4[0:P, 0:8], in_=gg, axis=AX.X, op=ALU.add), R(PKN), R("st4"))
                S.op("dve", lambda e: e.reciprocal(out=st4[0:P, 0:8], in_=st4[0:P, 0:8]), R("st4"), R("st4"))
                S.op("dve", lambda e: e.tensor_tensor(out=gg, in0=gg, in1=st4[0:P, 0:8].unsqueeze(2).to_broadcast([P, 8, 16]), op=ALU.mult), R(PKN, "st4"), R(PKN))
                GUr = [res["GU%d" % q] for q in range(NGU)]
                gn = [0]

                def gather(tab, slot):
                    q = gn[0] % NGU
                    gn[0] += 1
                    S.dma("pool", None, None, reads=R(EIN, "scratch"), writes=[GUr[q]],
                          fn=lambda e, q=q: e.indirect_dma_start(out=GU[q][0:P, :], out_offset=None, in_=tab,
                                                                 in_offset=bass.IndirectOffsetOnAxis(ap=eidx[0:P, slot:slot + 1], axis=0)))
                    return GU[q], GUr[q]

                yield "SPLIT"
                S.op("dve", lambda e: e.tensor_copy(out=xb16[0:P, :], in_=X[0:P, :]), R(XN), R("xb16"))
                pvb = [(PS[3], PSR[3]), (PS[4], PSR[4])]
                slres = [Res("sl%d" % q) for q in range(128)]
                pend = []
                for slot in range(128):
                    if slot % 4 == 0:
                        yield
                    gu, gur = gather(uvs[l][:, :], slot)
                    pb, pbr = prodb[slot % 2], res["prodb%d" % (slot % 2)]
                    dg, dgr = dgb[slot % 4], res["dgb%d" % (slot % 4)]
                    sr = slres[slot]
                    S.op("dve", lambda e, gu=gu, pb=pb: e.tensor_tensor(out=pb[0:P, :], in0=gu[0:P, 0:1024], in1=xb16[0:P, :], op=ALU.mult), [gur] + R("xb16"), [pbr])
                    S.op("act", lambda e, pb=pb, slot=slot: e.activation(out=pb[0:P, :], in_=pb[0:P, :], func=AF.Copy, accum_out=pk[0:P, 6, slot:slot + 1]), [pbr], [pbr, sr])
                    S.op("act", lambda e, slot=slot: e.activation(out=pk[0:P, 7, slot:slot + 1], in_=pk[0:P, 6, slot:slot + 1], func=AF.Gelu), [sr], [sr])
                    pend.append((slot, gu, gur, dg, dgr, sr))
                    while pend and (len(pend) > SKEW or slot == 127):
                        slot2, gu2, gur2, dg2, dgr2, sr2 = pend.pop(0)
                        S.op("dve", lambda e, dg2=dg2, slot2=slot2: e.tensor_scalar(out=dg2[0:P, 0:P], in0=cb[0:P, 128:128 + P], scalar1=pk[0:P, 7, slot2:slot2 + 1], scalar2=pk[0:P, 5, slot2:slot2 + 1],
                                                                                  op0=ALU.mult, op1=ALU.mult), R("cb", PKN) + [sr2], [dgr2])
                        for half in range(2):
                            S.op("pe", lambda e, dg2=dg2, gu2=gu2, half=half, slot2=slot2: e.matmul(pvb[half][0][0:P, 0:512], lhsT=dg2[0:P, 0:P], rhs=gu2[0:P, 1024 + half * 512:1024 + (half + 1) * 512],
                                                                                                 start=(slot2 == 0), stop=(slot2 == 127)), [dgr2, gur2], [pvb[half][1]])
                for half in range(2):
                    S.op("dve", lambda e, half=half: e.scalar_tensor_tensor(out=X[0:P, half * 512:(half + 1) * 512], in0=X[0:P, half * 512:(half + 1) * 512], scalar=DN_ALPHA,
                                                                           in1=pvb[half][0][0:P, 0:512], op0=ALU.mult, op1=ALU.add), R(XN) + [pvb[half][1]], R(XN))
                pass
                layernorm_tm(P, X, res[XN], lnp_d[l, 1], X[0:P, :], res[XN], lntB, "lntB", SBf[:, 0:1024], res["GU0"], lnrow[:, 1024:2048], "lnrowB")
                (fence() if samp else None)
            if not do_peer:
                yield "SPLIT"
            if l == nlayers - 1:
                S.dma("sp", y_d[t0:t0 + P, :], X[0:P, :], reads=R(XN))
            dbg_i[0] += 1

        def drain(g):
            if g is not None:
                for _ in g:
                    pass

        prompt_tiles = [t for t in all_tiles if t[0] == "p"]
        samp_tiles = [t for t in all_tiles if t[0] == "s"]
        seq = []
        for a in range(0, len(prompt_tiles), 2):
            pair = prompt_tiles[a:a + 2]
            for l in range(nlayers):
                for j, t in enumerate(pair):
                    seq.append((t[0], t[1], l, j))
        prev = None
        for i, (kind, ti, l, par) in enumerate(seq):
            g = tile_layer(kind, ti, l, par, i % 2)
            cnt = 0
            for tok in g:
                if tok == "SPLIT":
                    break
                cnt += 1
                if prev is not None and cnt % FB_RATIO == 0:
                    if next(prev, "END") == "END":
                        prev = None
            drain(prev)
            prev = g
        drain(prev)
        for kind, ti in samp_tiles:
            for l in range(nlayers):
                drain(tile_layer(kind, ti, l, 0, 0))
        S.finish()
    return nc


def make_consts():
    c = np.zeros((128, COW), np.float32)
    def put(n, a):
        c[:a.shape[0], CO[n][0]:CO[n][1]] = a
    s = np.arange(128)
    put("ident", np.eye(128, dtype=np.float32))
    put("leP", (s[:, None] <= s[None, :]).astype(np.float32))
    put("gtP", (s[:, None] > s[None, :]).astype(np.float32))
    q = np.arange(64)
    same = (q[:, None] // 4) == (q[None, :] // 4)
    put("leS", (same & (q[:, None] <= q[None, :])).astype(np.float32))
    put("gtS", (same & (q[:, None] > q[None, :])).astype(np.float32))
    put("ones", np.ones((128, 128), np.float32))
    put("segind", ((q[:, None] // 4) == np.arange(16)[None, :]).astype(np.float32))
    put("iota16", np.broadcast_to(np.arange(16, dtype=np.float32), (128, 16)))
    segm = ((np.arange(64)[None, :] // 4) == np.arange(16)[:, None]).astype(np.float32).reshape(1, 1024)
    put("halfm", np.stack([(s < 64), (s >= 64)], 1).astype(np.float32))
    return c, np.ascontiguousarray(np.broadcast_to(segm, (128, 1024))).astype(np.float32)

def prep(inp, core, nexp=16384):
    f = lambda a: np.ascontiguousarray(np.asarray(a, dtype=np.float32))
    x = np.concatenate([f(inp["x_prompt"][core]), f(inp["x_sample"][core * 16:(core + 1) * 16]).reshape(64, 1024)], 0)
    sl = slice(core * 16, (core + 1) * 16)
    d = dict(x=x)
    d["sC"] = f(inp["state_mlstm_C"][:, sl]); d["sn"] = f(inp["state_mlstm_n"][:, sl]); d["sm"] = f(inp["state_mlstm_m"][:, sl])
    d["sS"] = f(inp["state_gla_S"][:, sl]).reshape(2, 16, 256, 128)
    d["sh"] = f(inp["state_ssm_h"][:, sl]).reshape(2, 16, 512, 64)
    d["scv"] = f(inp["state_conv"][:, sl]).reshape(2, 48, 768)
    d["w_in"] = f(inp["w_in"]); d["w_br"] = f(inp["w_branch"]).reshape(2, 1536, 1024); d["w_out"] = f(inp["w_out"])
    d["p_wq"] = f(inp["p_wq"])
    d["keysT"] = f(np.transpose(np.asarray(inp["p_keys"]).reshape(2, 16, 128, 128), (0, 3, 1, 2)))
    for l in range(2):
        d["p_u%d" % l] = f(inp["p_u"][l][:nexp]); d["p_v%d" % l] = f(inp["p_v"][l][:nexp])
    rows = []
    for l in range(2):
        r = np.concatenate([f(inp[k][l]).reshape(-1) for k in ["m_i_bias", "m_f_bias", "m_norm", "g_a_bias", "g_norm", "s_dt_bias",
                                                              "s_A_log", "s_D", "s_norm"]])
        rows.append(np.broadcast_to(r, (128, r.size)))
    d["rowp"] = f(np.stack(rows))
    lnp = np.stack([np.stack([np.concatenate([f(inp["ln1_g"][l]), f(inp["ln1_b"][l])]), np.concatenate([f(inp["ln2_g"][l]), f(inp["ln2_b"][l])])]) for l in range(2)])
    d["lnp"] = f(np.broadcast_to(lnp[:, :, None, :], (2, 2, 128, 2048)))
    cw = f(inp["s_conv_w"]); cbb = f(inp["s_conv_b"])
    cp = np.concatenate([cw, cbb[:, None, :]], 1)
    d["convp"] = f(np.transpose(cp.reshape(2, 5, 6, 128), (0, 3, 2, 1)))
    d["gaup"] = f(inp["g_a_up"])
    d["cst"], d["segm"] = make_consts()
    return d


NEXP_USED = 16384


def kernel(**inp):
    nc = build(nexp=NEXP_USED)
    maps = [prep(inp, c, nexp=NEXP_USED) for c in range(NCORES)]
    r = run_bass_kernel_spmd(nc, maps, core_ids=list(range(NCORES))).results
    f = np.float32
    yp = np.stack([r[c]["y"][0:SEQ] for c in range(NCORES)], 0).astype(f)
    ys = np.concatenate([r[c]["y"][SEQ:].reshape(NSEG, LS, D) for c in range(NCORES)], 0).astype(f)
    st = lambda k, shp: np.stack([r[c][k].reshape(shp) for c in range(NCORES)], 1).astype(f)
    ct = lambda k, shp: np.concatenate([r[c][k].reshape(shp) for c in range(NCORES)], 1).astype(f)
    return (yp, ys,
            st("pC", (2, 4, 128, 128)), st("pn", (2, 4, 128)), st("pm", (2, 4)),
            st("pS", (2, 4, 64, 128)), st("ph", (2, 8, 64, 64)), st("pcv", (2, 3, 768)),
            ct("oC", (2, 16, 4, 128, 128)), ct("on", (2, 16, 4, 128)), ct("om", (2, 16, 4)),
            ct("oS", (2, 16, 4, 64, 128)), ct("oh", (2, 16, 8, 64, 64)), ct("ocv", (2, 16, 3, 768)))
```
